# Optimizing a Trainium2 kernel written in Bass

```python
import math
import jax, jax.numpy as jnp
from jax import lax
import numpy as np

D_MODEL = 1024
BATCH = 2
SEQ = 8192
DEPTH = 2

M_MLSTM = 512
MLSTM_HEADS = 4
MLSTM_HEAD_DIM = M_MLSTM // MLSTM_HEADS
CHUNK = 128
CONV_WIDTH = 5
FORGET_BIAS_LO = 3.0
FORGET_BIAS_HI = 6.0
M_FOURIER = 256
FOURIER_GROUPS = 4
FOURIER_GROUP_DIM = M_FOURIER // FOURIER_GROUPS
M_S5 = 256
S5_GROUP = 16
S5_GROUPS = M_S5 // S5_GROUP
S5_STATE = 64
DT_MIN = 1e-3
DT_MAX = 1e-1
N_BRANCHES = 3
D_FF = 4 * D_MODEL
EPS = 1e-6

OFF_Q = 0
OFF_K = OFF_Q + M_MLSTM
OFF_V = OFF_K + M_MLSTM
OFF_O = OFF_V + M_MLSTM
OFF_IG = OFF_O + M_MLSTM
OFF_FG = OFF_IG + 2 * MLSTM_HEADS
OFF_FOURIER = OFF_FG + 2 * MLSTM_HEADS
OFF_S5 = OFF_FOURIER + M_FOURIER
OFF_GATE = OFF_S5 + M_S5
N_IN = OFF_GATE + N_BRANCHES * D_MODEL

kernel_name = 'hybrid_mlstm_fnet_s5_encoder'

F32 = jnp.float32


def rmsnorm(x, g):
    xf = x.astype(F32)
    y = xf * lax.rsqrt(jnp.mean(xf * xf, axis=-1, keepdims=True) + EPS)
    return y * g.astype(F32)


def centred_dwconv(x, w, b):
    xf = x.astype(F32)
    y = lax.conv_general_dilated(xf, w.astype(F32)[:, None, :], window_strides=(1,), padding='SAME',
                                 dimension_numbers=('NWC', 'WIO', 'NWC'), feature_group_count=xf.shape[-1])
    return y + b.astype(F32)


def mlstm_direction(q, k, v, ig, fg):
    bsz, s, nh, dh = q.shape
    nc = s // CHUNK

    def chunk(t):
        return t.reshape(bsz, nc, CHUNK, nh, -1).transpose(0, 3, 1, 2, 4)

    def chunk_g(t):
        return t.reshape(bsz, nc, CHUNK, nh).transpose(0, 3, 1, 2)

    q = chunk(q) * (dh ** -0.5)
    k = chunk(k)
    v = chunk(v)
    ig = chunk_g(ig)
    logf = jax.nn.log_sigmoid(chunk_g(fg))
    b = jnp.cumsum(logf, axis=-1)
    g = b[..., -1]
    a = g[..., None] - b + ig
    a_max = jnp.max(a, axis=-1)
    w = jnp.exp(a - a_max[..., None])
    c_loc = jnp.einsum('bhcl,bhcle,bhclk->bhcek', w, v, k)
    n_loc = jnp.einsum('bhcl,bhclk->bhck', w, k)

    def step(carry, inp):
        c_st, n_st, m_st = carry
        cl, nl, am, gc = inp
        m_new = jnp.maximum(gc + m_st, am)
        s_old = jnp.exp(gc + m_st - m_new)
        s_new = jnp.exp(am - m_new)
        c_new = s_old[..., None, None] * c_st + s_new[..., None, None] * cl
        n_new = s_old[..., None] * n_st + s_new[..., None] * nl
        return (c_new, n_new, m_new), (c_st, n_st, m_st)

    init = (jnp.zeros((bsz, nh, dh, dh), F32), jnp.zeros((bsz, nh, dh), F32), jnp.zeros((bsz, nh), F32))
    xs = (jnp.moveaxis(c_loc, 2, 0), jnp.moveaxis(n_loc, 2, 0), jnp.moveaxis(a_max, 2, 0), jnp.moveaxis(g, 2, 0))
    _, (c_prev, n_prev, m_prev) = lax.scan(step, init, xs)
    c_prev = jnp.moveaxis(c_prev, 0, 2)
    n_prev = jnp.moveaxis(n_prev, 0, 2)
    m_prev = jnp.moveaxis(m_prev, 0, 2)

    mask = jnp.tril(jnp.ones((CHUNK, CHUNK), dtype=bool))
    d_log = jnp.where(mask, b[..., :, None] - b[..., None, :] + ig[..., None, :], -jnp.inf)
    inter_log = b + m_prev[..., None]
    m_t = jnp.maximum(inter_log, jnp.max(d_log, axis=-1))
    scores = jnp.einsum('bhcld,bhcjd->bhclj', q, k) * jnp.exp(d_log - m_t[..., None])
    inter_w = jnp.exp(inter_log - m_t)
    num = jnp.einsum('bhclj,bhcjd->bhcld', scores, v) + inter_w[..., None] * jnp.einsum('bhcek,bhclk->bhcle', c_prev, q)
    den = jnp.sum(scores, axis=-1) + inter_w * jnp.einsum('bhclk,bhck->bhcl', q, n_prev)
    h = num / jnp.maximum(jnp.abs(den), jnp.exp(-m_t))[..., None]
    return h.transpose(0, 2, 3, 1, 4).reshape(bsz, s, nh, dh)


def mlstm_branch(z_qk, z_v, z_o, z_ig, z_fg, conv_w, conv_b, norm_g):
    bsz, s, _ = z_v.shape
    nh, dh = MLSTM_HEADS, MLSTM_HEAD_DIM
    qk = jax.nn.silu(centred_dwconv(z_qk, conv_w, conv_b))
    q = qk[..., :M_MLSTM].reshape(bsz, s, nh, dh)
    k = qk[..., M_MLSTM:].reshape(bsz, s, nh, dh)
    v = z_v.astype(F32).reshape(bsz, s, nh, dh)
    ig = z_ig.astype(F32)
    fg = z_fg.astype(F32)
    flip = lambda t: jnp.flip(t, axis=1)
    h_fwd = mlstm_direction(q, k, v, ig[..., :nh], fg[..., :nh])
    h_bwd = flip(mlstm_direction(flip(q), flip(k), flip(v), flip(ig[..., nh:]), flip(fg[..., nh:])))
    h = h_fwd + h_bwd
    mu = jnp.mean(h, axis=-1, keepdims=True)
    var = jnp.mean(jnp.square(h - mu), axis=-1, keepdims=True)
    h = ((h - mu) * lax.rsqrt(var + EPS)).reshape(bsz, s, M_MLSTM) * norm_g.astype(F32)
    return h * jax.nn.sigmoid(z_o.astype(F32))


def fourier_branch(z_f):
    bsz, s, _ = z_f.shape
    u = z_f.astype(F32).reshape(bsz, s, FOURIER_GROUPS, FOURIER_GROUP_DIM)
    y = jnp.real(jnp.fft.fft2(u, axes=(1, 3), norm='ortho'))
    return y.reshape(bsz, s, M_FOURIER)


def s5_direction(u, lam_re, lam_im, log_dt, b_re, b_im, c_re, c_im, reverse):
    lam_re = lam_re.astype(F32)
    lam_im = lam_im.astype(F32)
    dt = jnp.exp(log_dt.astype(F32))[:, None]
    mag = jnp.exp(lam_re * dt)
    lb_re = mag * jnp.cos(lam_im * dt)
    lb_im = mag * jnp.sin(lam_im * dt)
    den = lam_re * lam_re + lam_im * lam_im
    f_re = ((lb_re - 1.0) * lam_re + lb_im * lam_im) / den
    f_im = (lb_im * lam_re - (lb_re - 1.0) * lam_im) / den
    b_re = b_re.astype(F32)
    b_im = b_im.astype(F32)
    bb_re = f_re[..., None] * b_re - f_im[..., None] * b_im
    bb_im = f_re[..., None] * b_im + f_im[..., None] * b_re
    bu_re = jnp.einsum('gpc,bsgc->bsgp', bb_re, u)
    bu_im = jnp.einsum('gpc,bsgc->bsgp', bb_im, u)
    a_re = jnp.broadcast_to(lb_re, bu_re.shape)
    a_im = jnp.broadcast_to(lb_im, bu_im.shape)

    def combine(e1, e2):
        a1r, a1i, x1r, x1i = e1
        a2r, a2i, x2r, x2i = e2
        return (a1r * a2r - a1i * a2i, a1r * a2i + a1i * a2r,
                a2r * x1r - a2i * x1i + x2r, a2r * x1i + a2i * x1r + x2i)

    _, _, x_re, x_im = lax.associative_scan(combine, (a_re, a_im, bu_re, bu_im), axis=1, reverse=reverse)
    return jnp.einsum('gcp,bsgp->bsgc', c_re.astype(F32), x_re) - jnp.einsum('gcp,bsgp->bsgc', c_im.astype(F32), x_im)


def s5_branch(z_s, lam_re, lam_im, log_dt, b_re, b_im, c_re, c_im, d_skip, w_glu, b_glu):
    bsz, s, _ = z_s.shape
    u = z_s.astype(F32).reshape(bsz, s, S5_GROUPS, S5_GROUP)
    y = (d_skip.astype(F32) * u
         + s5_direction(u, lam_re[0], lam_im[0], log_dt[0], b_re[0], b_im[0], c_re[0], c_im[0], False)
         + s5_direction(u, lam_re[1], lam_im[1], log_dt[1], b_re[1], b_im[1], c_re[1], c_im[1], True))
    y = jax.nn.gelu(y).reshape(bsz, s, M_S5)
    zz = y @ w_glu.astype(F32) + b_glu.astype(F32)
    return zz[..., :D_MODEL] * jax.nn.sigmoid(zz[..., D_MODEL:])


def setup_inputs(seed: int = 0) -> dict:
    key = jax.random.key(seed)
    ks = jax.random.split(key, 32)

    def nrm(k, shape, scale):
        return jax.random.normal(k, shape, F32) * scale

    fg_bias = jnp.tile(jnp.linspace(FORGET_BIAS_LO, FORGET_BIAS_HI, MLSTM_HEADS, dtype=F32), 2)[None, :]
    b_in = jnp.concatenate([
        nrm(ks[6], (DEPTH, OFF_FG), 0.02),
        fg_bias + nrm(ks[7], (DEPTH, 2 * MLSTM_HEADS), 0.1),
        nrm(ks[8], (DEPTH, N_IN - OFF_FOURIER), 0.02)], axis=-1)
    lam_im0 = jnp.pi * jnp.arange(S5_STATE, dtype=F32)
    return {
        'x': jax.random.normal(ks[0], (BATCH, SEQ, D_MODEL), F32),
        'g_mix_pre': 1.0 + nrm(ks[1], (DEPTH, D_MODEL), 0.02),
        'g_mix_post': 1.0 + nrm(ks[2], (DEPTH, D_MODEL), 0.02),
        'g_ffn_pre': 1.0 + nrm(ks[3], (DEPTH, D_MODEL), 0.02),
        'g_ffn_post': 1.0 + nrm(ks[4], (DEPTH, D_MODEL), 0.02),
        'w_in': nrm(ks[5], (DEPTH, D_MODEL, N_IN), D_MODEL ** -0.5),
        'b_in': b_in,
        'conv_w': nrm(ks[9], (DEPTH, CONV_WIDTH, 2 * M_MLSTM), CONV_WIDTH ** -0.5),
        'conv_b': nrm(ks[10], (DEPTH, 2 * M_MLSTM), 0.02),
        'mlstm_norm_g': 1.0 + nrm(ks[11], (DEPTH, M_MLSTM), 0.02),
        'w_up_mlstm': nrm(ks[12], (DEPTH, M_MLSTM, D_MODEL), M_MLSTM ** -0.5),
        'w_up_fourier': nrm(ks[13], (DEPTH, M_FOURIER, D_MODEL), M_FOURIER ** -0.5),
        's5_lam_re': -0.5 + nrm(ks[14], (DEPTH, 2, S5_GROUPS, S5_STATE), 0.01),
        's5_lam_im': lam_im0 + nrm(ks[15], (DEPTH, 2, S5_GROUPS, S5_STATE), 0.01),
        's5_log_dt': jax.random.uniform(ks[16], (DEPTH, 2, S5_GROUPS), F32, math.log(DT_MIN), math.log(DT_MAX)),
        's5_b_re': nrm(ks[17], (DEPTH, 2, S5_GROUPS, S5_STATE, S5_GROUP), (2 * S5_GROUP) ** -0.5),
        's5_b_im': nrm(ks[18], (DEPTH, 2, S5_GROUPS, S5_STATE, S5_GROUP), (2 * S5_GROUP) ** -0.5),
        's5_c_re': nrm(ks[19], (DEPTH, 2, S5_GROUPS, S5_GROUP, S5_STATE), S5_STATE ** -0.5),
        's5_c_im': nrm(ks[20], (DEPTH, 2, S5_GROUPS, S5_GROUP, S5_STATE), S5_STATE ** -0.5),
        's5_d': nrm(ks[21], (DEPTH, S5_GROUPS, S5_GROUP), 1.0),
        'w_glu': nrm(ks[22], (DEPTH, M_S5, 2 * D_MODEL), M_S5 ** -0.5),
        'b_glu': nrm(ks[23], (DEPTH, 2 * D_MODEL), 0.02),
        'w_out': nrm(ks[24], (DEPTH, D_MODEL, D_MODEL), D_MODEL ** -0.5),
        'w_ffn1': nrm(ks[25], (DEPTH, D_MODEL, D_FF), D_MODEL ** -0.5),
        'w_ffn2': nrm(ks[26], (DEPTH, D_FF, D_MODEL), D_FF ** -0.5),
    }


def reference(x, g_mix_pre, g_mix_post, g_ffn_pre, g_ffn_post, w_in, b_in, conv_w, conv_b, mlstm_norm_g,
              w_up_mlstm, w_up_fourier, s5_lam_re, s5_lam_im, s5_log_dt, s5_b_re, s5_b_im, s5_c_re, s5_c_im,
              s5_d, w_glu, b_glu, w_out, w_ffn1, w_ffn2):
    bsz, s, _ = x.shape
    for l in range(DEPTH):
        h = rmsnorm(x, g_mix_pre[l])
        z = h @ w_in[l].astype(F32) + b_in[l].astype(F32)
        y_m = mlstm_branch(z[..., OFF_Q:OFF_V], z[..., OFF_V:OFF_O], z[..., OFF_O:OFF_IG],
                           z[..., OFF_IG:OFF_FG], z[..., OFF_FG:OFF_FOURIER],
                           conv_w[l], conv_b[l], mlstm_norm_g[l]) @ w_up_mlstm[l].astype(F32)
        y_f = fourier_branch(z[..., OFF_FOURIER:OFF_S5]) @ w_up_fourier[l].astype(F32)
        y_s = s5_branch(z[..., OFF_S5:OFF_GATE], s5_lam_re[l], s5_lam_im[l], s5_log_dt[l], s5_b_re[l], s5_b_im[l],
                        s5_c_re[l], s5_c_im[l], s5_d[l], w_glu[l], b_glu[l])
        gates = jax.nn.sigmoid(z[..., OFF_GATE:]).reshape(bsz, s, N_BRANCHES, D_MODEL)
        mixed = gates[..., 0, :] * y_m + gates[..., 1, :] * y_f + gates[..., 2, :] * y_s
        x = x + rmsnorm(mixed @ w_out[l].astype(F32), g_mix_post[l]).astype(x.dtype)
        h = rmsnorm(x, g_ffn_pre[l])
        f = jnp.square(jax.nn.relu(h @ w_ffn1[l].astype(F32))) @ w_ffn2[l].astype(F32)
        x = x + rmsnorm(f, g_ffn_post[l]).astype(x.dtype)
    return x
```

```python
import numpy as np
import concourse.bass as bass
import concourse.mybir as mybir
from concourse.bass_utils import run_bass_kernel_spmd

F32 = mybir.dt.float32
BF16 = mybir.dt.bfloat16
AF = mybir.ActivationFunctionType
ALU = mybir.AluOpType
AX = mybir.AxisListType

ENGS = ['pe', 'act', 'pool', 'dve', 'sp']
DMA_POOL = 6
SAME_ENGINE_SYNC = True
SEM_EPOCH = 16000

D = 1024
TL = 2048
S = 8192
EPS = 1e-6
N_IN = 5648
XF_ROWS = 516
XB_ROWS = 256


class T:
    def __init__(self, name, handle, shape, space):
        self.name, self.h, self.shape, self.space = name, handle, shape, space
        self.recs = []
        self.track = True

    def v(self, p0=0, p1=None, f0=0, f1=None, fn=None):
        P = self.shape[0]
        F = int(np.prod(self.shape[1:]))
        p1 = P if p1 is None else p1
        f1 = F if f1 is None else f1
        ap = self.h[p0:p1, f0:f1]
        if fn is not None:
            ap = fn(ap)
        if self.space == 'psum':
            reg = (0, 128, 0, 1 << 30)
        else:
            reg = (p0, p1, f0, f1)
        return V(self, ap, reg)


class V:
    def __init__(self, t, ap, reg):
        self.t, self.ap, self.reg = t, ap, reg

    def f(self, fn):
        return V(self.t, fn(self.ap), self.reg)


def _ov(a, b):
    return a[0] < b[1] and b[0] < a[1] and a[2] < b[3] and b[2] < a[3]


def _cov(a, b):
    return a[0] <= b[0] and a[1] >= b[1] and a[2] <= b[2] and a[3] >= b[3]


class Sched:
    def __init__(self, nc):
        self.nc = nc
        self.ops = {e: [] for e in ENGS}
        self.cnt = {e: 0 for e in ENGS}
        self.waited = {e: {} for e in ENGS}
        self.dma_n = {e: 0 for e in ENGS}
        self.dma_last = {}
        self.ctx = []
        self.rr = 0
        self.epoch = {}
        self.final = {}
        self.dma_ep = {}
        self.prep_n = 0

    def mark(self):
        return len(self.ctx)

    def barrier(self):
        deps = dict(self.final)
        deps.update({k: v for k, v in self.dma_last.items() if k[2] < 100})
        for e in ENGS:
            w = self._waits(e, dict(deps))
            if w:
                self.ops[e].append((w, None, None, 0))

    def release(self, mark):
        self.barrier()
        while len(self.ctx) > mark:
            self.ctx.pop().__exit__(None, None, None)

    def sb(self, name, shape, dt):
        self.uid = getattr(self, 'uid', 0) + 1
        cm = self.nc.sbuf_tensor('sb%d_%s' % (self.uid, name), list(shape), dt)
        h = cm.__enter__()
        self.ctx.append(cm)
        return T(name, h, shape, 'sbuf')

    def ps(self, name, shape=(128, 512), dt=F32):
        cm = self.nc.psum_tensor('pp_' + name, list(shape), dt)
        h = cm.__enter__()
        self.ctx.append(cm)
        return T(name, h, shape, 'psum')

    def dram(self, name, shape, dt, kind):
        h = self.nc.dram_tensor(name, list(shape), dt, kind=kind)
        t = T(name, h.ap(), shape, 'dram')
        t.track = (kind == 'Internal')
        return t

    def _deps(self, reads, writes):
        deps = {}
        reads = [v for v in reads if v.t.track]
        writes = [v for v in writes if v.t.track]
        for vw in reads:
            for r in vw.t.recs:
                if r[0] and _ov(r[1:5], vw.reg) and deps.get(r[5], 0) < r[6]:
                    deps[r[5]] = r[6]
        for vw in writes:
            for r in vw.t.recs:
                if _ov(r[1:5], vw.reg) and deps.get(r[5], 0) < r[6]:
                    deps[r[5]] = r[6]
        return deps

    def _record(self, reads, writes, semkey, val):
        reads = [v for v in reads if v.t.track]
        writes = [v for v in writes if v.t.track]
        for vw in writes:
            t = vw.t
            reg = tuple(vw.reg)
            t.recs = [r for r in t.recs if not _cov(reg, r[1:5])]
            t.recs.append((True,) + reg + (semkey, val))
        for vw in reads:
            t = vw.t
            reg = tuple(vw.reg)
            t.recs = [r for r in t.recs if r[0] or r[5] != semkey or r[1:5] != reg]
            t.recs.append((False,) + reg + (semkey, val))

    def _waits(self, eng, deps):
        w = []
        for k, v in deps.items():
            if k[0] == 'c' and k[1] == eng and (eng == 'pe' or not SAME_ENGINE_SYNC):
                continue
            if self.waited[eng].get(k, 0) >= v:
                continue
            self.waited[eng][k] = v
            w.append((k, v))
        return w

    def op(self, eng, fn, reads=(), writes=()):
        writes = list(writes) + [v for v in reads if v.t.space == 'psum']
        reads = [v for v in reads if v.t.space != 'psum']
        deps = self._deps(reads, writes)
        waits = self._waits(eng, deps)
        if self.cnt[eng] >= SEM_EPOCH:
            self.epoch[eng] = self.epoch.get(eng, 0) + 1
            self.cnt[eng] = 0
        self.cnt[eng] += 1
        key = ('c', eng, self.epoch.get(eng, 0))
        self.final[key] = self.cnt[eng]
        self._record(reads, writes, key, self.cnt[eng])
        self.ops[eng].append((waits, fn, key, 1))

    def dma(self, eng, out, in_, prep=False, **kw):
        deps = self._deps([in_], [out])
        if prep:
            j = self.prep_n
            self.prep_n += 1
            slot = 100 + j % DMA_POOL
        else:
            j = self.dma_n[eng]
            self.dma_n[eng] += 1
            slot = j % DMA_POOL
        if self.dma_last.get(('d', eng, slot, self.dma_ep.get((eng, slot), 0)), 0) >= SEM_EPOCH:
            self.dma_ep[(eng, slot)] = self.dma_ep.get((eng, slot), 0) + 1
        key = ('d', eng, slot, self.dma_ep.get((eng, slot), 0))
        prev = self.dma_last.get(key, 0)
        if not prev and self.dma_ep.get((eng, slot), 0) > 0:
            pk = ('d', eng, slot, self.dma_ep[(eng, slot)] - 1)
            deps[pk] = max(deps.get(pk, 0), self.dma_last[pk])
        if prev:
            deps[key] = max(deps.get(key, 0), prev)
        waits = self._waits(eng, deps)
        val = prev + 16
        self.dma_last[key] = val
        self._record([in_], [out], key, val)
        oa, ia = out.ap, in_.ap
        self.ops[eng].append((waits, lambda e: e.dma_start(out=oa, in_=ia, **kw), key, 16))

    def finish(self):
        waits = self._waits('sp', dict(self.dma_last))
        self.ops['sp'].append((waits, None, None, 0))
        nc = self.nc
        semkeys = list(self.final.keys()) + list(self.dma_last.keys())
        sems = {}
        cms = []
        for k in semkeys:
            cm = nc.semaphore('s_' + '_'.join(str(x) for x in k))
            sems[k] = cm.__enter__()
            cms.append(cm)
        with nc.Block() as block:
            deco = dict(pe=block.tensor, act=block.scalar, pool=block.gpsimd, dve=block.vector, sp=block.sync)
            for e in ENGS:
                ops = self.ops[e]

                def body(engine, ops=ops):
                    for waits, fn, key, inc in ops:
                        for k, v in waits:
                            engine.wait_ge(sems[k], v)
                        if fn is not None:
                            fn(engine).then_inc(sems[key], inc)
                if ops:
                    deco[e](body)
        for cm in reversed(cms):
            cm.__exit__(None, None, None)
        for cm in reversed(self.ctx):
            cm.__exit__(None, None, None)

    def mm(self, out, lhsT, rhs, start=True, stop=True):
        self.op('pe', lambda e: e.matmul(out.ap, lhsT.ap, rhs.ap, start=start, stop=stop),
                reads=[lhsT, rhs], writes=[out])

    def tr(self, out, in_, ident):
        self.op('pe', lambda e: e.transpose(out.ap, in_.ap, ident.ap), reads=[in_, ident], writes=[out])

    def act(self, out, in_, func, bias=None, scale=None, eng='act'):
        kw = {}
        rd = [in_]
        if bias is not None:
            if isinstance(bias, V):
                kw['bias'] = bias.ap
                rd.append(bias)
            else:
                kw['bias'] = bias
        if scale is not None:
            if isinstance(scale, V):
                kw['scale'] = scale.ap
                rd.append(scale)
            else:
                kw['scale'] = scale
        self.op('act', lambda e: e.activation(out=out.ap, in_=in_.ap, func=func, **kw), reads=rd, writes=[out])

    def tt(self, out, in0, in1, op, eng='dve'):
        self.op(eng, lambda e: e.tensor_tensor(out=out.ap, in0=in0.ap, in1=in1.ap, op=op), reads=[in0, in1], writes=[out])

    def ts(self, out, in0, s1, op0, s2=None, op1=None, eng='dve'):
        rd = [in0]
        a1 = s1
        a2 = s2
        if isinstance(s1, V):
            rd.append(s1)
            a1 = s1.ap
        if isinstance(s2, V):
            rd.append(s2)
            a2 = s2.ap
        if op1 is None:
            self.op(eng, lambda e: e.tensor_scalar(out=out.ap, in0=in0.ap, scalar1=a1, scalar2=None, op0=op0), reads=rd, writes=[out])
        else:
            self.op(eng, lambda e: e.tensor_scalar(out=out.ap, in0=in0.ap, scalar1=a1, scalar2=a2, op0=op0, op1=op1), reads=rd, writes=[out])

    def stt(self, out, in0, sc, in1, op0, op1):
        rd = [in0, in1]
        a = sc
        if isinstance(sc, V):
            rd.append(sc)
            a = sc.ap
        self.op('dve', lambda e: e.scalar_tensor_tensor(out=out.ap, in0=in0.ap, scalar=a, in1=in1.ap, op0=op0, op1=op1),
                reads=rd, writes=[out])

    def copy(self, out, in_, eng='dve'):
        if eng == 'act':
            self.op('act', lambda e: e.activation(out=out.ap, in_=in_.ap, func=AF.Identity), reads=[in_], writes=[out])
        else:
            self.op(eng, lambda e: e.tensor_copy(out=out.ap, in_=in_.ap), reads=[in_], writes=[out])

    def evac_eng(self):
        self.rr += 1
        return 'act' if self.rr % 2 else 'dve'


class Ctx:
    pass


OFF_Q, OFF_K, OFF_V, OFF_O, OFF_G16, OFF_F, OFF_S5, OFF_GATE = 0, 512, 1024, 1536, 2048, 2064, 2320, 2576


def common_setup(s, C, ident_d):
    C.ident = s.sb('ident', [128, 128], F32)
    s.dma('sp', C.ident.v(), ident_d.v())
    C.identb = s.sb('identb', [128, 128], BF16)
    s.copy(C.identb.v(), C.ident.v())
    C.ones = s.sb('ones', [128, 128], BF16)
    s.op('dve', lambda e: e.memset(C.ones.h[:, :], 1.0), writes=[C.ones.v()])
    C.ps = [s.ps('ps%d' % i) for i in range(8)]
    C.psi = 0


def next_ps(C):
    C.psi = (C.psi + 1) % 8
    return C.ps[C.psi]


def load_xT(s, C, x_d, ntok):
    C.xT = [s.sb('xT%d' % k, [128, ntok], F32) for k in range(8)]
    stage = [s.sb('xst%d' % i, [128, 1024], F32) for i in range(8)]
    for tg in range(ntok // 512):
        for i in range(4):
            tt = tg * 4 + i
            s.dma('sp' if i % 2 == 0 else 'act', stage[(tg % 2) * 4 + i].v(), x_d.v(128 * tt, 128 * tt + 128))
        for kt in range(8):
            ps = next_ps(C)
            for i in range(4):
                s.tr(ps.v(0, 128, 128 * i, 128 * i + 128), stage[(tg % 2) * 4 + i].v(0, 128, 128 * kt, 128 * kt + 128), C.ident.v())
            s.copy(C.xT[kt].v(0, 128, 512 * tg, 512 * tg + 512), ps.v(), eng=s.evac_eng())


def rms_rstd(s, C, src, ntok, name):
    rstd = s.sb('rstd_' + name, [128, ntok], F32)
    sq = [s.sb('sq_%s%d' % (name, i), [128, 512], BF16) for i in range(2)]
    n = 0
    for tc in range(ntok // 512):
        ps = next_ps(C)
        for kt in range(8):
            q = sq[n % 2]
            n += 1
            s.act(q.v(), src[kt].v(0, 128, 512 * tc, 512 * tc + 512), AF.Square)
            s.mm(ps.v(), C.ones.v(), q.v(), start=(kt == 0), stop=(kt == 7))
        r = rstd.v(0, 128, 512 * tc, 512 * tc + 512)
        s.act(r, ps.v(), AF.Sqrt, bias=C.epsb.v(), scale=1.0 / D)
        s.op('dve', lambda e, r=r: e.reciprocal(out=r.ap, in_=r.ap), reads=[r], writes=[r])
    return rstd


def load_vec_cols(s, C, name, d_t, n):
    t = s.sb(name, [128, n], F32)
    for k in range(n):
        s.dma('sp', t.v(0, 128, k, k + 1), d_t.v(128 * k, 128 * k + 128))
    return t


def make_h(s, C, gcols, rstd, ntok, name):
    hT = [s.sb('%s%d' % (name, k), [128, ntok], BF16) for k in range(8)]
    for kt in range(8):
        s.stt(hT[kt].v(), C.xT[kt].v(), gcols.v(0, 128, kt, kt + 1), rstd.v(), ALU.mult, ALU.mult)
    return hT


class PW:
    def __init__(self, s, name, w_d, K, specs):
        self.w_d, self.K, self.specs = w_d, K, specs
        width = K * max(n for _, n in specs)
        self.t = s.dram(name, [len(specs) * 128, width], BF16, 'Internal')
        self.idx = {c0: i for i, (c0, n) in enumerate(specs)}

    def prep(self, s, i):
        c0, ncols = self.specs[i]
        K = self.K
        out = self.t.v(i * 128, i * 128 + 128, 0, K * ncols, fn=lambda a: a.rearrange('p (k n) -> p k n', k=K))
        in_ = self.w_d.v(0, K * 128, c0, c0 + ncols, fn=lambda a: a.rearrange('(k p) n -> p k n', p=128))
        s.dma('pool', out, in_, prep=True)


def load_w_cols(s, C, wt, w_d, K, c0, ncols, q='pool'):
    if isinstance(w_d, PW):
        i = w_d.idx[c0]
        s.dma('sp', wt.v(0, 128, 0, K * ncols), w_d.t.v(i * 128, i * 128 + 128, 0, K * ncols))
        return
    out = wt.v(0, 128, 0, K * ncols, fn=lambda a: a.rearrange('p (k n) -> p k n', k=K))
    in_ = w_d.v(0, K * 128, c0, c0 + ncols, fn=lambda a: a.rearrange('(k p) n -> p k n', p=128))
    s.dma(q, out, in_)


def build_T1():
    nc = bass.Bass("TRN2", target_bir_lowering=False)
    s = Sched(nc)
    C = Ctx()
    x_d = s.dram('x', [TL, D], F32, 'ExternalInput')
    w_d = s.dram('w_in', [D, N_IN], F32, 'ExternalInput')
    b_d = s.dram('b_in', [N_IN, 1], F32, 'ExternalInput')
    g_d = s.dram('g_pre', [D, 1], F32, 'ExternalInput')
    id_d = s.dram('ident', [128, 128], F32, 'ExternalInput')
    xf_d = s.dram('xf', [4 * XF_ROWS, TL], F32, 'ExternalOutput')
    common_setup(s, C, id_d)
    C.epsb = s.sb('epsb', [128, 1], F32)
    s.op('dve', lambda e: e.memset(C.epsb.h[:, :], EPS), writes=[C.epsb.v()])
    load_xT(s, C, x_d, TL)
    gcols = load_vec_cols(s, C, 'gpre', g_d, 8)
    rstd = rms_rstd(s, C, C.xT, TL, 'pre')
    hT = make_h(s, C, gcols, rstd, TL, 'hT')
    phase_T1_proj(s, C, hT, w_d, b_d, xf_d)
    s.finish()
    return nc


def phase_T1_proj(s, C, hT, w_d, b_d, xf_d):
    tiles = []
    for i in range(4):
        tiles.append((OFF_Q + 128 * i, 128, [(0, 128, i, 0)]))
        tiles.append((OFF_K + 128 * i, 128, [(0, 128, i, 128)]))
        tiles.append((OFF_V + 128 * i, 128, [(0, 128, i, 256)]))
    tiles.append((OFF_G16, 16, [(0, 16, None, 384)]))
    for i in range(2):
        tiles.append((OFF_F + 128 * i, 128, [(0, 64, 2 * i, 388), (64, 64, 2 * i + 1, 388)]))
        tiles.append((OFF_S5 + 128 * i, 128, [(0, 64, 2 * i, 452), (64, 64, 2 * i + 1, 452)]))
    wts = [s.sb('wT1_%d' % i, [128, 8 * 128], BF16) for i in range(2)]
    bts = [s.sb('bT1_%d' % i, [128, 1], F32) for i in range(2)]
    stg = [s.sb('zst%d' % i, [128, 512], F32) for i in range(4)]
    n = 0
    for ti, (c0, ncols, dsts) in enumerate(tiles):
        wt = wts[ti % 2]
        bt = bts[ti % 2]
        load_w_cols(s, C, wt, w_d, 8, c0, ncols)
        s.dma('sp', bt.v(0, ncols), b_d.v(c0, c0 + ncols))
        for tc in range(TL // 512):
            ps = next_ps(C)
            for kt in range(8):
                s.mm(ps.v(0, ncols), wt.v(0, 128, kt * ncols, (kt + 1) * ncols), hT[kt].v(0, 128, 512 * tc, 512 * tc + 512),
                     start=(kt == 0), stop=(kt == 7))
            st = stg[n % 4]
            n += 1
            s.act(st.v(0, ncols), ps.v(0, ncols), AF.Identity, bias=bt.v(0, ncols))
            for (r0, nr, dst, dr0) in dsts:
                if dst is None:
                    for dd in range(4):
                        o = xf_d.v(dd * XF_ROWS + dr0, dd * XF_ROWS + dr0 + 4, 512 * tc, 512 * tc + 512)
                        s.dma('sp' if dd % 2 else 'act', o, st.v(0, 16, fn=lambda a, dd=dd: a[dd::4, :]))
                else:
                    o = xf_d.v(dst * XF_ROWS + dr0, dst * XF_ROWS + dr0 + nr, 512 * tc, 512 * tc + 512)
                    s.dma('sp' if n % 2 else 'act', o, st.v(r0, r0 + nr))


class WPool:
    def __init__(self, s, name, ncols, n):
        self.b = [s.sb('%s%d' % (name, i), [128, ncols], BF16) for i in range(n)]
        self.i = 0

    def get(self):
        self.i += 1
        return self.b[self.i % len(self.b)]


def bias_cols(s, name, d_t, offs, n=128):
    t = s.sb(name, [128, len(offs)], F32)
    for k, o in enumerate(offs):
        s.dma('sp' if k % 2 else 'act', t.v(0, n, k, k + 1), d_t.v(o, o + n))
    return t


def store_xT(s, C, out_d, tok0, ntok, stage):
    for tt in range(ntok // 128):
        st = stage[tt % 2]
        for kg in range(2):
            ps = next_ps(C)
            for i in range(4):
                kt = kg * 4 + i
                s.tr(ps.v(0, 128, 128 * i, 128 * i + 128), C.xT[kt].v(0, 128, 128 * tt, 128 * tt + 128), C.ident.v())
            s.copy(st.v(0, 128, 512 * kg, 512 * kg + 512), ps.v(), eng=s.evac_eng())
        s.dma('sp' if tt % 2 else 'act', out_d.v(tok0 + 128 * tt, tok0 + 128 * tt + 128), st.v())


def t2_setup(s, C, W):
    NT = C.NT
    C.wp = WPool(s, 'wp', 1024, 8)
    C.wbig = WPool(s, 'wbig', 4096, 2)
    C.big = s.sb('big', [128, 32 * NT], BF16)
    C.h2T = s.sb('h2T', [128, 8 * NT], BF16)
    C.tmpf = [s.sb('tmpf%d' % i, [128, 512], F32) for i in range(6)]
    C.tmpi = 0
    C.hnst = [s.sb('hnst%d' % i, [128, NT], F32) for i in range(2)]
    C.b_o = bias_cols(s, 'b_o', W['b_in'], [OFF_O + 128 * i for i in range(4)])
    C.b_g = bias_cols(s, 'b_g', W['b_in'], [OFF_GATE + 128 * i for i in range(24)])
    C.b_glu = bias_cols(s, 'b_glu', W['b_glu'], [128 * i for i in range(16)])
    C.normg = bias_cols(s, 'normg', W['norm_g'], [128 * i for i in range(4)])
    C.g_post = bias_cols(s, 'g_post', W['g_post'], [128 * i for i in range(8)])
    C.g_f1 = bias_cols(s, 'g_f1', W['g_ffn_pre'], [128 * i for i in range(8)])
    C.g_f2 = bias_cols(s, 'g_f2', W['g_ffn_post'], [128 * i for i in range(8)])


def tmpf(C):
    C.tmpi += 1
    return C.tmpf[C.tmpi % len(C.tmpf)]


def ps6(C):
    C.psi = (C.psi + 1) % 6
    return C.ps[C.psi]


def big_view(C, tile, c0, c1):
    NT = C.NT
    return C.big.v(0, 128, tile * NT + c0, tile * NT + c1)


def phase_T2(s, C, W, xb_d, tok0):
    NT = C.NT
    NTC = NT // 512
    hT = C.hT
    MIX0, HG0, YF0, YS0, OB0 = 8, 16, 20, 22, 24

    def chunk(tc):
        return 512 * tc, 512 * tc + 512

    for (row0, dst0) in ((128, YF0), (192, YS0)):
        for half in range(2):
            st = C.hnst[half]
            for q in range(2):
                src = 2 * half + q
                s.dma('sp' if q else 'act', st.v(64 * q, 64 * q + 64),
                      xb_d.v(src * XB_ROWS + row0, src * XB_ROWS + row0 + 64, tok0, tok0 + NT))
            s.copy(big_view(C, dst0 + half, 0, NT), st.v(), eng='pool')
    for i in range(4):
        st = C.hnst[i % 2]
        s.dma('sp', st.v(), xb_d.v(i * XB_ROWS, i * XB_ROWS + 128, tok0, tok0 + NT))
        wt = C.wp.get()
        load_w_cols(s, C, wt, W['w_in'], 8, OFF_O + 128 * i, 128)
        for tc in range(NTC):
            c0, c1 = chunk(tc)
            ps = ps6(C)
            for kt in range(8):
                s.mm(ps.v(), wt.v(0, 128, 128 * kt, 128 * kt + 128), hT[kt].v(0, 128, c0, c1), start=(kt == 0), stop=(kt == 7))
            sg = tmpf(C)
            s.act(sg.v(), ps.v(), AF.Sigmoid, bias=C.b_o.v(0, 128, i, i + 1))
            s.stt(big_view(C, HG0 + i, c0, c1), st.v(0, 128, c0, c1), C.normg.v(0, 128, i, i + 1), sg.v(), ALU.mult, ALU.mult)
    for j in range(8):
        w_um = C.wp.get(); load_w_cols(s, C, w_um, W['w_up_m'], 4, 128 * j, 128)
        w_uf = C.wp.get(); load_w_cols(s, C, w_uf, W['w_up_f'], 2, 128 * j, 128)
        w_ga = C.wp.get(); load_w_cols(s, C, w_ga, W['w_glu'], 2, 128 * j, 128)
        w_gb = C.wp.get(); load_w_cols(s, C, w_gb, W['w_glu'], 2, 1024 + 128 * j, 128)
        w_g = []
        for b in range(3):
            w = C.wp.get(); load_w_cols(s, C, w, W['w_in'], 8, OFF_GATE + 1024 * b + 128 * j, 128)
            w_g.append(w)
        for tc in range(NTC):
            c0, c1 = chunk(tc)

            def gate(b):
                ps = ps6(C)
                for kt in range(8):
                    s.mm(ps.v(), w_g[b].v(0, 128, 128 * kt, 128 * kt + 128), hT[kt].v(0, 128, c0, c1), start=(kt == 0), stop=(kt == 7))
                g = tmpf(C)
                s.act(g.v(), ps.v(), AF.Sigmoid, bias=C.b_g.v(0, 128, 8 * b + j, 8 * b + j + 1))
                return g

            def small(wt, K, src0):
                ps = ps6(C)
                for kt in range(K):
                    s.mm(ps.v(), wt.v(0, 128, 128 * kt, 128 * kt + 128), big_view(C, src0 + kt, c0, c1), start=(kt == 0), stop=(kt == K - 1))
                return ps
            ps_ym = small(w_um, 4, HG0)
            g0 = gate(0)
            acc = tmpf(C)
            s.tt(acc.v(), g0.v(), ps_ym.v(), ALU.mult)
            ps_yf = small(w_uf, 2, YF0)
            g1 = gate(1)
            t1 = tmpf(C)
            s.tt(t1.v(), g1.v(), ps_yf.v(), ALU.mult)
            s.tt(acc.v(), acc.v(), t1.v(), ALU.add, eng='pool')
            ps_za = small(w_ga, 2, YS0)
            ps_zb = small(w_gb, 2, YS0)
            sb_ = tmpf(C)
            s.act(sb_.v(), ps_zb.v(), AF.Sigmoid, bias=C.b_glu.v(0, 128, 8 + j, 8 + j + 1))
            ys = tmpf(C)
            s.stt(ys.v(), ps_za.v(), C.b_glu.v(0, 128, j, j + 1), sb_.v(), ALU.add, ALU.mult)
            g2 = gate(2)
            s.tt(ys.v(), ys.v(), g2.v(), ALU.mult, eng='pool')
            s.tt(big_view(C, MIX0 + j, c0, c1), acc.v(), ys.v(), ALU.add)
    ss = [C.ps[6], C.ps[7]]
    sqb = [s_ for s_ in C.sqb]

    def proj_norm_add(wname, K, src_view, dst0_tile, dst_T, gcols, wpool, kcols):
        n = 0
        for j in range(8):
            wt = wpool.get()
            load_w_cols(s, C, wt, W[wname], K, 128 * j, 128)
            for tc in range(NTC):
                c0, c1 = chunk(tc)
                ps = ps6(C)
                for kt in range(K):
                    s.mm(ps.v(), wt.v(0, 128, 128 * kt, 128 * kt + 128), src_view(kt, c0, c1), start=(kt == 0), stop=(kt == K - 1))
                if dst_T is None:
                    ov = big_view(C, dst0_tile + j, c0, c1)
                else:
                    ov = dst_T.v(0, 128, j * NT + c0, j * NT + c1)
                s.copy(ov, ps.v(), eng='dve')
                q = sqb[n % 2]
                n += 1
                s.act(q.v(), ps.v(), AF.Square)
                s.mm(ss[tc].v(), C.ones.v(), q.v(), start=(j == 0), stop=(j == 7))
        for tc in range(NTC):
            c0, c1 = chunk(tc)
            r = C.rstd2.v(0, 128, c0, c1)
            s.act(r, ss[tc].v(), AF.Sqrt, bias=C.epsb.v(), scale=1.0 / D)
            s.op('dve', lambda e, r=r: e.reciprocal(out=r.ap, in_=r.ap), reads=[r], writes=[r])
        for j in range(8):
            for tc in range(NTC):
                c0, c1 = chunk(tc)
                if dst_T is None:
                    ov = big_view(C, dst0_tile + j, c0, c1)
                else:
                    ov = dst_T.v(0, 128, j * NT + c0, j * NT + c1)
                t = tmpf(C)
                s.stt(t.v(), ov, gcols.v(0, 128, j, j + 1), C.rstd2.v(0, 128, c0, c1), ALU.mult, ALU.mult)
                xv = C.xT[j].v(0, 128, C.xoff + c0, C.xoff + c1)
                s.tt(xv, xv, t.v(), ALU.add, eng='pool')

    proj_norm_add('w_out', 8, lambda kt, c0, c1: big_view(C, MIX0 + kt, c0, c1), OB0, None, C.g_post, C.wp, 128)
    for tc in range(NTC):
        c0, c1 = chunk(tc)
        ps = ps6(C)
        for kt in range(8):
            q = sqb[kt % 2]
            s.act(q.v(), C.xT[kt].v(0, 128, C.xoff + c0, C.xoff + c1), AF.Square)
            s.mm(ps.v(), C.ones.v(), q.v(), start=(kt == 0), stop=(kt == 7))
        r = C.rstd2.v(0, 128, c0, c1)
        s.act(r, ps.v(), AF.Sqrt, bias=C.epsb.v(), scale=1.0 / D)
        s.op('dve', lambda e, r=r: e.reciprocal(out=r.ap, in_=r.ap), reads=[r], writes=[r])
    for kt in range(8):
        s.stt(C.h2T.v(0, 128, kt * NT, kt * NT + NT), C.xT[kt].v(0, 128, C.xoff, C.xoff + NT), C.g_f1.v(0, 128, kt, kt + 1),
              C.rstd2.v(0, 128, 0, NT), ALU.mult, ALU.mult)
    for ng in range(8):
        wt = C.wbig.get()
        load_w_cols(s, C, wt, W['w_ffn1'], 8, 512 * ng, 512)
        for nn in range(4):
            n = 4 * ng + nn
            for tc in range(NTC):
                c0, c1 = chunk(tc)
                ps = ps6(C)
                for kt in range(8):
                    s.mm(ps.v(), wt.v(0, 128, 512 * kt + 128 * nn, 512 * kt + 128 * nn + 128), C.h2T.v(0, 128, kt * NT + c0, kt * NT + c1),
                         start=(kt == 0), stop=(kt == 7))
                t = tmpf(C)
                s.act(t.v(), ps.v(), AF.Relu)
                s.tt(big_view(C, n, c0, c1), t.v(), t.v(), ALU.mult, eng='pool' if (n + tc) % 2 else 'dve')
    proj_norm_add('w_ffn2', 32, lambda kt, c0, c1: big_view(C, kt, c0, c1), None, C.h2T, C.g_f2, C.wbig, 128)


def build_T2(NT=1024):
    nc = bass.Bass("TRN2", target_bir_lowering=False)
    s = Sched(nc)
    C = Ctx()
    C.NT = NT
    x_d = s.dram('x', [TL, D], F32, 'ExternalInput')
    xb_d = s.dram('xb', [4 * XB_ROWS, TL], F32, 'ExternalInput')
    W = {}
    for name, shape in [('w_in', [D, N_IN]), ('b_in', [N_IN, 1]), ('g_pre', [D, 1]), ('norm_g', [512, 1]), ('w_up_m', [512, D]),
                        ('w_up_f', [256, D]), ('w_glu', [256, 2 * D]), ('b_glu', [2 * D, 1]), ('w_out', [D, D]), ('g_post', [D, 1]),
                        ('g_ffn_pre', [D, 1]), ('g_ffn_post', [D, 1]), ('w_ffn1', [D, 4 * D]), ('w_ffn2', [4 * D, D])]:
        W[name] = s.dram(name, shape, F32, 'ExternalInput')
    id_d = s.dram('ident', [128, 128], F32, 'ExternalInput')
    out_d = s.dram('x_out', [TL, D], F32, 'ExternalOutput')
    common_setup(s, C, id_d)
    C.epsb = s.sb('epsb', [128, 1], F32)
    s.op('dve', lambda e: e.memset(C.epsb.h[:, :], EPS), writes=[C.epsb.v()])
    C.sqb = [s.sb('sqb%d' % i, [128, 512], BF16) for i in range(2)]
    C.rstd2 = s.sb('rstd2', [128, NT], F32)
    gcols = load_vec_cols(s, C, 'gpre', W['g_pre'], 8)
    t2_setup(s, C, W)
    C.xT = [s.sb('xT%d' % k, [128, NT], F32) for k in range(8)]
    C.hT = [s.sb('hT%d' % k, [128, NT], BF16) for k in range(8)]
    C.xoff = 0
    xstage = [s.sb('xst%d' % i, [128, 1024], F32) for i in range(4)]
    for hf in range(TL // NT):
        tok0 = hf * NT
        for tg in range(NT // 512):
            for i in range(4):
                tt = tg * 4 + i
                s.dma('sp' if i % 2 == 0 else 'act', xstage[i].v(), x_d.v(tok0 + 128 * tt, tok0 + 128 * tt + 128))
            for kt in range(8):
                ps = ps6(C)
                for i in range(4):
                    s.tr(ps.v(0, 128, 128 * i, 128 * i + 128), xstage[i].v(0, 128, 128 * kt, 128 * kt + 128), C.ident.v())
                s.copy(C.xT[kt].v(0, 128, 512 * tg, 512 * tg + 512), ps.v(), eng=s.evac_eng())
        for tc in range(NT // 512):
            ps = ps6(C)
            for kt in range(8):
                q = C.sqb[kt % 2]
                s.act(q.v(), C.xT[kt].v(0, 128, 512 * tc, 512 * tc + 512), AF.Square)
                s.mm(ps.v(), C.ones.v(), q.v(), start=(kt == 0), stop=(kt == 7))
            r = C.rstd2.v(0, 128, 512 * tc, 512 * tc + 512)
            s.act(r, ps.v(), AF.Sqrt, bias=C.epsb.v(), scale=1.0 / D)
            s.op('dve', lambda e, r=r: e.reciprocal(out=r.ap, in_=r.ap), reads=[r], writes=[r])
        for kt in range(8):
            s.stt(C.hT[kt].v(), C.xT[kt].v(), gcols.v(0, 128, kt, kt + 1), C.rstd2.v(), ALU.mult, ALU.mult)
        phase_T2(s, C, W, xb_d, tok0)
        store_xT(s, C, out_d, tok0, NT, xstage)
    s.finish()
    return nc


def fourier_tables():
    c = np.arange(64)
    a64 = 2 * np.pi * np.outer(c, c) / 64.0
    s1 = np.arange(128)
    a128 = 2 * np.pi * np.outer(s1, s1) / 128.0
    atw = 2 * np.pi * np.outer(s1, c) / 8192.0
    sc = 1.0 / np.sqrt(8192.0 * 64.0)
    z = np.zeros((64, 64))
    tabs = dict(
        f_f64=np.concatenate([np.cos(a64), -np.sin(a64)], 1),
        f_c128=np.cos(a128), f_s128=np.sin(a128), f_ns128=-np.sin(a128),
        f_tw=np.concatenate([np.cos(atw), -np.sin(atw)], 1),
        f_bdc=np.block([[np.cos(a64), z], [z, np.cos(a64)]]) * sc,
        f_bds=np.block([[np.sin(a64), z], [z, np.sin(a64)]]) * sc,
    )
    return {k: np.ascontiguousarray(v.astype(np.float32)) for k, v in tabs.items()}


FT_SHAPES = dict(f_f64=[64, 128], f_c128=[128, 128], f_s128=[128, 128], f_ns128=[128, 128], f_tw=[128, 128],
                 f_bdc=[128, 128], f_bds=[128, 128])


def phase_U_fourier(s, C, xf_d, xb_d, tabs_d):
    tb = {}
    for k in ['f_f64', 'f_c128', 'f_s128', 'f_ns128', 'f_bdc', 'f_bds']:
        tb[k] = s.sb(k, FT_SHAPES[k], BF16)
        s.dma('pool', tb[k].v(), tabs_d[k].v())
    tw = s.sb('f_tw', [128, 128], F32)
    s.dma('sp', tw.v(), tabs_d['f_tw'].v())
    UTb = s.sb('f_UTb', [64, S], BF16)
    for src in range(4):
        s.dma('pool', UTb.v(0, 64, 2048 * src, 2048 * src + 2048), xf_d.v(src * XF_ROWS + 388, src * XF_ROWS + 452))
    Zre = s.sb('f_Zre', [128, 4096], BF16)
    Zim = s.sb('f_Zim', [128, 4096], BF16)
    for g in range(16):
        ps = next_ps(C)
        for i in range(4):
            s2 = 4 * g + i
            lhsT = V(UTb, UTb.h[0:64, s2::64], (0, 64, 0, S))
            s.mm(ps.v(0, 128, 128 * i, 128 * i + 128), lhsT, tb['f_f64'].v())
        pv = lambda lo: ps.v(fn=lambda a: a.rearrange('p (s c) -> p s c', c=128)[:, :, lo:lo + 64])
        zv = lambda Zt: Zt.v(0, 128, 256 * g, 256 * g + 256, fn=lambda a: a.rearrange('p (s c) -> p s c', c=64))
        s.copy(zv(Zre), pv(0), eng='act')
        s.copy(zv(Zim), pv(64), eng='dve')
    ArP = s.sb('f_ArP', [128, 4096], BF16)
    AiP = s.sb('f_AiP', [128, 4096], BF16)
    tmp = [s.sb('f_tmp%d' % i, [128, 512], F32) for i in range(4)]
    for ch in range(8):
        c0, c1 = 512 * ch, 512 * ch + 512
        pr = next_ps(C)
        s.mm(pr.v(), tb['f_c128'].v(), Zre.v(0, 128, c0, c1), start=True, stop=False)
        s.mm(pr.v(), tb['f_s128'].v(), Zim.v(0, 128, c0, c1), start=False, stop=True)
        pi = next_ps(C)
        s.mm(pi.v(), tb['f_c128'].v(), Zim.v(0, 128, c0, c1), start=True, stop=False)
        s.mm(pi.v(), tb['f_ns128'].v(), Zre.v(0, 128, c0, c1), start=False, stop=True)
        r3 = lambda a: a.rearrange('p (s c) -> p s c', c=64)
        tre = tw.v(0, 128, 8 * ch, 8 * ch + 8, fn=lambda a: a.unsqueeze(2).to_broadcast([128, 8, 64]))
        tim = tw.v(0, 128, 64 + 8 * ch, 64 + 8 * ch + 8, fn=lambda a: a.unsqueeze(2).to_broadcast([128, 8, 64]))
        t = [x.v(fn=r3) for x in tmp]
        s.tt(t[0], pr.v(fn=r3), tre, ALU.mult)
        s.tt(t[1], pi.v(fn=r3), tim, ALU.mult)
        s.tt(t[2], pr.v(fn=r3), tim, ALU.mult)
        s.tt(t[3], pi.v(fn=r3), tre, ALU.mult)
        perm = lambda a: a.rearrange('p (c s) -> p s c', s=64)[:, 8 * ch:8 * ch + 8, :]
        s.tt(ArP.v(fn=perm), t[0], t[1], ALU.subtract, eng='pool')
        s.tt(AiP.v(fn=perm), t[2], t[3], ALU.add, eng='pool')
    ArT = s.sb('f_ArT', [128, 4096], BF16)
    AiT = s.sb('f_AiT', [128, 4096], BF16)
    for (src, dst) in ((ArP, ArT), (AiP, AiT)):
        for g in range(8):
            ps = next_ps(C)
            pb = lambda lo, hi: ps.v(fn=lambda a: a.bitcast(BF16)[:, lo:hi])
            for i in range(4):
                blk = 4 * g + i
                s.tr(pb(128 * i, 128 * i + 128), src.v(0, 128, 128 * blk, 128 * blk + 128), C.identb.v())
            s.copy(dst.v(0, 128, 512 * g, 512 * g + 512), pb(0, 512), eng=s.evac_eng())
    Y = s.sb('f_Y', [128, 4096], F32)
    for g in range(8):
        c0, c1 = 512 * g, 512 * g + 512
        ps = next_ps(C)
        s.mm(ps.v(), tb['f_bdc'].v(), ArT.v(0, 128, c0, c1), start=True, stop=False)
        s.mm(ps.v(), tb['f_bds'].v(), AiT.v(0, 128, c0, c1), start=False, stop=True)
        s.copy(Y.v(0, 128, c0, c1), ps.v(), eng=s.evac_eng())
    n = 0
    for cp in range(2):
        for dst in range(4):
            r0 = dst * XB_ROWS + 128 + cp
            o = xb_d.v(r0, r0 + 64, 0, TL, fn=lambda a: a[::2, :].rearrange('b (s j) -> s b j', j=128))
            i_ = Y.v(64 * cp + 16 * dst, 64 * cp + 16 * dst + 16, 0, 4096, fn=lambda a: a.rearrange('p (b j) -> p b j', j=128))
            s.dma('sp' if n % 2 else 'act', o, i_)
            n += 1


def build_U(parts=('fourier',)):
    nc = bass.Bass("TRN2", target_bir_lowering=False)
    s = Sched(nc)
    C = Ctx()
    xf_d = s.dram('xf', [4 * XF_ROWS, TL], F32, 'ExternalInput')
    id_d = s.dram('ident', [128, 128], F32, 'ExternalInput')
    xb_d = s.dram('xb', [4 * XB_ROWS, TL], F32, 'ExternalOutput')
    tabs_d = {k: s.dram(k, v, F32, 'ExternalInput') for k, v in FT_SHAPES.items()}
    md = {k: s.dram(k, v, F32, 'ExternalInput') for k, v in MT_SHAPES.items()}
    sd = {k: s.dram(k, v, F32, 'ExternalInput') for k, v in ST_SHAPES.items()}
    common_setup(s, C, id_d)
    C.epsb = s.sb('epsb', [128, 1], F32)
    s.op('dve', lambda e: e.memset(C.epsb.h[:, :], EPS), writes=[C.epsb.v()])
    C.oneb = s.sb('oneb', [128, 1], F32)
    s.op('dve', lambda e: e.memset(C.oneb.h[:, :], 1.0), writes=[C.oneb.v()])
    if 'dbg' in parts:
        C.dbg = {k: s.dram(k, sh, F32, 'ExternalOutput') for k, sh in [('d_q', [128, S]), ('d_k', [128, S]), ('d_H', [128, S]),
                                                                        ('d_sm', [128, 512]), ('d_va', [128, 64 * 129])]}
    mk = s.mark()
    if 'fourier' in parts:
        phase_U_fourier(s, C, xf_d, xb_d, tabs_d)
        s.release(mk)
    if 's5' in parts:
        phase_U_s5(s, C, xf_d, xb_d, sd)
        s.release(mk)
    if 'mlstm' in parts:
        phase_U_mlstm(s, C, xf_d, xb_d, md)
    s.finish()
    return nc


def mlstm_tables():
    j = np.arange(128)
    trif = (j[:, None] <= j[None, :]).astype(np.float32)
    sc = 128.0 ** -0.5
    return dict(m_trif=trif, m_trib=np.ascontiguousarray(trif.T), m_maskf=trif * sc, m_maskb=np.ascontiguousarray(trif.T) * sc,
                m_onesf=np.ones((128, 128), np.float32))


MT_SHAPES = dict(m_trif=[128, 128], m_trib=[128, 128], m_maskf=[128, 128], m_maskb=[128, 128], m_onesf=[128, 128],
                 m_cwq=[128, 5], m_cwk=[128, 5], m_cbq=[128, 1], m_cbk=[128, 1])


def phase_U_mlstm(s, C, xf_d, xb_d, md):
    NCH = S // 128
    SC = 128.0 ** -0.5
    tb = {}
    for k in MT_SHAPES:
        tb[k] = s.sb(k, MT_SHAPES[k], F32)
        s.dma('sp', tb[k].v(), md[k].v())
    G4 = s.sb('m_G4', [4, S], F32)
    for src in range(4):
        s.dma('act', G4.v(0, 4, 2048 * src, 2048 * src + 2048), xf_d.v(src * XF_ROWS + 384, src * XF_ROWS + 388))
    gT = s.sb('m_gT', [128, NCH * 4], F32)
    ps = next_ps(C)
    for c in range(NCH):
        s.tr(ps.v(0, 128, 4 * c, 4 * c + 4), G4.v(0, 4, 128 * c, 128 * c + 128), C.ident.v(0, 4, 0, 4))
    s.copy(gT.v(), ps.v(0, 128, 0, 4 * NCH))
    gcol = lambda g: gT.v(fn=lambda a: a.rearrange('p (c g) -> p c g', g=4)[:, :, g])
    sm = {}
    for nm in ['lf_f', 'lf_b', 'b_f', 'b_b', 'w_f', 'w_b', 'enb_f', 'enb_b', 'eg_f', 'eg_b', 'egs_f', 'egs_b', 'tmp']:
        sm[nm] = s.sb('m_' + nm, [128, NCH], F32)
    for d, gi in (('f', 2), ('b', 3)):
        lf = sm['lf_' + d]
        s.act(sm['tmp'].v(), gcol(gi), AF.Exp, scale=-1.0)
        s.act(lf.v(), sm['tmp'].v(), AF.Ln, bias=C.oneb.v(), scale=1.0)
        s.ts(lf.v(), lf.v(), -1.0, ALU.mult)
        pb = next_ps(C)
        s.mm(pb.v(0, 128, 0, NCH), tb['m_tri' + d].v(), lf.v())
        s.copy(sm['b_' + d].v(), pb.v(0, 128, 0, NCH))
        pg = next_ps(C)
        s.mm(pg.v(0, 128, 0, NCH), tb['m_onesf'].v(), lf.v())
        s.act(sm['eg_' + d].v(), pg.v(0, 128, 0, NCH), AF.Exp)
        s.ts(sm['egs_' + d].v(), sm['eg_' + d].v(), SC, ALU.mult)
        s.act(sm['enb_' + d].v(), sm['b_' + d].v(), AF.Exp, scale=-1.0)
        s.tt(sm['tmp'].v(), gcol(0 if d == 'f' else 1), sm['b_' + d].v(), ALU.subtract)
        s.act(sm['w_' + d].v(), sm['tmp'].v(), AF.Exp)
    zpad = s.sb('m_zpad', [128, S + 4], F32)
    s.op('dve', lambda e: e.memset(zpad.h[:, 0:2], 0.0), writes=[zpad.v(0, 128, 0, 2)])
    s.op('dve', lambda e: e.memset(zpad.h[:, S + 2:S + 4], 0.0), writes=[zpad.v(0, 128, S + 2, S + 4)])
    acc = [s.sb('m_acc%d' % i, [128, 2048], F32) for i in range(2)]
    qT = s.sb('m_qT', [128, S], BF16)
    kT = s.sb('m_kT', [128, S], BF16)
    for (row0, cw, cb, dstT) in ((0, tb['m_cwq'], tb['m_cbq'], qT), (128, tb['m_cwk'], tb['m_cbk'], kT)):
        for src in range(4):
            s.dma('sp' if src % 2 else 'act', zpad.v(0, 128, 2 + 2048 * src, 2 + 2048 * src + 2048),
                  xf_d.v(src * XF_ROWS + row0, src * XF_ROWS + row0 + 128))
        for pc in range(4):
            a = acc[pc % 2]
            t0 = 2048 * pc
            s.ts(a.v(), zpad.v(0, 128, t0, t0 + 2048), cw.v(0, 128, 0, 1), ALU.mult)
            for k in range(1, 5):
                s.stt(a.v(), zpad.v(0, 128, t0 + k, t0 + k + 2048), cw.v(0, 128, k, k + 1), a.v(), ALU.mult, ALU.add)
            s.act(dstT.v(0, 128, t0, t0 + 2048), a.v(), AF.Silu, bias=cb.v())
    ktok = s.sb('m_ktok', [128, S], BF16)
    for g in range(NCH // 4):
        ps = next_ps(C)
        pb = lambda lo, hi: ps.v(fn=lambda a: a.bitcast(BF16)[:, lo:hi])
        for i in range(4):
            c = 4 * g + i
            s.tr(pb(128 * i, 128 * i + 128), kT.v(0, 128, 128 * c, 128 * c + 128), C.identb.v())
        s.copy(ktok.v(0, 128, 512 * g, 512 * g + 512), pb(0, 512), eng=s.evac_eng())
    for src in range(4):
        s.dma('sp' if src % 2 else 'act', zpad.v(0, 128, 2 + 2048 * src, 2 + 2048 * src + 2048),
              xf_d.v(src * XF_ROWS + 256, src * XF_ROWS + 384))
    vaug = {d: s.sb('m_vaug_' + d, [128, NCH * 129], BF16) for d in 'fb'}
    for g in range(NCH // 4):
        ps = next_ps(C)
        for i in range(4):
            c = 4 * g + i
            s.tr(ps.v(0, 128, 128 * i, 128 * i + 128), zpad.v(0, 128, 2 + 128 * c, 2 + 128 * c + 128), C.ident.v())
        for i in range(4):
            c = 4 * g + i
            for d in 'fb':
                s.ts(vaug[d].v(0, 128, 129 * c, 129 * c + 128), ps.v(0, 128, 128 * i, 128 * i + 128), sm['w_' + d].v(0, 128, c, c + 1), ALU.mult)
    for d in 'fb':
        s.copy(vaug[d].v(fn=lambda a: a.rearrange('p (c e) -> p c e', e=129)[:, :, 128]), sm['w_' + d].v(), eng='pool')
    H = s.sb('m_H', [128, S], F32)
    P = {d: s.sb('m_P_' + d, [128, 129], F32) for d in 'fb'}
    Cb = {d: [s.sb('m_Cb_%s%d' % (d, i), [128, 129], BF16) for i in range(2)] for d in 'fb'}
    Sm = {d: [s.sb('m_Sm_%s%d' % (d, i), [128, 128], BF16) for i in range(3)] for d in 'fb'}
    den = {d: [s.sb('m_den_%s%d' % (d, i), [128, 1], F32) for i in range(3)] for d in 'fb'}
    mask = {'f': tb['m_maskf'], 'b': tb['m_maskb']}
    def dir_gen(d):
        for step in range(NCH):
            c = step if d == 'f' else NCH - 1 - step
            cprev = c - 1 if d == 'f' else c + 1
            va = vaug[d].v(0, 128, 129 * c, 129 * c + 129)
            if step < NCH - 1:
                ps_d = next_ps(C)
                s.mm(ps_d.v(0, 128, 0, 129), ktok.v(0, 128, 128 * c, 128 * c + 128), va)
                if step == 0:
                    s.copy(P[d].v(), ps_d.v(0, 128, 0, 129))
                else:
                    s.stt(P[d].v(), P[d].v(), sm['eg_' + d].v(0, 128, cprev, cprev + 1), ps_d.v(0, 128, 0, 129), ALU.mult, ALU.add)
                yield
                s.act(Cb[d][(step + 1) % 2].v(), P[d].v(), AF.Copy, scale=sm['egs_' + d].v(0, 128, c, c + 1))
                yield
            ps_s = next_ps(C)
            s.mm(ps_s.v(0, 128, 0, 128), kT.v(0, 128, 128 * c, 128 * c + 128), qT.v(0, 128, 128 * c, 128 * c + 128))
            smt = Sm[d][step % 3]
            s.tt(smt.v(), ps_s.v(0, 128, 0, 128), mask[d].v(), ALU.mult)
            yield
            ps_o = next_ps(C)
            s.mm(ps_o.v(0, 128, 0, 129), smt.v(), va, start=True, stop=(step == 0))
            if step > 0:
                s.mm(ps_o.v(0, 128, 0, 129), qT.v(0, 128, 128 * c, 128 * c + 128), Cb[d][step % 2].v(), start=False, stop=True)
            dn = den[d][step % 3]
            s.ts(dn.v(), ps_o.v(0, 128, 128, 129), -1.0, ALU.mult, sm['enb_' + d].v(0, 128, c, c + 1), ALU.max)
            yield
            s.tt(dn.v(), dn.v(), ps_o.v(0, 128, 128, 129), ALU.max)
            yield
            s.op('dve', lambda e, dn=dn: e.reciprocal(out=dn.h[:, :], in_=dn.h[:, :]), reads=[dn.v()], writes=[dn.v()])
            yield
            hv = H.v(0, 128, 128 * c, 128 * c + 128)
            if step < NCH // 2:
                s.act(hv, ps_o.v(0, 128, 0, 128), AF.Copy, scale=dn.v())
            else:
                s.stt(hv, ps_o.v(0, 128, 0, 128), dn.v(), hv, ALU.mult, ALU.add)
            yield

    gens = [dir_gen('f'), dir_gen('b')]
    while gens:
        for gnr in list(gens):
            try:
                next(gnr)
            except StopIteration:
                gens.remove(gnr)
    if getattr(C, 'dbg', None) is not None:
        s.dma('pool', C.dbg['d_q'].v(), qT.v())
        s.dma('pool', C.dbg['d_k'].v(), kT.v())
        s.dma('sp', C.dbg['d_H'].v(), H.v())
        for i, nm in enumerate(['lf_f', 'b_f', 'w_f', 'enb_f', 'eg_f', 'lf_b', 'b_b', 'w_b']):
            s.dma('sp', C.dbg['d_sm'].v(0, 128, 64 * i, 64 * i + 64), sm[nm].v())
        s.dma('pool', C.dbg['d_va'].v(), vaug['f'].v())
    H3 = lambda lo, hi: H.v(0, 128, 128 * lo, 128 * hi, fn=lambda a: a.rearrange('p (c e) -> p c e', e=128))
    mu = sm['tmp']
    s.op('dve', lambda e: e.tensor_reduce(out=mu.h[:, :], in_=H.h[:, :].rearrange('p (c e) -> p c e', e=128), axis=AX.X, op=ALU.add),
         reads=[H.v()], writes=[mu.v()])
    s.ts(mu.v(), mu.v(), 1.0 / 128, ALU.mult)
    bc = lambda t, lo, hi: t.v(0, 128, lo, hi, fn=lambda a: a.unsqueeze(2).to_broadcast([128, hi - lo, 128]))
    var = sm['lf_f']
    sq = zpad
    for pc in range(4):
        lo, hi = 16 * pc, 16 * pc + 16
        s.tt(H3(lo, hi), H3(lo, hi), bc(mu, lo, hi), ALU.subtract)
        s.act(sq.v(0, 128, 0, 2048), H.v(0, 128, 128 * lo, 128 * hi), AF.Square)
        s.op('dve', lambda e, lo=lo, hi=hi: e.tensor_reduce(out=var.h[:, lo:hi], in_=sq.h[:, 0:2048].rearrange('p (c e) -> p c e', e=128),
                                                           axis=AX.X, op=ALU.add),
             reads=[sq.v(0, 128, 0, 2048)], writes=[var.v(0, 128, lo, hi)])
    s.act(var.v(), var.v(), AF.Sqrt, bias=C.epsb.v(), scale=1.0 / 128)
    s.op('dve', lambda e: e.reciprocal(out=var.h[:, :], in_=var.h[:, :]), reads=[var.v()], writes=[var.v()])
    for pc in range(4):
        lo, hi = 16 * pc, 16 * pc + 16
        s.tt(H3(lo, hi), H3(lo, hi), bc(var, lo, hi), ALU.mult, eng='pool' if pc % 2 else 'dve')
    ost = acc
    for g in range(NCH // 4):
        ps = next_ps(C)
        for i in range(4):
            c = 4 * g + i
            s.tr(ps.v(0, 128, 128 * i, 128 * i + 128), H.v(0, 128, 128 * c, 128 * c + 128), C.ident.v())
        st = ost[g % 2]
        s.copy(st.v(0, 128, 0, 512), ps.v(), eng=s.evac_eng())
        dst = g // 4
        col = 512 * (g % 4)
        s.dma('sp' if g % 2 else 'act', xb_d.v(dst * XB_ROWS, dst * XB_ROWS + 128, col, col + 512), st.v(0, 128, 0, 512))


def u_extra_inputs(d, l, c):
    rp = c % 4
    m = {}
    m.update(mlstm_tables())
    cw = d['conv_w'][l]
    cb = d['conv_b'][l]
    m['m_cwq'] = np.ascontiguousarray(cw[:, 128 * rp:128 * rp + 128].T)
    m['m_cwk'] = np.ascontiguousarray(cw[:, 512 + 128 * rp:512 + 128 * rp + 128].T)
    m['m_cbq'] = np.ascontiguousarray(cb[128 * rp:128 * rp + 128].reshape(128, 1))
    m['m_cbk'] = np.ascontiguousarray(cb[512 + 128 * rp:512 + 128 * rp + 128].reshape(128, 1))
    m.update(s5_tables())
    m.update(s5_inputs(d, l, c))
    return m


NTAU = 152


def s5_tables():
    n72 = np.arange(72.0)
    n8 = np.arange(8.0)
    n64 = np.arange(64.0)
    dpow = 64.0 * 2.0 ** np.arange(7)
    tf = np.concatenate([n72 - 7, 7 - n8, 63 - n64, dpow, [1.0]])
    tbk = np.concatenate([64 - n72, n8, n64, dpow, [1.0]])
    tau = np.tile(np.concatenate([tf, tbk])[None, :], (128, 1))
    p = np.arange(128)
    imask = np.eye(128)
    jmask = np.roll(np.eye(128), 64, axis=1)
    blk = p // 16
    bmf = (blk[:, None] <= blk[None, :]) * 1.0
    bmb = (blk[:, None] >= blk[None, :]) * 1.0
    sg = np.concatenate([-np.ones(64), np.ones(64)])[:, None]
    sel = np.zeros((64, 4, 8, 128))
    for g in range(4):
        for i in range(8):
            for c in range(16):
                sel[16 * g + c, g, i, 16 * i + c] = 1.0
    selT = sel.transpose(3, 1, 2, 0).reshape(128, 4 * 8 * 64)
    tabs = dict(s_tau=tau, s_imask=imask, s_jmask=jmask, s_bmf=bmf, s_bmb=bmb, s_sg=sg, s_nsg=-sg,
                s_sel=sel.reshape(64, 4096), s_selT=selT)
    return {k: np.ascontiguousarray(v.astype(np.float32)) for k, v in tabs.items()}


ST_SHAPES = dict(s_tau=[128, 2 * NTAU], s_imask=[128, 128], s_jmask=[128, 128], s_bmf=[128, 128], s_bmb=[128, 128],
                 s_sg=[128, 1], s_nsg=[128, 1], s_sel=[64, 4096], s_selT=[128, 2048],
                 s_lam=[8 * 128, 2], s_logdt=[8 * 128, 1], s_X1=[8 * 128, 16], s_X2=[8 * 128, 16], s_Y1=[8 * 128, 16],
                 s_Y2=[8 * 128, 16], s_d=[4 * 128, 1])


def s5_inputs(d, l, c):
    rp = c % 4
    lam = np.zeros((8, 128, 2), np.float32)
    logdt = np.zeros((8, 128, 1), np.float32)
    X1 = np.zeros((8, 128, 16), np.float32)
    X2 = np.zeros((8, 128, 16), np.float32)
    Y1 = np.zeros((8, 128, 16), np.float32)
    Y2 = np.zeros((8, 128, 16), np.float32)
    dd = np.zeros((4, 128, 1), np.float32)
    for gl in range(4):
        g = 4 * rp + gl
        dd[gl, :, 0] = np.tile(d['s5_d'][l, g], 8)
        for di in range(2):
            u = 2 * gl + di
            lam[u, :, 0] = np.tile(d['s5_lam_re'][l, di, g], 2)
            lam[u, :, 1] = np.tile(d['s5_lam_im'][l, di, g], 2)
            logdt[u, :, 0] = d['s5_log_dt'][l, di, g]
            bre, bim = d['s5_b_re'][l, di, g], d['s5_b_im'][l, di, g]
            cre, cim = d['s5_c_re'][l, di, g].T, d['s5_c_im'][l, di, g].T
            X1[u] = np.concatenate([bre, bim], 0)
            X2[u] = np.concatenate([bim, bre], 0)
            Y1[u] = np.concatenate([cre, cim], 0)
            Y2[u] = np.concatenate([cim, cre], 0)
    return dict(s_lam=lam.reshape(1024, 2), s_logdt=logdt.reshape(1024, 1), s_X1=X1.reshape(1024, 16), s_X2=X2.reshape(1024, 16),
                s_Y1=Y1.reshape(1024, 16), s_Y2=Y2.reshape(1024, 16), s_d=dd.reshape(512, 1))


def phase_U_s5(s, C, xf_d, xb_d, sd):
    TWO_PI = 2.0 * np.pi
    cst = {}
    for k in ['s_tau', 's_imask', 's_jmask', 's_bmf', 's_bmb', 's_sg', 's_nsg']:
        cst[k] = s.sb(k, ST_SHAPES[k], F32)
        s.dma('sp', cst[k].v(), sd[k].v())
    sel = s.sb('s_sel', [64, 4096], BF16)
    s.dma('pool', sel.v(), sd['s_sel'].v())
    selT = s.sb('s_selT', [128, 2048], BF16)
    s.dma('pool', selT.v(), sd['s_selT'].v())
    zsb = s.sb('s_zsb', [64, S], BF16)
    for src in range(4):
        s.dma('pool', zsb.v(0, 64, 2048 * src, 2048 * src + 2048), xf_d.v(src * XF_ROWS + 452, src * XF_ROWS + 516))
    U8 = [s.sb('s_U8_%d' % g, [128, 1024], BF16) for g in range(4)]
    for g in range(4):
        for half in range(2):
            ps = next_ps(C)
            for i in range(8):
                rhs = V(zsb, zsb.h[0:64, 4096 * half + i:4096 * (half + 1):8], (0, 64, 4096 * half, 4096 * (half + 1)))
                s.mm(ps.v(0, 128, 0, 512), sel.v(0, 64, 128 * (8 * g + i), 128 * (8 * g + i) + 128), rhs, start=(i == 0), stop=(i == 7))
            s.copy(U8[g].v(0, 128, 512 * half, 512 * half + 512), ps.v(), eng=s.evac_eng())
    NS = 4

    def mk(nm, shape, dt=F32):
        return [s.sb('s_%s_%d' % (nm, b), shape, dt) for b in range(NS)]
    lamt_, ldt_ = mk('lamt', [128, 2]), mk('ldt', [128, 1])
    v1_ = {nm: mk(nm, [128, 1]) for nm in ['dt', 'lr', 'th', 'den', 'am1', 't1', 't2', 'fre', 'fim', 'fis', 'frs', 'dvec']}
    X1_, X2_, Y1_, Y2_ = [mk(nm, [128, 16]) for nm in ['X1', 'X2', 'Y1', 'Y2']]
    BA_, BB_, CA_, CB_, t16_ = [mk(nm, [128, 16]) for nm in ['BA', 'BB', 'CA', 'CB', 't16']]
    mag_, ang_, kf_, al_, be_ = [mk(nm, [128, NTAU]) for nm in ['mag', 'ang', 'kf', 'al', 'be']]
    ki_ = mk('ki', [128, NTAU], mybir.dt.int32)
    bsn_ = mk('bsn', [128, 7])
    Gt_, Ht_, tG_ = mk('G', [128, 72 * 16]), mk('H', [128, 72 * 16]), mk('tG', [128, 72 * 16])
    Gb_ = mk('Gb', [128, 72 * 16], BF16)
    Tm_, Wm_ = mk('T', [128, 1024], BF16), mk('W', [128, 1024], BF16)
    Dm_ = mk('D', [128, 7 * 128], BF16)
    tmp128_ = mk('tmp128', [128, 128])
    Xf_, Xb_, Xp_ = mk('Xf', [128, 128]), mk('Xb', [128, 128], BF16), mk('Xp', [128, 130], BF16)
    for b in range(NS):
        s.op('dve', lambda e, b=b: e.memset(Xp_[b].h[:, :], 0.0), writes=[Xp_[b].v()])
    Y8 = [s.sb('s_Y8_%d' % g, [128, 1024], BF16) for g in range(4)]
    gx = [s.sb('s_gx%d' % i, [128, 512], F32) for i in range(6)]

    def col(t, i):
        return t.v(0, 128, i, i + 1)

    def unit_gen(g, d):
        b = 2 * (g % 2) + d
        lamt, ldt = lamt_[b], ldt_[b]
        v1 = {k: v[b] for k, v in v1_.items()}
        X1, X2, Y1, Y2 = X1_[b], X2_[b], Y1_[b], Y2_[b]
        BA, BB, CA, CB, t16 = BA_[b], BB_[b], CA_[b], CB_[b], t16_[b]
        mag, ang, kf, al, be, ki, bsn = mag_[b], ang_[b], kf_[b], al_[b], be_[b], ki_[b], bsn_[b]
        Gt, Ht, tG, Gb, Tm, Wm, Dm, tmp128 = Gt_[b], Ht_[b], tG_[b], Gb_[b], Tm_[b], Wm_[b], Dm_[b], tmp128_[b]
        Xf, Xb, Xp = Xf_[b], Xb_[b], Xp_[b]
        u = 2 * g + d
        r0 = 128 * u
        s.dma('sp', lamt.v(), sd['s_lam'].v(r0, r0 + 128))
        s.dma('act', ldt.v(), sd['s_logdt'].v(r0, r0 + 128))
        s.dma('sp', X1.v(), sd['s_X1'].v(r0, r0 + 128))
        s.dma('act', X2.v(), sd['s_X2'].v(r0, r0 + 128))
        s.dma('sp', Y1.v(), sd['s_Y1'].v(r0, r0 + 128))
        s.dma('act', Y2.v(), sd['s_Y2'].v(r0, r0 + 128))
        if d == 0:
            s.dma('sp', v1['dvec'].v(), sd['s_d'].v(128 * g, 128 * g + 128))
        yield
        lre, lim = col(lamt, 0), col(lamt, 1)
        s.act(v1['dt'].v(), ldt.v(), AF.Exp); yield
        s.tt(v1['lr'].v(), lre, v1['dt'].v(), ALU.mult); yield
        s.tt(v1['th'].v(), lim, v1['dt'].v(), ALU.mult); yield
        tau = cst['s_tau'].v(0, 128, NTAU * d, NTAU * d + NTAU)
        s.ts(ang.v(), tau, v1['th'].v(), ALU.mult); yield
        s.ts(kf.v(), ang.v(), 1.0 / TWO_PI, ALU.mult); yield
        s.copy(ki.v(), kf.v()); yield
        s.copy(kf.v(), ki.v()); yield
        s.stt(ang.v(), kf.v(), -TWO_PI, ang.v(), ALU.mult, ALU.add); yield

        def wrap(dst, y):
            s.ts(mag.v(), y.v(), float(np.pi), ALU.is_gt, -TWO_PI, ALU.mult); yield
            s.tt(dst.v(), mag.v(), y.v(), ALU.add); yield
            s.ts(mag.v(), y.v(), -float(np.pi), ALU.is_lt, TWO_PI, ALU.mult); yield
            s.tt(dst.v(), dst.v(), mag.v(), ALU.add); yield
        yield from wrap(kf, ang)
        s.act(be.v(), kf.v(), AF.Sin); yield
        s.ts(ang.v(), ang.v(), float(np.pi / 2), ALU.add); yield
        yield from wrap(kf, ang)
        s.act(al.v(), kf.v(), AF.Sin); yield
        s.act(mag.v(), tau, AF.Exp, scale=v1['lr'].v()); yield
        s.tt(al.v(), al.v(), mag.v(), ALU.mult); yield
        s.tt(be.v(), be.v(), mag.v(), ALU.mult); yield
        a1, b1 = col(al, NTAU - 1), col(be, NTAU - 1)
        s.tt(v1['t1'].v(), lre, lre, ALU.mult); yield
        s.stt(v1['den'].v(), lim, lim, v1['t1'].v(), ALU.mult, ALU.add); yield
        s.op('dve', lambda e: e.reciprocal(out=v1['den'].h[:, :], in_=v1['den'].h[:, :]), reads=[v1['den'].v()], writes=[v1['den'].v()]); yield
        s.ts(v1['am1'].v(), a1, -1.0, ALU.add); yield
        s.tt(v1['t1'].v(), v1['am1'].v(), lre, ALU.mult); yield
        s.stt(v1['t1'].v(), b1, lim, v1['t1'].v(), ALU.mult, ALU.add); yield
        s.tt(v1['fre'].v(), v1['t1'].v(), v1['den'].v(), ALU.mult); yield
        s.tt(v1['t2'].v(), v1['am1'].v(), lim, ALU.mult); yield
        s.stt(v1['t2'].v(), b1, lre, v1['t2'].v(), ALU.mult, ALU.subtract); yield
        s.tt(v1['fim'].v(), v1['t2'].v(), v1['den'].v(), ALU.mult); yield
        s.tt(v1['fis'].v(), v1['fim'].v(), cst['s_sg'].v(), ALU.mult); yield
        s.tt(v1['frs'].v(), v1['fre'].v(), cst['s_sg'].v(), ALU.mult); yield
        s.ts(t16.v(), X1.v(), v1['fre'].v(), ALU.mult); yield
        s.stt(BA.v(), X2.v(), v1['fis'].v(), t16.v(), ALU.mult, ALU.add); yield
        s.ts(t16.v(), X1.v(), v1['fim'].v(), ALU.mult); yield
        s.stt(BB.v(), X2.v(), v1['frs'].v(), t16.v(), ALU.mult, ALU.subtract); yield
        s.ts(CA.v(), Y1.v(), cst['s_nsg'].v(), ALU.mult); yield
        s.ts(CB.v(), Y2.v(), -1.0, ALU.mult); yield
        b3 = lambda t: t.v(fn=lambda a: a.unsqueeze(1).to_broadcast([128, 72, 16]))
        a3 = lambda t, lo: t.v(0, 128, lo, lo + 72, fn=lambda a: a.unsqueeze(2).to_broadcast([128, 72, 16]))
        r3 = lambda t: t.v(fn=lambda a: a.rearrange('p (n c) -> p n c', c=16))
        s.tt(r3(Gt), b3(CA), a3(al, 0), ALU.mult); yield
        s.tt(r3(tG), b3(CB), a3(be, 0), ALU.mult); yield
        s.tt(Gt.v(), Gt.v(), tG.v(), ALU.add, eng='pool'); yield
        s.tt(r3(Ht), b3(BA), a3(al, 72), ALU.mult); yield
        s.tt(r3(tG), b3(BB), a3(be, 72), ALU.mult); yield
        s.tt(Ht.v(), Ht.v(), tG.v(), ALU.add, eng='pool'); yield
        s.copy(Gb.v(), Gt.v(), eng='act'); yield
        s.ts(bsn.v(), be.v(0, 128, 144, 151), cst['s_nsg'].v(), ALU.mult); yield
        for m in range(7):
            s.ts(tmp128.v(), cst['s_imask'].v(), col(al, 144 + m), ALU.mult, eng='pool'); yield
            s.stt(Dm.v(0, 128, 128 * m, 128 * m + 128), cst['s_jmask'].v(), col(bsn, m), tmp128.v(), ALU.mult, ALU.add); yield
        for dl in range(8):
            gs = 128 * dl if d == 0 else 128 * (8 - dl)
            ps = next_ps(C)
            s.mm(ps.v(0, 128, 0, 128), Ht.v(0, 128, 0, 128), Gt.v(0, 128, gs, gs + 128))
            tv = Tm.v(0, 128, 128 * dl, 128 * dl + 128)
            if dl == 0:
                bm = cst['s_bmf'] if d == 0 else cst['s_bmb']
                if d == 0:
                    s.tt(tmp128.v(), ps.v(0, 128, 0, 128), bm.v(), ALU.mult); yield
                    s.stt(tv, cst['s_imask'].v(), v1['dvec'].v(), tmp128.v(), ALU.mult, ALU.add)
                else:
                    s.tt(tv, ps.v(0, 128, 0, 128), bm.v(), ALU.mult)
            else:
                s.copy(tv, ps.v(0, 128, 0, 128), eng=s.evac_eng())
            yield
        for hh in range(2):
            ps = next_ps(C)
            for i in range(4):
                ih = 4 * hh + i
                s.tr(ps.v(0, 128, 128 * i, 128 * i + 128), Ht.v(0, 128, 128 + 128 * ih, 128 + 128 * ih + 128), C.ident.v())
            s.copy(Wm.v(0, 128, 512 * hh, 512 * hh + 512), ps.v(), eng=s.evac_eng()); yield
        ps = next_ps(C)
        for ih in range(8):
            rhs = V(U8[g], U8[g].h[:, ih::8], (0, 128, 0, 1024))
            s.mm(ps.v(0, 128, 0, 128), Wm.v(0, 128, 128 * ih, 128 * ih + 128), rhs, start=(ih == 0), stop=(ih == 7))
        s.copy(Xf.v(), ps.v(0, 128, 0, 128), eng='dve')
        s.copy(Xb.v(), ps.v(0, 128, 0, 128), eng='act'); yield
        for m in range(7):
            sh = 2 ** m
            ps = next_ps(C)
            if d == 0:
                s.mm(ps.v(0, 128, 0, 128 - sh), Dm.v(0, 128, 128 * m, 128 * m + 128), Xb.v(0, 128, 0, 128 - sh))
                xv = Xf.v(0, 128, sh, 128)
            else:
                s.mm(ps.v(0, 128, 0, 128 - sh), Dm.v(0, 128, 128 * m, 128 * m + 128), Xb.v(0, 128, sh, 128))
                xv = Xf.v(0, 128, 0, 128 - sh)
            s.tt(xv, xv, ps.v(0, 128, 0, 128 - sh), ALU.add); yield
            if m < 6:
                s.copy(Xb.v(), Xf.v(), eng='act'); yield
        s.copy(Xp.v(0, 128, 1, 129), Xf.v(), eng='act'); yield

    def out_gen(g):
        bf, bb = 2 * (g % 2), 2 * (g % 2) + 1
        for hh in range(2):
            ps = next_ps(C)
            for q in range(4):
                jh = 4 * hh + q
                o = ps.v(0, 128, 128 * q, 128 * q + 128)
                first = True
                for ih in range(jh + 1):
                    rhs = V(U8[g], U8[g].h[:, ih::8], (0, 128, 0, 1024))
                    s.mm(o, Tm_[bf].v(0, 128, 128 * (jh - ih), 128 * (jh - ih) + 128), rhs, start=first, stop=False)
                    first = False
                for ih in range(jh, 8):
                    rhs = V(U8[g], U8[g].h[:, ih::8], (0, 128, 0, 1024))
                    s.mm(o, Tm_[bb].v(0, 128, 128 * (ih - jh), 128 * (ih - jh) + 128), rhs, start=False, stop=False)
                s.mm(o, Gb_[bf].v(0, 128, 128 * (jh + 1), 128 * (jh + 1) + 128), Xp_[bf].v(0, 128, 0, 128), start=False, stop=False)
                s.mm(o, Gb_[bb].v(0, 128, 128 * jh, 128 * jh + 128), Xp_[bb].v(0, 128, 2, 130), start=False, stop=True)
            k3 = 3 * (g % 2)
            x_, t_, u_ = gx[k3], gx[k3 + 1], gx[k3 + 2]
            s.copy(x_.v(), ps.v(), eng='dve'); yield
            s.act(t_.v(), x_.v(), AF.Square); yield
            s.ts(t_.v(), t_.v(), 0.044715, ALU.mult, 1.0, ALU.add, eng='pool'); yield
            s.tt(t_.v(), t_.v(), x_.v(), ALU.mult, eng='pool'); yield
            s.act(u_.v(), t_.v(), AF.Sigmoid, scale=1.5957691216057308); yield
            ov = Y8[g].v(fn=lambda a, hh=hh: a.rearrange('p (k j) -> p j k', j=8)[:, 4 * hh:4 * hh + 4, :])
            s.tt(ov, u_.v(fn=lambda a: a.rearrange('p (j k) -> p j k', k=128)), x_.v(fn=lambda a: a.rearrange('p (j k) -> p j k', k=128)), ALU.mult); yield

    def run_interleaved(gens):
        gens = list(gens)
        while gens:
            for gnr in list(gens):
                try:
                    next(gnr)
                except StopIteration:
                    gens.remove(gnr)

    for gp in range(2):
        run_interleaved([unit_gen(2 * gp + gg, d) for gg in range(2) for d in range(2)])
        run_interleaved([out_gen(2 * gp + gg) for gg in range(2)])
    yst = [s.sb('s_yst%d' % i, [64, 512], F32) for i in range(2)]
    for tb_ in range(16):
        ps = next_ps(C)
        for j in range(8):
            for g in range(4):
                s.mm(ps.v(0, 64, 64 * j, 64 * j + 64), selT.v(0, 128, 64 * (8 * g + j), 64 * (8 * g + j) + 64),
                     Y8[g].v(0, 128, 64 * tb_, 64 * tb_ + 64), start=(g == 0), stop=(g == 3))
        st = yst[tb_ % 2]
        s.copy(st.v(fn=lambda a: a.rearrange('p (k j) -> p j k', j=8)), ps.v(0, 64, 0, 512, fn=lambda a: a.rearrange('p (j k) -> p j k', k=64)),
               eng=s.evac_eng())
        dst = tb_ // 4
        cc = 512 * (tb_ % 4)
        s.dma('sp' if tb_ % 2 else 'act', xb_d.v(dst * XB_ROWS + 192, dst * XB_ROWS + 256, cc, cc + 512), st.v())


def _col(a):
    return np.ascontiguousarray(np.asarray(a, np.float32).reshape(-1, 1))


def _c(a):
    return np.ascontiguousarray(np.asarray(a, np.float32))


def _layer_weights(inp, l):
    return dict(w_in=_c(inp['w_in'][l]), b_in=_col(inp['b_in'][l]), g_pre=_col(inp['g_mix_pre'][l]),
                norm_g=_col(inp['mlstm_norm_g'][l]), w_up_m=_c(inp['w_up_mlstm'][l]), w_up_f=_c(inp['w_up_fourier'][l]),
                w_glu=_c(inp['w_glu'][l]), b_glu=_col(inp['b_glu'][l]), w_out=_c(inp['w_out'][l]), g_post=_col(inp['g_mix_post'][l]),
                g_ffn_pre=_col(inp['g_ffn_pre'][l]), g_ffn_post=_col(inp['g_ffn_post'][l]), w_ffn1=_c(inp['w_ffn1'][l]),
                w_ffn2=_c(inp['w_ffn2'][l]))


def kernel_unfused(**inputs):
    inp = {k: np.asarray(v) for k, v in inputs.items()}
    ident = np.eye(128, dtype=np.float32)
    cores = list(range(8))
    xs = [_c(inp['x'][c // 4, TL * (c % 4):TL * (c % 4 + 1)]) for c in cores]
    ftab = fourier_tables()
    for l in range(2):
        Wl = _layer_weights(inp, l)
        nc = build_T1()
        maps = [dict(x=xs[c], w_in=Wl['w_in'], b_in=Wl['b_in'], g_pre=Wl['g_pre'], ident=ident) for c in cores]
        res = run_bass_kernel_spmd(nc, maps, core_ids=cores)
        xf_sent = [np.asarray(res.results[c]['xf']).reshape(4, XF_ROWS, TL) for c in cores]
        maps = []
        for c in cores:
            b, r = c // 4, c % 4
            xf = np.stack([xf_sent[4 * b + src][r] for src in range(4)], 0).reshape(4 * XF_ROWS, TL)
            m = dict(xf=np.ascontiguousarray(xf), ident=ident)
            m.update(ftab)
            m.update(u_extra_inputs(inp, l, c))
            maps.append(m)
        nc = build_U(('fourier', 'mlstm', 's5'))
        res = run_bass_kernel_spmd(nc, maps, core_ids=cores)
        xb_sent = [np.asarray(res.results[c]['xb']).reshape(4, XB_ROWS, TL) for c in cores]
        maps = []
        for c in cores:
            b, r = c // 4, c % 4
            xb = np.stack([xb_sent[4 * b + src][r] for src in range(4)], 0).reshape(4 * XB_ROWS, TL)
            m = dict(x=xs[c], xb=np.ascontiguousarray(xb), ident=ident)
            m.update(Wl)
            maps.append(m)
        nc = build_T2()
        res = run_bass_kernel_spmd(nc, maps, core_ids=cores)
        xs = [np.asarray(res.results[c]['x_out']) for c in cores]
    out = np.zeros((2, S, D), np.float32)
    for c in cores:
        out[c // 4, TL * (c % 4):TL * (c % 4 + 1)] = xs[c]
    return out


class BlockT:
    def __init__(self, parent, bases, B):
        self.parent, self.bases, self.B = parent, bases, B
        self.space = parent.space

    def v(self, p0=0, p1=None, f0=0, f1=None, fn=None):
        if p1 is None:
            p1 = self.B * len(self.bases)
        blk = p0 // self.B
        assert (p1 - 1) // self.B == blk, (p0, p1, self.B)
        off = self.bases[blk] - blk * self.B
        return self.parent.v(off + p0, off + p1, f0, f1, fn)


W_SHAPES = [('w_in', [D, N_IN]), ('b_in', [N_IN, 1]), ('g_pre', [D, 1]), ('norm_g', [512, 1]), ('w_up_m', [512, D]),
            ('w_up_f', [256, D]), ('w_glu', [256, 2 * D]), ('b_glu', [2 * D, 1]), ('w_out', [D, D]), ('g_post', [D, 1]),
            ('g_ffn_pre', [D, 1]), ('g_ffn_post', [D, 1]), ('w_ffn1', [D, 4 * D]), ('w_ffn2', [4 * D, D])]
M_CONST = ['m_trif', 'm_trib', 'm_maskf', 'm_maskb', 'm_onesf']
M_PARAM = ['m_cwq', 'm_cwk', 'm_cbq', 'm_cbk']
S_CONST = ['s_tau', 's_imask', 's_jmask', 's_bmf', 's_bmb', 's_sg', 's_nsg', 's_sel', 's_selT']
S_PARAM = ['s_lam', 's_logdt', 's_X1', 's_X2', 's_Y1', 's_Y2', 's_d']


def fused_T1(s, C, x_src, W, xf_view):
    mk = s.mark()
    load_xT(s, C, x_src, TL)
    gcols = load_vec_cols(s, C, 'gpre', W['g_pre'], 8)
    rstd = rms_rstd(s, C, C.xT, TL, 'pre')
    hT = make_h(s, C, gcols, rstd, TL, 'hT')
    phase_T1_proj(s, C, hT, W['w_in'], W['b_in'], xf_view)
    s.release(mk)


def fused_T2(s, C, x_src, W, xb_view, out_view, NT=1024):
    mk = s.mark()
    C.NT = NT
    C.sqb = [s.sb('sqb%d' % i, [128, 512], BF16) for i in range(2)]
    C.rstd2 = s.sb('rstd2', [128, NT], F32)
    gcols = load_vec_cols(s, C, 'gpre', W['g_pre'], 8)
    t2_setup(s, C, W)
    C.xT = [s.sb('xT%d' % k, [128, NT], F32) for k in range(8)]
    C.hT = [s.sb('hT%d' % k, [128, NT], BF16) for k in range(8)]
    C.xoff = 0
    xstage = [s.sb('xst%d' % i, [128, 1024], F32) for i in range(4)]
    for hf in range(TL // NT):
        tok0 = hf * NT
        for tg in range(NT // 512):
            for i in range(4):
                tt = tg * 4 + i
                s.dma('sp' if i % 2 == 0 else 'act', xstage[i].v(), x_src.v(tok0 + 128 * tt, tok0 + 128 * tt + 128))
            for kt in range(8):
                ps = ps6(C)
                for i in range(4):
                    s.tr(ps.v(0, 128, 128 * i, 128 * i + 128), xstage[i].v(0, 128, 128 * kt, 128 * kt + 128), C.ident.v())
                s.copy(C.xT[kt].v(0, 128, 512 * tg, 512 * tg + 512), ps.v(), eng=s.evac_eng())
        for tc in range(NT // 512):
            ps = ps6(C)
            for kt in range(8):
                q = C.sqb[kt % 2]
                s.act(q.v(), C.xT[kt].v(0, 128, 512 * tc, 512 * tc + 512), AF.Square)
                s.mm(ps.v(), C.ones.v(), q.v(), start=(kt == 0), stop=(kt == 7))
            r = C.rstd2.v(0, 128, 512 * tc, 512 * tc + 512)
            s.act(r, ps.v(), AF.Sqrt, bias=C.epsb.v(), scale=1.0 / D)
            s.op('dve', lambda e, r=r: e.reciprocal(out=r.ap, in_=r.ap), reads=[r], writes=[r])
        for kt in range(8):
            s.stt(C.hT[kt].v(), C.xT[kt].v(), gcols.v(0, 128, kt, kt + 1), C.rstd2.v(), ALU.mult, ALU.mult)
        phase_T2(s, C, W, xb_view, tok0)
        store_xT(s, C, out_view, tok0, NT, xstage)
    s.release(mk)


def fused_U(s, C, xf_view, xb_view, tabs_d, md, sd):
    mk = s.mark()
    phase_U_fourier(s, C, xf_view, xb_view, tabs_d)
    s.release(mk)
    phase_U_s5(s, C, xf_view, xb_view, sd)
    s.release(mk)
    phase_U_mlstm(s, C, xf_view, xb_view, md)
    s.release(mk)


def build_fused(n_layers=2):
    nc = bass.Bass("TRN2", target_bir_lowering=False)
    s = Sched(nc)
    C = Ctx()
    x_d = s.dram('x', [S, D], F32, 'ExternalInput')
    id_d = s.dram('ident', [128, 128], F32, 'ExternalInput')
    out_d = s.dram('out', [S, D], F32, 'ExternalOutput')
    Wl = [{n: s.dram('%s_l%d' % (n, l), sh, F32, 'ExternalInput') for n, sh in W_SHAPES} for l in range(n_layers)]
    tabs_d = {k: s.dram(k, v, F32, 'ExternalInput') for k, v in FT_SHAPES.items()}
    mconst = {k: s.dram(k, MT_SHAPES[k], F32, 'ExternalInput') for k in M_CONST}
    sconst = {k: s.dram(k, ST_SHAPES[k], F32, 'ExternalInput') for k in S_CONST}
    mpar = [{k: s.dram('%s_l%d' % (k, l), [4 * MT_SHAPES[k][0], MT_SHAPES[k][1]], F32, 'ExternalInput') for k in M_PARAM}
            for l in range(n_layers)]
    spar = [{k: s.dram('%s_l%d' % (k, l), [4 * ST_SHAPES[k][0], ST_SHAPES[k][1]], F32, 'ExternalInput') for k in S_PARAM}
            for l in range(n_layers)]
    xf_all = s.dram('xf_all', [16 * XF_ROWS, TL], F32, 'Internal')
    xb_all = s.dram('xb_all', [16 * XB_ROWS, TL], F32, 'Internal')
    xs1 = s.dram('xs1', [S, D], F32, 'Internal')
    common_setup(s, C, id_d)
    C.epsb = s.sb('epsb', [128, 1], F32)
    s.op('dve', lambda e: e.memset(C.epsb.h[:, :], EPS), writes=[C.epsb.v()])
    C.oneb = s.sb('oneb', [128, 1], F32)
    s.op('dve', lambda e: e.memset(C.oneb.h[:, :], 1.0), writes=[C.oneb.v()])
    for l in range(n_layers):
        xsrc = x_d if l == 0 else xs1
        xdst = out_d if l == n_layers - 1 else xs1
        Wf = Wl[l]
        t1cols = [(OFF_Q + 128 * i, 128) for i in range(4)] + [(OFF_K + 128 * i, 128) for i in range(4)] + \
                 [(OFF_V + 128 * i, 128) for i in range(4)] + [(OFF_G16, 16)] + [(OFF_F + 128 * i, 128) for i in range(2)] + \
                 [(OFF_S5 + 128 * i, 128) for i in range(2)]
        t2cols = [(OFF_O + 128 * i, 128) for i in range(4)] + [(OFF_GATE + 1024 * b + 128 * j, 128) for j in range(8) for b in range(3)]
        pw = dict(
            w_in=PW(s, 'pw_in_l%d' % l, Wf['w_in'], 8, t1cols + t2cols),
            w_up_m=PW(s, 'pw_um_l%d' % l, Wf['w_up_m'], 4, [(128 * j, 128) for j in range(8)]),
            w_up_f=PW(s, 'pw_uf_l%d' % l, Wf['w_up_f'], 2, [(128 * j, 128) for j in range(8)]),
            w_glu=PW(s, 'pw_glu_l%d' % l, Wf['w_glu'], 2, [(128 * j, 128) for j in range(16)]),
            w_out=PW(s, 'pw_out_l%d' % l, Wf['w_out'], 8, [(128 * j, 128) for j in range(8)]),
            w_ffn1=PW(s, 'pw_f1_l%d' % l, Wf['w_ffn1'], 8, [(512 * j, 512) for j in range(8)]),
            w_ffn2=PW(s, 'pw_f2_l%d' % l, Wf['w_ffn2'], 32, [(128 * j, 128) for j in range(8)]))
        for i in range(len(t1cols) + 4):
            pw['w_in'].prep(s, i)
        for j in range(8):
            pw['w_up_m'].prep(s, j)
            pw['w_up_f'].prep(s, j)
            pw['w_glu'].prep(s, j)
            pw['w_glu'].prep(s, 8 + j)
            for b in range(3):
                pw['w_in'].prep(s, len(t1cols) + 4 + 3 * j + b)
        for nm in ['w_out', 'w_ffn1', 'w_ffn2']:
            for j in range(8):
                pw[nm].prep(s, j)
        Wp = dict(Wf)
        Wp.update(pw)
        Wl[l] = Wp
        for q in range(4):
            fused_T1(s, C, BlockT(xsrc, [TL * q], TL), Wl[l], BlockT(xf_all, [(dst * 4 + q) * XF_ROWS for dst in range(4)], XF_ROWS))
        for u in range(4):
            md = dict(mconst)
            md.update({k: BlockT(mpar[l][k], [u * MT_SHAPES[k][0]], MT_SHAPES[k][0]) for k in M_PARAM})
            sd = dict(sconst)
            sd.update({k: BlockT(spar[l][k], [u * ST_SHAPES[k][0]], ST_SHAPES[k][0]) for k in S_PARAM})
            fused_U(s, C, BlockT(xf_all, [(u * 4 + src) * XF_ROWS for src in range(4)], XF_ROWS),
                    BlockT(xb_all, [(dst * 4 + u) * XB_ROWS for dst in range(4)], XB_ROWS), tabs_d, md, sd)
        for q in range(4):
            fused_T2(s, C, BlockT(xsrc, [TL * q], TL), Wl[l], BlockT(xb_all, [(q * 4 + src) * XB_ROWS for src in range(4)], XB_ROWS),
                     BlockT(xdst, [TL * q], TL))
    s.finish()
    return nc


def fused_inputs(inp, b, n_layers=2):
    m = dict(x=_c(inp['x'][b]), ident=np.eye(128, dtype=np.float32))
    m.update(fourier_tables())
    mt = mlstm_tables()
    m.update({k: mt[k] for k in M_CONST})
    st = s5_tables()
    m.update({k: st[k] for k in S_CONST})
    for l in range(n_layers):
        for k, v in _layer_weights(inp, l).items():
            m['%s_l%d' % (k, l)] = v
        units = [u_extra_inputs(inp, l, u) for u in range(4)]
        for k in M_PARAM + S_PARAM:
            m['%s_l%d' % (k, l)] = np.ascontiguousarray(np.concatenate([units[u][k] for u in range(4)], 0))
    return m


def kernel(**inputs):
    inp = {k: np.asarray(v) for k, v in inputs.items()}
    nc = build_fused()
    per_batch = [fused_inputs(inp, b) for b in range(2)]
    maps = [per_batch[c // 4] for c in range(8)]
    res = run_bass_kernel_spmd(nc, maps, core_ids=list(range(8)))
    out = np.stack([np.asarray(res.results[0]['out']), np.asarray(res.results[4]['out'])], 0)
    return out.astype(np.float32)
```

```python
import numpy as np
import concourse.bass as bass
import concourse.mybir as mybir
from concourse.bass_utils import run_bass_kernel_spmd

F32 = mybir.dt.float32
BF16 = mybir.dt.bfloat16
AF = mybir.ActivationFunctionType
ALU = mybir.AluOpType
AX = mybir.AxisListType

ENGS = ['pe', 'act', 'pool', 'dve', 'sp']
DMA_POOL = 6
SAME_ENGINE_SYNC = True
SEM_EPOCH = 16000

D = 1024
TL = 2048
S = 8192
EPS = 1e-6
N_IN = 5648
XF_ROWS = 516
XB_ROWS = 256


class T:
    def __init__(self, name, handle, shape, space):
        self.name, self.h, self.shape, self.space = name, handle, shape, space
        self.recs = []
        self.track = True

    def v(self, p0=0, p1=None, f0=0, f1=None, fn=None):
        P = self.shape[0]
        F = int(np.prod(self.shape[1:]))
        p1 = P if p1 is None else p1
        f1 = F if f1 is None else f1
        ap = self.h[p0:p1, f0:f1]
        if fn is not None:
            ap = fn(ap)
        if self.space == 'psum':
            reg = (0, 128, 0, 1 << 30)
        else:
            reg = (p0, p1, f0, f1)
        return V(self, ap, reg)


class V:
    def __init__(self, t, ap, reg):
        self.t, self.ap, self.reg = t, ap, reg

    def f(self, fn):
        return V(self.t, fn(self.ap), self.reg)


def _ov(a, b):
    return a[0] < b[1] and b[0] < a[1] and a[2] < b[3] and b[2] < a[3]


def _cov(a, b):
    return a[0] <= b[0] and a[1] >= b[1] and a[2] <= b[2] and a[3] >= b[3]


class Sched:
    def __init__(self, nc):
        self.nc = nc
        self.ops = {e: [] for e in ENGS}
        self.cnt = {e: 0 for e in ENGS}
        self.waited = {e: {} for e in ENGS}
        self.dma_n = {e: 0 for e in ENGS}
        self.dma_last = {}
        self.ctx = []
        self.rr = 0
        self.epoch = {}
        self.final = {}
        self.dma_ep = {}
        self.prep_n = 0

    def mark(self):
        return len(self.ctx)

    def barrier(self):
        deps = dict(self.final)
        deps.update({k: v for k, v in self.dma_last.items() if k[2] < 100})
        for e in ENGS:
            w = self._waits(e, dict(deps))
            if w:
                self.ops[e].append((w, None, None, 0))

    def release(self, mark):
        self.barrier()
        while len(self.ctx) > mark:
            self.ctx.pop().__exit__(None, None, None)

    def sb(self, name, shape, dt):
        self.uid = getattr(self, 'uid', 0) + 1
        cm = self.nc.sbuf_tensor('sb%d_%s' % (self.uid, name), list(shape), dt)
        h = cm.__enter__()
        self.ctx.append(cm)
        return T(name, h, shape, 'sbuf')

    def ps(self, name, shape=(128, 512), dt=F32):
        cm = self.nc.psum_tensor('pp_' + name, list(shape), dt)
        h = cm.__enter__()
        self.ctx.append(cm)
        return T(name, h, shape, 'psum')

    def dram(self, name, shape, dt, kind):
        h = self.nc.dram_tensor(name, list(shape), dt, kind=kind)
        t = T(name, h.ap(), shape, 'dram')
        t.track = (kind == 'Internal')
        return t

    def _deps(self, reads, writes):
        deps = {}
        reads = [v for v in reads if v.t.track]
        writes = [v for v in writes if v.t.track]
        for vw in reads:
            for r in vw.t.recs:
                if r[0] and _ov(r[1:5], vw.reg) and deps.get(r[5], 0) < r[6]:
                    deps[r[5]] = r[6]
        for vw in writes:
            for r in vw.t.recs:
                if _ov(r[1:5], vw.reg) and deps.get(r[5], 0) < r[6]:
                    deps[r[5]] = r[6]
        return deps

    def _record(self, reads, writes, semkey, val):
        reads = [v for v in reads if v.t.track]
        writes = [v for v in writes if v.t.track]
        for vw in writes:
            t = vw.t
            reg = tuple(vw.reg)
            t.recs = [r for r in t.recs if not _cov(reg, r[1:5])]
            t.recs.append((True,) + reg + (semkey, val))
        for vw in reads:
            t = vw.t
            reg = tuple(vw.reg)
            t.recs = [r for r in t.recs if r[0] or r[5] != semkey or r[1:5] != reg]
            t.recs.append((False,) + reg + (semkey, val))

    def _waits(self, eng, deps):
        w = []
        for k, v in deps.items():
            if k[0] == 'c' and k[1] == eng and (eng == 'pe' or not SAME_ENGINE_SYNC):
                continue
            if self.waited[eng].get(k, 0) >= v:
                continue
            self.waited[eng][k] = v
            w.append((k, v))
        return w

    def op(self, eng, fn, reads=(), writes=()):
        writes = list(writes) + [v for v in reads if v.t.space == 'psum']
        reads = [v for v in reads if v.t.space != 'psum']
        deps = self._deps(reads, writes)
        waits = self._waits(eng, deps)
        if self.cnt[eng] >= SEM_EPOCH:
            self.epoch[eng] = self.epoch.get(eng, 0) + 1
            self.cnt[eng] = 0
        self.cnt[eng] += 1
        key = ('c', eng, self.epoch.get(eng, 0))
        self.final[key] = self.cnt[eng]
        self._record(reads, writes, key, self.cnt[eng])
        self.ops[eng].append((waits, fn, key, 1))

    def dma(self, eng, out, in_, prep=False, **kw):
        deps = self._deps([in_], [out])
        if prep:
            j = self.prep_n
            self.prep_n += 1
            slot = 100 + j % DMA_POOL
        else:
            j = self.dma_n[eng]
            self.dma_n[eng] += 1
            slot = j % DMA_POOL
        if self.dma_last.get(('d', eng, slot, self.dma_ep.get((eng, slot), 0)), 0) >= SEM_EPOCH:
            self.dma_ep[(eng, slot)] = self.dma_ep.get((eng, slot), 0) + 1
        key = ('d', eng, slot, self.dma_ep.get((eng, slot), 0))
        prev = self.dma_last.get(key, 0)
        if not prev and self.dma_ep.get((eng, slot), 0) > 0:
            pk = ('d', eng, slot, self.dma_ep[(eng, slot)] - 1)
            deps[pk] = max(deps.get(pk, 0), self.dma_last[pk])
        if prev:
            deps[key] = max(deps.get(key, 0), prev)
        waits = self._waits(eng, deps)
        val = prev + 16
        self.dma_last[key] = val
        self._record([in_], [out], key, val)
        oa, ia = out.ap, in_.ap
        self.ops[eng].append((waits, lambda e: e.dma_start(out=oa, in_=ia, **kw), key, 16))

    def finish(self):
        waits = self._waits('sp', dict(self.dma_last))
        self.ops['sp'].append((waits, None, None, 0))
        nc = self.nc
        semkeys = list(self.final.keys()) + list(self.dma_last.keys())
        sems = {}
        cms = []
        for k in semkeys:
            cm = nc.semaphore('s_' + '_'.join(str(x) for x in k))
            sems[k] = cm.__enter__()
            cms.append(cm)
        with nc.Block() as block:
            deco = dict(pe=block.tensor, act=block.scalar, pool=block.gpsimd, dve=block.vector, sp=block.sync)
            for e in ENGS:
                ops = self.ops[e]

                def body(engine, ops=ops):
                    for waits, fn, key, inc in ops:
                        for k, v in waits:
                            engine.wait_ge(sems[k], v)
                        if fn is not None:
                            fn(engine).then_inc(sems[key], inc)
                if ops:
                    deco[e](body)
        for cm in reversed(cms):
            cm.__exit__(None, None, None)
        for cm in reversed(self.ctx):
            cm.__exit__(None, None, None)

    def mm(self, out, lhsT, rhs, start=True, stop=True):
        self.op('pe', lambda e: e.matmul(out.ap, lhsT.ap, rhs.ap, start=start, stop=stop),
                reads=[lhsT, rhs], writes=[out])

    def tr(self, out, in_, ident):
        self.op('pe', lambda e: e.transpose(out.ap, in_.ap, ident.ap), reads=[in_, ident], writes=[out])

    def act(self, out, in_, func, bias=None, scale=None, eng='act'):
        kw = {}
        rd = [in_]
        if bias is not None:
            if isinstance(bias, V):
                kw['bias'] = bias.ap
                rd.append(bias)
            else:
                kw['bias'] = bias
        if scale is not None:
            if isinstance(scale, V):
                kw['scale'] = scale.ap
                rd.append(scale)
            else:
                kw['scale'] = scale
        self.op('act', lambda e: e.activation(out=out.ap, in_=in_.ap, func=func, **kw), reads=rd, writes=[out])

    def tt(self, out, in0, in1, op, eng='dve'):
        self.op(eng, lambda e: e.tensor_tensor(out=out.ap, in0=in0.ap, in1=in1.ap, op=op), reads=[in0, in1], writes=[out])

    def ts(self, out, in0, s1, op0, s2=None, op1=None, eng='dve'):
        rd = [in0]
        a1 = s1
        a2 = s2
        if isinstance(s1, V):
            rd.append(s1)
            a1 = s1.ap
        if isinstance(s2, V):
            rd.append(s2)
            a2 = s2.ap
        if op1 is None:
            self.op(eng, lambda e: e.tensor_scalar(out=out.ap, in0=in0.ap, scalar1=a1, scalar2=None, op0=op0), reads=rd, writes=[out])
        else:
            self.op(eng, lambda e: e.tensor_scalar(out=out.ap, in0=in0.ap, scalar1=a1, scalar2=a2, op0=op0, op1=op1), reads=rd, writes=[out])

    def stt(self, out, in0, sc, in1, op0, op1):
        rd = [in0, in1]
        a = sc
        if isinstance(sc, V):
            rd.append(sc)
            a = sc.ap
        self.op('dve', lambda e: e.scalar_tensor_tensor(out=out.ap, in0=in0.ap, scalar=a, in1=in1.ap, op0=op0, op1=op1),
                reads=rd, writes=[out])

    def copy(self, out, in_, eng='dve'):
        if eng == 'act':
            self.op('act', lambda e: e.activation(out=out.ap, in_=in_.ap, func=AF.Identity), reads=[in_], writes=[out])
        else:
            self.op(eng, lambda e: e.tensor_copy(out=out.ap, in_=in_.ap), reads=[in_], writes=[out])

    def evac_eng(self):
        self.rr += 1
        return 'act' if self.rr % 2 else 'dve'


class Ctx:
    pass


OFF_Q, OFF_K, OFF_V, OFF_O, OFF_G16, OFF_F, OFF_S5, OFF_GATE = 0, 512, 1024, 1536, 2048, 2064, 2320, 2576


def common_setup(s, C, ident_d):
    C.ident = s.sb('ident', [128, 128], F32)
    s.dma('sp', C.ident.v(), ident_d.v())
    C.identb = s.sb('identb', [128, 128], BF16)
    s.copy(C.identb.v(), C.ident.v())
    C.ones = s.sb('ones', [128, 128], BF16)
    s.op('dve', lambda e: e.memset(C.ones.h[:, :], 1.0), writes=[C.ones.v()])
    C.ps = [s.ps('ps%d' % i) for i in range(8)]
    C.psi = 0


def next_ps(C):
    C.psi = (C.psi + 1) % 8
    return C.ps[C.psi]


def load_xT(s, C, x_d, ntok):
    C.xT = [s.sb('xT%d' % k, [128, ntok], F32) for k in range(8)]
    stage = [s.sb('xst%d' % i, [128, 1024], F32) for i in range(8)]
    for tg in range(ntok // 512):
        for i in range(4):
            tt = tg * 4 + i
            s.dma('sp' if i % 2 == 0 else 'act', stage[(tg % 2) * 4 + i].v(), x_d.v(128 * tt, 128 * tt + 128))
        for kt in range(8):
            ps = next_ps(C)
            for i in range(4):
                s.tr(ps.v(0, 128, 128 * i, 128 * i + 128), stage[(tg % 2) * 4 + i].v(0, 128, 128 * kt, 128 * kt + 128), C.ident.v())
            s.copy(C.xT[kt].v(0, 128, 512 * tg, 512 * tg + 512), ps.v(), eng=s.evac_eng())


def rms_rstd(s, C, src, ntok, name):
    rstd = s.sb('rstd_' + name, [128, ntok], F32)
    sq = [s.sb('sq_%s%d' % (name, i), [128, 512], BF16) for i in range(2)]
    n = 0
    for tc in range(ntok // 512):
        ps = next_ps(C)
        for kt in range(8):
            q = sq[n % 2]
            n += 1
            s.act(q.v(), src[kt].v(0, 128, 512 * tc, 512 * tc + 512), AF.Square)
            s.mm(ps.v(), C.ones.v(), q.v(), start=(kt == 0), stop=(kt == 7))
        r = rstd.v(0, 128, 512 * tc, 512 * tc + 512)
        s.act(r, ps.v(), AF.Sqrt, bias=C.epsb.v(), scale=1.0 / D)
        s.op('dve', lambda e, r=r: e.reciprocal(out=r.ap, in_=r.ap), reads=[r], writes=[r])
    return rstd


def load_vec_cols(s, C, name, d_t, n):
    t = s.sb(name, [128, n], F32)
    for k in range(n):
        s.dma('sp', t.v(0, 128, k, k + 1), d_t.v(128 * k, 128 * k + 128))
    return t


def make_h(s, C, gcols, rstd, ntok, name):
    hT = [s.sb('%s%d' % (name, k), [128, ntok], BF16) for k in range(8)]
    for kt in range(8):
        s.stt(hT[kt].v(), C.xT[kt].v(), gcols.v(0, 128, kt, kt + 1), rstd.v(), ALU.mult, ALU.mult)
    return hT


class PW:
    def __init__(self, s, name, w_d, K, specs):
        self.w_d, self.K, self.specs = w_d, K, specs
        width = K * max(n for _, n in specs)
        self.t = s.dram(name, [len(specs) * 128, width], BF16, 'Internal')
        self.idx = {c0: i for i, (c0, n) in enumerate(specs)}

    def prep(self, s, i):
        c0, ncols = self.specs[i]
        K = self.K
        out = self.t.v(i * 128, i * 128 + 128, 0, K * ncols, fn=lambda a: a.rearrange('p (k n) -> p k n', k=K))
        in_ = self.w_d.v(0, K * 128, c0, c0 + ncols, fn=lambda a: a.rearrange('(k p) n -> p k n', p=128))
        s.dma('pool', out, in_, prep=True)


def load_w_cols(s, C, wt, w_d, K, c0, ncols, q='pool'):
    if isinstance(w_d, PW):
        i = w_d.idx[c0]
        s.dma('sp', wt.v(0, 128, 0, K * ncols), w_d.t.v(i * 128, i * 128 + 128, 0, K * ncols))
        return
    out = wt.v(0, 128, 0, K * ncols, fn=lambda a: a.rearrange('p (k n) -> p k n', k=K))
    in_ = w_d.v(0, K * 128, c0, c0 + ncols, fn=lambda a: a.rearrange('(k p) n -> p k n', p=128))
    s.dma(q, out, in_)


def build_T1():
    nc = bass.Bass("TRN2", target_bir_lowering=False)
    s = Sched(nc)
    C = Ctx()
    x_d = s.dram('x', [TL, D], F32, 'ExternalInput')
    w_d = s.dram('w_in', [D, N_IN], F32, 'ExternalInput')
    b_d = s.dram('b_in', [N_IN, 1], F32, 'ExternalInput')
    g_d = s.dram('g_pre', [D, 1], F32, 'ExternalInput')
    id_d = s.dram('ident', [128, 128], F32, 'ExternalInput')
    xf_d = s.dram('xf', [4 * XF_ROWS, TL], F32, 'ExternalOutput')
    common_setup(s, C, id_d)
    C.epsb = s.sb('epsb', [128, 1], F32)
    s.op('dve', lambda e: e.memset(C.epsb.h[:, :], EPS), writes=[C.epsb.v()])
    load_xT(s, C, x_d, TL)
    gcols = load_vec_cols(s, C, 'gpre', g_d, 8)
    rstd = rms_rstd(s, C, C.xT, TL, 'pre')
    hT = make_h(s, C, gcols, rstd, TL, 'hT')
    phase_T1_proj(s, C, hT, w_d, b_d, xf_d)
    s.finish()
    return nc


def phase_T1_proj(s, C, hT, w_d, b_d, xf_d):
    tiles = []
    for i in range(4):
        tiles.append((OFF_Q + 128 * i, 128, [(0, 128, i, 0)]))
        tiles.append((OFF_K + 128 * i, 128, [(0, 128, i, 128)]))
        tiles.append((OFF_V + 128 * i, 128, [(0, 128, i, 256)]))
    tiles.append((OFF_G16, 16, [(0, 16, None, 384)]))
    for i in range(2):
        tiles.append((OFF_F + 128 * i, 128, [(0, 64, 2 * i, 388), (64, 64, 2 * i + 1, 388)]))
        tiles.append((OFF_S5 + 128 * i, 128, [(0, 64, 2 * i, 452), (64, 64, 2 * i + 1, 452)]))
    wts = [s.sb('wT1_%d' % i, [128, 8 * 128], BF16) for i in range(2)]
    bts = [s.sb('bT1_%d' % i, [128, 1], F32) for i in range(2)]
    stg = [s.sb('zst%d' % i, [128, 512], F32) for i in range(4)]
    n = 0
    for ti, (c0, ncols, dsts) in enumerate(tiles):
        wt = wts[ti % 2]
        bt = bts[ti % 2]
        load_w_cols(s, C, wt, w_d, 8, c0, ncols)
        s.dma('sp', bt.v(0, ncols), b_d.v(c0, c0 + ncols))
        for tc in range(TL // 512):
            ps = next_ps(C)
            for kt in range(8):
                s.mm(ps.v(0, ncols), wt.v(0, 128, kt * ncols, (kt + 1) * ncols), hT[kt].v(0, 128, 512 * tc, 512 * tc + 512),
                     start=(kt == 0), stop=(kt == 7))
            st = stg[n % 4]
            n += 1
            s.act(st.v(0, ncols), ps.v(0, ncols), AF.Identity, bias=bt.v(0, ncols))
            for (r0, nr, dst, dr0) in dsts:
                if dst is None:
                    for dd in range(4):
                        o = xf_d.v(dd * XF_ROWS + dr0, dd * XF_ROWS + dr0 + 4, 512 * tc, 512 * tc + 512)
                        s.dma('sp' if dd % 2 else 'act', o, st.v(0, 16, fn=lambda a, dd=dd: a[dd::4, :]))
                else:
                    o = xf_d.v(dst * XF_ROWS + dr0, dst * XF_ROWS + dr0 + nr, 512 * tc, 512 * tc + 512)
                    s.dma('sp' if n % 2 else 'act', o, st.v(r0, r0 + nr))


class WPool:
    def __init__(self, s, name, ncols, n):
        self.b = [s.sb('%s%d' % (name, i), [128, ncols], BF16) for i in range(n)]
        self.i = 0

    def get(self):
        self.i += 1
        return self.b[self.i % len(self.b)]


def bias_cols(s, name, d_t, offs, n=128):
    t = s.sb(name, [128, len(offs)], F32)
    for k, o in enumerate(offs):
        s.dma('sp' if k % 2 else 'act', t.v(0, n, k, k + 1), d_t.v(o, o + n))
    return t


def store_xT(s, C, out_d, tok0, ntok, stage):
    for tt in range(ntok // 128):
        st = stage[tt % 2]
        for kg in range(2):
            ps = next_ps(C)
            for i in range(4):
                kt = kg * 4 + i
                s.tr(ps.v(0, 128, 128 * i, 128 * i + 128), C.xT[kt].v(0, 128, 128 * tt, 128 * tt + 128), C.ident.v())
            s.copy(st.v(0, 128, 512 * kg, 512 * kg + 512), ps.v(), eng=s.evac_eng())
        s.dma('sp' if tt % 2 else 'act', out_d.v(tok0 + 128 * tt, tok0 + 128 * tt + 128), st.v())


def t2_setup(s, C, W):
    NT = C.NT
    C.wp = WPool(s, 'wp', 1024, 8)
    C.wbig = WPool(s, 'wbig', 4096, 2)
    C.big = s.sb('big', [128, 32 * NT], BF16)
    C.h2T = s.sb('h2T', [128, 8 * NT], BF16)
    C.tmpf = [s.sb('tmpf%d' % i, [128, 512], F32) for i in range(6)]
    C.tmpi = 0
    C.hnst = [s.sb('hnst%d' % i, [128, NT], F32) for i in range(2)]
    C.b_o = bias_cols(s, 'b_o', W['b_in'], [OFF_O + 128 * i for i in range(4)])
    C.b_g = bias_cols(s, 'b_g', W['b_in'], [OFF_GATE + 128 * i for i in range(24)])
    C.b_glu = bias_cols(s, 'b_glu', W['b_glu'], [128 * i for i in range(16)])
    C.normg = bias_cols(s, 'normg', W['norm_g'], [128 * i for i in range(4)])
    C.g_post = bias_cols(s, 'g_post', W['g_post'], [128 * i for i in range(8)])
    C.g_f1 = bias_cols(s, 'g_f1', W['g_ffn_pre'], [128 * i for i in range(8)])
    C.g_f2 = bias_cols(s, 'g_f2', W['g_ffn_post'], [128 * i for i in range(8)])


def tmpf(C):
    C.tmpi += 1
    return C.tmpf[C.tmpi % len(C.tmpf)]


def ps6(C):
    C.psi = (C.psi + 1) % 6
    return C.ps[C.psi]


def big_view(C, tile, c0, c1):
    NT = C.NT
    return C.big.v(0, 128, tile * NT + c0, tile * NT + c1)


def phase_T2(s, C, W, xb_d, tok0):
    NT = C.NT
    NTC = NT // 512
    hT = C.hT
    MIX0, HG0, YF0, YS0, OB0 = 8, 16, 20, 22, 24

    def chunk(tc):
        return 512 * tc, 512 * tc + 512

    for (row0, dst0) in ((128, YF0), (192, YS0)):
        for half in range(2):
            st = C.hnst[half]
            for q in range(2):
                src = 2 * half + q
                s.dma('sp' if q else 'act', st.v(64 * q, 64 * q + 64),
                      xb_d.v(src * XB_ROWS + row0, src * XB_ROWS + row0 + 64, tok0, tok0 + NT))
            s.copy(big_view(C, dst0 + half, 0, NT), st.v(), eng='pool')
    for i in range(4):
        st = C.hnst[i % 2]
        s.dma('sp', st.v(), xb_d.v(i * XB_ROWS, i * XB_ROWS + 128, tok0, tok0 + NT))
        wt = C.wp.get()
        load_w_cols(s, C, wt, W['w_in'], 8, OFF_O + 128 * i, 128)
        for tc in range(NTC):
            c0, c1 = chunk(tc)
            ps = ps6(C)
            for kt in range(8):
                s.mm(ps.v(), wt.v(0, 128, 128 * kt, 128 * kt + 128), hT[kt].v(0, 128, c0, c1), start=(kt == 0), stop=(kt == 7))
            sg = tmpf(C)
            s.act(sg.v(), ps.v(), AF.Sigmoid, bias=C.b_o.v(0, 128, i, i + 1))
            s.stt(big_view(C, HG0 + i, c0, c1), st.v(0, 128, c0, c1), C.normg.v(0, 128, i, i + 1), sg.v(), ALU.mult, ALU.mult)
    for j in range(8):
        w_um = C.wp.get(); load_w_cols(s, C, w_um, W['w_up_m'], 4, 128 * j, 128)
        w_uf = C.wp.get(); load_w_cols(s, C, w_uf, W['w_up_f'], 2, 128 * j, 128)
        w_ga = C.wp.get(); load_w_cols(s, C, w_ga, W['w_glu'], 2, 128 * j, 128)
        w_gb = C.wp.get(); load_w_cols(s, C, w_gb, W['w_glu'], 2, 1024 + 128 * j, 128)
        w_g = []
        for b in range(3):
            w = C.wp.get(); load_w_cols(s, C, w, W['w_in'], 8, OFF_GATE + 1024 * b + 128 * j, 128)
            w_g.append(w)
        for tc in range(NTC):
            c0, c1 = chunk(tc)

            def gate(b):
                ps = ps6(C)
                for kt in range(8):
                    s.mm(ps.v(), w_g[b].v(0, 128, 128 * kt, 128 * kt + 128), hT[kt].v(0, 128, c0, c1), start=(kt == 0), stop=(kt == 7))
                g = tmpf(C)
                s.act(g.v(), ps.v(), AF.Sigmoid, bias=C.b_g.v(0, 128, 8 * b + j, 8 * b + j + 1))
                return g

            def small(wt, K, src0):
                ps = ps6(C)
                for kt in range(K):
                    s.mm(ps.v(), wt.v(0, 128, 128 * kt, 128 * kt + 128), big_view(C, src0 + kt, c0, c1), start=(kt == 0), stop=(kt == K - 1))
                return ps
            ps_ym = small(w_um, 4, HG0)
            g0 = gate(0)
            acc = tmpf(C)
            s.tt(acc.v(), g0.v(), ps_ym.v(), ALU.mult)
            ps_yf = small(w_uf, 2, YF0)
            g1 = gate(1)
            t1 = tmpf(C)
            s.tt(t1.v(), g1.v(), ps_yf.v(), ALU.mult)
            s.tt(acc.v(), acc.v(), t1.v(), ALU.add, eng='pool')
            ps_za = small(w_ga, 2, YS0)
            ps_zb = small(w_gb, 2, YS0)
            sb_ = tmpf(C)
            s.act(sb_.v(), ps_zb.v(), AF.Sigmoid, bias=C.b_glu.v(0, 128, 8 + j, 8 + j + 1))
            ys = tmpf(C)
            s.stt(ys.v(), ps_za.v(), C.b_glu.v(0, 128, j, j + 1), sb_.v(), ALU.add, ALU.mult)
            g2 = gate(2)
            s.tt(ys.v(), ys.v(), g2.v(), ALU.mult, eng='pool')
            s.tt(big_view(C, MIX0 + j, c0, c1), acc.v(), ys.v(), ALU.add)
    ss = [C.ps[6], C.ps[7]]
    sqb = [s_ for s_ in C.sqb]

    def proj_norm_add(wname, K, src_view, dst0_tile, dst_T, gcols, wpool, kcols):
        n = 0
        for j in range(8):
            wt = wpool.get()
            load_w_cols(s, C, wt, W[wname], K, 128 * j, 128)
            for tc in range(NTC):
                c0, c1 = chunk(tc)
                ps = ps6(C)
                for kt in range(K):
                    s.mm(ps.v(), wt.v(0, 128, 128 * kt, 128 * kt + 128), src_view(kt, c0, c1), start=(kt == 0), stop=(kt == K - 1))
                if dst_T is None:
                    ov = big_view(C, dst0_tile + j, c0, c1)
                else:
                    ov = dst_T.v(0, 128, j * NT + c0, j * NT + c1)
                s.copy(ov, ps.v(), eng='dve')
                q = sqb[n % 2]
                n += 1
                s.act(q.v(), ps.v(), AF.Square)
                s.mm(ss[tc].v(), C.ones.v(), q.v(), start=(j == 0), stop=(j == 7))
        for tc in range(NTC):
            c0, c1 = chunk(tc)
            r = C.rstd2.v(0, 128, c0, c1)
            s.act(r, ss[tc].v(), AF.Sqrt, bias=C.epsb.v(), scale=1.0 / D)
            s.op('dve', lambda e, r=r: e.reciprocal(out=r.ap, in_=r.ap), reads=[r], writes=[r])
        for j in range(8):
            for tc in range(NTC):
                c0, c1 = chunk(tc)
                if dst_T is None:
                    ov = big_view(C, dst0_tile + j, c0, c1)
                else:
                    ov = dst_T.v(0, 128, j * NT + c0, j * NT + c1)
                t = tmpf(C)
                s.stt(t.v(), ov, gcols.v(0, 128, j, j + 1), C.rstd2.v(0, 128, c0, c1), ALU.mult, ALU.mult)
                xv = C.xT[j].v(0, 128, C.xoff + c0, C.xoff + c1)
                s.tt(xv, xv, t.v(), ALU.add, eng='pool')

    proj_norm_add('w_out', 8, lambda kt, c0, c1: big_view(C, MIX0 + kt, c0, c1), OB0, None, C.g_post, C.wp, 128)
    for tc in range(NTC):
        c0, c1 = chunk(tc)
        ps = ps6(C)
        for kt in range(8):
            q = sqb[kt % 2]
            s.act(q.v(), C.xT[kt].v(0, 128, C.xoff + c0, C.xoff + c1), AF.Square)
            s.mm(ps.v(), C.ones.v(), q.v(), start=(kt == 0), stop=(kt == 7))
        r = C.rstd2.v(0, 128, c0, c1)
        s.act(r, ps.v(), AF.Sqrt, bias=C.epsb.v(), scale=1.0 / D)
        s.op('dve', lambda e, r=r: e.reciprocal(out=r.ap, in_=r.ap), reads=[r], writes=[r])
    for kt in range(8):
        s.stt(C.h2T.v(0, 128, kt * NT, kt * NT + NT), C.xT[kt].v(0, 128, C.xoff, C.xoff + NT), C.g_f1.v(0, 128, kt, kt + 1),
              C.rstd2.v(0, 128, 0, NT), ALU.mult, ALU.mult)
    for ng in range(8):
        wt = C.wbig.get()
        load_w_cols(s, C, wt, W['w_ffn1'], 8, 512 * ng, 512)
        for nn in range(4):
            n = 4 * ng + nn
            for tc in range(NTC):
                c0, c1 = chunk(tc)
                ps = ps6(C)
                for kt in range(8):
                    s.mm(ps.v(), wt.v(0, 128, 512 * kt + 128 * nn, 512 * kt + 128 * nn + 128), C.h2T.v(0, 128, kt * NT + c0, kt * NT + c1),
                         start=(kt == 0), stop=(kt == 7))
                t = tmpf(C)
                s.act(t.v(), ps.v(), AF.Relu)
                s.tt(big_view(C, n, c0, c1), t.v(), t.v(), ALU.mult, eng='pool' if (n + tc) % 2 else 'dve')
    proj_norm_add('w_ffn2', 32, lambda kt, c0, c1: big_view(C, kt, c0, c1), None, C.h2T, C.g_f2, C.wbig, 128)


def build_T2(NT=1024):
    nc = bass.Bass("TRN2", target_bir_lowering=False)
    s = Sched(nc)
    C = Ctx()
    C.NT = NT
    x_d = s.dram('x', [TL, D], F32, 'ExternalInput')
    xb_d = s.dram('xb', [4 * XB_ROWS, TL], F32, 'ExternalInput')
    W = {}
    for name, shape in [('w_in', [D, N_IN]), ('b_in', [N_IN, 1]), ('g_pre', [D, 1]), ('norm_g', [512, 1]), ('w_up_m', [512, D]),
                        ('w_up_f', [256, D]), ('w_glu', [256, 2 * D]), ('b_glu', [2 * D, 1]), ('w_out', [D, D]), ('g_post', [D, 1]),
                        ('g_ffn_pre', [D, 1]), ('g_ffn_post', [D, 1]), ('w_ffn1', [D, 4 * D]), ('w_ffn2', [4 * D, D])]:
        W[name] = s.dram(name, shape, F32, 'ExternalInput')
    id_d = s.dram('ident', [128, 128], F32, 'ExternalInput')
    out_d = s.dram('x_out', [TL, D], F32, 'ExternalOutput')
    common_setup(s, C, id_d)
    C.epsb = s.sb('epsb', [128, 1], F32)
    s.op('dve', lambda e: e.memset(C.epsb.h[:, :], EPS), writes=[C.epsb.v()])
    C.sqb = [s.sb('sqb%d' % i, [128, 512], BF16) for i in range(2)]
    C.rstd2 = s.sb('rstd2', [128, NT], F32)
    gcols = load_vec_cols(s, C, 'gpre', W['g_pre'], 8)
    t2_setup(s, C, W)
    C.xT = [s.sb('xT%d' % k, [128, NT], F32) for k in range(8)]
    C.hT = [s.sb('hT%d' % k, [128, NT], BF16) for k in range(8)]
    C.xoff = 0
    xstage = [s.sb('xst%d' % i, [128, 1024], F32) for i in range(4)]
    for hf in range(TL // NT):
        tok0 = hf * NT
        for tg in range(NT // 512):
            for i in range(4):
                tt = tg * 4 + i
                s.dma('sp' if i % 2 == 0 else 'act', xstage[i].v(), x_d.v(tok0 + 128 * tt, tok0 + 128 * tt + 128))
            for kt in range(8):
                ps = ps6(C)
                for i in range(4):
                    s.tr(ps.v(0, 128, 128 * i, 128 * i + 128), xstage[i].v(0, 128, 128 * kt, 128 * kt + 128), C.ident.v())
                s.copy(C.xT[kt].v(0, 128, 512 * tg, 512 * tg + 512), ps.v(), eng=s.evac_eng())
        for tc in range(NT // 512):
            ps = ps6(C)
            for kt in range(8):
                q = C.sqb[kt % 2]
                s.act(q.v(), C.xT[kt].v(0, 128, 512 * tc, 512 * tc + 512), AF.Square)
                s.mm(ps.v(), C.ones.v(), q.v(), start=(kt == 0), stop=(kt == 7))
            r = C.rstd2.v(0, 128, 512 * tc, 512 * tc + 512)
            s.act(r, ps.v(), AF.Sqrt, bias=C.epsb.v(), scale=1.0 / D)
            s.op('dve', lambda e, r=r: e.reciprocal(out=r.ap, in_=r.ap), reads=[r], writes=[r])
        for kt in range(8):
            s.stt(C.hT[kt].v(), C.xT[kt].v(), gcols.v(0, 128, kt, kt + 1), C.rstd2.v(), ALU.mult, ALU.mult)
        phase_T2(s, C, W, xb_d, tok0)
        store_xT(s, C, out_d, tok0, NT, xstage)
    s.finish()
    return nc


def fourier_tables():
    c = np.arange(64)
    a64 = 2 * np.pi * np.outer(c, c) / 64.0
    s1 = np.arange(128)
    a128 = 2 * np.pi * np.outer(s1, s1) / 128.0
    atw = 2 * np.pi * np.outer(s1, c) / 8192.0
    sc = 1.0 / np.sqrt(8192.0 * 64.0)
    z = np.zeros((64, 64))
    tabs = dict(
        f_f64=np.concatenate([np.cos(a64), -np.sin(a64)], 1),
        f_c128=np.cos(a128), f_s128=np.sin(a128), f_ns128=-np.sin(a128),
        f_tw=np.concatenate([np.cos(atw), -np.sin(atw)], 1),
        f_bdc=np.block([[np.cos(a64), z], [z, np.cos(a64)]]) * sc,
        f_bds=np.block([[np.sin(a64), z], [z, np.sin(a64)]]) * sc,
    )
    return {k: np.ascontiguousarray(v.astype(np.float32)) for k, v in tabs.items()}


FT_SHAPES = dict(f_f64=[64, 128], f_c128=[128, 128], f_s128=[128, 128], f_ns128=[128, 128], f_tw=[128, 128],
                 f_bdc=[128, 128], f_bds=[128, 128])


def phase_U_fourier(s, C, xf_d, xb_d, tabs_d):
    tb = {}
    for k in ['f_f64', 'f_c128', 'f_s128', 'f_ns128', 'f_bdc', 'f_bds']:
        tb[k] = s.sb(k, FT_SHAPES[k], BF16)
        s.dma('pool', tb[k].v(), tabs_d[k].v())
    tw = s.sb('f_tw', [128, 128], F32)
    s.dma('sp', tw.v(), tabs_d['f_tw'].v())
    UTb = s.sb('f_UTb', [64, S], BF16)
    for src in range(4):
        s.dma('pool', UTb.v(0, 64, 2048 * src, 2048 * src + 2048), xf_d.v(src * XF_ROWS + 388, src * XF_ROWS + 452))
    Zre = s.sb('f_Zre', [128, 4096], BF16)
    Zim = s.sb('f_Zim', [128, 4096], BF16)
    for g in range(16):
        ps = next_ps(C)
        for i in range(4):
            s2 = 4 * g + i
            lhsT = V(UTb, UTb.h[0:64, s2::64], (0, 64, 0, S))
            s.mm(ps.v(0, 128, 128 * i, 128 * i + 128), lhsT, tb['f_f64'].v())
        pv = lambda lo: ps.v(fn=lambda a: a.rearrange('p (s c) -> p s c', c=128)[:, :, lo:lo + 64])
        zv = lambda Zt: Zt.v(0, 128, 256 * g, 256 * g + 256, fn=lambda a: a.rearrange('p (s c) -> p s c', c=64))
        s.copy(zv(Zre), pv(0), eng='act')
        s.copy(zv(Zim), pv(64), eng='dve')
    ArP = s.sb('f_ArP', [128, 4096], BF16)
    AiP = s.sb('f_AiP', [128, 4096], BF16)
    tmp = [s.sb('f_tmp%d' % i, [128, 512], F32) for i in range(4)]
    for ch in range(8):
        c0, c1 = 512 * ch, 512 * ch + 512
        pr = next_ps(C)
        s.mm(pr.v(), tb['f_c128'].v(), Zre.v(0, 128, c0, c1), start=True, stop=False)
        s.mm(pr.v(), tb['f_s128'].v(), Zim.v(0, 128, c0, c1), start=False, stop=True)
        pi = next_ps(C)
        s.mm(pi.v(), tb['f_c128'].v(), Zim.v(0, 128, c0, c1), start=True, stop=False)
        s.mm(pi.v(), tb['f_ns128'].v(), Zre.v(0, 128, c0, c1), start=False, stop=True)
        r3 = lambda a: a.rearrange('p (s c) -> p s c', c=64)
        tre = tw.v(0, 128, 8 * ch, 8 * ch + 8, fn=lambda a: a.unsqueeze(2).to_broadcast([128, 8, 64]))
        tim = tw.v(0, 128, 64 + 8 * ch, 64 + 8 * ch + 8, fn=lambda a: a.unsqueeze(2).to_broadcast([128, 8, 64]))
        t = [x.v(fn=r3) for x in tmp]
        s.tt(t[0], pr.v(fn=r3), tre, ALU.mult)
        s.tt(t[1], pi.v(fn=r3), tim, ALU.mult)
        s.tt(t[2], pr.v(fn=r3), tim, ALU.mult)
        s.tt(t[3], pi.v(fn=r3), tre, ALU.mult)
        perm = lambda a: a.rearrange('p (c s) -> p s c', s=64)[:, 8 * ch:8 * ch + 8, :]
        s.tt(ArP.v(fn=perm), t[0], t[1], ALU.subtract, eng='pool')
        s.tt(AiP.v(fn=perm), t[2], t[3], ALU.add, eng='pool')
    ArT = s.sb('f_ArT', [128, 4096], BF16)
    AiT = s.sb('f_AiT', [128, 4096], BF16)
    for (src, dst) in ((ArP, ArT), (AiP, AiT)):
        for g in range(8):
            ps = next_ps(C)
            pb = lambda lo, hi: ps.v(fn=lambda a: a.bitcast(BF16)[:, lo:hi])
            for i in range(4):
                blk = 4 * g + i
                s.tr(pb(128 * i, 128 * i + 128), src.v(0, 128, 128 * blk, 128 * blk + 128), C.identb.v())
            s.copy(dst.v(0, 128, 512 * g, 512 * g + 512), pb(0, 512), eng=s.evac_eng())
    Y = s.sb('f_Y', [128, 4096], F32)
    for g in range(8):
        c0, c1 = 512 * g, 512 * g + 512
        ps = next_ps(C)
        s.mm(ps.v(), tb['f_bdc'].v(), ArT.v(0, 128, c0, c1), start=True, stop=False)
        s.mm(ps.v(), tb['f_bds'].v(), AiT.v(0, 128, c0, c1), start=False, stop=True)
        s.copy(Y.v(0, 128, c0, c1), ps.v(), eng=s.evac_eng())
    n = 0
    for cp in range(2):
        for dst in range(4):
            r0 = dst * XB_ROWS + 128 + cp
            o = xb_d.v(r0, r0 + 64, 0, TL, fn=lambda a: a[::2, :].rearrange('b (s j) -> s b j', j=128))
            i_ = Y.v(64 * cp + 16 * dst, 64 * cp + 16 * dst + 16, 0, 4096, fn=lambda a: a.rearrange('p (b j) -> p b j', j=128))
            s.dma('sp' if n % 2 else 'act', o, i_)
            n += 1


def build_U(parts=('fourier',)):
    nc = bass.Bass("TRN2", target_bir_lowering=False)
    s = Sched(nc)
    C = Ctx()
    xf_d = s.dram('xf', [4 * XF_ROWS, TL], F32, 'ExternalInput')
    id_d = s.dram('ident', [128, 128], F32, 'ExternalInput')
    xb_d = s.dram('xb', [4 * XB_ROWS, TL], F32, 'ExternalOutput')
    tabs_d = {k: s.dram(k, v, F32, 'ExternalInput') for k, v in FT_SHAPES.items()}
    md = {k: s.dram(k, v, F32, 'ExternalInput') for k, v in MT_SHAPES.items()}
    sd = {k: s.dram(k, v, F32, 'ExternalInput') for k, v in ST_SHAPES.items()}
    common_setup(s, C, id_d)
    C.epsb = s.sb('epsb', [128, 1], F32)
    s.op('dve', lambda e: e.memset(C.epsb.h[:, :], EPS), writes=[C.epsb.v()])
    C.oneb = s.sb('oneb', [128, 1], F32)
    s.op('dve', lambda e: e.memset(C.oneb.h[:, :], 1.0), writes=[C.oneb.v()])
    if 'dbg' in parts:
        C.dbg = {k: s.dram(k, sh, F32, 'ExternalOutput') for k, sh in [('d_q', [128, S]), ('d_k', [128, S]), ('d_H', [128, S]),
                                                                        ('d_sm', [128, 512]), ('d_va', [128, 64 * 129])]}
    mk = s.mark()
    if 'fourier' in parts:
        phase_U_fourier(s, C, xf_d, xb_d, tabs_d)
        s.release(mk)
    if 's5' in parts:
        phase_U_s5(s, C, xf_d, xb_d, sd)
        s.release(mk)
    if 'mlstm' in parts:
        phase_U_mlstm(s, C, xf_d, xb_d, md)
    s.finish()
    return nc


def mlstm_tables():
    j = np.arange(128)
    trif = (j[:, None] <= j[None, :]).astype(np.float32)
    sc = 128.0 ** -0.5
    return dict(m_trif=trif, m_trib=np.ascontiguousarray(trif.T), m_maskf=trif * sc, m_maskb=np.ascontiguousarray(trif.T) * sc,
                m_onesf=np.ones((128, 128), np.float32))


MT_SHAPES = dict(m_trif=[128, 128], m_trib=[128, 128], m_maskf=[128, 128], m_maskb=[128, 128], m_onesf=[128, 128],
                 m_cwq=[128, 5], m_cwk=[128, 5], m_cbq=[128, 1], m_cbk=[128, 1])


def phase_U_mlstm(s, C, xf_d, xb_d, md):
    NCH = S // 128
    SC = 128.0 ** -0.5
    tb = {}
    for k in MT_SHAPES:
        tb[k] = s.sb(k, MT_SHAPES[k], F32)
        s.dma('sp', tb[k].v(), md[k].v())
    G4 = s.sb('m_G4', [4, S], F32)
    for src in range(4):
        s.dma('act', G4.v(0, 4, 2048 * src, 2048 * src + 2048), xf_d.v(src * XF_ROWS + 384, src * XF_ROWS + 388))
    gT = s.sb('m_gT', [128, NCH * 4], F32)
    ps = next_ps(C)
    for c in range(NCH):
        s.tr(ps.v(0, 128, 4 * c, 4 * c + 4), G4.v(0, 4, 128 * c, 128 * c + 128), C.ident.v(0, 4, 0, 4))
    s.copy(gT.v(), ps.v(0, 128, 0, 4 * NCH))
    gcol = lambda g: gT.v(fn=lambda a: a.rearrange('p (c g) -> p c g', g=4)[:, :, g])
    sm = {}
    for nm in ['lf_f', 'lf_b', 'b_f', 'b_b', 'w_f', 'w_b', 'enb_f', 'enb_b', 'eg_f', 'eg_b', 'egs_f', 'egs_b', 'tmp']:
        sm[nm] = s.sb('m_' + nm, [128, NCH], F32)
    for d, gi in (('f', 2), ('b', 3)):
        lf = sm['lf_' + d]
        s.act(sm['tmp'].v(), gcol(gi), AF.Exp, scale=-1.0)
        s.act(lf.v(), sm['tmp'].v(), AF.Ln, bias=C.oneb.v(), scale=1.0)
        s.ts(lf.v(), lf.v(), -1.0, ALU.mult)
        pb = next_ps(C)
        s.mm(pb.v(0, 128, 0, NCH), tb['m_tri' + d].v(), lf.v())
        s.copy(sm['b_' + d].v(), pb.v(0, 128, 0, NCH))
        pg = next_ps(C)
        s.mm(pg.v(0, 128, 0, NCH), tb['m_onesf'].v(), lf.v())
        s.act(sm['eg_' + d].v(), pg.v(0, 128, 0, NCH), AF.Exp)
        s.ts(sm['egs_' + d].v(), sm['eg_' + d].v(), SC, ALU.mult)
        s.act(sm['enb_' + d].v(), sm['b_' + d].v(), AF.Exp, scale=-1.0)
        s.tt(sm['tmp'].v(), gcol(0 if d == 'f' else 1), sm['b_' + d].v(), ALU.subtract)
        s.act(sm['w_' + d].v(), sm['tmp'].v(), AF.Exp)
    zpb = s.sb('m_zpb', [128, S + 4], BF16)
    s.op('dve', lambda e: e.memset(zpb.h[:, 0:2], 0.0), writes=[zpb.v(0, 128, 0, 2)])
    s.op('dve', lambda e: e.memset(zpb.h[:, S + 2:S + 4], 0.0), writes=[zpb.v(0, 128, S + 2, S + 4)])
    acc0 = s.sb('m_acc0', [128, 2048], F32)
    vst = s.sb('m_vst', [128, 4096], F32)
    dg = s.sb('m_dg', [128, 10 * 128], BF16)
    qT = s.sb('m_qT', [128, S], BF16)
    kT = s.sb('m_kT', [128, S], BF16)
    for (qi, row0, cw, cb, dstT) in ((0, 0, tb['m_cwq'], tb['m_cbq'], qT), (1, 128, tb['m_cwk'], tb['m_cbk'], kT)):
        for k in range(5):
            s.ts(dg.v(0, 128, 128 * (5 * qi + k), 128 * (5 * qi + k) + 128), C.ident.v(), cw.v(0, 128, k, k + 1), ALU.mult)
        for src in range(4):
            s.dma('pool', zpb.v(0, 128, 2 + 2048 * src, 2 + 2048 * src + 2048),
                  xf_d.v(src * XF_ROWS + row0, src * XF_ROWS + row0 + 128))
        for ch in range(S // 512):
            ps = next_ps(C)
            for k in range(5):
                s.mm(ps.v(), dg.v(0, 128, 128 * (5 * qi + k), 128 * (5 * qi + k) + 128),
                     zpb.v(0, 128, 512 * ch + k, 512 * ch + k + 512), start=(k == 0), stop=(k == 4))
            s.act(dstT.v(0, 128, 512 * ch, 512 * ch + 512), ps.v(), AF.Silu, bias=cb.v())
    ktok = s.sb('m_ktok', [128, S], BF16)
    for g in range(NCH // 4):
        ps = next_ps(C)
        pb = lambda lo, hi: ps.v(fn=lambda a: a.bitcast(BF16)[:, lo:hi])
        for i in range(4):
            c = 4 * g + i
            s.tr(pb(128 * i, 128 * i + 128), kT.v(0, 128, 128 * c, 128 * c + 128), C.identb.v())
        s.copy(ktok.v(0, 128, 512 * g, 512 * g + 512), pb(0, 512), eng=s.evac_eng())
    vaug = {d: s.sb('m_vaug_' + d, [128, NCH * 129], BF16) for d in 'fb'}
    for g in range(NCH // 4):
        if g % 8 == 0:
            hv_ = g // 8
            for q2 in range(2):
                src = 2 * hv_ + q2
                s.dma('sp' if q2 else 'act', vst.v(0, 128, 2048 * q2, 2048 * q2 + 2048),
                      xf_d.v(src * XF_ROWS + 256, src * XF_ROWS + 384))
        ps = next_ps(C)
        for i in range(4):
            c = 4 * g + i
            cl = c % 32
            s.tr(ps.v(0, 128, 128 * i, 128 * i + 128), vst.v(0, 128, 128 * cl, 128 * cl + 128), C.ident.v())
        for i in range(4):
            c = 4 * g + i
            for d in 'fb':
                s.ts(vaug[d].v(0, 128, 129 * c, 129 * c + 128), ps.v(0, 128, 128 * i, 128 * i + 128), sm['w_' + d].v(0, 128, c, c + 1), ALU.mult)
    for d in 'fb':
        s.copy(vaug[d].v(fn=lambda a: a.rearrange('p (c e) -> p c e', e=129)[:, :, 128]), sm['w_' + d].v(), eng='pool')
    H = s.sb('m_H', [128, S], F32)
    P = {d: s.sb('m_P_' + d, [128, 129], F32) for d in 'fb'}
    Cb = {d: [s.sb('m_Cb_%s%d' % (d, i), [128, 129], BF16) for i in range(2)] for d in 'fb'}
    Sm = {d: [s.sb('m_Sm_%s%d' % (d, i), [128, 128], BF16) for i in range(3)] for d in 'fb'}
    den = {d: [s.sb('m_den_%s%d' % (d, i), [128, 1], F32) for i in range(3)] for d in 'fb'}
    mask = {'f': tb['m_maskf'], 'b': tb['m_maskb']}
    def dir_gen(d):
        for step in range(NCH):
            c = step if d == 'f' else NCH - 1 - step
            cprev = c - 1 if d == 'f' else c + 1
            va = vaug[d].v(0, 128, 129 * c, 129 * c + 129)
            if step < NCH - 1:
                ps_d = next_ps(C)
                s.mm(ps_d.v(0, 128, 0, 129), ktok.v(0, 128, 128 * c, 128 * c + 128), va)
                if step == 0:
                    s.copy(P[d].v(), ps_d.v(0, 128, 0, 129))
                else:
                    s.stt(P[d].v(), P[d].v(), sm['eg_' + d].v(0, 128, cprev, cprev + 1), ps_d.v(0, 128, 0, 129), ALU.mult, ALU.add)
                yield
                s.act(Cb[d][(step + 1) % 2].v(), P[d].v(), AF.Copy, scale=sm['egs_' + d].v(0, 128, c, c + 1))
                yield
            ps_s = next_ps(C)
            s.mm(ps_s.v(0, 128, 0, 128), kT.v(0, 128, 128 * c, 128 * c + 128), qT.v(0, 128, 128 * c, 128 * c + 128))
            smt = Sm[d][step % 3]
            s.tt(smt.v(), ps_s.v(0, 128, 0, 128), mask[d].v(), ALU.mult)
            yield
            ps_o = next_ps(C)
            s.mm(ps_o.v(0, 128, 0, 129), smt.v(), va, start=True, stop=(step == 0))
            if step > 0:
                s.mm(ps_o.v(0, 128, 0, 129), qT.v(0, 128, 128 * c, 128 * c + 128), Cb[d][step % 2].v(), start=False, stop=True)
            dn = den[d][step % 3]
            s.ts(dn.v(), ps_o.v(0, 128, 128, 129), -1.0, ALU.mult, sm['enb_' + d].v(0, 128, c, c + 1), ALU.max)
            yield
            s.tt(dn.v(), dn.v(), ps_o.v(0, 128, 128, 129), ALU.max)
            yield
            s.op('dve', lambda e, dn=dn: e.reciprocal(out=dn.h[:, :], in_=dn.h[:, :]), reads=[dn.v()], writes=[dn.v()])
            yield
            hv = H.v(0, 128, 128 * c, 128 * c + 128)
            if step < NCH // 2:
                s.act(hv, ps_o.v(0, 128, 0, 128), AF.Copy, scale=dn.v())
            else:
                s.stt(hv, ps_o.v(0, 128, 0, 128), dn.v(), hv, ALU.mult, ALU.add)
            yield

    gens = [dir_gen('f'), dir_gen('b')]
    while gens:
        for gnr in list(gens):
            try:
                next(gnr)
            except StopIteration:
                gens.remove(gnr)
    if getattr(C, 'dbg', None) is not None:
        s.dma('pool', C.dbg['d_q'].v(), qT.v())
        s.dma('pool', C.dbg['d_k'].v(), kT.v())
        s.dma('sp', C.dbg['d_H'].v(), H.v())
        for i, nm in enumerate(['lf_f', 'b_f', 'w_f', 'enb_f', 'eg_f', 'lf_b', 'b_b', 'w_b']):
            s.dma('sp', C.dbg['d_sm'].v(0, 128, 64 * i, 64 * i + 64), sm[nm].v())
        s.dma('pool', C.dbg['d_va'].v(), vaug['f'].v())
    H3 = lambda lo, hi: H.v(0, 128, 128 * lo, 128 * hi, fn=lambda a: a.rearrange('p (c e) -> p c e', e=128))
    mu = sm['tmp']
    s.op('dve', lambda e: e.tensor_reduce(out=mu.h[:, :], in_=H.h[:, :].rearrange('p (c e) -> p c e', e=128), axis=AX.X, op=ALU.add),
         reads=[H.v()], writes=[mu.v()])
    s.ts(mu.v(), mu.v(), 1.0 / 128, ALU.mult)
    bc = lambda t, lo, hi: t.v(0, 128, lo, hi, fn=lambda a: a.unsqueeze(2).to_broadcast([128, hi - lo, 128]))
    var = sm['lf_f']
    sq = vst
    for pc in range(4):
        lo, hi = 16 * pc, 16 * pc + 16
        s.tt(H3(lo, hi), H3(lo, hi), bc(mu, lo, hi), ALU.subtract)
        s.act(sq.v(0, 128, 0, 2048), H.v(0, 128, 128 * lo, 128 * hi), AF.Square)
        s.op('dve', lambda e, lo=lo, hi=hi: e.tensor_reduce(out=var.h[:, lo:hi], in_=sq.h[:, 0:2048].rearrange('p (c e) -> p c e', e=128),
                                                           axis=AX.X, op=ALU.add),
             reads=[sq.v(0, 128, 0, 2048)], writes=[var.v(0, 128, lo, hi)])
    s.act(var.v(), var.v(), AF.Sqrt, bias=C.epsb.v(), scale=1.0 / 128)
    s.op('dve', lambda e: e.reciprocal(out=var.h[:, :], in_=var.h[:, :]), reads=[var.v()], writes=[var.v()])
    for pc in range(4):
        lo, hi = 16 * pc, 16 * pc + 16
        s.tt(H3(lo, hi), H3(lo, hi), bc(var, lo, hi), ALU.mult, eng='pool' if pc % 2 else 'dve')
    for g in range(NCH // 4):
        ps = next_ps(C)
        for i in range(4):
            c = 4 * g + i
            s.tr(ps.v(0, 128, 128 * i, 128 * i + 128), H.v(0, 128, 128 * c, 128 * c + 128), C.ident.v())
        stv = acc0.v(0, 128, 512 * (g % 4), 512 * (g % 4) + 512)
        s.copy(stv, ps.v(), eng=s.evac_eng())
        dst = g // 4
        col = 512 * (g % 4)
        s.dma('sp' if g % 2 else 'act', xb_d.v(dst * XB_ROWS, dst * XB_ROWS + 128, col, col + 512), stv)


def u_extra_inputs(d, l, c):
    rp = c % 4
    m = {}
    m.update(mlstm_tables())
    cw = d['conv_w'][l]
    cb = d['conv_b'][l]
    m['m_cwq'] = np.ascontiguousarray(cw[:, 128 * rp:128 * rp + 128].T)
    m['m_cwk'] = np.ascontiguousarray(cw[:, 512 + 128 * rp:512 + 128 * rp + 128].T)
    m['m_cbq'] = np.ascontiguousarray(cb[128 * rp:128 * rp + 128].reshape(128, 1))
    m['m_cbk'] = np.ascontiguousarray(cb[512 + 128 * rp:512 + 128 * rp + 128].reshape(128, 1))
    m.update(s5_tables())
    m.update(s5_inputs(d, l, c))
    return m


NTAU = 152


def s5_tables():
    n72 = np.arange(72.0)
    n8 = np.arange(8.0)
    n64 = np.arange(64.0)
    dpow = 64.0 * 2.0 ** np.arange(7)
    tf = np.concatenate([n72 - 7, 7 - n8, 63 - n64, dpow, [1.0]])
    tbk = np.concatenate([64 - n72, n8, n64, dpow, [1.0]])
    tau = np.tile(np.concatenate([tf, tbk])[None, :], (128, 1))
    p = np.arange(128)
    imask = np.eye(128)
    jmask = np.roll(np.eye(128), 64, axis=1)
    blk = p // 16
    bmf = (blk[:, None] <= blk[None, :]) * 1.0
    bmb = (blk[:, None] >= blk[None, :]) * 1.0
    sg = np.concatenate([-np.ones(64), np.ones(64)])[:, None]
    sel = np.zeros((64, 4, 8, 128))
    for g in range(4):
        for i in range(8):
            for c in range(16):
                sel[16 * g + c, g, i, 16 * i + c] = 1.0
    selT = sel.transpose(3, 1, 2, 0).reshape(128, 4 * 8 * 64)
    tabs = dict(s_tau=tau, s_imask=imask, s_jmask=jmask, s_bmf=bmf, s_bmb=bmb, s_sg=sg, s_nsg=-sg,
                s_sel=sel.reshape(64, 4096), s_selT=selT)
    return {k: np.ascontiguousarray(v.astype(np.float32)) for k, v in tabs.items()}


ST_SHAPES = dict(s_tau=[128, 2 * NTAU], s_imask=[128, 128], s_jmask=[128, 128], s_bmf=[128, 128], s_bmb=[128, 128],
                 s_sg=[128, 1], s_nsg=[128, 1], s_sel=[64, 4096], s_selT=[128, 2048],
                 s_lam=[8 * 128, 2], s_logdt=[8 * 128, 1], s_X1=[8 * 128, 16], s_X2=[8 * 128, 16], s_Y1=[8 * 128, 16],
                 s_Y2=[8 * 128, 16], s_d=[4 * 128, 1])


def s5_inputs(d, l, c):
    rp = c % 4
    lam = np.zeros((8, 128, 2), np.float32)
    logdt = np.zeros((8, 128, 1), np.float32)
    X1 = np.zeros((8, 128, 16), np.float32)
    X2 = np.zeros((8, 128, 16), np.float32)
    Y1 = np.zeros((8, 128, 16), np.float32)
    Y2 = np.zeros((8, 128, 16), np.float32)
    dd = np.zeros((4, 128, 1), np.float32)
    for gl in range(4):
        g = 4 * rp + gl
        dd[gl, :, 0] = np.tile(d['s5_d'][l, g], 8)
        for di in range(2):
            u = 2 * gl + di
            lam[u, :, 0] = np.tile(d['s5_lam_re'][l, di, g], 2)
            lam[u, :, 1] = np.tile(d['s5_lam_im'][l, di, g], 2)
            logdt[u, :, 0] = d['s5_log_dt'][l, di, g]
            bre, bim = d['s5_b_re'][l, di, g], d['s5_b_im'][l, di, g]
            cre, cim = d['s5_c_re'][l, di, g].T, d['s5_c_im'][l, di, g].T
            X1[u] = np.concatenate([bre, bim], 0)
            X2[u] = np.concatenate([bim, bre], 0)
            Y1[u] = np.concatenate([cre, cim], 0)
            Y2[u] = np.concatenate([cim, cre], 0)
    return dict(s_lam=lam.reshape(1024, 2), s_logdt=logdt.reshape(1024, 1), s_X1=X1.reshape(1024, 16), s_X2=X2.reshape(1024, 16),
                s_Y1=Y1.reshape(1024, 16), s_Y2=Y2.reshape(1024, 16), s_d=dd.reshape(512, 1))


def phase_U_s5(s, C, xf_d, xb_d, sd):
    TWO_PI = 2.0 * np.pi
    cst = {}
    for k in ['s_tau', 's_imask', 's_jmask', 's_bmf', 's_bmb', 's_sg', 's_nsg']:
        cst[k] = s.sb(k, ST_SHAPES[k], F32)
        s.dma('sp', cst[k].v(), sd[k].v())
    sel = s.sb('s_sel', [64, 4096], BF16)
    s.dma('pool', sel.v(), sd['s_sel'].v())
    selT = s.sb('s_selT', [128, 2048], BF16)
    s.dma('pool', selT.v(), sd['s_selT'].v())
    zsb = s.sb('s_zsb', [64, S], BF16)
    for src in range(4):
        s.dma('pool', zsb.v(0, 64, 2048 * src, 2048 * src + 2048), xf_d.v(src * XF_ROWS + 452, src * XF_ROWS + 516))
    U8 = [s.sb('s_U8_%d' % g, [128, 1024], BF16) for g in range(4)]
    for g in range(4):
        for half in range(2):
            ps = next_ps(C)
            for i in range(8):
                rhs = V(zsb, zsb.h[0:64, 4096 * half + i:4096 * (half + 1):8], (0, 64, 4096 * half, 4096 * (half + 1)))
                s.mm(ps.v(0, 128, 0, 512), sel.v(0, 64, 128 * (8 * g + i), 128 * (8 * g + i) + 128), rhs, start=(i == 0), stop=(i == 7))
            s.copy(U8[g].v(0, 128, 512 * half, 512 * half + 512), ps.v(), eng=s.evac_eng())
    NS = 4

    def mk(nm, shape, dt=F32):
        return [s.sb('s_%s_%d' % (nm, b), shape, dt) for b in range(NS)]
    lamt_, ldt_ = mk('lamt', [128, 2]), mk('ldt', [128, 1])
    v1_ = {nm: mk(nm, [128, 1]) for nm in ['dt', 'lr', 'th', 'den', 'am1', 't1', 't2', 'fre', 'fim', 'fis', 'frs', 'dvec']}
    X1_, X2_, Y1_, Y2_ = [mk(nm, [128, 16]) for nm in ['X1', 'X2', 'Y1', 'Y2']]
    BA_, BB_, CA_, CB_, t16_ = [mk(nm, [128, 16]) for nm in ['BA', 'BB', 'CA', 'CB', 't16']]
    mag_, ang_, kf_, al_, be_ = [mk(nm, [128, NTAU]) for nm in ['mag', 'ang', 'kf', 'al', 'be']]
    ki_ = mk('ki', [128, NTAU], mybir.dt.int32)
    bsn_ = mk('bsn', [128, 7])
    Gt_, Ht_, tG_ = mk('G', [128, 72 * 16]), mk('H', [128, 72 * 16]), mk('tG', [128, 72 * 16])
    Gb_ = mk('Gb', [128, 72 * 16], BF16)
    Tm_, Wm_ = mk('T', [128, 1024], BF16), mk('W', [128, 1024], BF16)
    Dm_ = mk('D', [128, 7 * 128], BF16)
    tmp128_ = mk('tmp128', [128, 128])
    Xf_, Xb_, Xp_ = mk('Xf', [128, 128]), mk('Xb', [128, 128], BF16), mk('Xp', [128, 130], BF16)
    for b in range(NS):
        s.op('dve', lambda e, b=b: e.memset(Xp_[b].h[:, :], 0.0), writes=[Xp_[b].v()])
    Y8 = [s.sb('s_Y8_%d' % g, [128, 1024], BF16) for g in range(4)]
    gx = [s.sb('s_gx%d' % i, [128, 512], F32) for i in range(6)]

    def col(t, i):
        return t.v(0, 128, i, i + 1)

    def unit_gen(g, d):
        b = 2 * (g % 2) + d
        lamt, ldt = lamt_[b], ldt_[b]
        v1 = {k: v[b] for k, v in v1_.items()}
        X1, X2, Y1, Y2 = X1_[b], X2_[b], Y1_[b], Y2_[b]
        BA, BB, CA, CB, t16 = BA_[b], BB_[b], CA_[b], CB_[b], t16_[b]
        mag, ang, kf, al, be, ki, bsn = mag_[b], ang_[b], kf_[b], al_[b], be_[b], ki_[b], bsn_[b]
        Gt, Ht, tG, Gb, Tm, Wm, Dm, tmp128 = Gt_[b], Ht_[b], tG_[b], Gb_[b], Tm_[b], Wm_[b], Dm_[b], tmp128_[b]
        Xf, Xb, Xp = Xf_[b], Xb_[b], Xp_[b]
        u = 2 * g + d
        r0 = 128 * u
        s.dma('sp', lamt.v(), sd['s_lam'].v(r0, r0 + 128))
        s.dma('act', ldt.v(), sd['s_logdt'].v(r0, r0 + 128))
        s.dma('sp', X1.v(), sd['s_X1'].v(r0, r0 + 128))
        s.dma('act', X2.v(), sd['s_X2'].v(r0, r0 + 128))
        s.dma('sp', Y1.v(), sd['s_Y1'].v(r0, r0 + 128))
        s.dma('act', Y2.v(), sd['s_Y2'].v(r0, r0 + 128))
        if d == 0:
            s.dma('sp', v1['dvec'].v(), sd['s_d'].v(128 * g, 128 * g + 128))
        yield
        lre, lim = col(lamt, 0), col(lamt, 1)
        s.act(v1['dt'].v(), ldt.v(), AF.Exp); yield
        s.tt(v1['lr'].v(), lre, v1['dt'].v(), ALU.mult); yield
        s.tt(v1['th'].v(), lim, v1['dt'].v(), ALU.mult); yield
        tau = cst['s_tau'].v(0, 128, NTAU * d, NTAU * d + NTAU)
        s.ts(ang.v(), tau, v1['th'].v(), ALU.mult); yield
        s.ts(kf.v(), ang.v(), 1.0 / TWO_PI, ALU.mult); yield
        s.copy(ki.v(), kf.v()); yield
        s.copy(kf.v(), ki.v()); yield
        s.stt(ang.v(), kf.v(), -TWO_PI, ang.v(), ALU.mult, ALU.add); yield

        def wrap(dst, y):
            s.ts(mag.v(), y.v(), float(np.pi), ALU.is_gt, -TWO_PI, ALU.mult); yield
            s.tt(dst.v(), mag.v(), y.v(), ALU.add); yield
            s.ts(mag.v(), y.v(), -float(np.pi), ALU.is_lt, TWO_PI, ALU.mult); yield
            s.tt(dst.v(), dst.v(), mag.v(), ALU.add); yield
        yield from wrap(kf, ang)
        s.act(be.v(), kf.v(), AF.Sin); yield
        s.ts(ang.v(), ang.v(), float(np.pi / 2), ALU.add); yield
        yield from wrap(kf, ang)
        s.act(al.v(), kf.v(), AF.Sin); yield
        s.act(mag.v(), tau, AF.Exp, scale=v1['lr'].v()); yield
        s.tt(al.v(), al.v(), mag.v(), ALU.mult); yield
        s.tt(be.v(), be.v(), mag.v(), ALU.mult); yield
        a1, b1 = col(al, NTAU - 1), col(be, NTAU - 1)
        s.tt(v1['t1'].v(), lre, lre, ALU.mult); yield
        s.stt(v1['den'].v(), lim, lim, v1['t1'].v(), ALU.mult, ALU.add); yield
        s.op('dve', lambda e: e.reciprocal(out=v1['den'].h[:, :], in_=v1['den'].h[:, :]), reads=[v1['den'].v()], writes=[v1['den'].v()]); yield
        s.ts(v1['am1'].v(), a1, -1.0, ALU.add); yield
        s.tt(v1['t1'].v(), v1['am1'].v(), lre, ALU.mult); yield
        s.stt(v1['t1'].v(), b1, lim, v1['t1'].v(), ALU.mult, ALU.add); yield
        s.tt(v1['fre'].v(), v1['t1'].v(), v1['den'].v(), ALU.mult); yield
        s.tt(v1['t2'].v(), v1['am1'].v(), lim, ALU.mult); yield
        s.stt(v1['t2'].v(), b1, lre, v1['t2'].v(), ALU.mult, ALU.subtract); yield
        s.tt(v1['fim'].v(), v1['t2'].v(), v1['den'].v(), ALU.mult); yield
        s.tt(v1['fis'].v(), v1['fim'].v(), cst['s_sg'].v(), ALU.mult); yield
        s.tt(v1['frs'].v(), v1['fre'].v(), cst['s_sg'].v(), ALU.mult); yield
        s.ts(t16.v(), X1.v(), v1['fre'].v(), ALU.mult); yield
        s.stt(BA.v(), X2.v(), v1['fis'].v(), t16.v(), ALU.mult, ALU.add); yield
        s.ts(t16.v(), X1.v(), v1['fim'].v(), ALU.mult); yield
        s.stt(BB.v(), X2.v(), v1['frs'].v(), t16.v(), ALU.mult, ALU.subtract); yield
        s.ts(CA.v(), Y1.v(), cst['s_nsg'].v(), ALU.mult); yield
        s.ts(CB.v(), Y2.v(), -1.0, ALU.mult); yield
        b3 = lambda t: t.v(fn=lambda a: a.unsqueeze(1).to_broadcast([128, 72, 16]))
        a3 = lambda t, lo: t.v(0, 128, lo, lo + 72, fn=lambda a: a.unsqueeze(2).to_broadcast([128, 72, 16]))
        r3 = lambda t: t.v(fn=lambda a: a.rearrange('p (n c) -> p n c', c=16))
        s.tt(r3(Gt), b3(CA), a3(al, 0), ALU.mult); yield
        s.tt(r3(tG), b3(CB), a3(be, 0), ALU.mult); yield
        s.tt(Gt.v(), Gt.v(), tG.v(), ALU.add, eng='pool'); yield
        s.tt(r3(Ht), b3(BA), a3(al, 72), ALU.mult); yield
        s.tt(r3(tG), b3(BB), a3(be, 72), ALU.mult); yield
        s.tt(Ht.v(), Ht.v(), tG.v(), ALU.add, eng='pool'); yield
        s.copy(Gb.v(), Gt.v(), eng='act'); yield
        s.ts(bsn.v(), be.v(0, 128, 144, 151), cst['s_nsg'].v(), ALU.mult); yield
        for m in range(7):
            s.ts(tmp128.v(), cst['s_imask'].v(), col(al, 144 + m), ALU.mult, eng='pool'); yield
            s.stt(Dm.v(0, 128, 128 * m, 128 * m + 128), cst['s_jmask'].v(), col(bsn, m), tmp128.v(), ALU.mult, ALU.add); yield
        for dl in range(8):
            gs = 128 * dl if d == 0 else 128 * (8 - dl)
            ps = next_ps(C)
            s.mm(ps.v(0, 128, 0, 128), Ht.v(0, 128, 0, 128), Gt.v(0, 128, gs, gs + 128))
            tv = Tm.v(0, 128, 128 * dl, 128 * dl + 128)
            if dl == 0:
                bm = cst['s_bmf'] if d == 0 else cst['s_bmb']
                if d == 0:
                    s.tt(tmp128.v(), ps.v(0, 128, 0, 128), bm.v(), ALU.mult); yield
                    s.stt(tv, cst['s_imask'].v(), v1['dvec'].v(), tmp128.v(), ALU.mult, ALU.add)
                else:
                    s.tt(tv, ps.v(0, 128, 0, 128), bm.v(), ALU.mult)
            else:
                s.copy(tv, ps.v(0, 128, 0, 128), eng=s.evac_eng())
            yield
        for hh in range(2):
            ps = next_ps(C)
            for i in range(4):
                ih = 4 * hh + i
                s.tr(ps.v(0, 128, 128 * i, 128 * i + 128), Ht.v(0, 128, 128 + 128 * ih, 128 + 128 * ih + 128), C.ident.v())
            s.copy(Wm.v(0, 128, 512 * hh, 512 * hh + 512), ps.v(), eng=s.evac_eng()); yield
        ps = next_ps(C)
        for ih in range(8):
            rhs = V(U8[g], U8[g].h[:, ih::8], (0, 128, 0, 1024))
            s.mm(ps.v(0, 128, 0, 128), Wm.v(0, 128, 128 * ih, 128 * ih + 128), rhs, start=(ih == 0), stop=(ih == 7))
        s.copy(Xf.v(), ps.v(0, 128, 0, 128), eng='dve')
        s.copy(Xb.v(), ps.v(0, 128, 0, 128), eng='act'); yield
        for m in range(7):
            sh = 2 ** m
            ps = next_ps(C)
            if d == 0:
                s.mm(ps.v(0, 128, 0, 128 - sh), Dm.v(0, 128, 128 * m, 128 * m + 128), Xb.v(0, 128, 0, 128 - sh))
                xv = Xf.v(0, 128, sh, 128)
            else:
                s.mm(ps.v(0, 128, 0, 128 - sh), Dm.v(0, 128, 128 * m, 128 * m + 128), Xb.v(0, 128, sh, 128))
                xv = Xf.v(0, 128, 0, 128 - sh)
            s.tt(xv, xv, ps.v(0, 128, 0, 128 - sh), ALU.add); yield
            if m < 6:
                s.copy(Xb.v(), Xf.v(), eng='act'); yield
        s.copy(Xp.v(0, 128, 1, 129), Xf.v(), eng='act'); yield

    def out_gen(g):
        bf, bb = 2 * (g % 2), 2 * (g % 2) + 1
        for hh in range(2):
            ps = next_ps(C)
            for q in range(4):
                jh = 4 * hh + q
                o = ps.v(0, 128, 128 * q, 128 * q + 128)
                first = True
                for ih in range(jh + 1):
                    rhs = V(U8[g], U8[g].h[:, ih::8], (0, 128, 0, 1024))
                    s.mm(o, Tm_[bf].v(0, 128, 128 * (jh - ih), 128 * (jh - ih) + 128), rhs, start=first, stop=False)
                    first = False
                for ih in range(jh, 8):
                    rhs = V(U8[g], U8[g].h[:, ih::8], (0, 128, 0, 1024))
                    s.mm(o, Tm_[bb].v(0, 128, 128 * (ih - jh), 128 * (ih - jh) + 128), rhs, start=False, stop=False)
                s.mm(o, Gb_[bf].v(0, 128, 128 * (jh + 1), 128 * (jh + 1) + 128), Xp_[bf].v(0, 128, 0, 128), start=False, stop=False)
                s.mm(o, Gb_[bb].v(0, 128, 128 * jh, 128 * jh + 128), Xp_[bb].v(0, 128, 2, 130), start=False, stop=True)
            k3 = 3 * (g % 2)
            x_, t_, u_ = gx[k3], gx[k3 + 1], gx[k3 + 2]
            s.copy(x_.v(), ps.v(), eng='dve'); yield
            s.act(t_.v(), x_.v(), AF.Square); yield
            s.ts(t_.v(), t_.v(), 0.044715, ALU.mult, 1.0, ALU.add, eng='pool'); yield
            s.tt(t_.v(), t_.v(), x_.v(), ALU.mult, eng='pool'); yield
            s.act(u_.v(), t_.v(), AF.Sigmoid, scale=1.5957691216057308); yield
            ov = Y8[g].v(fn=lambda a, hh=hh: a.rearrange('p (k j) -> p j k', j=8)[:, 4 * hh:4 * hh + 4, :])
            s.tt(ov, u_.v(fn=lambda a: a.rearrange('p (j k) -> p j k', k=128)), x_.v(fn=lambda a: a.rearrange('p (j k) -> p j k', k=128)), ALU.mult); yield

    def run_interleaved(gens):
        gens = list(gens)
        while gens:
            for gnr in list(gens):
                try:
                    next(gnr)
                except StopIteration:
                    gens.remove(gnr)

    for gp in range(2):
        run_interleaved([unit_gen(2 * gp + gg, d) for gg in range(2) for d in range(2)])
        run_interleaved([out_gen(2 * gp + gg) for gg in range(2)])
    yst = [s.sb('s_yst%d' % i, [64, 512], F32) for i in range(2)]
    for tb_ in range(16):
        ps = next_ps(C)
        for j in range(8):
            for g in range(4):
                s.mm(ps.v(0, 64, 64 * j, 64 * j + 64), selT.v(0, 128, 64 * (8 * g + j), 64 * (8 * g + j) + 64),
                     Y8[g].v(0, 128, 64 * tb_, 64 * tb_ + 64), start=(g == 0), stop=(g == 3))
        st = yst[tb_ % 2]
        s.copy(st.v(fn=lambda a: a.rearrange('p (k j) -> p j k', j=8)), ps.v(0, 64, 0, 512, fn=lambda a: a.rearrange('p (j k) -> p j k', k=64)),
               eng=s.evac_eng())
        dst = tb_ // 4
        cc = 512 * (tb_ % 4)
        s.dma('sp' if tb_ % 2 else 'act', xb_d.v(dst * XB_ROWS + 192, dst * XB_ROWS + 256, cc, cc + 512), st.v())


def _col(a):
    return np.ascontiguousarray(np.asarray(a, np.float32).reshape(-1, 1))


def _c(a):
    return np.ascontiguousarray(np.asarray(a, np.float32))


def _layer_weights(inp, l):
    return dict(w_in=_c(inp['w_in'][l]), b_in=_col(inp['b_in'][l]), g_pre=_col(inp['g_mix_pre'][l]),
                norm_g=_col(inp['mlstm_norm_g'][l]), w_up_m=_c(inp['w_up_mlstm'][l]), w_up_f=_c(inp['w_up_fourier'][l]),
                w_glu=_c(inp['w_glu'][l]), b_glu=_col(inp['b_glu'][l]), w_out=_c(inp['w_out'][l]), g_post=_col(inp['g_mix_post'][l]),
                g_ffn_pre=_col(inp['g_ffn_pre'][l]), g_ffn_post=_col(inp['g_ffn_post'][l]), w_ffn1=_c(inp['w_ffn1'][l]),
                w_ffn2=_c(inp['w_ffn2'][l]))


def kernel_unfused(**inputs):
    inp = {k: np.asarray(v) for k, v in inputs.items()}
    ident = np.eye(128, dtype=np.float32)
    cores = list(range(8))
    xs = [_c(inp['x'][c // 4, TL * (c % 4):TL * (c % 4 + 1)]) for c in cores]
    ftab = fourier_tables()
    for l in range(2):
        Wl = _layer_weights(inp, l)
        nc = build_T1()
        maps = [dict(x=xs[c], w_in=Wl['w_in'], b_in=Wl['b_in'], g_pre=Wl['g_pre'], ident=ident) for c in cores]
        res = run_bass_kernel_spmd(nc, maps, core_ids=cores)
        xf_sent = [np.asarray(res.results[c]['xf']).reshape(4, XF_ROWS, TL) for c in cores]
        maps = []
        for c in cores:
            b, r = c // 4, c % 4
            xf = np.stack([xf_sent[4 * b + src][r] for src in range(4)], 0).reshape(4 * XF_ROWS, TL)
            m = dict(xf=np.ascontiguousarray(xf), ident=ident)
            m.update(ftab)
            m.update(u_extra_inputs(inp, l, c))
            maps.append(m)
        nc = build_U(('fourier', 'mlstm', 's5'))
        res = run_bass_kernel_spmd(nc, maps, core_ids=cores)
        xb_sent = [np.asarray(res.results[c]['xb']).reshape(4, XB_ROWS, TL) for c in cores]
        maps = []
        for c in cores:
            b, r = c // 4, c % 4
            xb = np.stack([xb_sent[4 * b + src][r] for src in range(4)], 0).reshape(4 * XB_ROWS, TL)
            m = dict(x=xs[c], xb=np.ascontiguousarray(xb), ident=ident)
            m.update(Wl)
            maps.append(m)
        nc = build_T2()
        res = run_bass_kernel_spmd(nc, maps, core_ids=cores)
        xs = [np.asarray(res.results[c]['x_out']) for c in cores]
    out = np.zeros((2, S, D), np.float32)
    for c in cores:
        out[c // 4, TL * (c % 4):TL * (c % 4 + 1)] = xs[c]
    return out


class BlockT:
    def __init__(self, parent, bases, B):
        self.parent, self.bases, self.B = parent, bases, B
        self.space = parent.space

    def v(self, p0=0, p1=None, f0=0, f1=None, fn=None):
        if p1 is None:
            p1 = self.B * len(self.bases)
        blk = p0 // self.B
        assert (p1 - 1) // self.B == blk, (p0, p1, self.B)
        off = self.bases[blk] - blk * self.B
        return self.parent.v(off + p0, off + p1, f0, f1, fn)


W_SHAPES = [('w_in', [D, N_IN]), ('b_in', [N_IN, 1]), ('g_pre', [D, 1]), ('norm_g', [512, 1]), ('w_up_m', [512, D]),
            ('w_up_f', [256, D]), ('w_glu', [256, 2 * D]), ('b_glu', [2 * D, 1]), ('w_out', [D, D]), ('g_post', [D, 1]),
            ('g_ffn_pre', [D, 1]), ('g_ffn_post', [D, 1]), ('w_ffn1', [D, 4 * D]), ('w_ffn2', [4 * D, D])]
M_CONST = ['m_trif', 'm_trib', 'm_maskf', 'm_maskb', 'm_onesf']
M_PARAM = ['m_cwq', 'm_cwk', 'm_cbq', 'm_cbk']
S_CONST = ['s_tau', 's_imask', 's_jmask', 's_bmf', 's_bmb', 's_sg', 's_nsg', 's_sel', 's_selT']
S_PARAM = ['s_lam', 's_logdt', 's_X1', 's_X2', 's_Y1', 's_Y2', 's_d']


def fused_T1(s, C, x_src, W, xf_view):
    mk = s.mark()
    load_xT(s, C, x_src, TL)
    gcols = load_vec_cols(s, C, 'gpre', W['g_pre'], 8)
    rstd = rms_rstd(s, C, C.xT, TL, 'pre')
    hT = make_h(s, C, gcols, rstd, TL, 'hT')
    phase_T1_proj(s, C, hT, W['w_in'], W['b_in'], xf_view)
    s.release(mk)


def fused_T2(s, C, x_src, W, xb_view, out_view, NT=1024):
    mk = s.mark()
    C.NT = NT
    C.sqb = [s.sb('sqb%d' % i, [128, 512], BF16) for i in range(2)]
    C.rstd2 = s.sb('rstd2', [128, NT], F32)
    gcols = load_vec_cols(s, C, 'gpre', W['g_pre'], 8)
    t2_setup(s, C, W)
    C.xT = [s.sb('xT%d' % k, [128, NT], F32) for k in range(8)]
    C.hT = [s.sb('hT%d' % k, [128, NT], BF16) for k in range(8)]
    C.xoff = 0
    xstage = [s.sb('xst%d' % i, [128, 1024], F32) for i in range(4)]
    for hf in range(TL // NT):
        tok0 = hf * NT
        for tg in range(NT // 512):
            for i in range(4):
                tt = tg * 4 + i
                s.dma('sp' if i % 2 == 0 else 'act', xstage[i].v(), x_src.v(tok0 + 128 * tt, tok0 + 128 * tt + 128))
            for kt in range(8):
                ps = ps6(C)
                for i in range(4):
                    s.tr(ps.v(0, 128, 128 * i, 128 * i + 128), xstage[i].v(0, 128, 128 * kt, 128 * kt + 128), C.ident.v())
                s.copy(C.xT[kt].v(0, 128, 512 * tg, 512 * tg + 512), ps.v(), eng=s.evac_eng())
        for tc in range(NT // 512):
            ps = ps6(C)
            for kt in range(8):
                q = C.sqb[kt % 2]
                s.act(q.v(), C.xT[kt].v(0, 128, 512 * tc, 512 * tc + 512), AF.Square)
                s.mm(ps.v(), C.ones.v(), q.v(), start=(kt == 0), stop=(kt == 7))
            r = C.rstd2.v(0, 128, 512 * tc, 512 * tc + 512)
            s.act(r, ps.v(), AF.Sqrt, bias=C.epsb.v(), scale=1.0 / D)
            s.op('dve', lambda e, r=r: e.reciprocal(out=r.ap, in_=r.ap), reads=[r], writes=[r])
        for kt in range(8):
            s.stt(C.hT[kt].v(), C.xT[kt].v(), gcols.v(0, 128, kt, kt + 1), C.rstd2.v(), ALU.mult, ALU.mult)
        phase_T2(s, C, W, xb_view, tok0)
        store_xT(s, C, out_view, tok0, NT, xstage)
    s.release(mk)


def fused_U(s, C, xf_view, xb_view, tabs_d, md, sd):
    mk = s.mark()
    phase_U_fourier(s, C, xf_view, xb_view, tabs_d)
    s.release(mk)
    phase_U_s5(s, C, xf_view, xb_view, sd)
    s.release(mk)
    phase_U_mlstm(s, C, xf_view, xb_view, md)
    s.release(mk)


def build_fused(n_layers=2):
    nc = bass.Bass("TRN2", target_bir_lowering=False)
    s = Sched(nc)
    C = Ctx()
    x_d = s.dram('x', [S, D], F32, 'ExternalInput')
    id_d = s.dram('ident', [128, 128], F32, 'ExternalInput')
    out_d = s.dram('out', [S, D], F32, 'ExternalOutput')
    Wl = [{n: s.dram('%s_l%d' % (n, l), sh, F32, 'ExternalInput') for n, sh in W_SHAPES} for l in range(n_layers)]
    tabs_d = {k: s.dram(k, v, F32, 'ExternalInput') for k, v in FT_SHAPES.items()}
    mconst = {k: s.dram(k, MT_SHAPES[k], F32, 'ExternalInput') for k in M_CONST}
    sconst = {k: s.dram(k, ST_SHAPES[k], F32, 'ExternalInput') for k in S_CONST}
    mpar = [{k: s.dram('%s_l%d' % (k, l), [4 * MT_SHAPES[k][0], MT_SHAPES[k][1]], F32, 'ExternalInput') for k in M_PARAM}
            for l in range(n_layers)]
    spar = [{k: s.dram('%s_l%d' % (k, l), [4 * ST_SHAPES[k][0], ST_SHAPES[k][1]], F32, 'ExternalInput') for k in S_PARAM}
            for l in range(n_layers)]
    xf_all = s.dram('xf_all', [16 * XF_ROWS, TL], F32, 'Internal')
    xb_all = s.dram('xb_all', [16 * XB_ROWS, TL], F32, 'Internal')
    xs1 = s.dram('xs1', [S, D], F32, 'Internal')
    common_setup(s, C, id_d)
    C.epsb = s.sb('epsb', [128, 1], F32)
    s.op('dve', lambda e: e.memset(C.epsb.h[:, :], EPS), writes=[C.epsb.v()])
    C.oneb = s.sb('oneb', [128, 1], F32)
    s.op('dve', lambda e: e.memset(C.oneb.h[:, :], 1.0), writes=[C.oneb.v()])
    for l in range(n_layers):
        xsrc = x_d if l == 0 else xs1
        xdst = out_d if l == n_layers - 1 else xs1
        Wf = Wl[l]
        t1cols = [(OFF_Q + 128 * i, 128) for i in range(4)] + [(OFF_K + 128 * i, 128) for i in range(4)] + \
                 [(OFF_V + 128 * i, 128) for i in range(4)] + [(OFF_G16, 16)] + [(OFF_F + 128 * i, 128) for i in range(2)] + \
                 [(OFF_S5 + 128 * i, 128) for i in range(2)]
        t2cols = [(OFF_O + 128 * i, 128) for i in range(4)] + [(OFF_GATE + 1024 * b + 128 * j, 128) for j in range(8) for b in range(3)]
        pw = dict(
            w_in=PW(s, 'pw_in_l%d' % l, Wf['w_in'], 8, t1cols + t2cols),
            w_up_m=PW(s, 'pw_um_l%d' % l, Wf['w_up_m'], 4, [(128 * j, 128) for j in range(8)]),
            w_up_f=PW(s, 'pw_uf_l%d' % l, Wf['w_up_f'], 2, [(128 * j, 128) for j in range(8)]),
            w_glu=PW(s, 'pw_glu_l%d' % l, Wf['w_glu'], 2, [(128 * j, 128) for j in range(16)]),
            w_out=PW(s, 'pw_out_l%d' % l, Wf['w_out'], 8, [(128 * j, 128) for j in range(8)]),
            w_ffn1=PW(s, 'pw_f1_l%d' % l, Wf['w_ffn1'], 8, [(512 * j, 512) for j in range(8)]),
            w_ffn2=PW(s, 'pw_f2_l%d' % l, Wf['w_ffn2'], 32, [(128 * j, 128) for j in range(8)]))
        for i in range(len(t1cols) + 4):
            pw['w_in'].prep(s, i)
        for j in range(8):
            pw['w_up_m'].prep(s, j)
            pw['w_up_f'].prep(s, j)
            pw['w_glu'].prep(s, j)
            pw['w_glu'].prep(s, 8 + j)
            for b in range(3):
                pw['w_in'].prep(s, len(t1cols) + 4 + 3 * j + b)
        for nm in ['w_out', 'w_ffn1', 'w_ffn2']:
            for j in range(8):
                pw[nm].prep(s, j)
        Wp = dict(Wf)
        Wp.update(pw)
        Wl[l] = Wp
        for q in range(4):
            fused_T1(s, C, BlockT(xsrc, [TL * q], TL), Wl[l], BlockT(xf_all, [(dst * 4 + q) * XF_ROWS for dst in range(4)], XF_ROWS))
        for u in range(4):
            md = dict(mconst)
            md.update({k: BlockT(mpar[l][k], [u * MT_SHAPES[k][0]], MT_SHAPES[k][0]) for k in M_PARAM})
            sd = dict(sconst)
            sd.update({k: BlockT(spar[l][k], [u * ST_SHAPES[k][0]], ST_SHAPES[k][0]) for k in S_PARAM})
            fused_U(s, C, BlockT(xf_all, [(u * 4 + src) * XF_ROWS for src in range(4)], XF_ROWS),
                    BlockT(xb_all, [(dst * 4 + u) * XB_ROWS for dst in range(4)], XB_ROWS), tabs_d, md, sd)
        for q in range(4):
            fused_T2(s, C, BlockT(xsrc, [TL * q], TL), Wl[l], BlockT(xb_all, [(q * 4 + src) * XB_ROWS for src in range(4)], XB_ROWS),
                     BlockT(xdst, [TL * q], TL))
    s.finish()
    return nc


def fused_inputs(inp, b, n_layers=2):
    m = dict(x=_c(inp['x'][b]), ident=np.eye(128, dtype=np.float32))
    m.update(fourier_tables())
    mt = mlstm_tables()
    m.update({k: mt[k] for k in M_CONST})
    st = s5_tables()
    m.update({k: st[k] for k in S_CONST})
    for l in range(n_layers):
        for k, v in _layer_weights(inp, l).items():
            m['%s_l%d' % (k, l)] = v
        units = [u_extra_inputs(inp, l, u) for u in range(4)]
        for k in M_PARAM + S_PARAM:
            m['%s_l%d' % (k, l)] = np.ascontiguousarray(np.concatenate([units[u][k] for u in range(4)], 0))
    return m


def kernel(**inputs):
    inp = {k: np.asarray(v) for k, v in inputs.items()}
    nc = build_fused()
    per_batch = [fused_inputs(inp, b) for b in range(2)]
    maps = [per_batch[c // 4] for c in range(8)]
    res = run_bass_kernel_spmd(nc, maps, core_ids=list(range(8)))
    out = np.stack([np.asarray(res.results[0]['out']), np.asarray(res.results[4]['out'])], 0)
    return out.astype(np.float32)
```

```python
import numpy as np
import concourse.bass as bass
import concourse.mybir as mybir
from concourse.bass_utils import run_bass_kernel_spmd

F32 = mybir.dt.float32
BF16 = mybir.dt.bfloat16
AF = mybir.ActivationFunctionType
ALU = mybir.AluOpType
AX = mybir.AxisListType

ENGS = ['pe', 'act', 'pool', 'dve', 'sp']
DMA_POOL = 6
SAME_ENGINE_SYNC = True
SEM_EPOCH = 16000

D = 1024
TL = 2048
S = 8192
EPS = 1e-6
N_IN = 5648
XF_ROWS = 516
XB_ROWS = 256


class T:
    def __init__(self, name, handle, shape, space):
        self.name, self.h, self.shape, self.space = name, handle, shape, space
        self.recs = []
        self.track = True

    def v(self, p0=0, p1=None, f0=0, f1=None, fn=None):
        P = self.shape[0]
        F = int(np.prod(self.shape[1:]))
        p1 = P if p1 is None else p1
        f1 = F if f1 is None else f1
        ap = self.h[p0:p1, f0:f1]
        if fn is not None:
            ap = fn(ap)
        if self.space == 'psum':
            reg = (0, 128, 0, 1 << 30)
        else:
            reg = (p0, p1, f0, f1)
        return V(self, ap, reg)


class V:
    def __init__(self, t, ap, reg):
        self.t, self.ap, self.reg = t, ap, reg

    def f(self, fn):
        return V(self.t, fn(self.ap), self.reg)


def _ov(a, b):
    return a[0] < b[1] and b[0] < a[1] and a[2] < b[3] and b[2] < a[3]


def _cov(a, b):
    return a[0] <= b[0] and a[1] >= b[1] and a[2] <= b[2] and a[3] >= b[3]


class Sched:
    def __init__(self, nc):
        self.nc = nc
        self.ops = {e: [] for e in ENGS}
        self.cnt = {e: 0 for e in ENGS}
        self.waited = {e: {} for e in ENGS}
        self.dma_n = {e: 0 for e in ENGS}
        self.dma_last = {}
        self.ctx = []
        self.rr = 0
        self.epoch = {}
        self.final = {}
        self.dma_ep = {}
        self.prep_n = 0

    def mark(self):
        return len(self.ctx)

    def barrier(self):
        deps = dict(self.final)
        deps.update({k: v for k, v in self.dma_last.items() if k[2] < 100})
        for e in ENGS:
            w = self._waits(e, dict(deps))
            if w:
                self.ops[e].append((w, None, None, 0))

    def release(self, mark):
        self.barrier()
        while len(self.ctx) > mark:
            self.ctx.pop().__exit__(None, None, None)

    def sb(self, name, shape, dt):
        self.uid = getattr(self, 'uid', 0) + 1
        cm = self.nc.sbuf_tensor('sb%d_%s' % (self.uid, name), list(shape), dt)
        h = cm.__enter__()
        self.ctx.append(cm)
        return T(name, h, shape, 'sbuf')

    def ps(self, name, shape=(128, 512), dt=F32):
        cm = self.nc.psum_tensor('pp_' + name, list(shape), dt)
        h = cm.__enter__()
        self.ctx.append(cm)
        return T(name, h, shape, 'psum')

    def dram(self, name, shape, dt, kind):
        h = self.nc.dram_tensor(name, list(shape), dt, kind=kind)
        t = T(name, h.ap(), shape, 'dram')
        t.track = (kind == 'Internal')
        return t

    def _deps(self, reads, writes):
        deps = {}
        reads = [v for v in reads if v.t.track]
        writes = [v for v in writes if v.t.track]
        for vw in reads:
            for r in vw.t.recs:
                if r[0] and _ov(r[1:5], vw.reg) and deps.get(r[5], 0) < r[6]:
                    deps[r[5]] = r[6]
        for vw in writes:
            for r in vw.t.recs:
                if _ov(r[1:5], vw.reg) and deps.get(r[5], 0) < r[6]:
                    deps[r[5]] = r[6]
        return deps

    def _record(self, reads, writes, semkey, val):
        reads = [v for v in reads if v.t.track]
        writes = [v for v in writes if v.t.track]
        for vw in writes:
            t = vw.t
            reg = tuple(vw.reg)
            t.recs = [r for r in t.recs if not _cov(reg, r[1:5])]
            t.recs.append((True,) + reg + (semkey, val))
        for vw in reads:
            t = vw.t
            reg = tuple(vw.reg)
            t.recs = [r for r in t.recs if r[0] or r[5] != semkey or r[1:5] != reg]
            t.recs.append((False,) + reg + (semkey, val))

    def _waits(self, eng, deps):
        w = []
        for k, v in deps.items():
            if k[0] == 'c' and k[1] == eng and (eng == 'pe' or not SAME_ENGINE_SYNC):
                continue
            if self.waited[eng].get(k, 0) >= v:
                continue
            self.waited[eng][k] = v
            w.append((k, v))
        return w

    def op(self, eng, fn, reads=(), writes=()):
        writes = list(writes) + [v for v in reads if v.t.space == 'psum']
        reads = [v for v in reads if v.t.space != 'psum']
        deps = self._deps(reads, writes)
        waits = self._waits(eng, deps)
        if self.cnt[eng] >= SEM_EPOCH:
            self.epoch[eng] = self.epoch.get(eng, 0) + 1
            self.cnt[eng] = 0
        self.cnt[eng] += 1
        key = ('c', eng, self.epoch.get(eng, 0))
        self.final[key] = self.cnt[eng]
        self._record(reads, writes, key, self.cnt[eng])
        self.ops[eng].append((waits, fn, key, 1))

    def dma(self, eng, out, in_, prep=False, **kw):
        deps = self._deps([in_], [out])
        if prep:
            j = self.prep_n
            self.prep_n += 1
            slot = 100 + j % DMA_POOL
        else:
            j = self.dma_n[eng]
            self.dma_n[eng] += 1
            slot = j % DMA_POOL
        if self.dma_last.get(('d', eng, slot, self.dma_ep.get((eng, slot), 0)), 0) >= SEM_EPOCH:
            self.dma_ep[(eng, slot)] = self.dma_ep.get((eng, slot), 0) + 1
        key = ('d', eng, slot, self.dma_ep.get((eng, slot), 0))
        prev = self.dma_last.get(key, 0)
        if not prev and self.dma_ep.get((eng, slot), 0) > 0:
            pk = ('d', eng, slot, self.dma_ep[(eng, slot)] - 1)
            deps[pk] = max(deps.get(pk, 0), self.dma_last[pk])
        if prev:
            deps[key] = max(deps.get(key, 0), prev)
        waits = self._waits(eng, deps)
        val = prev + 16
        self.dma_last[key] = val
        self._record([in_], [out], key, val)
        oa, ia = out.ap, in_.ap
        self.ops[eng].append((waits, lambda e: e.dma_start(out=oa, in_=ia, **kw), key, 16))

    def finish(self):
        waits = self._waits('sp', dict(self.dma_last))
        self.ops['sp'].append((waits, None, None, 0))
        nc = self.nc
        semkeys = list(self.final.keys()) + list(self.dma_last.keys())
        sems = {}
        cms = []
        for k in semkeys:
            cm = nc.semaphore('s_' + '_'.join(str(x) for x in k))
            sems[k] = cm.__enter__()
            cms.append(cm)
        with nc.Block() as block:
            deco = dict(pe=block.tensor, act=block.scalar, pool=block.gpsimd, dve=block.vector, sp=block.sync)
            for e in ENGS:
                ops = self.ops[e]

                def body(engine, ops=ops):
                    for waits, fn, key, inc in ops:
                        for k, v in waits:
                            engine.wait_ge(sems[k], v)
                        if fn is not None:
                            fn(engine).then_inc(sems[key], inc)
                if ops:
                    deco[e](body)
        for cm in reversed(cms):
            cm.__exit__(None, None, None)
        for cm in reversed(self.ctx):
            cm.__exit__(None, None, None)

    def mm(self, out, lhsT, rhs, start=True, stop=True):
        self.op('pe', lambda e: e.matmul(out.ap, lhsT.ap, rhs.ap, start=start, stop=stop),
                reads=[lhsT, rhs], writes=[out])

    def tr(self, out, in_, ident):
        self.op('pe', lambda e: e.transpose(out.ap, in_.ap, ident.ap), reads=[in_, ident], writes=[out])

    def act(self, out, in_, func, bias=None, scale=None, eng='act'):
        kw = {}
        rd = [in_]
        if bias is not None:
            if isinstance(bias, V):
                kw['bias'] = bias.ap
                rd.append(bias)
            else:
                kw['bias'] = bias
        if scale is not None:
            if isinstance(scale, V):
                kw['scale'] = scale.ap
                rd.append(scale)
            else:
                kw['scale'] = scale
        self.op('act', lambda e: e.activation(out=out.ap, in_=in_.ap, func=func, **kw), reads=rd, writes=[out])

    def tt(self, out, in0, in1, op, eng='dve'):
        self.op(eng, lambda e: e.tensor_tensor(out=out.ap, in0=in0.ap, in1=in1.ap, op=op), reads=[in0, in1], writes=[out])

    def ts(self, out, in0, s1, op0, s2=None, op1=None, eng='dve'):
        rd = [in0]
        a1 = s1
        a2 = s2
        if isinstance(s1, V):
            rd.append(s1)
            a1 = s1.ap
        if isinstance(s2, V):
            rd.append(s2)
            a2 = s2.ap
        if op1 is None:
            self.op(eng, lambda e: e.tensor_scalar(out=out.ap, in0=in0.ap, scalar1=a1, scalar2=None, op0=op0), reads=rd, writes=[out])
        else:
            self.op(eng, lambda e: e.tensor_scalar(out=out.ap, in0=in0.ap, scalar1=a1, scalar2=a2, op0=op0, op1=op1), reads=rd, writes=[out])

    def stt(self, out, in0, sc, in1, op0, op1):
        rd = [in0, in1]
        a = sc
        if isinstance(sc, V):
            rd.append(sc)
            a = sc.ap
        self.op('dve', lambda e: e.scalar_tensor_tensor(out=out.ap, in0=in0.ap, scalar=a, in1=in1.ap, op0=op0, op1=op1),
                reads=rd, writes=[out])

    def copy(self, out, in_, eng='dve'):
        if eng == 'act':
            self.op('act', lambda e: e.activation(out=out.ap, in_=in_.ap, func=AF.Identity), reads=[in_], writes=[out])
        else:
            self.op(eng, lambda e: e.tensor_copy(out=out.ap, in_=in_.ap), reads=[in_], writes=[out])

    def evac_eng(self):
        self.rr += 1
        return 'act' if self.rr % 2 else 'dve'


class Ctx:
    pass


OFF_Q, OFF_K, OFF_V, OFF_O, OFF_G16, OFF_F, OFF_S5, OFF_GATE = 0, 512, 1024, 1536, 2048, 2064, 2320, 2576


def common_setup(s, C, ident_d):
    C.ident = s.sb('ident', [128, 128], F32)
    s.dma('sp', C.ident.v(), ident_d.v())
    C.identb = s.sb('identb', [128, 128], BF16)
    s.copy(C.identb.v(), C.ident.v())
    C.ones = s.sb('ones', [128, 128], BF16)
    s.op('dve', lambda e: e.memset(C.ones.h[:, :], 1.0), writes=[C.ones.v()])
    C.ps = [s.ps('ps%d' % i) for i in range(8)]
    C.psi = 0


def next_ps(C):
    C.psi = (C.psi + 1) % 8
    return C.ps[C.psi]


def load_xT(s, C, x_d, ntok):
    C.xT = [s.sb('xT%d' % k, [128, ntok], F32) for k in range(8)]
    stage = [s.sb('xst%d' % i, [128, 1024], F32) for i in range(8)]
    for tg in range(ntok // 512):
        for i in range(4):
            tt = tg * 4 + i
            s.dma('sp' if i % 2 == 0 else 'act', stage[(tg % 2) * 4 + i].v(), x_d.v(128 * tt, 128 * tt + 128))
        for kt in range(8):
            ps = next_ps(C)
            for i in range(4):
                s.tr(ps.v(0, 128, 128 * i, 128 * i + 128), stage[(tg % 2) * 4 + i].v(0, 128, 128 * kt, 128 * kt + 128), C.ident.v())
            s.copy(C.xT[kt].v(0, 128, 512 * tg, 512 * tg + 512), ps.v(), eng=s.evac_eng())


def rms_rstd(s, C, src, ntok, name):
    rstd = s.sb('rstd_' + name, [128, ntok], F32)
    sq = [s.sb('sq_%s%d' % (name, i), [128, 512], BF16) for i in range(2)]
    n = 0
    for tc in range(ntok // 512):
        ps = next_ps(C)
        for kt in range(8):
            q = sq[n % 2]
            n += 1
            s.act(q.v(), src[kt].v(0, 128, 512 * tc, 512 * tc + 512), AF.Square)
            s.mm(ps.v(), C.ones.v(), q.v(), start=(kt == 0), stop=(kt == 7))
        r = rstd.v(0, 128, 512 * tc, 512 * tc + 512)
        s.act(r, ps.v(), AF.Sqrt, bias=C.epsb.v(), scale=1.0 / D)
        s.op('dve', lambda e, r=r: e.reciprocal(out=r.ap, in_=r.ap), reads=[r], writes=[r])
    return rstd


def load_vec_cols(s, C, name, d_t, n):
    t = s.sb(name, [128, n], F32)
    for k in range(n):
        s.dma('sp', t.v(0, 128, k, k + 1), d_t.v(128 * k, 128 * k + 128))
    return t


def make_h(s, C, gcols, rstd, ntok, name):
    hT = [s.sb('%s%d' % (name, k), [128, ntok], BF16) for k in range(8)]
    for kt in range(8):
        s.stt(hT[kt].v(), C.xT[kt].v(), gcols.v(0, 128, kt, kt + 1), rstd.v(), ALU.mult, ALU.mult)
    return hT


class PW:
    def __init__(self, s, name, w_d, K, specs):
        self.w_d, self.K, self.specs = w_d, K, specs
        width = K * max(n for _, n in specs)
        self.t = s.dram(name, [len(specs) * 128, width], BF16, 'Internal')
        self.idx = {c0: i for i, (c0, n) in enumerate(specs)}

    def prep(self, s, i):
        c0, ncols = self.specs[i]
        K = self.K
        out = self.t.v(i * 128, i * 128 + 128, 0, K * ncols, fn=lambda a: a.rearrange('p (k n) -> p k n', k=K))
        in_ = self.w_d.v(0, K * 128, c0, c0 + ncols, fn=lambda a: a.rearrange('(k p) n -> p k n', p=128))
        s.dma('pool', out, in_, prep=True)


def load_w_cols(s, C, wt, w_d, K, c0, ncols, q='pool'):
    if isinstance(w_d, PW):
        i = w_d.idx[c0]
        s.dma('sp', wt.v(0, 128, 0, K * ncols), w_d.t.v(i * 128, i * 128 + 128, 0, K * ncols))
        return
    out = wt.v(0, 128, 0, K * ncols, fn=lambda a: a.rearrange('p (k n) -> p k n', k=K))
    in_ = w_d.v(0, K * 128, c0, c0 + ncols, fn=lambda a: a.rearrange('(k p) n -> p k n', p=128))
    s.dma(q, out, in_)


def build_T1():
    nc = bass.Bass("TRN2", target_bir_lowering=False)
    s = Sched(nc)
    C = Ctx()
    x_d = s.dram('x', [TL, D], F32, 'ExternalInput')
    w_d = s.dram('w_in', [D, N_IN], F32, 'ExternalInput')
    b_d = s.dram('b_in', [N_IN, 1], F32, 'ExternalInput')
    g_d = s.dram('g_pre', [D, 1], F32, 'ExternalInput')
    id_d = s.dram('ident', [128, 128], F32, 'ExternalInput')
    xf_d = s.dram('xf', [4 * XF_ROWS, TL], F32, 'ExternalOutput')
    common_setup(s, C, id_d)
    C.epsb = s.sb('epsb', [128, 1], F32)
    s.op('dve', lambda e: e.memset(C.epsb.h[:, :], EPS), writes=[C.epsb.v()])
    load_xT(s, C, x_d, TL)
    gcols = load_vec_cols(s, C, 'gpre', g_d, 8)
    rstd = rms_rstd(s, C, C.xT, TL, 'pre')
    hT = make_h(s, C, gcols, rstd, TL, 'hT')
    phase_T1_proj(s, C, hT, w_d, b_d, xf_d)
    s.finish()
    return nc


def phase_T1_proj(s, C, hT, w_d, b_d, xf_d):
    tiles = []
    for i in range(4):
        tiles.append((OFF_Q + 128 * i, 128, [(0, 128, i, 0)]))
        tiles.append((OFF_K + 128 * i, 128, [(0, 128, i, 128)]))
        tiles.append((OFF_V + 128 * i, 128, [(0, 128, i, 256)]))
    tiles.append((OFF_G16, 16, [(0, 16, None, 384)]))
    for i in range(2):
        tiles.append((OFF_F + 128 * i, 128, [(0, 64, 2 * i, 388), (64, 64, 2 * i + 1, 388)]))
        tiles.append((OFF_S5 + 128 * i, 128, [(0, 64, 2 * i, 452), (64, 64, 2 * i + 1, 452)]))
    wts = [s.sb('wT1_%d' % i, [128, 8 * 128], BF16) for i in range(2)]
    bts = [s.sb('bT1_%d' % i, [128, 1], F32) for i in range(2)]
    stg = [s.sb('zst%d' % i, [128, 512], F32) for i in range(4)]
    n = 0
    for ti, (c0, ncols, dsts) in enumerate(tiles):
        wt = wts[ti % 2]
        bt = bts[ti % 2]
        load_w_cols(s, C, wt, w_d, 8, c0, ncols)
        s.dma('sp', bt.v(0, ncols), b_d.v(c0, c0 + ncols))
        for tc in range(TL // 512):
            ps = next_ps(C)
            for kt in range(8):
                s.mm(ps.v(0, ncols), wt.v(0, 128, kt * ncols, (kt + 1) * ncols), hT[kt].v(0, 128, 512 * tc, 512 * tc + 512),
                     start=(kt == 0), stop=(kt == 7))
            st = stg[n % 4]
            n += 1
            s.act(st.v(0, ncols), ps.v(0, ncols), AF.Identity, bias=bt.v(0, ncols))
            for (r0, nr, dst, dr0) in dsts:
                if dst is None:
                    for dd in range(4):
                        o = xf_d.v(dd * XF_ROWS + dr0, dd * XF_ROWS + dr0 + 4, 512 * tc, 512 * tc + 512)
                        s.dma('sp' if dd % 2 else 'act', o, st.v(0, 16, fn=lambda a, dd=dd: a[dd::4, :]))
                else:
                    o = xf_d.v(dst * XF_ROWS + dr0, dst * XF_ROWS + dr0 + nr, 512 * tc, 512 * tc + 512)
                    s.dma('sp' if n % 2 else 'act', o, st.v(r0, r0 + nr))


class WPool:
    def __init__(self, s, name, ncols, n):
        self.b = [s.sb('%s%d' % (name, i), [128, ncols], BF16) for i in range(n)]
        self.i = 0

    def get(self):
        self.i += 1
        return self.b[self.i % len(self.b)]


def bias_cols(s, name, d_t, offs, n=128):
    t = s.sb(name, [128, len(offs)], F32)
    for k, o in enumerate(offs):
        s.dma('sp' if k % 2 else 'act', t.v(0, n, k, k + 1), d_t.v(o, o + n))
    return t


def store_xT(s, C, out_d, tok0, ntok, stage):
    for tt in range(ntok // 128):
        st = stage[tt % 2]
        for kg in range(2):
            ps = next_ps(C)
            for i in range(4):
                kt = kg * 4 + i
                s.tr(ps.v(0, 128, 128 * i, 128 * i + 128), C.xT[kt].v(0, 128, 128 * tt, 128 * tt + 128), C.ident.v())
            s.copy(st.v(0, 128, 512 * kg, 512 * kg + 512), ps.v(), eng=s.evac_eng())
        s.dma('sp' if tt % 2 else 'act', out_d.v(tok0 + 128 * tt, tok0 + 128 * tt + 128), st.v())


def t2_setup(s, C, W):
    NT = C.NT
    C.wp = WPool(s, 'wp', 1024, 8)
    C.wbig = WPool(s, 'wbig', 4096, getattr(C, 'n_wbig', 2))
    C.big = s.sb('big', [128, 32 * NT], BF16)
    C.h2T = s.sb('h2T', [128, 8 * NT], BF16)
    C.tmpf = [s.sb('tmpf%d' % i, [128, 512], F32) for i in range(6)]
    C.tmpi = 0
    C.hnst = [s.sb('hnst%d' % i, [128, NT], F32) for i in range(2)]
    C.b_o = bias_cols(s, 'b_o', W['b_in'], [OFF_O + 128 * i for i in range(4)])
    C.b_g = bias_cols(s, 'b_g', W['b_in'], [OFF_GATE + 128 * i for i in range(24)])
    C.b_glu = bias_cols(s, 'b_glu', W['b_glu'], [128 * i for i in range(16)])
    C.normg = bias_cols(s, 'normg', W['norm_g'], [128 * i for i in range(4)])
    C.g_post = bias_cols(s, 'g_post', W['g_post'], [128 * i for i in range(8)])
    C.g_f1 = bias_cols(s, 'g_f1', W['g_ffn_pre'], [128 * i for i in range(8)])
    C.g_f2 = bias_cols(s, 'g_f2', W['g_ffn_post'], [128 * i for i in range(8)])


def tmpf(C):
    C.tmpi += 1
    return C.tmpf[C.tmpi % len(C.tmpf)]


def ps6(C):
    C.psi = (C.psi + 1) % 6
    return C.ps[C.psi]


def big_view(C, tile, c0, c1):
    NT = C.NT
    return C.big.v(0, 128, tile * NT + c0, tile * NT + c1)


def phase_T2(s, C, W, xb_d, tok0):
    NT = C.NT
    NTC = NT // 512
    hT = C.hT
    MIX0, HG0, YF0, YS0, OB0 = 8, 16, 20, 22, 24

    def chunk(tc):
        return 512 * tc, 512 * tc + 512

    for (row0, dst0) in ((128, YF0), (192, YS0)):
        for half in range(2):
            st = C.hnst[half]
            for q in range(2):
                src = 2 * half + q
                s.dma('sp' if q else 'act', st.v(64 * q, 64 * q + 64),
                      xb_d.v(src * XB_ROWS + row0, src * XB_ROWS + row0 + 64, tok0, tok0 + NT))
            s.copy(big_view(C, dst0 + half, 0, NT), st.v(), eng='pool')
    for i in range(4):
        st = C.hnst[i % 2]
        s.dma('sp', st.v(), xb_d.v(i * XB_ROWS, i * XB_ROWS + 128, tok0, tok0 + NT))
        wt = C.wp.get()
        load_w_cols(s, C, wt, W['w_in'], 8, OFF_O + 128 * i, 128)
        for tc in range(NTC):
            c0, c1 = chunk(tc)
            ps = ps6(C)
            for kt in range(8):
                s.mm(ps.v(), wt.v(0, 128, 128 * kt, 128 * kt + 128), hT[kt].v(0, 128, c0, c1), start=(kt == 0), stop=(kt == 7))
            sg = tmpf(C)
            s.act(sg.v(), ps.v(), AF.Sigmoid, bias=C.b_o.v(0, 128, i, i + 1))
            s.stt(big_view(C, HG0 + i, c0, c1), st.v(0, 128, c0, c1), C.normg.v(0, 128, i, i + 1), sg.v(), ALU.mult, ALU.mult)
    for j in range(8):
        w_um = C.wp.get(); load_w_cols(s, C, w_um, W['w_up_m'], 4, 128 * j, 128)
        w_uf = C.wp.get(); load_w_cols(s, C, w_uf, W['w_up_f'], 2, 128 * j, 128)
        w_ga = C.wp.get(); load_w_cols(s, C, w_ga, W['w_glu'], 2, 128 * j, 128)
        w_gb = C.wp.get(); load_w_cols(s, C, w_gb, W['w_glu'], 2, 1024 + 128 * j, 128)
        w_g = []
        for b in range(3):
            w = C.wp.get(); load_w_cols(s, C, w, W['w_in'], 8, OFF_GATE + 1024 * b + 128 * j, 128)
            w_g.append(w)
        for tc in range(NTC):
            c0, c1 = chunk(tc)

            def gate(b):
                ps = ps6(C)
                for kt in range(8):
                    s.mm(ps.v(), w_g[b].v(0, 128, 128 * kt, 128 * kt + 128), hT[kt].v(0, 128, c0, c1), start=(kt == 0), stop=(kt == 7))
                g = tmpf(C)
                s.act(g.v(), ps.v(), AF.Sigmoid, bias=C.b_g.v(0, 128, 8 * b + j, 8 * b + j + 1))
                return g

            def small(wt, K, src0):
                ps = ps6(C)
                for kt in range(K):
                    s.mm(ps.v(), wt.v(0, 128, 128 * kt, 128 * kt + 128), big_view(C, src0 + kt, c0, c1), start=(kt == 0), stop=(kt == K - 1))
                return ps
            ps_ym = small(w_um, 4, HG0)
            g0 = gate(0)
            acc = tmpf(C)
            s.tt(acc.v(), g0.v(), ps_ym.v(), ALU.mult)
            ps_yf = small(w_uf, 2, YF0)
            g1 = gate(1)
            t1 = tmpf(C)
            s.tt(t1.v(), g1.v(), ps_yf.v(), ALU.mult)
            s.tt(acc.v(), acc.v(), t1.v(), ALU.add, eng='pool')
            ps_za = small(w_ga, 2, YS0)
            ps_zb = small(w_gb, 2, YS0)
            sb_ = tmpf(C)
            s.act(sb_.v(), ps_zb.v(), AF.Sigmoid, bias=C.b_glu.v(0, 128, 8 + j, 8 + j + 1))
            ys = tmpf(C)
            s.stt(ys.v(), ps_za.v(), C.b_glu.v(0, 128, j, j + 1), sb_.v(), ALU.add, ALU.mult)
            g2 = gate(2)
            s.tt(ys.v(), ys.v(), g2.v(), ALU.mult, eng='pool')
            s.tt(big_view(C, MIX0 + j, c0, c1), acc.v(), ys.v(), ALU.add)
    ss = [C.ps[6], C.ps[7]]
    sqb = [s_ for s_ in C.sqb]

    def proj_norm_add(wname, K, src_view, dst0_tile, dst_T, gcols, wpool, kcols):
        n = 0
        for j in range(8):
            wt = wpool.get()
            load_w_cols(s, C, wt, W[wname], K, 128 * j, 128)
            for tc in range(NTC):
                c0, c1 = chunk(tc)
                ps = ps6(C)
                for kt in range(K):
                    s.mm(ps.v(), wt.v(0, 128, 128 * kt, 128 * kt + 128), src_view(kt, c0, c1), start=(kt == 0), stop=(kt == K - 1))
                if dst_T is None:
                    ov = big_view(C, dst0_tile + j, c0, c1)
                else:
                    ov = dst_T.v(0, 128, j * NT + c0, j * NT + c1)
                s.copy(ov, ps.v(), eng='dve')
                q = sqb[n % 2]
                n += 1
                s.act(q.v(), ps.v(), AF.Square)
                s.mm(ss[tc].v(), C.ones.v(), q.v(), start=(j == 0), stop=(j == 7))
        for tc in range(NTC):
            c0, c1 = chunk(tc)
            r = C.rstd2.v(0, 128, c0, c1)
            s.act(r, ss[tc].v(), AF.Sqrt, bias=C.epsb.v(), scale=1.0 / D)
            s.op('dve', lambda e, r=r: e.reciprocal(out=r.ap, in_=r.ap), reads=[r], writes=[r])
        for j in range(8):
            for tc in range(NTC):
                c0, c1 = chunk(tc)
                if dst_T is None:
                    ov = big_view(C, dst0_tile + j, c0, c1)
                else:
                    ov = dst_T.v(0, 128, j * NT + c0, j * NT + c1)
                t = tmpf(C)
                s.stt(t.v(), ov, gcols.v(0, 128, j, j + 1), C.rstd2.v(0, 128, c0, c1), ALU.mult, ALU.mult)
                xv = C.xT[j].v(0, 128, C.xoff + c0, C.xoff + c1)
                s.tt(xv, xv, t.v(), ALU.add, eng='pool')

    proj_norm_add('w_out', 8, lambda kt, c0, c1: big_view(C, MIX0 + kt, c0, c1), OB0, None, C.g_post, C.wp, 128)
    for tc in range(NTC):
        c0, c1 = chunk(tc)
        ps = ps6(C)
        for kt in range(8):
            q = sqb[kt % 2]
            s.act(q.v(), C.xT[kt].v(0, 128, C.xoff + c0, C.xoff + c1), AF.Square)
            s.mm(ps.v(), C.ones.v(), q.v(), start=(kt == 0), stop=(kt == 7))
        r = C.rstd2.v(0, 128, c0, c1)
        s.act(r, ps.v(), AF.Sqrt, bias=C.epsb.v(), scale=1.0 / D)
        s.op('dve', lambda e, r=r: e.reciprocal(out=r.ap, in_=r.ap), reads=[r], writes=[r])
    for kt in range(8):
        s.stt(C.h2T.v(0, 128, kt * NT, kt * NT + NT), C.xT[kt].v(0, 128, C.xoff, C.xoff + NT), C.g_f1.v(0, 128, kt, kt + 1),
              C.rstd2.v(0, 128, 0, NT), ALU.mult, ALU.mult)
    for ng in range(8):
        wt = C.wbig.get()
        load_w_cols(s, C, wt, W['w_ffn1'], 8, 512 * ng, 512)
        for nn in range(4):
            n = 4 * ng + nn
            for tc in range(NTC):
                c0, c1 = chunk(tc)
                ps = ps6(C)
                for kt in range(8):
                    s.mm(ps.v(), wt.v(0, 128, 512 * kt + 128 * nn, 512 * kt + 128 * nn + 128), C.h2T.v(0, 128, kt * NT + c0, kt * NT + c1),
                         start=(kt == 0), stop=(kt == 7))
                t = tmpf(C)
                s.act(t.v(), ps.v(), AF.Relu)
                s.tt(big_view(C, n, c0, c1), t.v(), t.v(), ALU.mult, eng='pool' if (n + tc) % 2 else 'dve')
    proj_norm_add('w_ffn2', 32, lambda kt, c0, c1: big_view(C, kt, c0, c1), None, C.h2T, C.g_f2, C.wbig, 128)


def build_T2(NT=1024):
    nc = bass.Bass("TRN2", target_bir_lowering=False)
    s = Sched(nc)
    C = Ctx()
    C.NT = NT
    x_d = s.dram('x', [TL, D], F32, 'ExternalInput')
    xb_d = s.dram('xb', [4 * XB_ROWS, TL], F32, 'ExternalInput')
    W = {}
    for name, shape in [('w_in', [D, N_IN]), ('b_in', [N_IN, 1]), ('g_pre', [D, 1]), ('norm_g', [512, 1]), ('w_up_m', [512, D]),
                        ('w_up_f', [256, D]), ('w_glu', [256, 2 * D]), ('b_glu', [2 * D, 1]), ('w_out', [D, D]), ('g_post', [D, 1]),
                        ('g_ffn_pre', [D, 1]), ('g_ffn_post', [D, 1]), ('w_ffn1', [D, 4 * D]), ('w_ffn2', [4 * D, D])]:
        W[name] = s.dram(name, shape, F32, 'ExternalInput')
    id_d = s.dram('ident', [128, 128], F32, 'ExternalInput')
    out_d = s.dram('x_out', [TL, D], F32, 'ExternalOutput')
    common_setup(s, C, id_d)
    C.epsb = s.sb('epsb', [128, 1], F32)
    s.op('dve', lambda e: e.memset(C.epsb.h[:, :], EPS), writes=[C.epsb.v()])
    C.sqb = [s.sb('sqb%d' % i, [128, 512], BF16) for i in range(2)]
    C.rstd2 = s.sb('rstd2', [128, NT], F32)
    gcols = load_vec_cols(s, C, 'gpre', W['g_pre'], 8)
    t2_setup(s, C, W)
    C.xT = [s.sb('xT%d' % k, [128, NT], F32) for k in range(8)]
    C.hT = [s.sb('hT%d' % k, [128, NT], BF16) for k in range(8)]
    C.xoff = 0
    xstage = [s.sb('xst%d' % i, [128, 1024], F32) for i in range(4)]
    for hf in range(TL // NT):
        tok0 = hf * NT
        for tg in range(NT // 512):
            for i in range(4):
                tt = tg * 4 + i
                s.dma('sp' if i % 2 == 0 else 'act', xstage[i].v(), x_d.v(tok0 + 128 * tt, tok0 + 128 * tt + 128))
            for kt in range(8):
                ps = ps6(C)
                for i in range(4):
                    s.tr(ps.v(0, 128, 128 * i, 128 * i + 128), xstage[i].v(0, 128, 128 * kt, 128 * kt + 128), C.ident.v())
                s.copy(C.xT[kt].v(0, 128, 512 * tg, 512 * tg + 512), ps.v(), eng=s.evac_eng())
        for tc in range(NT // 512):
            ps = ps6(C)
            for kt in range(8):
                q = C.sqb[kt % 2]
                s.act(q.v(), C.xT[kt].v(0, 128, 512 * tc, 512 * tc + 512), AF.Square)
                s.mm(ps.v(), C.ones.v(), q.v(), start=(kt == 0), stop=(kt == 7))
            r = C.rstd2.v(0, 128, 512 * tc, 512 * tc + 512)
            s.act(r, ps.v(), AF.Sqrt, bias=C.epsb.v(), scale=1.0 / D)
            s.op('dve', lambda e, r=r: e.reciprocal(out=r.ap, in_=r.ap), reads=[r], writes=[r])
        for kt in range(8):
            s.stt(C.hT[kt].v(), C.xT[kt].v(), gcols.v(0, 128, kt, kt + 1), C.rstd2.v(), ALU.mult, ALU.mult)
        phase_T2(s, C, W, xb_d, tok0)
        store_xT(s, C, out_d, tok0, NT, xstage)
    s.finish()
    return nc


def fourier_tables():
    c = np.arange(64)
    a64 = 2 * np.pi * np.outer(c, c) / 64.0
    s1 = np.arange(128)
    a128 = 2 * np.pi * np.outer(s1, s1) / 128.0
    atw = 2 * np.pi * np.outer(s1, c) / 8192.0
    sc = 1.0 / np.sqrt(8192.0 * 64.0)
    z = np.zeros((64, 64))
    tabs = dict(
        f_f64=np.concatenate([np.cos(a64), -np.sin(a64)], 1),
        f_c128=np.cos(a128), f_s128=np.sin(a128), f_ns128=-np.sin(a128),
        f_tw=np.concatenate([np.cos(atw), -np.sin(atw)], 1),
        f_bdc=np.block([[np.cos(a64), z], [z, np.cos(a64)]]) * sc,
        f_bds=np.block([[np.sin(a64), z], [z, np.sin(a64)]]) * sc,
    )
    return {k: np.ascontiguousarray(v.astype(np.float32)) for k, v in tabs.items()}


FT_SHAPES = dict(f_f64=[64, 128], f_c128=[128, 128], f_s128=[128, 128], f_ns128=[128, 128], f_tw=[128, 128],
                 f_bdc=[128, 128], f_bds=[128, 128])


def phase_U_fourier(s, C, xf_d, xb_d, tabs_d):
    tb = {}
    for k in ['f_f64', 'f_c128', 'f_s128', 'f_ns128', 'f_bdc', 'f_bds']:
        tb[k] = s.sb(k, FT_SHAPES[k], BF16)
        s.dma('pool', tb[k].v(), tabs_d[k].v())
    tw = s.sb('f_tw', [128, 128], F32)
    s.dma('sp', tw.v(), tabs_d['f_tw'].v())
    UTb = s.sb('f_UTb', [64, S], BF16)
    for src in range(4):
        s.dma('pool', UTb.v(0, 64, 2048 * src, 2048 * src + 2048), xf_d.v(src * XF_ROWS + 388, src * XF_ROWS + 452))
    Zre = s.sb('f_Zre', [128, 4096], BF16)
    Zim = s.sb('f_Zim', [128, 4096], BF16)
    for g in range(16):
        ps = next_ps(C)
        for i in range(4):
            s2 = 4 * g + i
            lhsT = V(UTb, UTb.h[0:64, s2::64], (0, 64, 0, S))
            s.mm(ps.v(0, 128, 128 * i, 128 * i + 128), lhsT, tb['f_f64'].v())
        pv = lambda lo: ps.v(fn=lambda a: a.rearrange('p (s c) -> p s c', c=128)[:, :, lo:lo + 64])
        zv = lambda Zt: Zt.v(0, 128, 256 * g, 256 * g + 256, fn=lambda a: a.rearrange('p (s c) -> p s c', c=64))
        s.copy(zv(Zre), pv(0), eng='act')
        s.copy(zv(Zim), pv(64), eng='dve')
    ArP = s.sb('f_ArP', [128, 4096], BF16)
    AiP = s.sb('f_AiP', [128, 4096], BF16)
    tmp = [s.sb('f_tmp%d' % i, [128, 512], F32) for i in range(4)]
    for ch in range(8):
        c0, c1 = 512 * ch, 512 * ch + 512
        pr = next_ps(C)
        s.mm(pr.v(), tb['f_c128'].v(), Zre.v(0, 128, c0, c1), start=True, stop=False)
        s.mm(pr.v(), tb['f_s128'].v(), Zim.v(0, 128, c0, c1), start=False, stop=True)
        pi = next_ps(C)
        s.mm(pi.v(), tb['f_c128'].v(), Zim.v(0, 128, c0, c1), start=True, stop=False)
        s.mm(pi.v(), tb['f_ns128'].v(), Zre.v(0, 128, c0, c1), start=False, stop=True)
        r3 = lambda a: a.rearrange('p (s c) -> p s c', c=64)
        tre = tw.v(0, 128, 8 * ch, 8 * ch + 8, fn=lambda a: a.unsqueeze(2).to_broadcast([128, 8, 64]))
        tim = tw.v(0, 128, 64 + 8 * ch, 64 + 8 * ch + 8, fn=lambda a: a.unsqueeze(2).to_broadcast([128, 8, 64]))
        t = [x.v(fn=r3) for x in tmp]
        s.tt(t[0], pr.v(fn=r3), tre, ALU.mult)
        s.tt(t[1], pi.v(fn=r3), tim, ALU.mult)
        s.tt(t[2], pr.v(fn=r3), tim, ALU.mult)
        s.tt(t[3], pi.v(fn=r3), tre, ALU.mult)
        perm = lambda a: a.rearrange('p (c s) -> p s c', s=64)[:, 8 * ch:8 * ch + 8, :]
        s.tt(ArP.v(fn=perm), t[0], t[1], ALU.subtract, eng='pool')
        s.tt(AiP.v(fn=perm), t[2], t[3], ALU.add, eng='pool')
    ArT = s.sb('f_ArT', [128, 4096], BF16)
    AiT = s.sb('f_AiT', [128, 4096], BF16)
    for (src, dst) in ((ArP, ArT), (AiP, AiT)):
        for g in range(8):
            ps = next_ps(C)
            pb = lambda lo, hi: ps.v(fn=lambda a: a.bitcast(BF16)[:, lo:hi])
            for i in range(4):
                blk = 4 * g + i
                s.tr(pb(128 * i, 128 * i + 128), src.v(0, 128, 128 * blk, 128 * blk + 128), C.identb.v())
            s.copy(dst.v(0, 128, 512 * g, 512 * g + 512), pb(0, 512), eng=s.evac_eng())
    Y = s.sb('f_Y', [128, 4096], F32)
    for g in range(8):
        c0, c1 = 512 * g, 512 * g + 512
        ps = next_ps(C)
        s.mm(ps.v(), tb['f_bdc'].v(), ArT.v(0, 128, c0, c1), start=True, stop=False)
        s.mm(ps.v(), tb['f_bds'].v(), AiT.v(0, 128, c0, c1), start=False, stop=True)
        s.copy(Y.v(0, 128, c0, c1), ps.v(), eng=s.evac_eng())
    n = 0
    for cp in range(2):
        for dst in range(4):
            r0 = dst * XB_ROWS + 128 + cp
            o = xb_d.v(r0, r0 + 64, 0, TL, fn=lambda a: a[::2, :].rearrange('b (s j) -> s b j', j=128))
            i_ = Y.v(64 * cp + 16 * dst, 64 * cp + 16 * dst + 16, 0, 4096, fn=lambda a: a.rearrange('p (b j) -> p b j', j=128))
            s.dma('sp' if n % 2 else 'act', o, i_)
            n += 1


def build_U(parts=('fourier',)):
    nc = bass.Bass("TRN2", target_bir_lowering=False)
    s = Sched(nc)
    C = Ctx()
    xf_d = s.dram('xf', [4 * XF_ROWS, TL], F32, 'ExternalInput')
    id_d = s.dram('ident', [128, 128], F32, 'ExternalInput')
    xb_d = s.dram('xb', [4 * XB_ROWS, TL], F32, 'ExternalOutput')
    tabs_d = {k: s.dram(k, v, F32, 'ExternalInput') for k, v in FT_SHAPES.items()}
    md = {k: s.dram(k, v, F32, 'ExternalInput') for k, v in MT_SHAPES.items()}
    sd = {k: s.dram(k, v, F32, 'ExternalInput') for k, v in ST_SHAPES.items()}
    common_setup(s, C, id_d)
    C.epsb = s.sb('epsb', [128, 1], F32)
    s.op('dve', lambda e: e.memset(C.epsb.h[:, :], EPS), writes=[C.epsb.v()])
    C.oneb = s.sb('oneb', [128, 1], F32)
    s.op('dve', lambda e: e.memset(C.oneb.h[:, :], 1.0), writes=[C.oneb.v()])
    if 'dbg' in parts:
        C.dbg = {k: s.dram(k, sh, F32, 'ExternalOutput') for k, sh in [('d_q', [128, S]), ('d_k', [128, S]), ('d_H', [128, S]),
                                                                        ('d_sm', [128, 512]), ('d_va', [128, 64 * 129])]}
    mk = s.mark()
    if 'fourier' in parts:
        phase_U_fourier(s, C, xf_d, xb_d, tabs_d)
        s.release(mk)
    if 's5' in parts:
        phase_U_s5(s, C, xf_d, xb_d, sd)
        s.release(mk)
    if 'mlstm' in parts:
        phase_U_mlstm(s, C, xf_d, xb_d, md)
    s.finish()
    return nc


def mlstm_tables():
    j = np.arange(128)
    trif = (j[:, None] <= j[None, :]).astype(np.float32)
    sc = 128.0 ** -0.5
    return dict(m_trif=trif, m_trib=np.ascontiguousarray(trif.T), m_maskf=trif * sc, m_maskb=np.ascontiguousarray(trif.T) * sc,
                m_onesf=np.ones((128, 128), np.float32))


MT_SHAPES = dict(m_trif=[128, 128], m_trib=[128, 128], m_maskf=[128, 128], m_maskb=[128, 128], m_onesf=[128, 128],
                 m_cwq=[128, 5], m_cwk=[128, 5], m_cbq=[128, 1], m_cbk=[128, 1])


def phase_U_mlstm(s, C, xf_d, xb_d, md):
    NCH = S // 128
    SC = 128.0 ** -0.5
    tb = {}
    for k in MT_SHAPES:
        tb[k] = s.sb(k, MT_SHAPES[k], F32)
        s.dma('sp', tb[k].v(), md[k].v())
    G4 = s.sb('m_G4', [4, S], F32)
    for src in range(4):
        s.dma('act', G4.v(0, 4, 2048 * src, 2048 * src + 2048), xf_d.v(src * XF_ROWS + 384, src * XF_ROWS + 388))
    gT = s.sb('m_gT', [128, NCH * 4], F32)
    ps = next_ps(C)
    for c in range(NCH):
        s.tr(ps.v(0, 128, 4 * c, 4 * c + 4), G4.v(0, 4, 128 * c, 128 * c + 128), C.ident.v(0, 4, 0, 4))
    s.copy(gT.v(), ps.v(0, 128, 0, 4 * NCH))
    gcol = lambda g: gT.v(fn=lambda a: a.rearrange('p (c g) -> p c g', g=4)[:, :, g])
    sm = {}
    for nm in ['lf_f', 'lf_b', 'b_f', 'b_b', 'w_f', 'w_b', 'enb_f', 'enb_b', 'eg_f', 'eg_b', 'egs_f', 'egs_b', 'tmp']:
        sm[nm] = s.sb('m_' + nm, [128, NCH], F32)
    for d, gi in (('f', 2), ('b', 3)):
        lf = sm['lf_' + d]
        s.act(sm['tmp'].v(), gcol(gi), AF.Exp, scale=-1.0)
        s.act(lf.v(), sm['tmp'].v(), AF.Ln, bias=C.oneb.v(), scale=1.0)
        s.ts(lf.v(), lf.v(), -1.0, ALU.mult)
        pb = next_ps(C)
        s.mm(pb.v(0, 128, 0, NCH), tb['m_tri' + d].v(), lf.v())
        s.copy(sm['b_' + d].v(), pb.v(0, 128, 0, NCH))
        pg = next_ps(C)
        s.mm(pg.v(0, 128, 0, NCH), tb['m_onesf'].v(), lf.v())
        s.act(sm['eg_' + d].v(), pg.v(0, 128, 0, NCH), AF.Exp)
        s.ts(sm['egs_' + d].v(), sm['eg_' + d].v(), SC, ALU.mult)
        s.act(sm['enb_' + d].v(), sm['b_' + d].v(), AF.Exp, scale=-1.0)
        s.tt(sm['tmp'].v(), gcol(0 if d == 'f' else 1), sm['b_' + d].v(), ALU.subtract)
        s.act(sm['w_' + d].v(), sm['tmp'].v(), AF.Exp)
    zpb = s.sb('m_zpb', [128, S + 4], BF16)
    s.op('dve', lambda e: e.memset(zpb.h[:, 0:2], 0.0), writes=[zpb.v(0, 128, 0, 2)])
    s.op('dve', lambda e: e.memset(zpb.h[:, S + 2:S + 4], 0.0), writes=[zpb.v(0, 128, S + 2, S + 4)])
    acc0 = s.sb('m_acc0', [128, 2048], F32)
    vst = s.sb('m_vst', [128, 4096], F32)
    dg = s.sb('m_dg', [128, 10 * 128], BF16)
    qT = s.sb('m_qT', [128, S], BF16)
    kT = s.sb('m_kT', [128, S], BF16)
    for (qi, row0, cw, cb, dstT) in ((0, 0, tb['m_cwq'], tb['m_cbq'], qT), (1, 128, tb['m_cwk'], tb['m_cbk'], kT)):
        for k in range(5):
            s.ts(dg.v(0, 128, 128 * (5 * qi + k), 128 * (5 * qi + k) + 128), C.ident.v(), cw.v(0, 128, k, k + 1), ALU.mult)
        for src in range(4):
            s.dma('pool', zpb.v(0, 128, 2 + 2048 * src, 2 + 2048 * src + 2048),
                  xf_d.v(src * XF_ROWS + row0, src * XF_ROWS + row0 + 128))
        for ch in range(S // 512):
            ps = next_ps(C)
            for k in range(5):
                s.mm(ps.v(), dg.v(0, 128, 128 * (5 * qi + k), 128 * (5 * qi + k) + 128),
                     zpb.v(0, 128, 512 * ch + k, 512 * ch + k + 512), start=(k == 0), stop=(k == 4))
            s.act(dstT.v(0, 128, 512 * ch, 512 * ch + 512), ps.v(), AF.Silu, bias=cb.v())
    ktok = s.sb('m_ktok', [128, S], BF16)
    for g in range(NCH // 4):
        ps = next_ps(C)
        pb = lambda lo, hi: ps.v(fn=lambda a: a.bitcast(BF16)[:, lo:hi])
        for i in range(4):
            c = 4 * g + i
            s.tr(pb(128 * i, 128 * i + 128), kT.v(0, 128, 128 * c, 128 * c + 128), C.identb.v())
        s.copy(ktok.v(0, 128, 512 * g, 512 * g + 512), pb(0, 512), eng=s.evac_eng())
    vaug = {d: s.sb('m_vaug_' + d, [128, NCH * 129], BF16) for d in 'fb'}
    for g in range(NCH // 4):
        if g % 8 == 0:
            hv_ = g // 8
            for q2 in range(2):
                src = 2 * hv_ + q2
                s.dma('sp' if q2 else 'act', vst.v(0, 128, 2048 * q2, 2048 * q2 + 2048),
                      xf_d.v(src * XF_ROWS + 256, src * XF_ROWS + 384))
        ps = next_ps(C)
        for i in range(4):
            c = 4 * g + i
            cl = c % 32
            s.tr(ps.v(0, 128, 128 * i, 128 * i + 128), vst.v(0, 128, 128 * cl, 128 * cl + 128), C.ident.v())
        for i in range(4):
            c = 4 * g + i
            for d in 'fb':
                s.ts(vaug[d].v(0, 128, 129 * c, 129 * c + 128), ps.v(0, 128, 128 * i, 128 * i + 128), sm['w_' + d].v(0, 128, c, c + 1), ALU.mult)
    for d in 'fb':
        s.copy(vaug[d].v(fn=lambda a: a.rearrange('p (c e) -> p c e', e=129)[:, :, 128]), sm['w_' + d].v(), eng='pool')
    H = s.sb('m_H', [128, S], F32)
    P = {d: s.sb('m_P_' + d, [128, 129], F32) for d in 'fb'}
    Cb = {d: [s.sb('m_Cb_%s%d' % (d, i), [128, 129], BF16) for i in range(2)] for d in 'fb'}
    Sm = {d: [s.sb('m_Sm_%s%d' % (d, i), [128, 128], BF16) for i in range(3)] for d in 'fb'}
    den = {d: [s.sb('m_den_%s%d' % (d, i), [128, 1], F32) for i in range(3)] for d in 'fb'}
    mask = {'f': tb['m_maskf'], 'b': tb['m_maskb']}
    def dir_gen(d):
        for step in range(NCH):
            c = step if d == 'f' else NCH - 1 - step
            cprev = c - 1 if d == 'f' else c + 1
            va = vaug[d].v(0, 128, 129 * c, 129 * c + 129)
            if step < NCH - 1:
                ps_d = next_ps(C)
                s.mm(ps_d.v(0, 128, 0, 129), ktok.v(0, 128, 128 * c, 128 * c + 128), va)
                if step == 0:
                    s.copy(P[d].v(), ps_d.v(0, 128, 0, 129))
                else:
                    s.stt(P[d].v(), P[d].v(), sm['eg_' + d].v(0, 128, cprev, cprev + 1), ps_d.v(0, 128, 0, 129), ALU.mult, ALU.add)
                yield
                s.act(Cb[d][(step + 1) % 2].v(), P[d].v(), AF.Copy, scale=sm['egs_' + d].v(0, 128, c, c + 1))
                yield
            ps_s = next_ps(C)
            s.mm(ps_s.v(0, 128, 0, 128), kT.v(0, 128, 128 * c, 128 * c + 128), qT.v(0, 128, 128 * c, 128 * c + 128))
            smt = Sm[d][step % 3]
            s.tt(smt.v(), ps_s.v(0, 128, 0, 128), mask[d].v(), ALU.mult)
            yield
            ps_o = next_ps(C)
            s.mm(ps_o.v(0, 128, 0, 129), smt.v(), va, start=True, stop=(step == 0))
            if step > 0:
                s.mm(ps_o.v(0, 128, 0, 129), qT.v(0, 128, 128 * c, 128 * c + 128), Cb[d][step % 2].v(), start=False, stop=True)
            dn = den[d][step % 3]
            s.ts(dn.v(), ps_o.v(0, 128, 128, 129), -1.0, ALU.mult, sm['enb_' + d].v(0, 128, c, c + 1), ALU.max)
            yield
            s.tt(dn.v(), dn.v(), ps_o.v(0, 128, 128, 129), ALU.max)
            yield
            s.op('dve', lambda e, dn=dn: e.reciprocal(out=dn.h[:, :], in_=dn.h[:, :]), reads=[dn.v()], writes=[dn.v()])
            yield
            hv = H.v(0, 128, 128 * c, 128 * c + 128)
            if step < NCH // 2:
                s.act(hv, ps_o.v(0, 128, 0, 128), AF.Copy, scale=dn.v())
            else:
                s.stt(hv, ps_o.v(0, 128, 0, 128), dn.v(), hv, ALU.mult, ALU.add)
            yield

    gens = [dir_gen('f'), dir_gen('b')]
    while gens:
        for gnr in list(gens):
            try:
                next(gnr)
            except StopIteration:
                gens.remove(gnr)
    if getattr(C, 'dbg', None) is not None:
        s.dma('pool', C.dbg['d_q'].v(), qT.v())
        s.dma('pool', C.dbg['d_k'].v(), kT.v())
        s.dma('sp', C.dbg['d_H'].v(), H.v())
        for i, nm in enumerate(['lf_f', 'b_f', 'w_f', 'enb_f', 'eg_f', 'lf_b', 'b_b', 'w_b']):
            s.dma('sp', C.dbg['d_sm'].v(0, 128, 64 * i, 64 * i + 64), sm[nm].v())
        s.dma('pool', C.dbg['d_va'].v(), vaug['f'].v())
    H3 = lambda lo, hi: H.v(0, 128, 128 * lo, 128 * hi, fn=lambda a: a.rearrange('p (c e) -> p c e', e=128))
    mu = sm['tmp']
    s.op('dve', lambda e: e.tensor_reduce(out=mu.h[:, :], in_=H.h[:, :].rearrange('p (c e) -> p c e', e=128), axis=AX.X, op=ALU.add),
         reads=[H.v()], writes=[mu.v()])
    s.ts(mu.v(), mu.v(), 1.0 / 128, ALU.mult)
    bc = lambda t, lo, hi: t.v(0, 128, lo, hi, fn=lambda a: a.unsqueeze(2).to_broadcast([128, hi - lo, 128]))
    var = sm['lf_f']
    sq = vst
    for pc in range(4):
        lo, hi = 16 * pc, 16 * pc + 16
        s.tt(H3(lo, hi), H3(lo, hi), bc(mu, lo, hi), ALU.subtract)
        s.act(sq.v(0, 128, 0, 2048), H.v(0, 128, 128 * lo, 128 * hi), AF.Square)
        s.op('dve', lambda e, lo=lo, hi=hi: e.tensor_reduce(out=var.h[:, lo:hi], in_=sq.h[:, 0:2048].rearrange('p (c e) -> p c e', e=128),
                                                           axis=AX.X, op=ALU.add),
             reads=[sq.v(0, 128, 0, 2048)], writes=[var.v(0, 128, lo, hi)])
    s.act(var.v(), var.v(), AF.Sqrt, bias=C.epsb.v(), scale=1.0 / 128)
    s.op('dve', lambda e: e.reciprocal(out=var.h[:, :], in_=var.h[:, :]), reads=[var.v()], writes=[var.v()])
    for pc in range(4):
        lo, hi = 16 * pc, 16 * pc + 16
        s.tt(H3(lo, hi), H3(lo, hi), bc(var, lo, hi), ALU.mult, eng='pool' if pc % 2 else 'dve')
    for g in range(NCH // 4):
        ps = next_ps(C)
        for i in range(4):
            c = 4 * g + i
            s.tr(ps.v(0, 128, 128 * i, 128 * i + 128), H.v(0, 128, 128 * c, 128 * c + 128), C.ident.v())
        stv = acc0.v(0, 128, 512 * (g % 4), 512 * (g % 4) + 512)
        s.copy(stv, ps.v(), eng=s.evac_eng())
        dst = g // 4
        col = 512 * (g % 4)
        s.dma('sp' if g % 2 else 'act', xb_d.v(dst * XB_ROWS, dst * XB_ROWS + 128, col, col + 512), stv)


def u_extra_inputs(d, l, c):
    rp = c % 4
    m = {}
    m.update(mlstm_tables())
    cw = d['conv_w'][l]
    cb = d['conv_b'][l]
    m['m_cwq'] = np.ascontiguousarray(cw[:, 128 * rp:128 * rp + 128].T)
    m['m_cwk'] = np.ascontiguousarray(cw[:, 512 + 128 * rp:512 + 128 * rp + 128].T)
    m['m_cbq'] = np.ascontiguousarray(cb[128 * rp:128 * rp + 128].reshape(128, 1))
    m['m_cbk'] = np.ascontiguousarray(cb[512 + 128 * rp:512 + 128 * rp + 128].reshape(128, 1))
    m.update(s5_tables())
    m.update(s5_inputs(d, l, c))
    return m


NTAU = 152


def s5_tables():
    n72 = np.arange(72.0)
    n8 = np.arange(8.0)
    n64 = np.arange(64.0)
    dpow = 64.0 * 2.0 ** np.arange(7)
    tf = np.concatenate([n72 - 7, 7 - n8, 63 - n64, dpow, [1.0]])
    tbk = np.concatenate([64 - n72, n8, n64, dpow, [1.0]])
    tau = np.tile(np.concatenate([tf, tbk])[None, :], (128, 1))
    p = np.arange(128)
    imask = np.eye(128)
    jmask = np.roll(np.eye(128), 64, axis=1)
    blk = p // 16
    bmf = (blk[:, None] <= blk[None, :]) * 1.0
    bmb = (blk[:, None] >= blk[None, :]) * 1.0
    sg = np.concatenate([-np.ones(64), np.ones(64)])[:, None]
    sel = np.zeros((64, 4, 8, 128))
    for g in range(4):
        for i in range(8):
            for c in range(16):
                sel[16 * g + c, g, i, 16 * i + c] = 1.0
    selT = sel.transpose(3, 1, 2, 0).reshape(128, 4 * 8 * 64)
    tabs = dict(s_tau=tau, s_imask=imask, s_jmask=jmask, s_bmf=bmf, s_bmb=bmb, s_sg=sg, s_nsg=-sg,
                s_sel=sel.reshape(64, 4096), s_selT=selT)
    return {k: np.ascontiguousarray(v.astype(np.float32)) for k, v in tabs.items()}


ST_SHAPES = dict(s_tau=[128, 2 * NTAU], s_imask=[128, 128], s_jmask=[128, 128], s_bmf=[128, 128], s_bmb=[128, 128],
                 s_sg=[128, 1], s_nsg=[128, 1], s_sel=[64, 4096], s_selT=[128, 2048],
                 s_lam=[8 * 128, 2], s_logdt=[8 * 128, 1], s_X1=[8 * 128, 16], s_X2=[8 * 128, 16], s_Y1=[8 * 128, 16],
                 s_Y2=[8 * 128, 16], s_d=[4 * 128, 1])


def s5_inputs(d, l, c):
    rp = c % 4
    lam = np.zeros((8, 128, 2), np.float32)
    logdt = np.zeros((8, 128, 1), np.float32)
    X1 = np.zeros((8, 128, 16), np.float32)
    X2 = np.zeros((8, 128, 16), np.float32)
    Y1 = np.zeros((8, 128, 16), np.float32)
    Y2 = np.zeros((8, 128, 16), np.float32)
    dd = np.zeros((4, 128, 1), np.float32)
    for gl in range(4):
        g = 4 * rp + gl
        dd[gl, :, 0] = np.tile(d['s5_d'][l, g], 8)
        for di in range(2):
            u = 2 * gl + di
            lam[u, :, 0] = np.tile(d['s5_lam_re'][l, di, g], 2)
            lam[u, :, 1] = np.tile(d['s5_lam_im'][l, di, g], 2)
            logdt[u, :, 0] = d['s5_log_dt'][l, di, g]
            bre, bim = d['s5_b_re'][l, di, g], d['s5_b_im'][l, di, g]
            cre, cim = d['s5_c_re'][l, di, g].T, d['s5_c_im'][l, di, g].T
            X1[u] = np.concatenate([bre, bim], 0)
            X2[u] = np.concatenate([bim, bre], 0)
            Y1[u] = np.concatenate([cre, cim], 0)
            Y2[u] = np.concatenate([cim, cre], 0)
    return dict(s_lam=lam.reshape(1024, 2), s_logdt=logdt.reshape(1024, 1), s_X1=X1.reshape(1024, 16), s_X2=X2.reshape(1024, 16),
                s_Y1=Y1.reshape(1024, 16), s_Y2=Y2.reshape(1024, 16), s_d=dd.reshape(512, 1))


def phase_U_s5(s, C, xf_d, xb_d, sd):
    TWO_PI = 2.0 * np.pi
    cst = {}
    for k in ['s_tau', 's_imask', 's_jmask', 's_bmf', 's_bmb', 's_sg', 's_nsg']:
        cst[k] = s.sb(k, ST_SHAPES[k], F32)
        s.dma('sp', cst[k].v(), sd[k].v())
    sel = s.sb('s_sel', [64, 4096], BF16)
    s.dma('pool', sel.v(), sd['s_sel'].v())
    selT = s.sb('s_selT', [128, 2048], BF16)
    s.dma('pool', selT.v(), sd['s_selT'].v())
    zsb = s.sb('s_zsb', [64, S], BF16)
    for src in range(4):
        s.dma('pool', zsb.v(0, 64, 2048 * src, 2048 * src + 2048), xf_d.v(src * XF_ROWS + 452, src * XF_ROWS + 516))
    U8 = [s.sb('s_U8_%d' % g, [128, 1024], BF16) for g in range(4)]
    for g in range(4):
        for half in range(2):
            ps = next_ps(C)
            for i in range(8):
                rhs = V(zsb, zsb.h[0:64, 4096 * half + i:4096 * (half + 1):8], (0, 64, 4096 * half, 4096 * (half + 1)))
                s.mm(ps.v(0, 128, 0, 512), sel.v(0, 64, 128 * (8 * g + i), 128 * (8 * g + i) + 128), rhs, start=(i == 0), stop=(i == 7))
            s.copy(U8[g].v(0, 128, 512 * half, 512 * half + 512), ps.v(), eng=s.evac_eng())
    NS = 4

    def mk(nm, shape, dt=F32):
        return [s.sb('s_%s_%d' % (nm, b), shape, dt) for b in range(NS)]
    lamt_, ldt_ = mk('lamt', [128, 2]), mk('ldt', [128, 1])
    v1_ = {nm: mk(nm, [128, 1]) for nm in ['dt', 'lr', 'th', 'den', 'am1', 't1', 't2', 'fre', 'fim', 'fis', 'frs', 'dvec']}
    X1_, X2_, Y1_, Y2_ = [mk(nm, [128, 16]) for nm in ['X1', 'X2', 'Y1', 'Y2']]
    BA_, BB_, CA_, CB_, t16_ = [mk(nm, [128, 16]) for nm in ['BA', 'BB', 'CA', 'CB', 't16']]
    mag_, ang_, kf_, al_, be_ = [mk(nm, [128, NTAU]) for nm in ['mag', 'ang', 'kf', 'al', 'be']]
    ki_ = mk('ki', [128, NTAU], mybir.dt.int32)
    bsn_ = mk('bsn', [128, 7])
    Gt_, Ht_, tG_ = mk('G', [128, 72 * 16]), mk('H', [128, 72 * 16]), mk('tG', [128, 72 * 16])
    Gb_ = mk('Gb', [128, 72 * 16], BF16)
    Tm_, Wm_ = mk('T', [128, 1024], BF16), mk('W', [128, 1024], BF16)
    Dm_ = mk('D', [128, 7 * 128], BF16)
    tmp128_ = mk('tmp128', [128, 128])
    Xf_, Xb_, Xp_ = mk('Xf', [128, 128]), mk('Xb', [128, 128], BF16), mk('Xp', [128, 130], BF16)
    for b in range(NS):
        s.op('dve', lambda e, b=b: e.memset(Xp_[b].h[:, :], 0.0), writes=[Xp_[b].v()])
    Y8 = [s.sb('s_Y8_%d' % g, [128, 1024], BF16) for g in range(4)]
    gx = [s.sb('s_gx%d' % i, [128, 512], F32) for i in range(6)]

    def col(t, i):
        return t.v(0, 128, i, i + 1)

    def unit_gen(g, d):
        b = 2 * (g % 2) + d
        lamt, ldt = lamt_[b], ldt_[b]
        v1 = {k: v[b] for k, v in v1_.items()}
        X1, X2, Y1, Y2 = X1_[b], X2_[b], Y1_[b], Y2_[b]
        BA, BB, CA, CB, t16 = BA_[b], BB_[b], CA_[b], CB_[b], t16_[b]
        mag, ang, kf, al, be, ki, bsn = mag_[b], ang_[b], kf_[b], al_[b], be_[b], ki_[b], bsn_[b]
        Gt, Ht, tG, Gb, Tm, Wm, Dm, tmp128 = Gt_[b], Ht_[b], tG_[b], Gb_[b], Tm_[b], Wm_[b], Dm_[b], tmp128_[b]
        Xf, Xb, Xp = Xf_[b], Xb_[b], Xp_[b]
        u = 2 * g + d
        r0 = 128 * u
        s.dma('sp', lamt.v(), sd['s_lam'].v(r0, r0 + 128))
        s.dma('act', ldt.v(), sd['s_logdt'].v(r0, r0 + 128))
        s.dma('sp', X1.v(), sd['s_X1'].v(r0, r0 + 128))
        s.dma('act', X2.v(), sd['s_X2'].v(r0, r0 + 128))
        s.dma('sp', Y1.v(), sd['s_Y1'].v(r0, r0 + 128))
        s.dma('act', Y2.v(), sd['s_Y2'].v(r0, r0 + 128))
        if d == 0:
            s.dma('sp', v1['dvec'].v(), sd['s_d'].v(128 * g, 128 * g + 128))
        yield
        lre, lim = col(lamt, 0), col(lamt, 1)
        s.act(v1['dt'].v(), ldt.v(), AF.Exp); yield
        s.tt(v1['lr'].v(), lre, v1['dt'].v(), ALU.mult); yield
        s.tt(v1['th'].v(), lim, v1['dt'].v(), ALU.mult); yield
        tau = cst['s_tau'].v(0, 128, NTAU * d, NTAU * d + NTAU)
        s.ts(ang.v(), tau, v1['th'].v(), ALU.mult); yield
        s.ts(kf.v(), ang.v(), 1.0 / TWO_PI, ALU.mult); yield
        s.copy(ki.v(), kf.v()); yield
        s.copy(kf.v(), ki.v()); yield
        s.stt(ang.v(), kf.v(), -TWO_PI, ang.v(), ALU.mult, ALU.add); yield

        def wrap(dst, y):
            s.ts(mag.v(), y.v(), float(np.pi), ALU.is_gt, -TWO_PI, ALU.mult); yield
            s.tt(dst.v(), mag.v(), y.v(), ALU.add); yield
            s.ts(mag.v(), y.v(), -float(np.pi), ALU.is_lt, TWO_PI, ALU.mult); yield
            s.tt(dst.v(), dst.v(), mag.v(), ALU.add); yield
        yield from wrap(kf, ang)
        s.act(be.v(), kf.v(), AF.Sin); yield
        s.ts(ang.v(), ang.v(), float(np.pi / 2), ALU.add); yield
        yield from wrap(kf, ang)
        s.act(al.v(), kf.v(), AF.Sin); yield
        s.act(mag.v(), tau, AF.Exp, scale=v1['lr'].v()); yield
        s.tt(al.v(), al.v(), mag.v(), ALU.mult); yield
        s.tt(be.v(), be.v(), mag.v(), ALU.mult); yield
        a1, b1 = col(al, NTAU - 1), col(be, NTAU - 1)
        s.tt(v1['t1'].v(), lre, lre, ALU.mult); yield
        s.stt(v1['den'].v(), lim, lim, v1['t1'].v(), ALU.mult, ALU.add); yield
        s.op('dve', lambda e: e.reciprocal(out=v1['den'].h[:, :], in_=v1['den'].h[:, :]), reads=[v1['den'].v()], writes=[v1['den'].v()]); yield
        s.ts(v1['am1'].v(), a1, -1.0, ALU.add); yield
        s.tt(v1['t1'].v(), v1['am1'].v(), lre, ALU.mult); yield
        s.stt(v1['t1'].v(), b1, lim, v1['t1'].v(), ALU.mult, ALU.add); yield
        s.tt(v1['fre'].v(), v1['t1'].v(), v1['den'].v(), ALU.mult); yield
        s.tt(v1['t2'].v(), v1['am1'].v(), lim, ALU.mult); yield
        s.stt(v1['t2'].v(), b1, lre, v1['t2'].v(), ALU.mult, ALU.subtract); yield
        s.tt(v1['fim'].v(), v1['t2'].v(), v1['den'].v(), ALU.mult); yield
        s.tt(v1['fis'].v(), v1['fim'].v(), cst['s_sg'].v(), ALU.mult); yield
        s.tt(v1['frs'].v(), v1['fre'].v(), cst['s_sg'].v(), ALU.mult); yield
        s.ts(t16.v(), X1.v(), v1['fre'].v(), ALU.mult); yield
        s.stt(BA.v(), X2.v(), v1['fis'].v(), t16.v(), ALU.mult, ALU.add); yield
        s.ts(t16.v(), X1.v(), v1['fim'].v(), ALU.mult); yield
        s.stt(BB.v(), X2.v(), v1['frs'].v(), t16.v(), ALU.mult, ALU.subtract); yield
        s.ts(CA.v(), Y1.v(), cst['s_nsg'].v(), ALU.mult); yield
        s.ts(CB.v(), Y2.v(), -1.0, ALU.mult); yield
        b3 = lambda t: t.v(fn=lambda a: a.unsqueeze(1).to_broadcast([128, 72, 16]))
        a3 = lambda t, lo: t.v(0, 128, lo, lo + 72, fn=lambda a: a.unsqueeze(2).to_broadcast([128, 72, 16]))
        r3 = lambda t: t.v(fn=lambda a: a.rearrange('p (n c) -> p n c', c=16))
        s.tt(r3(Gt), b3(CA), a3(al, 0), ALU.mult); yield
        s.tt(r3(tG), b3(CB), a3(be, 0), ALU.mult); yield
        s.tt(Gt.v(), Gt.v(), tG.v(), ALU.add, eng='pool'); yield
        s.tt(r3(Ht), b3(BA), a3(al, 72), ALU.mult); yield
        s.tt(r3(tG), b3(BB), a3(be, 72), ALU.mult); yield
        s.tt(Ht.v(), Ht.v(), tG.v(), ALU.add, eng='pool'); yield
        s.copy(Gb.v(), Gt.v(), eng='act'); yield
        s.ts(bsn.v(), be.v(0, 128, 144, 151), cst['s_nsg'].v(), ALU.mult); yield
        for m in range(7):
            s.ts(tmp128.v(), cst['s_imask'].v(), col(al, 144 + m), ALU.mult, eng='pool'); yield
            s.stt(Dm.v(0, 128, 128 * m, 128 * m + 128), cst['s_jmask'].v(), col(bsn, m), tmp128.v(), ALU.mult, ALU.add); yield
        for dl in range(8):
            gs = 128 * dl if d == 0 else 128 * (8 - dl)
            ps = next_ps(C)
            s.mm(ps.v(0, 128, 0, 128), Ht.v(0, 128, 0, 128), Gt.v(0, 128, gs, gs + 128))
            tv = Tm.v(0, 128, 128 * dl, 128 * dl + 128)
            if dl == 0:
                bm = cst['s_bmf'] if d == 0 else cst['s_bmb']
                if d == 0:
                    s.tt(tmp128.v(), ps.v(0, 128, 0, 128), bm.v(), ALU.mult); yield
                    s.stt(tv, cst['s_imask'].v(), v1['dvec'].v(), tmp128.v(), ALU.mult, ALU.add)
                else:
                    s.tt(tv, ps.v(0, 128, 0, 128), bm.v(), ALU.mult)
            else:
                s.copy(tv, ps.v(0, 128, 0, 128), eng=s.evac_eng())
            yield
        for hh in range(2):
            ps = next_ps(C)
            for i in range(4):
                ih = 4 * hh + i
                s.tr(ps.v(0, 128, 128 * i, 128 * i + 128), Ht.v(0, 128, 128 + 128 * ih, 128 + 128 * ih + 128), C.ident.v())
            s.copy(Wm.v(0, 128, 512 * hh, 512 * hh + 512), ps.v(), eng=s.evac_eng()); yield
        ps = next_ps(C)
        for ih in range(8):
            rhs = V(U8[g], U8[g].h[:, ih::8], (0, 128, 0, 1024))
            s.mm(ps.v(0, 128, 0, 128), Wm.v(0, 128, 128 * ih, 128 * ih + 128), rhs, start=(ih == 0), stop=(ih == 7))
        s.copy(Xf.v(), ps.v(0, 128, 0, 128), eng='dve')
        s.copy(Xb.v(), ps.v(0, 128, 0, 128), eng='act'); yield
        for m in range(7):
            sh = 2 ** m
            ps = next_ps(C)
            if d == 0:
                s.mm(ps.v(0, 128, 0, 128 - sh), Dm.v(0, 128, 128 * m, 128 * m + 128), Xb.v(0, 128, 0, 128 - sh))
                xv = Xf.v(0, 128, sh, 128)
            else:
                s.mm(ps.v(0, 128, 0, 128 - sh), Dm.v(0, 128, 128 * m, 128 * m + 128), Xb.v(0, 128, sh, 128))
                xv = Xf.v(0, 128, 0, 128 - sh)
            s.tt(xv, xv, ps.v(0, 128, 0, 128 - sh), ALU.add); yield
            if m < 6:
                s.copy(Xb.v(), Xf.v(), eng='act'); yield
        s.copy(Xp.v(0, 128, 1, 129), Xf.v(), eng='act'); yield

    def out_gen(g):
        bf, bb = 2 * (g % 2), 2 * (g % 2) + 1
        for hh in range(2):
            ps = next_ps(C)
            for q in range(4):
                jh = 4 * hh + q
                o = ps.v(0, 128, 128 * q, 128 * q + 128)
                first = True
                for ih in range(jh + 1):
                    rhs = V(U8[g], U8[g].h[:, ih::8], (0, 128, 0, 1024))
                    s.mm(o, Tm_[bf].v(0, 128, 128 * (jh - ih), 128 * (jh - ih) + 128), rhs, start=first, stop=False)
                    first = False
                for ih in range(jh, 8):
                    rhs = V(U8[g], U8[g].h[:, ih::8], (0, 128, 0, 1024))
                    s.mm(o, Tm_[bb].v(0, 128, 128 * (ih - jh), 128 * (ih - jh) + 128), rhs, start=False, stop=False)
                s.mm(o, Gb_[bf].v(0, 128, 128 * (jh + 1), 128 * (jh + 1) + 128), Xp_[bf].v(0, 128, 0, 128), start=False, stop=False)
                s.mm(o, Gb_[bb].v(0, 128, 128 * jh, 128 * jh + 128), Xp_[bb].v(0, 128, 2, 130), start=False, stop=True)
            k3 = 3 * (g % 2)
            x_, t_, u_ = gx[k3], gx[k3 + 1], gx[k3 + 2]
            s.copy(x_.v(), ps.v(), eng='dve'); yield
            s.act(t_.v(), x_.v(), AF.Square); yield
            s.ts(t_.v(), t_.v(), 0.044715, ALU.mult, 1.0, ALU.add, eng='pool'); yield
            s.tt(t_.v(), t_.v(), x_.v(), ALU.mult, eng='pool'); yield
            s.act(u_.v(), t_.v(), AF.Sigmoid, scale=1.5957691216057308); yield
            ov = Y8[g].v(fn=lambda a, hh=hh: a.rearrange('p (k j) -> p j k', j=8)[:, 4 * hh:4 * hh + 4, :])
            s.tt(ov, u_.v(fn=lambda a: a.rearrange('p (j k) -> p j k', k=128)), x_.v(fn=lambda a: a.rearrange('p (j k) -> p j k', k=128)), ALU.mult); yield

    def run_interleaved(gens):
        gens = list(gens)
        while gens:
            for gnr in list(gens):
                try:
                    next(gnr)
                except StopIteration:
                    gens.remove(gnr)

    for gp in range(2):
        run_interleaved([unit_gen(2 * gp + gg, d) for gg in range(2) for d in range(2)])
        run_interleaved([out_gen(2 * gp + gg) for gg in range(2)])
    yst = [s.sb('s_yst%d' % i, [64, 512], F32) for i in range(2)]
    for tb_ in range(16):
        ps = next_ps(C)
        for j in range(8):
            for g in range(4):
                s.mm(ps.v(0, 64, 64 * j, 64 * j + 64), selT.v(0, 128, 64 * (8 * g + j), 64 * (8 * g + j) + 64),
                     Y8[g].v(0, 128, 64 * tb_, 64 * tb_ + 64), start=(g == 0), stop=(g == 3))
        st = yst[tb_ % 2]
        s.copy(st.v(fn=lambda a: a.rearrange('p (k j) -> p j k', j=8)), ps.v(0, 64, 0, 512, fn=lambda a: a.rearrange('p (j k) -> p j k', k=64)),
               eng=s.evac_eng())
        dst = tb_ // 4
        cc = 512 * (tb_ % 4)
        s.dma('sp' if tb_ % 2 else 'act', xb_d.v(dst * XB_ROWS + 192, dst * XB_ROWS + 256, cc, cc + 512), st.v())


def _col(a):
    return np.ascontiguousarray(np.asarray(a, np.float32).reshape(-1, 1))


def _c(a):
    return np.ascontiguousarray(np.asarray(a, np.float32))


def _layer_weights(inp, l):
    return dict(w_in=_c(inp['w_in'][l]), b_in=_col(inp['b_in'][l]), g_pre=_col(inp['g_mix_pre'][l]),
                norm_g=_col(inp['mlstm_norm_g'][l]), w_up_m=_c(inp['w_up_mlstm'][l]), w_up_f=_c(inp['w_up_fourier'][l]),
                w_glu=_c(inp['w_glu'][l]), b_glu=_col(inp['b_glu'][l]), w_out=_c(inp['w_out'][l]), g_post=_col(inp['g_mix_post'][l]),
                g_ffn_pre=_col(inp['g_ffn_pre'][l]), g_ffn_post=_col(inp['g_ffn_post'][l]), w_ffn1=_c(inp['w_ffn1'][l]),
                w_ffn2=_c(inp['w_ffn2'][l]))


def kernel_unfused(**inputs):
    inp = {k: np.asarray(v) for k, v in inputs.items()}
    ident = np.eye(128, dtype=np.float32)
    cores = list(range(8))
    xs = [_c(inp['x'][c // 4, TL * (c % 4):TL * (c % 4 + 1)]) for c in cores]
    ftab = fourier_tables()
    for l in range(2):
        Wl = _layer_weights(inp, l)
        nc = build_T1()
        maps = [dict(x=xs[c], w_in=Wl['w_in'], b_in=Wl['b_in'], g_pre=Wl['g_pre'], ident=ident) for c in cores]
        res = run_bass_kernel_spmd(nc, maps, core_ids=cores)
        xf_sent = [np.asarray(res.results[c]['xf']).reshape(4, XF_ROWS, TL) for c in cores]
        maps = []
        for c in cores:
            b, r = c // 4, c % 4
            xf = np.stack([xf_sent[4 * b + src][r] for src in range(4)], 0).reshape(4 * XF_ROWS, TL)
            m = dict(xf=np.ascontiguousarray(xf), ident=ident)
            m.update(ftab)
            m.update(u_extra_inputs(inp, l, c))
            maps.append(m)
        nc = build_U(('fourier', 'mlstm', 's5'))
        res = run_bass_kernel_spmd(nc, maps, core_ids=cores)
        xb_sent = [np.asarray(res.results[c]['xb']).reshape(4, XB_ROWS, TL) for c in cores]
        maps = []
        for c in cores:
            b, r = c // 4, c % 4
            xb = np.stack([xb_sent[4 * b + src][r] for src in range(4)], 0).reshape(4 * XB_ROWS, TL)
            m = dict(x=xs[c], xb=np.ascontiguousarray(xb), ident=ident)
            m.update(Wl)
            maps.append(m)
        nc = build_T2()
        res = run_bass_kernel_spmd(nc, maps, core_ids=cores)
        xs = [np.asarray(res.results[c]['x_out']) for c in cores]
    out = np.zeros((2, S, D), np.float32)
    for c in cores:
        out[c // 4, TL * (c % 4):TL * (c % 4 + 1)] = xs[c]
    return out


class BlockT:
    def __init__(self, parent, bases, B):
        self.parent, self.bases, self.B = parent, bases, B
        self.space = parent.space

    def v(self, p0=0, p1=None, f0=0, f1=None, fn=None):
        if p1 is None:
            p1 = self.B * len(self.bases)
        blk = p0 // self.B
        assert (p1 - 1) // self.B == blk, (p0, p1, self.B)
        off = self.bases[blk] - blk * self.B
        return self.parent.v(off + p0, off + p1, f0, f1, fn)


W_SHAPES = [('w_in', [D, N_IN]), ('b_in', [N_IN, 1]), ('g_pre', [D, 1]), ('norm_g', [512, 1]), ('w_up_m', [512, D]),
            ('w_up_f', [256, D]), ('w_glu', [256, 2 * D]), ('b_glu', [2 * D, 1]), ('w_out', [D, D]), ('g_post', [D, 1]),
            ('g_ffn_pre', [D, 1]), ('g_ffn_post', [D, 1]), ('w_ffn1', [D, 4 * D]), ('w_ffn2', [4 * D, D])]
M_CONST = ['m_trif', 'm_trib', 'm_maskf', 'm_maskb', 'm_onesf']
M_PARAM = ['m_cwq', 'm_cwk', 'm_cbq', 'm_cbk']
S_CONST = ['s_tau', 's_imask', 's_jmask', 's_bmf', 's_bmb', 's_sg', 's_nsg', 's_sel', 's_selT']
S_PARAM = ['s_lam', 's_logdt', 's_X1', 's_X2', 's_Y1', 's_Y2', 's_d']


class ColT:
    def __init__(self, parent, col0, n):
        self.parent, self.col0, self.n = parent, col0, n
        self.space = parent.space

    def v(self, p0=0, p1=None, f0=0, f1=None, fn=None):
        f1 = self.n if f1 is None else f1
        return self.parent.v(p0, p1, self.col0 + f0, self.col0 + f1, fn)


def fused_T1(s, C, x_src, W, xf_view):
    mk = s.mark()
    load_xT(s, C, x_src, TL)
    gcols = load_vec_cols(s, C, 'gpre', W['g_pre'], 8)
    rstd = rms_rstd(s, C, C.xT, TL, 'pre')
    hT = make_h(s, C, gcols, rstd, TL, 'hT')
    phase_T1_proj(s, C, hT, W['w_in'], W['b_in'], xf_view)
    s.release(mk)


def fused_T2(s, C, x_src, W, xb_view, out_view, NT=1024):
    mk = s.mark()
    C.NT = NT
    C.n_wbig = 3
    C.sqb = [s.sb('sqb%d' % i, [128, 512], BF16) for i in range(2)]
    C.rstd2 = s.sb('rstd2', [128, NT], F32)
    gcols = load_vec_cols(s, C, 'gpre', W['g_pre'], 8)
    t2_setup(s, C, W)
    C.xT = [s.sb('xT%d' % k, [128, NT], F32) for k in range(8)]
    C.hT = [ColT(C.h2T, k * NT, NT) for k in range(8)]
    C.xoff = 0
    xstage = [s.sb('xst%d' % i, [128, 1024], F32) for i in range(4)]
    for hf in range(TL // NT):
        tok0 = hf * NT
        for tg in range(NT // 512):
            for i in range(4):
                tt = tg * 4 + i
                s.dma('sp' if i % 2 == 0 else 'act', xstage[i].v(), x_src.v(tok0 + 128 * tt, tok0 + 128 * tt + 128))
            for kt in range(8):
                ps = ps6(C)
                for i in range(4):
                    s.tr(ps.v(0, 128, 128 * i, 128 * i + 128), xstage[i].v(0, 128, 128 * kt, 128 * kt + 128), C.ident.v())
                s.copy(C.xT[kt].v(0, 128, 512 * tg, 512 * tg + 512), ps.v(), eng=s.evac_eng())
        for tc in range(NT // 512):
            ps = ps6(C)
            for kt in range(8):
                q = C.sqb[kt % 2]
                s.act(q.v(), C.xT[kt].v(0, 128, 512 * tc, 512 * tc + 512), AF.Square)
                s.mm(ps.v(), C.ones.v(), q.v(), start=(kt == 0), stop=(kt == 7))
            r = C.rstd2.v(0, 128, 512 * tc, 512 * tc + 512)
            s.act(r, ps.v(), AF.Sqrt, bias=C.epsb.v(), scale=1.0 / D)
            s.op('dve', lambda e, r=r: e.reciprocal(out=r.ap, in_=r.ap), reads=[r], writes=[r])
        for kt in range(8):
            s.stt(C.hT[kt].v(), C.xT[kt].v(), gcols.v(0, 128, kt, kt + 1), C.rstd2.v(), ALU.mult, ALU.mult)
        phase_T2(s, C, W, xb_view, tok0)
        store_xT(s, C, out_view, tok0, NT, xstage)
    s.release(mk)


def fused_U(s, C, xf_view, xb_view, tabs_d, md, sd):
    mk = s.mark()
    phase_U_fourier(s, C, xf_view, xb_view, tabs_d)
    s.release(mk)
    phase_U_s5(s, C, xf_view, xb_view, sd)
    s.release(mk)
    phase_U_mlstm(s, C, xf_view, xb_view, md)
    s.release(mk)


def build_fused(n_layers=2):
    nc = bass.Bass("TRN2", target_bir_lowering=False)
    s = Sched(nc)
    C = Ctx()
    x_d = s.dram('x', [S, D], F32, 'ExternalInput')
    id_d = s.dram('ident', [128, 128], F32, 'ExternalInput')
    out_d = s.dram('out', [S, D], F32, 'ExternalOutput')
    Wl = [{n: s.dram('%s_l%d' % (n, l), sh, F32, 'ExternalInput') for n, sh in W_SHAPES} for l in range(n_layers)]
    tabs_d = {k: s.dram(k, v, F32, 'ExternalInput') for k, v in FT_SHAPES.items()}
    mconst = {k: s.dram(k, MT_SHAPES[k], F32, 'ExternalInput') for k in M_CONST}
    sconst = {k: s.dram(k, ST_SHAPES[k], F32, 'ExternalInput') for k in S_CONST}
    mpar = [{k: s.dram('%s_l%d' % (k, l), [4 * MT_SHAPES[k][0], MT_SHAPES[k][1]], F32, 'ExternalInput') for k in M_PARAM}
            for l in range(n_layers)]
    spar = [{k: s.dram('%s_l%d' % (k, l), [4 * ST_SHAPES[k][0], ST_SHAPES[k][1]], F32, 'ExternalInput') for k in S_PARAM}
            for l in range(n_layers)]
    xf_all = s.dram('xf_all', [16 * XF_ROWS, TL], F32, 'Internal')
    xb_all = s.dram('xb_all', [16 * XB_ROWS, TL], F32, 'Internal')
    xs1 = s.dram('xs1', [S, D], F32, 'Internal')
    common_setup(s, C, id_d)
    C.epsb = s.sb('epsb', [128, 1], F32)
    s.op('dve', lambda e: e.memset(C.epsb.h[:, :], EPS), writes=[C.epsb.v()])
    C.oneb = s.sb('oneb', [128, 1], F32)
    s.op('dve', lambda e: e.memset(C.oneb.h[:, :], 1.0), writes=[C.oneb.v()])
    for l in range(n_layers):
        xsrc = x_d if l == 0 else xs1
        xdst = out_d if l == n_layers - 1 else xs1
        Wf = Wl[l]
        t1cols = [(OFF_Q + 128 * i, 128) for i in range(4)] + [(OFF_K + 128 * i, 128) for i in range(4)] + \
                 [(OFF_V + 128 * i, 128) for i in range(4)] + [(OFF_G16, 16)] + [(OFF_F + 128 * i, 128) for i in range(2)] + \
                 [(OFF_S5 + 128 * i, 128) for i in range(2)]
        t2cols = [(OFF_O + 128 * i, 128) for i in range(4)] + [(OFF_GATE + 1024 * b + 128 * j, 128) for j in range(8) for b in range(3)]
        pw = dict(
            w_in=PW(s, 'pw_in_l%d' % l, Wf['w_in'], 8, t1cols + t2cols),
            w_up_m=PW(s, 'pw_um_l%d' % l, Wf['w_up_m'], 4, [(128 * j, 128) for j in range(8)]),
            w_up_f=PW(s, 'pw_uf_l%d' % l, Wf['w_up_f'], 2, [(128 * j, 128) for j in range(8)]),
            w_glu=PW(s, 'pw_glu_l%d' % l, Wf['w_glu'], 2, [(128 * j, 128) for j in range(16)]),
            w_out=PW(s, 'pw_out_l%d' % l, Wf['w_out'], 8, [(128 * j, 128) for j in range(8)]),
            w_ffn1=PW(s, 'pw_f1_l%d' % l, Wf['w_ffn1'], 8, [(512 * j, 512) for j in range(8)]),
            w_ffn2=PW(s, 'pw_f2_l%d' % l, Wf['w_ffn2'], 32, [(128 * j, 128) for j in range(8)]))
        for i in range(len(t1cols) + 4):
            pw['w_in'].prep(s, i)
        for j in range(8):
            pw['w_up_m'].prep(s, j)
            pw['w_up_f'].prep(s, j)
            pw['w_glu'].prep(s, j)
            pw['w_glu'].prep(s, 8 + j)
            for b in range(3):
                pw['w_in'].prep(s, len(t1cols) + 4 + 3 * j + b)
        for nm in ['w_out', 'w_ffn1', 'w_ffn2']:
            for j in range(8):
                pw[nm].prep(s, j)
        Wp = dict(Wf)
        Wp.update(pw)
        Wl[l] = Wp
        for q in range(4):
            fused_T1(s, C, BlockT(xsrc, [TL * q], TL), Wl[l], BlockT(xf_all, [(dst * 4 + q) * XF_ROWS for dst in range(4)], XF_ROWS))
        for u in range(4):
            md = dict(mconst)
            md.update({k: BlockT(mpar[l][k], [u * MT_SHAPES[k][0]], MT_SHAPES[k][0]) for k in M_PARAM})
            sd = dict(sconst)
            sd.update({k: BlockT(spar[l][k], [u * ST_SHAPES[k][0]], ST_SHAPES[k][0]) for k in S_PARAM})
            fused_U(s, C, BlockT(xf_all, [(u * 4 + src) * XF_ROWS for src in range(4)], XF_ROWS),
                    BlockT(xb_all, [(dst * 4 + u) * XB_ROWS for dst in range(4)], XB_ROWS), tabs_d, md, sd)
        for q in range(4):
            fused_T2(s, C, BlockT(xsrc, [TL * q], TL), Wl[l], BlockT(xb_all, [(q * 4 + src) * XB_ROWS for src in range(4)], XB_ROWS),
                     BlockT(xdst, [TL * q], TL))
    s.finish()
    return nc


def fused_inputs(inp, b, n_layers=2):
    m = dict(x=_c(inp['x'][b]), ident=np.eye(128, dtype=np.float32))
    m.update(fourier_tables())
    mt = mlstm_tables()
    m.update({k: mt[k] for k in M_CONST})
    st = s5_tables()
    m.update({k: st[k] for k in S_CONST})
    for l in range(n_layers):
        for k, v in _layer_weights(inp, l).items():
            m['%s_l%d' % (k, l)] = v
        units = [u_extra_inputs(inp, l, u) for u in range(4)]
        for k in M_PARAM + S_PARAM:
            m['%s_l%d' % (k, l)] = np.ascontiguousarray(np.concatenate([units[u][k] for u in range(4)], 0))
    return m


def kernel(**inputs):
    inp = {k: np.asarray(v) for k, v in inputs.items()}
    nc = build_fused()
    per_batch = [fused_inputs(inp, b) for b in range(2)]
    maps = [per_batch[c // 4] for c in range(8)]
    res = run_bass_kernel_spmd(nc, maps, core_ids=list(range(8)))
    out = np.stack([np.asarray(res.results[0]['out']), np.asarray(res.results[4]['out'])], 0)
    return out.astype(np.float32)
```

```python
import numpy as np
import concourse.bass as bass
import concourse.mybir as mybir
from concourse.bass_utils import run_bass_kernel_spmd

F32 = mybir.dt.float32
BF16 = mybir.dt.bfloat16
AF = mybir.ActivationFunctionType
ALU = mybir.AluOpType
AX = mybir.AxisListType

ENGS = ['pe', 'act', 'pool', 'dve', 'sp']
DMA_POOL = 6
SAME_ENGINE_SYNC = True
SEM_EPOCH = 16000

D = 1024
TL = 2048
S = 8192
EPS = 1e-6
N_IN = 5648
XF_ROWS = 516
XB_ROWS = 256


class T:
    def __init__(self, name, handle, shape, space):
        self.name, self.h, self.shape, self.space = name, handle, shape, space
        self.recs = []
        self.track = True

    def v(self, p0=0, p1=None, f0=0, f1=None, fn=None):
        P = self.shape[0]
        F = int(np.prod(self.shape[1:]))
        p1 = P if p1 is None else p1
        f1 = F if f1 is None else f1
        ap = self.h[p0:p1, f0:f1]
        if fn is not None:
            ap = fn(ap)
        if self.space == 'psum':
            reg = (0, 128, 0, 1 << 30)
        else:
            reg = (p0, p1, f0, f1)
        return V(self, ap, reg)


class V:
    def __init__(self, t, ap, reg):
        self.t, self.ap, self.reg = t, ap, reg

    def f(self, fn):
        return V(self.t, fn(self.ap), self.reg)


def _ov(a, b):
    return a[0] < b[1] and b[0] < a[1] and a[2] < b[3] and b[2] < a[3]


def _cov(a, b):
    return a[0] <= b[0] and a[1] >= b[1] and a[2] <= b[2] and a[3] >= b[3]


class Sched:
    def __init__(self, nc):
        self.nc = nc
        self.ops = {e: [] for e in ENGS}
        self.cnt = {e: 0 for e in ENGS}
        self.waited = {e: {} for e in ENGS}
        self.dma_n = {e: 0 for e in ENGS}
        self.dma_last = {}
        self.ctx = []
        self.rr = 0
        self.epoch = {}
        self.final = {}
        self.dma_ep = {}
        self.prep_n = 0

    def mark(self):
        return len(self.ctx)

    def barrier(self):
        deps = dict(self.final)
        deps.update({k: v for k, v in self.dma_last.items() if k[2] < 100})
        for e in ENGS:
            w = self._waits(e, dict(deps))
            if w:
                self.ops[e].append((w, None, None, 0))

    def release(self, mark):
        self.barrier()
        while len(self.ctx) > mark:
            self.ctx.pop().__exit__(None, None, None)

    def sb(self, name, shape, dt):
        self.uid = getattr(self, 'uid', 0) + 1
        cm = self.nc.sbuf_tensor('sb%d_%s' % (self.uid, name), list(shape), dt)
        h = cm.__enter__()
        self.ctx.append(cm)
        return T(name, h, shape, 'sbuf')

    def ps(self, name, shape=(128, 512), dt=F32):
        cm = self.nc.psum_tensor('pp_' + name, list(shape), dt)
        h = cm.__enter__()
        self.ctx.append(cm)
        return T(name, h, shape, 'psum')

    def dram(self, name, shape, dt, kind):
        h = self.nc.dram_tensor(name, list(shape), dt, kind=kind)
        t = T(name, h.ap(), shape, 'dram')
        t.track = (kind == 'Internal')
        return t

    def _deps(self, reads, writes):
        deps = {}
        reads = [v for v in reads if v.t.track]
        writes = [v for v in writes if v.t.track]
        for vw in reads:
            for r in vw.t.recs:
                if r[0] and _ov(r[1:5], vw.reg) and deps.get(r[5], 0) < r[6]:
                    deps[r[5]] = r[6]
        for vw in writes:
            for r in vw.t.recs:
                if _ov(r[1:5], vw.reg) and deps.get(r[5], 0) < r[6]:
                    deps[r[5]] = r[6]
        return deps

    def _record(self, reads, writes, semkey, val):
        reads = [v for v in reads if v.t.track]
        writes = [v for v in writes if v.t.track]
        for vw in writes:
            t = vw.t
            reg = tuple(vw.reg)
            t.recs = [r for r in t.recs if not _cov(reg, r[1:5])]
            t.recs.append((True,) + reg + (semkey, val))
        for vw in reads:
            t = vw.t
            reg = tuple(vw.reg)
            t.recs = [r for r in t.recs if r[0] or r[5] != semkey or r[1:5] != reg]
            t.recs.append((False,) + reg + (semkey, val))

    def _waits(self, eng, deps):
        w = []
        for k, v in deps.items():
            if k[0] == 'c' and k[1] == eng and (eng == 'pe' or not SAME_ENGINE_SYNC):
                continue
            if self.waited[eng].get(k, 0) >= v:
                continue
            self.waited[eng][k] = v
            w.append((k, v))
        return w

    def op(self, eng, fn, reads=(), writes=()):
        writes = list(writes) + [v for v in reads if v.t.space == 'psum']
        reads = [v for v in reads if v.t.space != 'psum']
        deps = self._deps(reads, writes)
        waits = self._waits(eng, deps)
        if self.cnt[eng] >= SEM_EPOCH:
            self.epoch[eng] = self.epoch.get(eng, 0) + 1
            self.cnt[eng] = 0
        self.cnt[eng] += 1
        key = ('c', eng, self.epoch.get(eng, 0))
        self.final[key] = self.cnt[eng]
        self._record(reads, writes, key, self.cnt[eng])
        self.ops[eng].append((waits, fn, key, 1))

    def dma(self, eng, out, in_, prep=False, **kw):
        deps = self._deps([in_], [out])
        if prep:
            j = self.prep_n
            self.prep_n += 1
            slot = 100 + j % DMA_POOL
        else:
            j = self.dma_n[eng]
            self.dma_n[eng] += 1
            slot = j % DMA_POOL
        if self.dma_last.get(('d', eng, slot, self.dma_ep.get((eng, slot), 0)), 0) >= SEM_EPOCH:
            self.dma_ep[(eng, slot)] = self.dma_ep.get((eng, slot), 0) + 1
        key = ('d', eng, slot, self.dma_ep.get((eng, slot), 0))
        prev = self.dma_last.get(key, 0)
        if not prev and self.dma_ep.get((eng, slot), 0) > 0:
            pk = ('d', eng, slot, self.dma_ep[(eng, slot)] - 1)
            deps[pk] = max(deps.get(pk, 0), self.dma_last[pk])
        if prev:
            deps[key] = max(deps.get(key, 0), prev)
        waits = self._waits(eng, deps)
        val = prev + 16
        self.dma_last[key] = val
        self._record([in_], [out], key, val)
        oa, ia = out.ap, in_.ap
        self.ops[eng].append((waits, lambda e: e.dma_start(out=oa, in_=ia, **kw), key, 16))

    def finish(self):
        waits = self._waits('sp', dict(self.dma_last))
        self.ops['sp'].append((waits, None, None, 0))
        nc = self.nc
        semkeys = list(self.final.keys()) + list(self.dma_last.keys())
        sems = {}
        cms = []
        for k in semkeys:
            cm = nc.semaphore('s_' + '_'.join(str(x) for x in k))
            sems[k] = cm.__enter__()
            cms.append(cm)
        with nc.Block() as block:
            deco = dict(pe=block.tensor, act=block.scalar, pool=block.gpsimd, dve=block.vector, sp=block.sync)
            for e in ENGS:
                ops = self.ops[e]

                def body(engine, ops=ops):
                    for waits, fn, key, inc in ops:
                        for k, v in waits:
                            engine.wait_ge(sems[k], v)
                        if fn is not None:
                            fn(engine).then_inc(sems[key], inc)
                if ops:
                    deco[e](body)
        for cm in reversed(cms):
            cm.__exit__(None, None, None)
        for cm in reversed(self.ctx):
            cm.__exit__(None, None, None)

    def mm(self, out, lhsT, rhs, start=True, stop=True):
        self.op('pe', lambda e: e.matmul(out.ap, lhsT.ap, rhs.ap, start=start, stop=stop),
                reads=[lhsT, rhs], writes=[out])

    def tr(self, out, in_, ident):
        self.op('pe', lambda e: e.transpose(out.ap, in_.ap, ident.ap), reads=[in_, ident], writes=[out])

    def act(self, out, in_, func, bias=None, scale=None, eng='act'):
        kw = {}
        rd = [in_]
        if bias is not None:
            if isinstance(bias, V):
                kw['bias'] = bias.ap
                rd.append(bias)
            else:
                kw['bias'] = bias
        if scale is not None:
            if isinstance(scale, V):
                kw['scale'] = scale.ap
                rd.append(scale)
            else:
                kw['scale'] = scale
        self.op('act', lambda e: e.activation(out=out.ap, in_=in_.ap, func=func, **kw), reads=rd, writes=[out])

    def tt(self, out, in0, in1, op, eng='dve'):
        self.op(eng, lambda e: e.tensor_tensor(out=out.ap, in0=in0.ap, in1=in1.ap, op=op), reads=[in0, in1], writes=[out])

    def ts(self, out, in0, s1, op0, s2=None, op1=None, eng='dve'):
        rd = [in0]
        a1 = s1
        a2 = s2
        if isinstance(s1, V):
            rd.append(s1)
            a1 = s1.ap
        if isinstance(s2, V):
            rd.append(s2)
            a2 = s2.ap
        if op1 is None:
            self.op(eng, lambda e: e.tensor_scalar(out=out.ap, in0=in0.ap, scalar1=a1, scalar2=None, op0=op0), reads=rd, writes=[out])
        else:
            self.op(eng, lambda e: e.tensor_scalar(out=out.ap, in0=in0.ap, scalar1=a1, scalar2=a2, op0=op0, op1=op1), reads=rd, writes=[out])

    def stt(self, out, in0, sc, in1, op0, op1):
        rd = [in0, in1]
        a = sc
        if isinstance(sc, V):
            rd.append(sc)
            a = sc.ap
        self.op('dve', lambda e: e.scalar_tensor_tensor(out=out.ap, in0=in0.ap, scalar=a, in1=in1.ap, op0=op0, op1=op1),
                reads=rd, writes=[out])

    def copy(self, out, in_, eng='dve'):
        if eng == 'act':
            self.op('act', lambda e: e.activation(out=out.ap, in_=in_.ap, func=AF.Identity), reads=[in_], writes=[out])
        else:
            self.op(eng, lambda e: e.tensor_copy(out=out.ap, in_=in_.ap), reads=[in_], writes=[out])

    def evac_eng(self):
        self.rr += 1
        return 'act' if self.rr % 2 else 'dve'


class Ctx:
    pass


OFF_Q, OFF_K, OFF_V, OFF_O, OFF_G16, OFF_F, OFF_S5, OFF_GATE = 0, 512, 1024, 1536, 2048, 2064, 2320, 2576


def common_setup(s, C, ident_d):
    C.ident = s.sb('ident', [128, 128], F32)
    s.dma('sp', C.ident.v(), ident_d.v())
    C.identb = s.sb('identb', [128, 128], BF16)
    s.copy(C.identb.v(), C.ident.v())
    C.ones = s.sb('ones', [128, 128], BF16)
    s.op('dve', lambda e: e.memset(C.ones.h[:, :], 1.0), writes=[C.ones.v()])
    C.ps = [s.ps('ps%d' % i) for i in range(8)]
    C.psi = 0


def next_ps(C):
    C.psi = (C.psi + 1) % 8
    return C.ps[C.psi]


def load_xT(s, C, x_d, ntok):
    C.xT = [s.sb('xT%d' % k, [128, ntok], F32) for k in range(8)]
    stage = [s.sb('xst%d' % i, [128, 1024], F32) for i in range(8)]
    for tg in range(ntok // 512):
        for i in range(4):
            tt = tg * 4 + i
            s.dma('sp' if i % 2 == 0 else 'act', stage[(tg % 2) * 4 + i].v(), x_d.v(128 * tt, 128 * tt + 128))
        for kt in range(8):
            ps = next_ps(C)
            for i in range(4):
                s.tr(ps.v(0, 128, 128 * i, 128 * i + 128), stage[(tg % 2) * 4 + i].v(0, 128, 128 * kt, 128 * kt + 128), C.ident.v())
            s.copy(C.xT[kt].v(0, 128, 512 * tg, 512 * tg + 512), ps.v(), eng=s.evac_eng())


def rms_rstd(s, C, src, ntok, name):
    rstd = s.sb('rstd_' + name, [128, ntok], F32)
    sq = [s.sb('sq_%s%d' % (name, i), [128, 512], BF16) for i in range(2)]
    n = 0
    for tc in range(ntok // 512):
        ps = next_ps(C)
        for kt in range(8):
            q = sq[n % 2]
            n += 1
            s.act(q.v(), src[kt].v(0, 128, 512 * tc, 512 * tc + 512), AF.Square)
            s.mm(ps.v(), C.ones.v(), q.v(), start=(kt == 0), stop=(kt == 7))
        r = rstd.v(0, 128, 512 * tc, 512 * tc + 512)
        s.act(r, ps.v(), AF.Sqrt, bias=C.epsb.v(), scale=1.0 / D)
        s.op('dve', lambda e, r=r: e.reciprocal(out=r.ap, in_=r.ap), reads=[r], writes=[r])
    return rstd


def load_vec_cols(s, C, name, d_t, n):
    t = s.sb(name, [128, n], F32)
    for k in range(n):
        s.dma('sp', t.v(0, 128, k, k + 1), d_t.v(128 * k, 128 * k + 128))
    return t


def make_h(s, C, gcols, rstd, ntok, name):
    hT = [s.sb('%s%d' % (name, k), [128, ntok], BF16) for k in range(8)]
    for kt in range(8):
        s.stt(hT[kt].v(), C.xT[kt].v(), gcols.v(0, 128, kt, kt + 1), rstd.v(), ALU.mult, ALU.mult)
    return hT


class PW:
    def __init__(self, s, name, w_d, K, specs):
        self.w_d, self.K, self.specs = w_d, K, specs
        width = K * max(n for _, n in specs)
        self.t = s.dram(name, [len(specs) * 128, width], BF16, 'Internal')
        self.idx = {c0: i for i, (c0, n) in enumerate(specs)}

    def prep(self, s, i):
        c0, ncols = self.specs[i]
        K = self.K
        out = self.t.v(i * 128, i * 128 + 128, 0, K * ncols, fn=lambda a: a.rearrange('p (k n) -> p k n', k=K))
        in_ = self.w_d.v(0, K * 128, c0, c0 + ncols, fn=lambda a: a.rearrange('(k p) n -> p k n', p=128))
        s.dma('pool', out, in_, prep=True)


def load_w_cols(s, C, wt, w_d, K, c0, ncols, q='pool'):
    if isinstance(w_d, PW):
        i = w_d.idx[c0]
        s.dma('sp', wt.v(0, 128, 0, K * ncols), w_d.t.v(i * 128, i * 128 + 128, 0, K * ncols))
        return
    out = wt.v(0, 128, 0, K * ncols, fn=lambda a: a.rearrange('p (k n) -> p k n', k=K))
    in_ = w_d.v(0, K * 128, c0, c0 + ncols, fn=lambda a: a.rearrange('(k p) n -> p k n', p=128))
    s.dma(q, out, in_)


def build_T1():
    nc = bass.Bass("TRN2", target_bir_lowering=False)
    s = Sched(nc)
    C = Ctx()
    x_d = s.dram('x', [TL, D], F32, 'ExternalInput')
    w_d = s.dram('w_in', [D, N_IN], F32, 'ExternalInput')
    b_d = s.dram('b_in', [N_IN, 1], F32, 'ExternalInput')
    g_d = s.dram('g_pre', [D, 1], F32, 'ExternalInput')
    id_d = s.dram('ident', [128, 128], F32, 'ExternalInput')
    xf_d = s.dram('xf', [4 * XF_ROWS, TL], F32, 'ExternalOutput')
    common_setup(s, C, id_d)
    C.epsb = s.sb('epsb', [128, 1], F32)
    s.op('dve', lambda e: e.memset(C.epsb.h[:, :], EPS), writes=[C.epsb.v()])
    load_xT(s, C, x_d, TL)
    gcols = load_vec_cols(s, C, 'gpre', g_d, 8)
    rstd = rms_rstd(s, C, C.xT, TL, 'pre')
    hT = make_h(s, C, gcols, rstd, TL, 'hT')
    phase_T1_proj(s, C, hT, w_d, b_d, xf_d)
    s.finish()
    return nc


def phase_T1_proj(s, C, hT, w_d, b_d, xf_d):
    tiles = []
    for i in range(4):
        tiles.append((OFF_Q + 128 * i, 128, [(0, 128, i, 0)]))
        tiles.append((OFF_K + 128 * i, 128, [(0, 128, i, 128)]))
        tiles.append((OFF_V + 128 * i, 128, [(0, 128, i, 256)]))
    tiles.append((OFF_G16, 16, [(0, 16, None, 384)]))
    for i in range(2):
        tiles.append((OFF_F + 128 * i, 128, [(0, 64, 2 * i, 388), (64, 64, 2 * i + 1, 388)]))
        tiles.append((OFF_S5 + 128 * i, 128, [(0, 64, 2 * i, 452), (64, 64, 2 * i + 1, 452)]))
    wts = [s.sb('wT1_%d' % i, [128, 8 * 128], BF16) for i in range(6)]
    bts = [s.sb('bT1_%d' % i, [128, 1], F32) for i in range(6)]
    stg = [s.sb('zst%d' % i, [128, 512], F32) for i in range(8)]
    n = 0
    for ti, (c0, ncols, dsts) in enumerate(tiles):
        wt = wts[ti % 6]
        bt = bts[ti % 6]
        load_w_cols(s, C, wt, w_d, 8, c0, ncols)
        s.dma('sp', bt.v(0, ncols), b_d.v(c0, c0 + ncols))
        for tc in range(TL // 512):
            ps = next_ps(C)
            for kt in range(8):
                s.mm(ps.v(0, ncols), wt.v(0, 128, kt * ncols, (kt + 1) * ncols), hT[kt].v(0, 128, 512 * tc, 512 * tc + 512),
                     start=(kt == 0), stop=(kt == 7))
            st = stg[n % 8]
            n += 1
            s.act(st.v(0, ncols), ps.v(0, ncols), AF.Identity, bias=bt.v(0, ncols))
            for (r0, nr, dst, dr0) in dsts:
                if dst is None:
                    for dd in range(4):
                        o = xf_d.v(dd * XF_ROWS + dr0, dd * XF_ROWS + dr0 + 4, 512 * tc, 512 * tc + 512)
                        s.dma('sp' if dd % 2 else 'act', o, st.v(0, 16, fn=lambda a, dd=dd: a[dd::4, :]))
                else:
                    o = xf_d.v(dst * XF_ROWS + dr0, dst * XF_ROWS + dr0 + nr, 512 * tc, 512 * tc + 512)
                    s.dma('sp' if n % 2 else 'act', o, st.v(r0, r0 + nr))


class WPool:
    def __init__(self, s, name, ncols, n):
        self.b = [s.sb('%s%d' % (name, i), [128, ncols], BF16) for i in range(n)]
        self.i = 0

    def get(self):
        self.i += 1
        return self.b[self.i % len(self.b)]


def bias_cols(s, name, d_t, offs, n=128):
    t = s.sb(name, [128, len(offs)], F32)
    for k, o in enumerate(offs):
        s.dma('sp' if k % 2 else 'act', t.v(0, n, k, k + 1), d_t.v(o, o + n))
    return t


def store_xT(s, C, out_d, tok0, ntok, stage):
    for tt in range(ntok // 128):
        st = stage[tt % 2]
        for kg in range(2):
            ps = next_ps(C)
            for i in range(4):
                kt = kg * 4 + i
                s.tr(ps.v(0, 128, 128 * i, 128 * i + 128), C.xT[kt].v(0, 128, 128 * tt, 128 * tt + 128), C.ident.v())
            s.copy(st.v(0, 128, 512 * kg, 512 * kg + 512), ps.v(), eng=s.evac_eng())
        s.dma('sp' if tt % 2 else 'act', out_d.v(tok0 + 128 * tt, tok0 + 128 * tt + 128), st.v())


def t2_setup(s, C, W):
    NT = C.NT
    C.wp = WPool(s, 'wp', 1024, getattr(C, 'n_wp', 8))
    C.wbig = WPool(s, 'wbig', 4096, getattr(C, 'n_wbig', 2))
    C.big = s.sb('big', [128, 32 * NT], BF16)
    C.h2T = s.sb('h2T', [128, 8 * NT], BF16)
    C.tmpf = [s.sb('tmpf%d' % i, [128, 512], F32) for i in range(6)]
    C.tmpi = 0
    C.hnst = [s.sb('hnst%d' % i, [128, NT], F32) for i in range(2)]
    C.b_o = bias_cols(s, 'b_o', W['b_in'], [OFF_O + 128 * i for i in range(4)])
    C.b_g = bias_cols(s, 'b_g', W['b_in'], [OFF_GATE + 128 * i for i in range(24)])
    C.b_glu = bias_cols(s, 'b_glu', W['b_glu'], [128 * i for i in range(16)])
    C.normg = bias_cols(s, 'normg', W['norm_g'], [128 * i for i in range(4)])
    C.g_post = bias_cols(s, 'g_post', W['g_post'], [128 * i for i in range(8)])
    C.g_f1 = bias_cols(s, 'g_f1', W['g_ffn_pre'], [128 * i for i in range(8)])
    C.g_f2 = bias_cols(s, 'g_f2', W['g_ffn_post'], [128 * i for i in range(8)])


def tmpf(C):
    C.tmpi += 1
    return C.tmpf[C.tmpi % len(C.tmpf)]


def ps6(C):
    C.psi = (C.psi + 1) % 6
    return C.ps[C.psi]


def big_view(C, tile, c0, c1):
    NT = C.NT
    return C.big.v(0, 128, tile * NT + c0, tile * NT + c1)


def phase_T2(s, C, W, xb_d, tok0):
    NT = C.NT
    NTC = NT // 512
    hT = C.hT
    MIX0, HG0, YF0, YS0, OB0 = 8, 16, 20, 22, 24

    def chunk(tc):
        return 512 * tc, 512 * tc + 512

    for (row0, dst0) in ((128, YF0), (192, YS0)):
        for half in range(2):
            st = C.hnst[half]
            for q in range(2):
                src = 2 * half + q
                s.dma('sp' if q else 'act', st.v(64 * q, 64 * q + 64),
                      xb_d.v(src * XB_ROWS + row0, src * XB_ROWS + row0 + 64, tok0, tok0 + NT))
            s.copy(big_view(C, dst0 + half, 0, NT), st.v(), eng='pool')
    for i in range(4):
        st = C.hnst[i % 2]
        s.dma('sp', st.v(), xb_d.v(i * XB_ROWS, i * XB_ROWS + 128, tok0, tok0 + NT))
        wt = C.wp.get()
        load_w_cols(s, C, wt, W['w_in'], 8, OFF_O + 128 * i, 128)
        for tc in range(NTC):
            c0, c1 = chunk(tc)
            ps = ps6(C)
            for kt in range(8):
                s.mm(ps.v(), wt.v(0, 128, 128 * kt, 128 * kt + 128), hT[kt].v(0, 128, c0, c1), start=(kt == 0), stop=(kt == 7))
            sg = tmpf(C)
            s.act(sg.v(), ps.v(), AF.Sigmoid, bias=C.b_o.v(0, 128, i, i + 1))
            s.stt(big_view(C, HG0 + i, c0, c1), st.v(0, 128, c0, c1), C.normg.v(0, 128, i, i + 1), sg.v(), ALU.mult, ALU.mult)
    for j in range(8):
        w_um = C.wp.get(); load_w_cols(s, C, w_um, W['w_up_m'], 4, 128 * j, 128)
        w_uf = C.wp.get(); load_w_cols(s, C, w_uf, W['w_up_f'], 2, 128 * j, 128)
        w_ga = C.wp.get(); load_w_cols(s, C, w_ga, W['w_glu'], 2, 128 * j, 128)
        w_gb = C.wp.get(); load_w_cols(s, C, w_gb, W['w_glu'], 2, 1024 + 128 * j, 128)
        w_g = []
        for b in range(3):
            w = C.wp.get(); load_w_cols(s, C, w, W['w_in'], 8, OFF_GATE + 1024 * b + 128 * j, 128)
            w_g.append(w)
        for tc in range(NTC):
            c0, c1 = chunk(tc)

            def gate(b):
                ps = ps6(C)
                for kt in range(8):
                    s.mm(ps.v(), w_g[b].v(0, 128, 128 * kt, 128 * kt + 128), hT[kt].v(0, 128, c0, c1), start=(kt == 0), stop=(kt == 7))
                g = tmpf(C)
                s.act(g.v(), ps.v(), AF.Sigmoid, bias=C.b_g.v(0, 128, 8 * b + j, 8 * b + j + 1))
                return g

            def small(wt, K, src0):
                ps = ps6(C)
                for kt in range(K):
                    s.mm(ps.v(), wt.v(0, 128, 128 * kt, 128 * kt + 128), big_view(C, src0 + kt, c0, c1), start=(kt == 0), stop=(kt == K - 1))
                return ps
            ps_ym = small(w_um, 4, HG0)
            g0 = gate(0)
            acc = tmpf(C)
            s.tt(acc.v(), g0.v(), ps_ym.v(), ALU.mult)
            ps_yf = small(w_uf, 2, YF0)
            g1 = gate(1)
            t1 = tmpf(C)
            s.tt(t1.v(), g1.v(), ps_yf.v(), ALU.mult)
            s.tt(acc.v(), acc.v(), t1.v(), ALU.add, eng='pool')
            ps_za = small(w_ga, 2, YS0)
            ps_zb = small(w_gb, 2, YS0)
            sb_ = tmpf(C)
            s.act(sb_.v(), ps_zb.v(), AF.Sigmoid, bias=C.b_glu.v(0, 128, 8 + j, 8 + j + 1))
            ys = tmpf(C)
            s.stt(ys.v(), ps_za.v(), C.b_glu.v(0, 128, j, j + 1), sb_.v(), ALU.add, ALU.mult)
            g2 = gate(2)
            s.tt(ys.v(), ys.v(), g2.v(), ALU.mult, eng='pool')
            s.tt(big_view(C, MIX0 + j, c0, c1), acc.v(), ys.v(), ALU.add)
    ss = [C.ps[6], C.ps[7]]
    sqb = [s_ for s_ in C.sqb]

    def proj_norm_add(wname, K, src_view, dst0_tile, dst_T, gcols, wpool, kcols):
        n = 0
        for j in range(8):
            wt = wpool.get()
            load_w_cols(s, C, wt, W[wname], K, 128 * j, 128)
            for tc in range(NTC):
                c0, c1 = chunk(tc)
                ps = ps6(C)
                for kt in range(K):
                    s.mm(ps.v(), wt.v(0, 128, 128 * kt, 128 * kt + 128), src_view(kt, c0, c1), start=(kt == 0), stop=(kt == K - 1))
                if dst_T is None:
                    ov = big_view(C, dst0_tile + j, c0, c1)
                else:
                    ov = dst_T.v(0, 128, j * NT + c0, j * NT + c1)
                s.copy(ov, ps.v(), eng='dve')
                q = sqb[n % 2]
                n += 1
                s.act(q.v(), ps.v(), AF.Square)
                s.mm(ss[tc].v(), C.ones.v(), q.v(), start=(j == 0), stop=(j == 7))
        for tc in range(NTC):
            c0, c1 = chunk(tc)
            r = C.rstd2.v(0, 128, c0, c1)
            s.act(r, ss[tc].v(), AF.Sqrt, bias=C.epsb.v(), scale=1.0 / D)
            s.op('dve', lambda e, r=r: e.reciprocal(out=r.ap, in_=r.ap), reads=[r], writes=[r])
        for j in range(8):
            for tc in range(NTC):
                c0, c1 = chunk(tc)
                if dst_T is None:
                    ov = big_view(C, dst0_tile + j, c0, c1)
                else:
                    ov = dst_T.v(0, 128, j * NT + c0, j * NT + c1)
                t = tmpf(C)
                s.stt(t.v(), ov, gcols.v(0, 128, j, j + 1), C.rstd2.v(0, 128, c0, c1), ALU.mult, ALU.mult)
                xv = C.xT[j].v(0, 128, C.xoff + c0, C.xoff + c1)
                s.tt(xv, xv, t.v(), ALU.add, eng='pool')

    proj_norm_add('w_out', 8, lambda kt, c0, c1: big_view(C, MIX0 + kt, c0, c1), OB0, None, C.g_post, C.wp, 128)
    for tc in range(NTC):
        c0, c1 = chunk(tc)
        ps = ps6(C)
        for kt in range(8):
            q = sqb[kt % 2]
            s.act(q.v(), C.xT[kt].v(0, 128, C.xoff + c0, C.xoff + c1), AF.Square)
            s.mm(ps.v(), C.ones.v(), q.v(), start=(kt == 0), stop=(kt == 7))
        r = C.rstd2.v(0, 128, c0, c1)
        s.act(r, ps.v(), AF.Sqrt, bias=C.epsb.v(), scale=1.0 / D)
        s.op('dve', lambda e, r=r: e.reciprocal(out=r.ap, in_=r.ap), reads=[r], writes=[r])
    for kt in range(8):
        s.stt(C.h2T.v(0, 128, kt * NT, kt * NT + NT), C.xT[kt].v(0, 128, C.xoff, C.xoff + NT), C.g_f1.v(0, 128, kt, kt + 1),
              C.rstd2.v(0, 128, 0, NT), ALU.mult, ALU.mult)
    for ng in range(8):
        wt = C.wbig.get()
        load_w_cols(s, C, wt, W['w_ffn1'], 8, 512 * ng, 512)
        for nn in range(4):
            n = 4 * ng + nn
            for tc in range(NTC):
                c0, c1 = chunk(tc)
                ps = ps6(C)
                for kt in range(8):
                    s.mm(ps.v(), wt.v(0, 128, 512 * kt + 128 * nn, 512 * kt + 128 * nn + 128), C.h2T.v(0, 128, kt * NT + c0, kt * NT + c1),
                         start=(kt == 0), stop=(kt == 7))
                t = tmpf(C)
                s.act(t.v(), ps.v(), AF.Relu)
                s.tt(big_view(C, n, c0, c1), t.v(), t.v(), ALU.mult, eng='pool' if (n + tc) % 2 else 'dve')
    proj_norm_add('w_ffn2', 32, lambda kt, c0, c1: big_view(C, kt, c0, c1), None, C.h2T, C.g_f2, C.wbig, 128)


def build_T2(NT=1024):
    nc = bass.Bass("TRN2", target_bir_lowering=False)
    s = Sched(nc)
    C = Ctx()
    C.NT = NT
    x_d = s.dram('x', [TL, D], F32, 'ExternalInput')
    xb_d = s.dram('xb', [4 * XB_ROWS, TL], F32, 'ExternalInput')
    W = {}
    for name, shape in [('w_in', [D, N_IN]), ('b_in', [N_IN, 1]), ('g_pre', [D, 1]), ('norm_g', [512, 1]), ('w_up_m', [512, D]),
                        ('w_up_f', [256, D]), ('w_glu', [256, 2 * D]), ('b_glu', [2 * D, 1]), ('w_out', [D, D]), ('g_post', [D, 1]),
                        ('g_ffn_pre', [D, 1]), ('g_ffn_post', [D, 1]), ('w_ffn1', [D, 4 * D]), ('w_ffn2', [4 * D, D])]:
        W[name] = s.dram(name, shape, F32, 'ExternalInput')
    id_d = s.dram('ident', [128, 128], F32, 'ExternalInput')
    out_d = s.dram('x_out', [TL, D], F32, 'ExternalOutput')
    common_setup(s, C, id_d)
    C.epsb = s.sb('epsb', [128, 1], F32)
    s.op('dve', lambda e: e.memset(C.epsb.h[:, :], EPS), writes=[C.epsb.v()])
    C.sqb = [s.sb('sqb%d' % i, [128, 512], BF16) for i in range(2)]
    C.rstd2 = s.sb('rstd2', [128, NT], F32)
    gcols = load_vec_cols(s, C, 'gpre', W['g_pre'], 8)
    t2_setup(s, C, W)
    C.xT = [s.sb('xT%d' % k, [128, NT], F32) for k in range(8)]
    C.hT = [s.sb('hT%d' % k, [128, NT], BF16) for k in range(8)]
    C.xoff = 0
    xstage = [s.sb('xst%d' % i, [128, 1024], F32) for i in range(4)]
    for hf in range(TL // NT):
        tok0 = hf * NT
        for tg in range(NT // 512):
            for i in range(4):
                tt = tg * 4 + i
                s.dma('sp' if i % 2 == 0 else 'act', xstage[i].v(), x_d.v(tok0 + 128 * tt, tok0 + 128 * tt + 128))
            for kt in range(8):
                ps = ps6(C)
                for i in range(4):
                    s.tr(ps.v(0, 128, 128 * i, 128 * i + 128), xstage[i].v(0, 128, 128 * kt, 128 * kt + 128), C.ident.v())
                s.copy(C.xT[kt].v(0, 128, 512 * tg, 512 * tg + 512), ps.v(), eng=s.evac_eng())
        for tc in range(NT // 512):
            ps = ps6(C)
            for kt in range(8):
                q = C.sqb[kt % 2]
                s.act(q.v(), C.xT[kt].v(0, 128, 512 * tc, 512 * tc + 512), AF.Square)
                s.mm(ps.v(), C.ones.v(), q.v(), start=(kt == 0), stop=(kt == 7))
            r = C.rstd2.v(0, 128, 512 * tc, 512 * tc + 512)
            s.act(r, ps.v(), AF.Sqrt, bias=C.epsb.v(), scale=1.0 / D)
            s.op('dve', lambda e, r=r: e.reciprocal(out=r.ap, in_=r.ap), reads=[r], writes=[r])
        for kt in range(8):
            s.stt(C.hT[kt].v(), C.xT[kt].v(), gcols.v(0, 128, kt, kt + 1), C.rstd2.v(), ALU.mult, ALU.mult)
        phase_T2(s, C, W, xb_d, tok0)
        store_xT(s, C, out_d, tok0, NT, xstage)
    s.finish()
    return nc


def fourier_tables():
    c = np.arange(64)
    a64 = 2 * np.pi * np.outer(c, c) / 64.0
    s1 = np.arange(128)
    a128 = 2 * np.pi * np.outer(s1, s1) / 128.0
    atw = 2 * np.pi * np.outer(s1, c) / 8192.0
    sc = 1.0 / np.sqrt(8192.0 * 64.0)
    z = np.zeros((64, 64))
    tabs = dict(
        f_f64=np.concatenate([np.cos(a64), -np.sin(a64)], 1),
        f_c128=np.cos(a128), f_s128=np.sin(a128), f_ns128=-np.sin(a128),
        f_tw=np.concatenate([np.cos(atw), -np.sin(atw)], 1),
        f_bdc=np.block([[np.cos(a64), z], [z, np.cos(a64)]]) * sc,
        f_bds=np.block([[np.sin(a64), z], [z, np.sin(a64)]]) * sc,
    )
    return {k: np.ascontiguousarray(v.astype(np.float32)) for k, v in tabs.items()}


FT_SHAPES = dict(f_f64=[64, 128], f_c128=[128, 128], f_s128=[128, 128], f_ns128=[128, 128], f_tw=[128, 128],
                 f_bdc=[128, 128], f_bds=[128, 128])


def phase_U_fourier(s, C, xf_d, xb_d, tabs_d):
    tb = {}
    for k in ['f_f64', 'f_c128', 'f_s128', 'f_ns128', 'f_bdc', 'f_bds']:
        tb[k] = s.sb(k, FT_SHAPES[k], BF16)
        s.dma('pool', tb[k].v(), tabs_d[k].v())
    tw = s.sb('f_tw', [128, 128], F32)
    s.dma('sp', tw.v(), tabs_d['f_tw'].v())
    UTb = s.sb('f_UTb', [64, S], BF16)
    for src in range(4):
        s.dma('pool', UTb.v(0, 64, 2048 * src, 2048 * src + 2048), xf_d.v(src * XF_ROWS + 388, src * XF_ROWS + 452))
    Zre = s.sb('f_Zre', [128, 4096], BF16)
    Zim = s.sb('f_Zim', [128, 4096], BF16)
    for g in range(16):
        ps = next_ps(C)
        for i in range(4):
            s2 = 4 * g + i
            lhsT = V(UTb, UTb.h[0:64, s2::64], (0, 64, 0, S))
            s.mm(ps.v(0, 128, 128 * i, 128 * i + 128), lhsT, tb['f_f64'].v())
        pv = lambda lo: ps.v(fn=lambda a: a.rearrange('p (s c) -> p s c', c=128)[:, :, lo:lo + 64])
        zv = lambda Zt: Zt.v(0, 128, 256 * g, 256 * g + 256, fn=lambda a: a.rearrange('p (s c) -> p s c', c=64))
        s.copy(zv(Zre), pv(0), eng='act')
        s.copy(zv(Zim), pv(64), eng='dve')
    ArP = s.sb('f_ArP', [128, 4096], BF16)
    AiP = s.sb('f_AiP', [128, 4096], BF16)
    tmp = [s.sb('f_tmp%d' % i, [128, 512], F32) for i in range(4)]
    for ch in range(8):
        c0, c1 = 512 * ch, 512 * ch + 512
        pr = next_ps(C)
        s.mm(pr.v(), tb['f_c128'].v(), Zre.v(0, 128, c0, c1), start=True, stop=False)
        s.mm(pr.v(), tb['f_s128'].v(), Zim.v(0, 128, c0, c1), start=False, stop=True)
        pi = next_ps(C)
        s.mm(pi.v(), tb['f_c128'].v(), Zim.v(0, 128, c0, c1), start=True, stop=False)
        s.mm(pi.v(), tb['f_ns128'].v(), Zre.v(0, 128, c0, c1), start=False, stop=True)
        r3 = lambda a: a.rearrange('p (s c) -> p s c', c=64)
        tre = tw.v(0, 128, 8 * ch, 8 * ch + 8, fn=lambda a: a.unsqueeze(2).to_broadcast([128, 8, 64]))
        tim = tw.v(0, 128, 64 + 8 * ch, 64 + 8 * ch + 8, fn=lambda a: a.unsqueeze(2).to_broadcast([128, 8, 64]))
        t = [x.v(fn=r3) for x in tmp]
        s.tt(t[0], pr.v(fn=r3), tre, ALU.mult)
        s.tt(t[1], pi.v(fn=r3), tim, ALU.mult)
        s.tt(t[2], pr.v(fn=r3), tim, ALU.mult)
        s.tt(t[3], pi.v(fn=r3), tre, ALU.mult)
        perm = lambda a: a.rearrange('p (c s) -> p s c', s=64)[:, 8 * ch:8 * ch + 8, :]
        s.tt(ArP.v(fn=perm), t[0], t[1], ALU.subtract, eng='pool')
        s.tt(AiP.v(fn=perm), t[2], t[3], ALU.add, eng='pool')
    ArT = s.sb('f_ArT', [128, 4096], BF16)
    AiT = s.sb('f_AiT', [128, 4096], BF16)
    for (src, dst) in ((ArP, ArT), (AiP, AiT)):
        for g in range(8):
            ps = next_ps(C)
            pb = lambda lo, hi: ps.v(fn=lambda a: a.bitcast(BF16)[:, lo:hi])
            for i in range(4):
                blk = 4 * g + i
                s.tr(pb(128 * i, 128 * i + 128), src.v(0, 128, 128 * blk, 128 * blk + 128), C.identb.v())
            s.copy(dst.v(0, 128, 512 * g, 512 * g + 512), pb(0, 512), eng=s.evac_eng())
    Y = s.sb('f_Y', [128, 4096], F32)
    for g in range(8):
        c0, c1 = 512 * g, 512 * g + 512
        ps = next_ps(C)
        s.mm(ps.v(), tb['f_bdc'].v(), ArT.v(0, 128, c0, c1), start=True, stop=False)
        s.mm(ps.v(), tb['f_bds'].v(), AiT.v(0, 128, c0, c1), start=False, stop=True)
        s.copy(Y.v(0, 128, c0, c1), ps.v(), eng=s.evac_eng())
    n = 0
    for cp in range(2):
        for dst in range(4):
            r0 = dst * XB_ROWS + 128 + cp
            o = xb_d.v(r0, r0 + 64, 0, TL, fn=lambda a: a[::2, :].rearrange('b (s j) -> s b j', j=128))
            i_ = Y.v(64 * cp + 16 * dst, 64 * cp + 16 * dst + 16, 0, 4096, fn=lambda a: a.rearrange('p (b j) -> p b j', j=128))
            s.dma('sp' if n % 2 else 'act', o, i_)
            n += 1


def build_U(parts=('fourier',)):
    nc = bass.Bass("TRN2", target_bir_lowering=False)
    s = Sched(nc)
    C = Ctx()
    xf_d = s.dram('xf', [4 * XF_ROWS, TL], F32, 'ExternalInput')
    id_d = s.dram('ident', [128, 128], F32, 'ExternalInput')
    xb_d = s.dram('xb', [4 * XB_ROWS, TL], F32, 'ExternalOutput')
    tabs_d = {k: s.dram(k, v, F32, 'ExternalInput') for k, v in FT_SHAPES.items()}
    md = {k: s.dram(k, v, F32, 'ExternalInput') for k, v in MT_SHAPES.items()}
    sd = {k: s.dram(k, v, F32, 'ExternalInput') for k, v in ST_SHAPES.items()}
    common_setup(s, C, id_d)
    C.epsb = s.sb('epsb', [128, 1], F32)
    s.op('dve', lambda e: e.memset(C.epsb.h[:, :], EPS), writes=[C.epsb.v()])
    C.oneb = s.sb('oneb', [128, 1], F32)
    s.op('dve', lambda e: e.memset(C.oneb.h[:, :], 1.0), writes=[C.oneb.v()])
    if 'dbg' in parts:
        C.dbg = {k: s.dram(k, sh, F32, 'ExternalOutput') for k, sh in [('d_q', [128, S]), ('d_k', [128, S]), ('d_H', [128, S]),
                                                                        ('d_sm', [128, 512]), ('d_va', [128, 64 * 129])]}
    mk = s.mark()
    if 'fourier' in parts:
        phase_U_fourier(s, C, xf_d, xb_d, tabs_d)
        s.release(mk)
    if 's5' in parts:
        phase_U_s5(s, C, xf_d, xb_d, sd)
        s.release(mk)
    if 'mlstm' in parts:
        phase_U_mlstm(s, C, xf_d, xb_d, md)
    s.finish()
    return nc


def mlstm_tables():
    j = np.arange(128)
    trif = (j[:, None] <= j[None, :]).astype(np.float32)
    sc = 128.0 ** -0.5
    return dict(m_trif=trif, m_trib=np.ascontiguousarray(trif.T), m_maskf=trif * sc, m_maskb=np.ascontiguousarray(trif.T) * sc,
                m_onesf=np.ones((128, 128), np.float32))


MT_SHAPES = dict(m_trif=[128, 128], m_trib=[128, 128], m_maskf=[128, 128], m_maskb=[128, 128], m_onesf=[128, 128],
                 m_cwq=[128, 5], m_cwk=[128, 5], m_cbq=[128, 1], m_cbk=[128, 1])


def phase_U_mlstm(s, C, xf_d, xb_d, md):
    NCH = S // 128
    SC = 128.0 ** -0.5
    tb = {}
    for k in MT_SHAPES:
        tb[k] = s.sb(k, MT_SHAPES[k], F32)
        s.dma('sp', tb[k].v(), md[k].v())
    G4 = s.sb('m_G4', [4, S], F32)
    for src in range(4):
        s.dma('act', G4.v(0, 4, 2048 * src, 2048 * src + 2048), xf_d.v(src * XF_ROWS + 384, src * XF_ROWS + 388))
    gT = s.sb('m_gT', [128, NCH * 4], F32)
    ps = next_ps(C)
    for c in range(NCH):
        s.tr(ps.v(0, 128, 4 * c, 4 * c + 4), G4.v(0, 4, 128 * c, 128 * c + 128), C.ident.v(0, 4, 0, 4))
    s.copy(gT.v(), ps.v(0, 128, 0, 4 * NCH))
    gcol = lambda g: gT.v(fn=lambda a: a.rearrange('p (c g) -> p c g', g=4)[:, :, g])
    sm = {}
    for nm in ['lf_f', 'lf_b', 'b_f', 'b_b', 'w_f', 'w_b', 'enb_f', 'enb_b', 'eg_f', 'eg_b', 'egs_f', 'egs_b', 'tmp']:
        sm[nm] = s.sb('m_' + nm, [128, NCH], F32)
    for d, gi in (('f', 2), ('b', 3)):
        lf = sm['lf_' + d]
        s.act(sm['tmp'].v(), gcol(gi), AF.Exp, scale=-1.0)
        s.act(lf.v(), sm['tmp'].v(), AF.Ln, bias=C.oneb.v(), scale=1.0)
        s.ts(lf.v(), lf.v(), -1.0, ALU.mult)
        pb = next_ps(C)
        s.mm(pb.v(0, 128, 0, NCH), tb['m_tri' + d].v(), lf.v())
        s.copy(sm['b_' + d].v(), pb.v(0, 128, 0, NCH))
        pg = next_ps(C)
        s.mm(pg.v(0, 128, 0, NCH), tb['m_onesf'].v(), lf.v())
        s.act(sm['eg_' + d].v(), pg.v(0, 128, 0, NCH), AF.Exp)
        s.ts(sm['egs_' + d].v(), sm['eg_' + d].v(), SC, ALU.mult)
        s.act(sm['enb_' + d].v(), sm['b_' + d].v(), AF.Exp, scale=-1.0)
        s.tt(sm['tmp'].v(), gcol(0 if d == 'f' else 1), sm['b_' + d].v(), ALU.subtract)
        s.act(sm['w_' + d].v(), sm['tmp'].v(), AF.Exp)
    zpb = s.sb('m_zpb', [128, S + 4], BF16)
    s.op('dve', lambda e: e.memset(zpb.h[:, 0:2], 0.0), writes=[zpb.v(0, 128, 0, 2)])
    s.op('dve', lambda e: e.memset(zpb.h[:, S + 2:S + 4], 0.0), writes=[zpb.v(0, 128, S + 2, S + 4)])
    acc0 = s.sb('m_acc0', [128, 2048], F32)
    vst = s.sb('m_vst', [128, 4096], F32)
    dg = s.sb('m_dg', [128, 10 * 128], BF16)
    qT = s.sb('m_qT', [128, S], BF16)
    kT = s.sb('m_kT', [128, S], BF16)
    for (qi, row0, cw, cb, dstT) in ((0, 0, tb['m_cwq'], tb['m_cbq'], qT), (1, 128, tb['m_cwk'], tb['m_cbk'], kT)):
        for k in range(5):
            s.ts(dg.v(0, 128, 128 * (5 * qi + k), 128 * (5 * qi + k) + 128), C.ident.v(), cw.v(0, 128, k, k + 1), ALU.mult)
        for src in range(4):
            s.dma('pool', zpb.v(0, 128, 2 + 2048 * src, 2 + 2048 * src + 2048),
                  xf_d.v(src * XF_ROWS + row0, src * XF_ROWS + row0 + 128))
        for ch in range(S // 512):
            ps = next_ps(C)
            for k in range(5):
                s.mm(ps.v(), dg.v(0, 128, 128 * (5 * qi + k), 128 * (5 * qi + k) + 128),
                     zpb.v(0, 128, 512 * ch + k, 512 * ch + k + 512), start=(k == 0), stop=(k == 4))
            s.act(dstT.v(0, 128, 512 * ch, 512 * ch + 512), ps.v(), AF.Silu, bias=cb.v())
    ktok = s.sb('m_ktok', [128, S], BF16)
    for g in range(NCH // 4):
        ps = next_ps(C)
        pb = lambda lo, hi: ps.v(fn=lambda a: a.bitcast(BF16)[:, lo:hi])
        for i in range(4):
            c = 4 * g + i
            s.tr(pb(128 * i, 128 * i + 128), kT.v(0, 128, 128 * c, 128 * c + 128), C.identb.v())
        s.copy(ktok.v(0, 128, 512 * g, 512 * g + 512), pb(0, 512), eng=s.evac_eng())
    vaug = {d: s.sb('m_vaug_' + d, [128, NCH * 129], BF16) for d in 'fb'}
    for g in range(NCH // 4):
        if g % 8 == 0:
            hv_ = g // 8
            for q2 in range(2):
                src = 2 * hv_ + q2
                s.dma('sp' if q2 else 'act', vst.v(0, 128, 2048 * q2, 2048 * q2 + 2048),
                      xf_d.v(src * XF_ROWS + 256, src * XF_ROWS + 384))
        ps = next_ps(C)
        for i in range(4):
            c = 4 * g + i
            cl = c % 32
            s.tr(ps.v(0, 128, 128 * i, 128 * i + 128), vst.v(0, 128, 128 * cl, 128 * cl + 128), C.ident.v())
        for i in range(4):
            c = 4 * g + i
            for d in 'fb':
                s.ts(vaug[d].v(0, 128, 129 * c, 129 * c + 128), ps.v(0, 128, 128 * i, 128 * i + 128), sm['w_' + d].v(0, 128, c, c + 1), ALU.mult)
    for d in 'fb':
        s.copy(vaug[d].v(fn=lambda a: a.rearrange('p (c e) -> p c e', e=129)[:, :, 128]), sm['w_' + d].v(), eng='pool')
    H = s.sb('m_H', [128, S], F32)
    P = {d: s.sb('m_P_' + d, [128, 129], F32) for d in 'fb'}
    Cb = {d: [s.sb('m_Cb_%s%d' % (d, i), [128, 129], BF16) for i in range(2)] for d in 'fb'}
    Sm = {d: [s.sb('m_Sm_%s%d' % (d, i), [128, 128], BF16) for i in range(3)] for d in 'fb'}
    den = {d: [s.sb('m_den_%s%d' % (d, i), [128, 1], F32) for i in range(3)] for d in 'fb'}
    mask = {'f': tb['m_maskf'], 'b': tb['m_maskb']}
    def dir_gen(d):
        for step in range(NCH):
            c = step if d == 'f' else NCH - 1 - step
            cprev = c - 1 if d == 'f' else c + 1
            va = vaug[d].v(0, 128, 129 * c, 129 * c + 129)
            if step < NCH - 1:
                ps_d = next_ps(C)
                s.mm(ps_d.v(0, 128, 0, 129), ktok.v(0, 128, 128 * c, 128 * c + 128), va)
                if step == 0:
                    s.copy(P[d].v(), ps_d.v(0, 128, 0, 129))
                else:
                    s.stt(P[d].v(), P[d].v(), sm['eg_' + d].v(0, 128, cprev, cprev + 1), ps_d.v(0, 128, 0, 129), ALU.mult, ALU.add)
                yield
                s.act(Cb[d][(step + 1) % 2].v(), P[d].v(), AF.Copy, scale=sm['egs_' + d].v(0, 128, c, c + 1))
                yield
            ps_s = next_ps(C)
            s.mm(ps_s.v(0, 128, 0, 128), kT.v(0, 128, 128 * c, 128 * c + 128), qT.v(0, 128, 128 * c, 128 * c + 128))
            smt = Sm[d][step % 3]
            s.tt(smt.v(), ps_s.v(0, 128, 0, 128), mask[d].v(), ALU.mult)
            yield
            ps_o = next_ps(C)
            s.mm(ps_o.v(0, 128, 0, 129), smt.v(), va, start=True, stop=(step == 0))
            if step > 0:
                s.mm(ps_o.v(0, 128, 0, 129), qT.v(0, 128, 128 * c, 128 * c + 128), Cb[d][step % 2].v(), start=False, stop=True)
            dn = den[d][step % 3]
            s.ts(dn.v(), ps_o.v(0, 128, 128, 129), -1.0, ALU.mult, sm['enb_' + d].v(0, 128, c, c + 1), ALU.max)
            yield
            s.tt(dn.v(), dn.v(), ps_o.v(0, 128, 128, 129), ALU.max)
            yield
            s.op('dve', lambda e, dn=dn: e.reciprocal(out=dn.h[:, :], in_=dn.h[:, :]), reads=[dn.v()], writes=[dn.v()])
            yield
            hv = H.v(0, 128, 128 * c, 128 * c + 128)
            if step < NCH // 2:
                s.act(hv, ps_o.v(0, 128, 0, 128), AF.Copy, scale=dn.v())
            else:
                s.stt(hv, ps_o.v(0, 128, 0, 128), dn.v(), hv, ALU.mult, ALU.add)
            yield

    gens = [dir_gen('f'), dir_gen('b')]
    while gens:
        for gnr in list(gens):
            try:
                next(gnr)
            except StopIteration:
                gens.remove(gnr)
    if getattr(C, 'dbg', None) is not None:
        s.dma('pool', C.dbg['d_q'].v(), qT.v())
        s.dma('pool', C.dbg['d_k'].v(), kT.v())
        s.dma('sp', C.dbg['d_H'].v(), H.v())
        for i, nm in enumerate(['lf_f', 'b_f', 'w_f', 'enb_f', 'eg_f', 'lf_b', 'b_b', 'w_b']):
            s.dma('sp', C.dbg['d_sm'].v(0, 128, 64 * i, 64 * i + 64), sm[nm].v())
        s.dma('pool', C.dbg['d_va'].v(), vaug['f'].v())
    H3 = lambda lo, hi: H.v(0, 128, 128 * lo, 128 * hi, fn=lambda a: a.rearrange('p (c e) -> p c e', e=128))
    mu = sm['tmp']
    s.op('dve', lambda e: e.tensor_reduce(out=mu.h[:, :], in_=H.h[:, :].rearrange('p (c e) -> p c e', e=128), axis=AX.X, op=ALU.add),
         reads=[H.v()], writes=[mu.v()])
    s.ts(mu.v(), mu.v(), 1.0 / 128, ALU.mult)
    bc = lambda t, lo, hi: t.v(0, 128, lo, hi, fn=lambda a: a.unsqueeze(2).to_broadcast([128, hi - lo, 128]))
    var = sm['lf_f']
    sq = vst
    for pc in range(4):
        lo, hi = 16 * pc, 16 * pc + 16
        s.tt(H3(lo, hi), H3(lo, hi), bc(mu, lo, hi), ALU.subtract)
        s.act(sq.v(0, 128, 0, 2048), H.v(0, 128, 128 * lo, 128 * hi), AF.Square)
        s.op('dve', lambda e, lo=lo, hi=hi: e.tensor_reduce(out=var.h[:, lo:hi], in_=sq.h[:, 0:2048].rearrange('p (c e) -> p c e', e=128),
                                                           axis=AX.X, op=ALU.add),
             reads=[sq.v(0, 128, 0, 2048)], writes=[var.v(0, 128, lo, hi)])
    s.act(var.v(), var.v(), AF.Sqrt, bias=C.epsb.v(), scale=1.0 / 128)
    s.op('dve', lambda e: e.reciprocal(out=var.h[:, :], in_=var.h[:, :]), reads=[var.v()], writes=[var.v()])
    for pc in range(4):
        lo, hi = 16 * pc, 16 * pc + 16
        s.tt(H3(lo, hi), H3(lo, hi), bc(var, lo, hi), ALU.mult, eng='pool' if pc % 2 else 'dve')
    for g in range(NCH // 4):
        ps = next_ps(C)
        for i in range(4):
            c = 4 * g + i
            s.tr(ps.v(0, 128, 128 * i, 128 * i + 128), H.v(0, 128, 128 * c, 128 * c + 128), C.ident.v())
        stv = acc0.v(0, 128, 512 * (g % 4), 512 * (g % 4) + 512)
        s.copy(stv, ps.v(), eng=s.evac_eng())
        dst = g // 4
        col = 512 * (g % 4)
        s.dma('sp' if g % 2 else 'act', xb_d.v(dst * XB_ROWS, dst * XB_ROWS + 128, col, col + 512), stv)


def u_extra_inputs(d, l, c):
    rp = c % 4
    m = {}
    m.update(mlstm_tables())
    cw = d['conv_w'][l]
    cb = d['conv_b'][l]
    m['m_cwq'] = np.ascontiguousarray(cw[:, 128 * rp:128 * rp + 128].T)
    m['m_cwk'] = np.ascontiguousarray(cw[:, 512 + 128 * rp:512 + 128 * rp + 128].T)
    m['m_cbq'] = np.ascontiguousarray(cb[128 * rp:128 * rp + 128].reshape(128, 1))
    m['m_cbk'] = np.ascontiguousarray(cb[512 + 128 * rp:512 + 128 * rp + 128].reshape(128, 1))
    m.update(s5_tables())
    m.update(s5_inputs(d, l, c))
    return m


NTAU = 152


def s5_tables():
    n72 = np.arange(72.0)
    n8 = np.arange(8.0)
    n64 = np.arange(64.0)
    dpow = 64.0 * 2.0 ** np.arange(7)
    tf = np.concatenate([n72 - 7, 7 - n8, 63 - n64, dpow, [1.0]])
    tbk = np.concatenate([64 - n72, n8, n64, dpow, [1.0]])
    tau = np.tile(np.concatenate([tf, tbk])[None, :], (128, 1))
    p = np.arange(128)
    imask = np.eye(128)
    jmask = np.roll(np.eye(128), 64, axis=1)
    blk = p // 16
    bmf = (blk[:, None] <= blk[None, :]) * 1.0
    bmb = (blk[:, None] >= blk[None, :]) * 1.0
    sg = np.concatenate([-np.ones(64), np.ones(64)])[:, None]
    sel = np.zeros((64, 4, 8, 128))
    for g in range(4):
        for i in range(8):
            for c in range(16):
                sel[16 * g + c, g, i, 16 * i + c] = 1.0
    selT = sel.transpose(3, 1, 2, 0).reshape(128, 4 * 8 * 64)
    tabs = dict(s_tau=tau, s_imask=imask, s_jmask=jmask, s_bmf=bmf, s_bmb=bmb, s_sg=sg, s_nsg=-sg,
                s_sel=sel.reshape(64, 4096), s_selT=selT)
    return {k: np.ascontiguousarray(v.astype(np.float32)) for k, v in tabs.items()}


ST_SHAPES = dict(s_tau=[128, 2 * NTAU], s_imask=[128, 128], s_jmask=[128, 128], s_bmf=[128, 128], s_bmb=[128, 128],
                 s_sg=[128, 1], s_nsg=[128, 1], s_sel=[64, 4096], s_selT=[128, 2048],
                 s_lam=[8 * 128, 2], s_logdt=[8 * 128, 1], s_X1=[8 * 128, 16], s_X2=[8 * 128, 16], s_Y1=[8 * 128, 16],
                 s_Y2=[8 * 128, 16], s_d=[4 * 128, 1])


def s5_inputs(d, l, c):
    rp = c % 4
    lam = np.zeros((8, 128, 2), np.float32)
    logdt = np.zeros((8, 128, 1), np.float32)
    X1 = np.zeros((8, 128, 16), np.float32)
    X2 = np.zeros((8, 128, 16), np.float32)
    Y1 = np.zeros((8, 128, 16), np.float32)
    Y2 = np.zeros((8, 128, 16), np.float32)
    dd = np.zeros((4, 128, 1), np.float32)
    for gl in range(4):
        g = 4 * rp + gl
        dd[gl, :, 0] = np.tile(d['s5_d'][l, g], 8)
        for di in range(2):
            u = 2 * gl + di
            lam[u, :, 0] = np.tile(d['s5_lam_re'][l, di, g], 2)
            lam[u, :, 1] = np.tile(d['s5_lam_im'][l, di, g], 2)
            logdt[u, :, 0] = d['s5_log_dt'][l, di, g]
            bre, bim = d['s5_b_re'][l, di, g], d['s5_b_im'][l, di, g]
            cre, cim = d['s5_c_re'][l, di, g].T, d['s5_c_im'][l, di, g].T
            X1[u] = np.concatenate([bre, bim], 0)
            X2[u] = np.concatenate([bim, bre], 0)
            Y1[u] = np.concatenate([cre, cim], 0)
            Y2[u] = np.concatenate([cim, cre], 0)
    return dict(s_lam=lam.reshape(1024, 2), s_logdt=logdt.reshape(1024, 1), s_X1=X1.reshape(1024, 16), s_X2=X2.reshape(1024, 16),
                s_Y1=Y1.reshape(1024, 16), s_Y2=Y2.reshape(1024, 16), s_d=dd.reshape(512, 1))


def phase_U_s5(s, C, xf_d, xb_d, sd):
    TWO_PI = 2.0 * np.pi
    cst = {}
    for k in ['s_tau', 's_imask', 's_jmask', 's_bmf', 's_bmb', 's_sg', 's_nsg']:
        cst[k] = s.sb(k, ST_SHAPES[k], F32)
        s.dma('sp', cst[k].v(), sd[k].v())
    sel = s.sb('s_sel', [64, 4096], BF16)
    s.dma('pool', sel.v(), sd['s_sel'].v())
    selT = s.sb('s_selT', [128, 2048], BF16)
    s.dma('pool', selT.v(), sd['s_selT'].v())
    zsb = s.sb('s_zsb', [64, S], BF16)
    for src in range(4):
        s.dma('pool', zsb.v(0, 64, 2048 * src, 2048 * src + 2048), xf_d.v(src * XF_ROWS + 452, src * XF_ROWS + 516))
    U8 = [s.sb('s_U8_%d' % g, [128, 1024], BF16) for g in range(4)]
    for g in range(4):
        for half in range(2):
            ps = next_ps(C)
            for i in range(8):
                rhs = V(zsb, zsb.h[0:64, 4096 * half + i:4096 * (half + 1):8], (0, 64, 4096 * half, 4096 * (half + 1)))
                s.mm(ps.v(0, 128, 0, 512), sel.v(0, 64, 128 * (8 * g + i), 128 * (8 * g + i) + 128), rhs, start=(i == 0), stop=(i == 7))
            s.copy(U8[g].v(0, 128, 512 * half, 512 * half + 512), ps.v(), eng=s.evac_eng())
    NS = 4

    def mk(nm, shape, dt=F32):
        return [s.sb('s_%s_%d' % (nm, b), shape, dt) for b in range(NS)]
    lamt_, ldt_ = mk('lamt', [128, 2]), mk('ldt', [128, 1])
    v1_ = {nm: mk(nm, [128, 1]) for nm in ['dt', 'lr', 'th', 'den', 'am1', 't1', 't2', 'fre', 'fim', 'fis', 'frs', 'dvec']}
    X1_, X2_, Y1_, Y2_ = [mk(nm, [128, 16]) for nm in ['X1', 'X2', 'Y1', 'Y2']]
    BA_, BB_, CA_, CB_, t16_ = [mk(nm, [128, 16]) for nm in ['BA', 'BB', 'CA', 'CB', 't16']]
    mag_, ang_, kf_, al_, be_ = [mk(nm, [128, NTAU]) for nm in ['mag', 'ang', 'kf', 'al', 'be']]
    ki_ = mk('ki', [128, NTAU], mybir.dt.int32)
    bsn_ = mk('bsn', [128, 7])
    Gt_, Ht_, tG_ = mk('G', [128, 72 * 16]), mk('H', [128, 72 * 16]), mk('tG', [128, 72 * 16])
    Gb_ = mk('Gb', [128, 72 * 16], BF16)
    Tm_, Wm_ = mk('T', [128, 1024], BF16), mk('W', [128, 1024], BF16)
    Dm_ = mk('D', [128, 7 * 128], BF16)
    tmp128_ = mk('tmp128', [128, 128])
    Xf_, Xb_, Xp_ = mk('Xf', [128, 128]), mk('Xb', [128, 128], BF16), mk('Xp', [128, 130], BF16)
    for b in range(NS):
        s.op('dve', lambda e, b=b: e.memset(Xp_[b].h[:, :], 0.0), writes=[Xp_[b].v()])
    Y8 = [s.sb('s_Y8_%d' % g, [128, 1024], BF16) for g in range(4)]
    gx = [s.sb('s_gx%d' % i, [128, 512], F32) for i in range(6)]

    def col(t, i):
        return t.v(0, 128, i, i + 1)

    def unit_gen(g, d):
        b = 2 * (g % 2) + d
        lamt, ldt = lamt_[b], ldt_[b]
        v1 = {k: v[b] for k, v in v1_.items()}
        X1, X2, Y1, Y2 = X1_[b], X2_[b], Y1_[b], Y2_[b]
        BA, BB, CA, CB, t16 = BA_[b], BB_[b], CA_[b], CB_[b], t16_[b]
        mag, ang, kf, al, be, ki, bsn = mag_[b], ang_[b], kf_[b], al_[b], be_[b], ki_[b], bsn_[b]
        Gt, Ht, tG, Gb, Tm, Wm, Dm, tmp128 = Gt_[b], Ht_[b], tG_[b], Gb_[b], Tm_[b], Wm_[b], Dm_[b], tmp128_[b]
        Xf, Xb, Xp = Xf_[b], Xb_[b], Xp_[b]
        u = 2 * g + d
        r0 = 128 * u
        s.dma('sp', lamt.v(), sd['s_lam'].v(r0, r0 + 128))
        s.dma('act', ldt.v(), sd['s_logdt'].v(r0, r0 + 128))
        s.dma('sp', X1.v(), sd['s_X1'].v(r0, r0 + 128))
        s.dma('act', X2.v(), sd['s_X2'].v(r0, r0 + 128))
        s.dma('sp', Y1.v(), sd['s_Y1'].v(r0, r0 + 128))
        s.dma('act', Y2.v(), sd['s_Y2'].v(r0, r0 + 128))
        if d == 0:
            s.dma('sp', v1['dvec'].v(), sd['s_d'].v(128 * g, 128 * g + 128))
        yield
        lre, lim = col(lamt, 0), col(lamt, 1)
        s.act(v1['dt'].v(), ldt.v(), AF.Exp); yield
        s.tt(v1['lr'].v(), lre, v1['dt'].v(), ALU.mult); yield
        s.tt(v1['th'].v(), lim, v1['dt'].v(), ALU.mult); yield
        tau = cst['s_tau'].v(0, 128, NTAU * d, NTAU * d + NTAU)
        s.ts(ang.v(), tau, v1['th'].v(), ALU.mult); yield
        s.ts(kf.v(), ang.v(), 1.0 / TWO_PI, ALU.mult); yield
        s.copy(ki.v(), kf.v()); yield
        s.copy(kf.v(), ki.v()); yield
        s.stt(ang.v(), kf.v(), -TWO_PI, ang.v(), ALU.mult, ALU.add); yield

        def wrap(dst, y):
            s.ts(mag.v(), y.v(), float(np.pi), ALU.is_gt, -TWO_PI, ALU.mult); yield
            s.tt(dst.v(), mag.v(), y.v(), ALU.add); yield
            s.ts(mag.v(), y.v(), -float(np.pi), ALU.is_lt, TWO_PI, ALU.mult); yield
            s.tt(dst.v(), dst.v(), mag.v(), ALU.add); yield
        yield from wrap(kf, ang)
        s.act(be.v(), kf.v(), AF.Sin); yield
        s.ts(ang.v(), ang.v(), float(np.pi / 2), ALU.add); yield
        yield from wrap(kf, ang)
        s.act(al.v(), kf.v(), AF.Sin); yield
        s.act(mag.v(), tau, AF.Exp, scale=v1['lr'].v()); yield
        s.tt(al.v(), al.v(), mag.v(), ALU.mult); yield
        s.tt(be.v(), be.v(), mag.v(), ALU.mult); yield
        a1, b1 = col(al, NTAU - 1), col(be, NTAU - 1)
        s.tt(v1['t1'].v(), lre, lre, ALU.mult); yield
        s.stt(v1['den'].v(), lim, lim, v1['t1'].v(), ALU.mult, ALU.add); yield
        s.op('dve', lambda e: e.reciprocal(out=v1['den'].h[:, :], in_=v1['den'].h[:, :]), reads=[v1['den'].v()], writes=[v1['den'].v()]); yield
        s.ts(v1['am1'].v(), a1, -1.0, ALU.add); yield
        s.tt(v1['t1'].v(), v1['am1'].v(), lre, ALU.mult); yield
        s.stt(v1['t1'].v(), b1, lim, v1['t1'].v(), ALU.mult, ALU.add); yield
        s.tt(v1['fre'].v(), v1['t1'].v(), v1['den'].v(), ALU.mult); yield
        s.tt(v1['t2'].v(), v1['am1'].v(), lim, ALU.mult); yield
        s.stt(v1['t2'].v(), b1, lre, v1['t2'].v(), ALU.mult, ALU.subtract); yield
        s.tt(v1['fim'].v(), v1['t2'].v(), v1['den'].v(), ALU.mult); yield
        s.tt(v1['fis'].v(), v1['fim'].v(), cst['s_sg'].v(), ALU.mult); yield
        s.tt(v1['frs'].v(), v1['fre'].v(), cst['s_sg'].v(), ALU.mult); yield
        s.ts(t16.v(), X1.v(), v1['fre'].v(), ALU.mult); yield
        s.stt(BA.v(), X2.v(), v1['fis'].v(), t16.v(), ALU.mult, ALU.add); yield
        s.ts(t16.v(), X1.v(), v1['fim'].v(), ALU.mult); yield
        s.stt(BB.v(), X2.v(), v1['frs'].v(), t16.v(), ALU.mult, ALU.subtract); yield
        s.ts(CA.v(), Y1.v(), cst['s_nsg'].v(), ALU.mult); yield
        s.ts(CB.v(), Y2.v(), -1.0, ALU.mult); yield
        b3 = lambda t: t.v(fn=lambda a: a.unsqueeze(1).to_broadcast([128, 72, 16]))
        a3 = lambda t, lo: t.v(0, 128, lo, lo + 72, fn=lambda a: a.unsqueeze(2).to_broadcast([128, 72, 16]))
        r3 = lambda t: t.v(fn=lambda a: a.rearrange('p (n c) -> p n c', c=16))
        s.tt(r3(Gt), b3(CA), a3(al, 0), ALU.mult); yield
        s.tt(r3(tG), b3(CB), a3(be, 0), ALU.mult); yield
        s.tt(Gt.v(), Gt.v(), tG.v(), ALU.add, eng='pool'); yield
        s.tt(r3(Ht), b3(BA), a3(al, 72), ALU.mult); yield
        s.tt(r3(tG), b3(BB), a3(be, 72), ALU.mult); yield
        s.tt(Ht.v(), Ht.v(), tG.v(), ALU.add, eng='pool'); yield
        s.copy(Gb.v(), Gt.v(), eng='act'); yield
        s.ts(bsn.v(), be.v(0, 128, 144, 151), cst['s_nsg'].v(), ALU.mult); yield
        for m in range(7):
            s.ts(tmp128.v(), cst['s_imask'].v(), col(al, 144 + m), ALU.mult, eng='pool'); yield
            s.stt(Dm.v(0, 128, 128 * m, 128 * m + 128), cst['s_jmask'].v(), col(bsn, m), tmp128.v(), ALU.mult, ALU.add); yield
        for dl in range(8):
            gs = 128 * dl if d == 0 else 128 * (8 - dl)
            ps = next_ps(C)
            s.mm(ps.v(0, 128, 0, 128), Ht.v(0, 128, 0, 128), Gt.v(0, 128, gs, gs + 128))
            tv = Tm.v(0, 128, 128 * dl, 128 * dl + 128)
            if dl == 0:
                bm = cst['s_bmf'] if d == 0 else cst['s_bmb']
                if d == 0:
                    s.tt(tmp128.v(), ps.v(0, 128, 0, 128), bm.v(), ALU.mult); yield
                    s.stt(tv, cst['s_imask'].v(), v1['dvec'].v(), tmp128.v(), ALU.mult, ALU.add)
                else:
                    s.tt(tv, ps.v(0, 128, 0, 128), bm.v(), ALU.mult)
            else:
                s.copy(tv, ps.v(0, 128, 0, 128), eng=s.evac_eng())
            yield
        for hh in range(2):
            ps = next_ps(C)
            for i in range(4):
                ih = 4 * hh + i
                s.tr(ps.v(0, 128, 128 * i, 128 * i + 128), Ht.v(0, 128, 128 + 128 * ih, 128 + 128 * ih + 128), C.ident.v())
            s.copy(Wm.v(0, 128, 512 * hh, 512 * hh + 512), ps.v(), eng=s.evac_eng()); yield
        ps = next_ps(C)
        for ih in range(8):
            rhs = V(U8[g], U8[g].h[:, ih::8], (0, 128, 0, 1024))
            s.mm(ps.v(0, 128, 0, 128), Wm.v(0, 128, 128 * ih, 128 * ih + 128), rhs, start=(ih == 0), stop=(ih == 7))
        s.copy(Xf.v(), ps.v(0, 128, 0, 128), eng='dve')
        s.copy(Xb.v(), ps.v(0, 128, 0, 128), eng='act'); yield
        for m in range(7):
            sh = 2 ** m
            ps = next_ps(C)
            if d == 0:
                s.mm(ps.v(0, 128, 0, 128 - sh), Dm.v(0, 128, 128 * m, 128 * m + 128), Xb.v(0, 128, 0, 128 - sh))
                xv = Xf.v(0, 128, sh, 128)
            else:
                s.mm(ps.v(0, 128, 0, 128 - sh), Dm.v(0, 128, 128 * m, 128 * m + 128), Xb.v(0, 128, sh, 128))
                xv = Xf.v(0, 128, 0, 128 - sh)
            s.tt(xv, xv, ps.v(0, 128, 0, 128 - sh), ALU.add); yield
            if m < 6:
                s.copy(Xb.v(), Xf.v(), eng='act'); yield
        s.copy(Xp.v(0, 128, 1, 129), Xf.v(), eng='act'); yield

    def out_gen(g):
        bf, bb = 2 * (g % 2), 2 * (g % 2) + 1
        for hh in range(2):
            ps = next_ps(C)
            for q in range(4):
                jh = 4 * hh + q
                o = ps.v(0, 128, 128 * q, 128 * q + 128)
                first = True
                for ih in range(jh + 1):
                    rhs = V(U8[g], U8[g].h[:, ih::8], (0, 128, 0, 1024))
                    s.mm(o, Tm_[bf].v(0, 128, 128 * (jh - ih), 128 * (jh - ih) + 128), rhs, start=first, stop=False)
                    first = False
                for ih in range(jh, 8):
                    rhs = V(U8[g], U8[g].h[:, ih::8], (0, 128, 0, 1024))
                    s.mm(o, Tm_[bb].v(0, 128, 128 * (ih - jh), 128 * (ih - jh) + 128), rhs, start=False, stop=False)
                s.mm(o, Gb_[bf].v(0, 128, 128 * (jh + 1), 128 * (jh + 1) + 128), Xp_[bf].v(0, 128, 0, 128), start=False, stop=False)
                s.mm(o, Gb_[bb].v(0, 128, 128 * jh, 128 * jh + 128), Xp_[bb].v(0, 128, 2, 130), start=False, stop=True)
            k3 = 3 * (g % 2)
            x_, t_, u_ = gx[k3], gx[k3 + 1], gx[k3 + 2]
            s.copy(x_.v(), ps.v(), eng='dve'); yield
            s.act(t_.v(), x_.v(), AF.Square); yield
            s.ts(t_.v(), t_.v(), 0.044715, ALU.mult, 1.0, ALU.add, eng='pool'); yield
            s.tt(t_.v(), t_.v(), x_.v(), ALU.mult, eng='pool'); yield
            s.act(u_.v(), t_.v(), AF.Sigmoid, scale=1.5957691216057308); yield
            ov = Y8[g].v(fn=lambda a, hh=hh: a.rearrange('p (k j) -> p j k', j=8)[:, 4 * hh:4 * hh + 4, :])
            s.tt(ov, u_.v(fn=lambda a: a.rearrange('p (j k) -> p j k', k=128)), x_.v(fn=lambda a: a.rearrange('p (j k) -> p j k', k=128)), ALU.mult); yield

    def run_interleaved(gens):
        gens = list(gens)
        while gens:
            for gnr in list(gens):
                try:
                    next(gnr)
                except StopIteration:
                    gens.remove(gnr)

    for gp in range(2):
        run_interleaved([unit_gen(2 * gp + gg, d) for gg in range(2) for d in range(2)])
        run_interleaved([out_gen(2 * gp + gg) for gg in range(2)])
    yst = [s.sb('s_yst%d' % i, [64, 512], F32) for i in range(2)]
    for tb_ in range(16):
        ps = next_ps(C)
        for j in range(8):
            for g in range(4):
                s.mm(ps.v(0, 64, 64 * j, 64 * j + 64), selT.v(0, 128, 64 * (8 * g + j), 64 * (8 * g + j) + 64),
                     Y8[g].v(0, 128, 64 * tb_, 64 * tb_ + 64), start=(g == 0), stop=(g == 3))
        st = yst[tb_ % 2]
        s.copy(st.v(fn=lambda a: a.rearrange('p (k j) -> p j k', j=8)), ps.v(0, 64, 0, 512, fn=lambda a: a.rearrange('p (j k) -> p j k', k=64)),
               eng=s.evac_eng())
        dst = tb_ // 4
        cc = 512 * (tb_ % 4)
        s.dma('sp' if tb_ % 2 else 'act', xb_d.v(dst * XB_ROWS + 192, dst * XB_ROWS + 256, cc, cc + 512), st.v())


def _col(a):
    return np.ascontiguousarray(np.asarray(a, np.float32).reshape(-1, 1))


def _c(a):
    return np.ascontiguousarray(np.asarray(a, np.float32))


def _layer_weights(inp, l):
    return dict(w_in=_c(inp['w_in'][l]), b_in=_col(inp['b_in'][l]), g_pre=_col(inp['g_mix_pre'][l]),
                norm_g=_col(inp['mlstm_norm_g'][l]), w_up_m=_c(inp['w_up_mlstm'][l]), w_up_f=_c(inp['w_up_fourier'][l]),
                w_glu=_c(inp['w_glu'][l]), b_glu=_col(inp['b_glu'][l]), w_out=_c(inp['w_out'][l]), g_post=_col(inp['g_mix_post'][l]),
                g_ffn_pre=_col(inp['g_ffn_pre'][l]), g_ffn_post=_col(inp['g_ffn_post'][l]), w_ffn1=_c(inp['w_ffn1'][l]),
                w_ffn2=_c(inp['w_ffn2'][l]))


def kernel_unfused(**inputs):
    inp = {k: np.asarray(v) for k, v in inputs.items()}
    ident = np.eye(128, dtype=np.float32)
    cores = list(range(8))
    xs = [_c(inp['x'][c // 4, TL * (c % 4):TL * (c % 4 + 1)]) for c in cores]
    ftab = fourier_tables()
    for l in range(2):
        Wl = _layer_weights(inp, l)
        nc = build_T1()
        maps = [dict(x=xs[c], w_in=Wl['w_in'], b_in=Wl['b_in'], g_pre=Wl['g_pre'], ident=ident) for c in cores]
        res = run_bass_kernel_spmd(nc, maps, core_ids=cores)
        xf_sent = [np.asarray(res.results[c]['xf']).reshape(4, XF_ROWS, TL) for c in cores]
        maps = []
        for c in cores:
            b, r = c // 4, c % 4
            xf = np.stack([xf_sent[4 * b + src][r] for src in range(4)], 0).reshape(4 * XF_ROWS, TL)
            m = dict(xf=np.ascontiguousarray(xf), ident=ident)
            m.update(ftab)
            m.update(u_extra_inputs(inp, l, c))
            maps.append(m)
        nc = build_U(('fourier', 'mlstm', 's5'))
        res = run_bass_kernel_spmd(nc, maps, core_ids=cores)
        xb_sent = [np.asarray(res.results[c]['xb']).reshape(4, XB_ROWS, TL) for c in cores]
        maps = []
        for c in cores:
            b, r = c // 4, c % 4
            xb = np.stack([xb_sent[4 * b + src][r] for src in range(4)], 0).reshape(4 * XB_ROWS, TL)
            m = dict(x=xs[c], xb=np.ascontiguousarray(xb), ident=ident)
            m.update(Wl)
            maps.append(m)
        nc = build_T2()
        res = run_bass_kernel_spmd(nc, maps, core_ids=cores)
        xs = [np.asarray(res.results[c]['x_out']) for c in cores]
    out = np.zeros((2, S, D), np.float32)
    for c in cores:
        out[c // 4, TL * (c % 4):TL * (c % 4 + 1)] = xs[c]
    return out


class BlockT:
    def __init__(self, parent, bases, B):
        self.parent, self.bases, self.B = parent, bases, B
        self.space = parent.space

    def v(self, p0=0, p1=None, f0=0, f1=None, fn=None):
        if p1 is None:
            p1 = self.B * len(self.bases)
        blk = p0 // self.B
        assert (p1 - 1) // self.B == blk, (p0, p1, self.B)
        off = self.bases[blk] - blk * self.B
        return self.parent.v(off + p0, off + p1, f0, f1, fn)


W_SHAPES = [('w_in', [D, N_IN]), ('b_in', [N_IN, 1]), ('g_pre', [D, 1]), ('norm_g', [512, 1]), ('w_up_m', [512, D]),
            ('w_up_f', [256, D]), ('w_glu', [256, 2 * D]), ('b_glu', [2 * D, 1]), ('w_out', [D, D]), ('g_post', [D, 1]),
            ('g_ffn_pre', [D, 1]), ('g_ffn_post', [D, 1]), ('w_ffn1', [D, 4 * D]), ('w_ffn2', [4 * D, D])]
M_CONST = ['m_trif', 'm_trib', 'm_maskf', 'm_maskb', 'm_onesf']
M_PARAM = ['m_cwq', 'm_cwk', 'm_cbq', 'm_cbk']
S_CONST = ['s_tau', 's_imask', 's_jmask', 's_bmf', 's_bmb', 's_sg', 's_nsg', 's_sel', 's_selT']
S_PARAM = ['s_lam', 's_logdt', 's_X1', 's_X2', 's_Y1', 's_Y2', 's_d']


class ColT:
    def __init__(self, parent, col0, n):
        self.parent, self.col0, self.n = parent, col0, n
        self.space = parent.space

    def v(self, p0=0, p1=None, f0=0, f1=None, fn=None):
        f1 = self.n if f1 is None else f1
        return self.parent.v(p0, p1, self.col0 + f0, self.col0 + f1, fn)


def fused_T1(s, C, x_src, W, xf_view):
    mk = s.mark()
    load_xT(s, C, x_src, TL)
    gcols = load_vec_cols(s, C, 'gpre', W['g_pre'], 8)
    rstd = rms_rstd(s, C, C.xT, TL, 'pre')
    hT = make_h(s, C, gcols, rstd, TL, 'hT')
    phase_T1_proj(s, C, hT, W['w_in'], W['b_in'], xf_view)
    s.release(mk)


def fused_T2(s, C, x_src, W, xb_view, out_view, NT=1024):
    mk = s.mark()
    C.NT = NT
    C.n_wbig = 3
    C.n_wp = 11
    C.sqb = [s.sb('sqb%d' % i, [128, 512], BF16) for i in range(2)]
    C.rstd2 = s.sb('rstd2', [128, NT], F32)
    gcols = load_vec_cols(s, C, 'gpre', W['g_pre'], 8)
    t2_setup(s, C, W)
    C.xT = [s.sb('xT%d' % k, [128, NT], F32) for k in range(8)]
    C.hT = [ColT(C.h2T, k * NT, NT) for k in range(8)]
    C.xoff = 0
    xstage = [s.sb('xst%d' % i, [128, 1024], F32) for i in range(4)]
    for hf in range(TL // NT):
        tok0 = hf * NT
        for tg in range(NT // 512):
            for i in range(4):
                tt = tg * 4 + i
                s.dma('sp' if i % 2 == 0 else 'act', xstage[i].v(), x_src.v(tok0 + 128 * tt, tok0 + 128 * tt + 128))
            for kt in range(8):
                ps = ps6(C)
                for i in range(4):
                    s.tr(ps.v(0, 128, 128 * i, 128 * i + 128), xstage[i].v(0, 128, 128 * kt, 128 * kt + 128), C.ident.v())
                s.copy(C.xT[kt].v(0, 128, 512 * tg, 512 * tg + 512), ps.v(), eng=s.evac_eng())
        for tc in range(NT // 512):
            ps = ps6(C)
            for kt in range(8):
                q = C.sqb[kt % 2]
                s.act(q.v(), C.xT[kt].v(0, 128, 512 * tc, 512 * tc + 512), AF.Square)
                s.mm(ps.v(), C.ones.v(), q.v(), start=(kt == 0), stop=(kt == 7))
            r = C.rstd2.v(0, 128, 512 * tc, 512 * tc + 512)
            s.act(r, ps.v(), AF.Sqrt, bias=C.epsb.v(), scale=1.0 / D)
            s.op('dve', lambda e, r=r: e.reciprocal(out=r.ap, in_=r.ap), reads=[r], writes=[r])
        for kt in range(8):
            s.stt(C.hT[kt].v(), C.xT[kt].v(), gcols.v(0, 128, kt, kt + 1), C.rstd2.v(), ALU.mult, ALU.mult)
        phase_T2(s, C, W, xb_view, tok0)
        store_xT(s, C, out_view, tok0, NT, xstage)
    s.release(mk)


def fused_U(s, C, xf_view, xb_view, tabs_d, md, sd):
    mk = s.mark()
    phase_U_fourier(s, C, xf_view, xb_view, tabs_d)
    s.release(mk)
    phase_U_s5(s, C, xf_view, xb_view, sd)
    s.release(mk)
    phase_U_mlstm(s, C, xf_view, xb_view, md)
    s.release(mk)


def build_fused(n_layers=2):
    nc = bass.Bass("TRN2", target_bir_lowering=False)
    s = Sched(nc)
    C = Ctx()
    x_d = s.dram('x', [S, D], F32, 'ExternalInput')
    id_d = s.dram('ident', [128, 128], F32, 'ExternalInput')
    out_d = s.dram('out', [S, D], F32, 'ExternalOutput')
    Wl = [{n: s.dram('%s_l%d' % (n, l), sh, F32, 'ExternalInput') for n, sh in W_SHAPES} for l in range(n_layers)]
    tabs_d = {k: s.dram(k, v, F32, 'ExternalInput') for k, v in FT_SHAPES.items()}
    mconst = {k: s.dram(k, MT_SHAPES[k], F32, 'ExternalInput') for k in M_CONST}
    sconst = {k: s.dram(k, ST_SHAPES[k], F32, 'ExternalInput') for k in S_CONST}
    mpar = [{k: s.dram('%s_l%d' % (k, l), [4 * MT_SHAPES[k][0], MT_SHAPES[k][1]], F32, 'ExternalInput') for k in M_PARAM}
            for l in range(n_layers)]
    spar = [{k: s.dram('%s_l%d' % (k, l), [4 * ST_SHAPES[k][0], ST_SHAPES[k][1]], F32, 'ExternalInput') for k in S_PARAM}
            for l in range(n_layers)]
    xf_all = s.dram('xf_all', [16 * XF_ROWS, TL], F32, 'Internal')
    xb_all = s.dram('xb_all', [16 * XB_ROWS, TL], F32, 'Internal')
    xs1 = s.dram('xs1', [S, D], F32, 'Internal')
    common_setup(s, C, id_d)
    C.epsb = s.sb('epsb', [128, 1], F32)
    s.op('dve', lambda e: e.memset(C.epsb.h[:, :], EPS), writes=[C.epsb.v()])
    C.oneb = s.sb('oneb', [128, 1], F32)
    s.op('dve', lambda e: e.memset(C.oneb.h[:, :], 1.0), writes=[C.oneb.v()])
    for l in range(n_layers):
        xsrc = x_d if l == 0 else xs1
        xdst = out_d if l == n_layers - 1 else xs1
        Wf = Wl[l]
        t1cols = [(OFF_Q + 128 * i, 128) for i in range(4)] + [(OFF_K + 128 * i, 128) for i in range(4)] + \
                 [(OFF_V + 128 * i, 128) for i in range(4)] + [(OFF_G16, 16)] + [(OFF_F + 128 * i, 128) for i in range(2)] + \
                 [(OFF_S5 + 128 * i, 128) for i in range(2)]
        t2cols = [(OFF_O + 128 * i, 128) for i in range(4)] + [(OFF_GATE + 1024 * b + 128 * j, 128) for j in range(8) for b in range(3)]
        pw = dict(
            w_in=PW(s, 'pw_in_l%d' % l, Wf['w_in'], 8, t1cols + t2cols),
            w_up_m=PW(s, 'pw_um_l%d' % l, Wf['w_up_m'], 4, [(128 * j, 128) for j in range(8)]),
            w_up_f=PW(s, 'pw_uf_l%d' % l, Wf['w_up_f'], 2, [(128 * j, 128) for j in range(8)]),
            w_glu=PW(s, 'pw_glu_l%d' % l, Wf['w_glu'], 2, [(128 * j, 128) for j in range(16)]),
            w_out=PW(s, 'pw_out_l%d' % l, Wf['w_out'], 8, [(128 * j, 128) for j in range(8)]),
            w_ffn1=PW(s, 'pw_f1_l%d' % l, Wf['w_ffn1'], 8, [(512 * j, 512) for j in range(8)]),
            w_ffn2=PW(s, 'pw_f2_l%d' % l, Wf['w_ffn2'], 32, [(128 * j, 128) for j in range(8)]))
        for i in range(len(t1cols) + 4):
            pw['w_in'].prep(s, i)
        for j in range(8):
            pw['w_up_m'].prep(s, j)
            pw['w_up_f'].prep(s, j)
            pw['w_glu'].prep(s, j)
            pw['w_glu'].prep(s, 8 + j)
            for b in range(3):
                pw['w_in'].prep(s, len(t1cols) + 4 + 3 * j + b)
        for nm in ['w_out', 'w_ffn1', 'w_ffn2']:
            for j in range(8):
                pw[nm].prep(s, j)
        Wp = dict(Wf)
        Wp.update(pw)
        Wl[l] = Wp
        for q in range(4):
            fused_T1(s, C, BlockT(xsrc, [TL * q], TL), Wl[l], BlockT(xf_all, [(dst * 4 + q) * XF_ROWS for dst in range(4)], XF_ROWS))
        for u in range(4):
            md = dict(mconst)
            md.update({k: BlockT(mpar[l][k], [u * MT_SHAPES[k][0]], MT_SHAPES[k][0]) for k in M_PARAM})
            sd = dict(sconst)
            sd.update({k: BlockT(spar[l][k], [u * ST_SHAPES[k][0]], ST_SHAPES[k][0]) for k in S_PARAM})
            fused_U(s, C, BlockT(xf_all, [(u * 4 + src) * XF_ROWS for src in range(4)], XF_ROWS),
                    BlockT(xb_all, [(dst * 4 + u) * XB_ROWS for dst in range(4)], XB_ROWS), tabs_d, md, sd)
        for q in range(4):
            fused_T2(s, C, BlockT(xsrc, [TL * q], TL), Wl[l], BlockT(xb_all, [(q * 4 + src) * XB_ROWS for src in range(4)], XB_ROWS),
                     BlockT(xdst, [TL * q], TL))
    s.finish()
    return nc


def fused_inputs(inp, b, n_layers=2):
    m = dict(x=_c(inp['x'][b]), ident=np.eye(128, dtype=np.float32))
    m.update(fourier_tables())
    mt = mlstm_tables()
    m.update({k: mt[k] for k in M_CONST})
    st = s5_tables()
    m.update({k: st[k] for k in S_CONST})
    for l in range(n_layers):
        for k, v in _layer_weights(inp, l).items():
            m['%s_l%d' % (k, l)] = v
        units = [u_extra_inputs(inp, l, u) for u in range(4)]
        for k in M_PARAM + S_PARAM:
            m['%s_l%d' % (k, l)] = np.ascontiguousarray(np.concatenate([units[u][k] for u in range(4)], 0))
    return m


def kernel(**inputs):
    inp = {k: np.asarray(v) for k, v in inputs.items()}
    nc = build_fused()
    per_batch = [fused_inputs(inp, b) for b in range(2)]
    maps = [per_batch[c // 4] for c in range(8)]
    res = run_bass_kernel_spmd(nc, maps, core_ids=list(range(8)))
    out = np.stack([np.asarray(res.results[0]['out']), np.asarray(res.results[4]['out'])], 0)
    return out.astype(np.float32)
```

```python
import numpy as np
import concourse.bass as bass
import concourse.mybir as mybir
from concourse.bass_utils import run_bass_kernel_spmd

F32 = mybir.dt.float32
BF16 = mybir.dt.bfloat16
AF = mybir.ActivationFunctionType
ALU = mybir.AluOpType
AX = mybir.AxisListType

ENGS = ['pe', 'act', 'pool', 'dve', 'sp']
DMA_POOL = 6
SAME_ENGINE_SYNC = True
SEM_EPOCH = 16000

D = 1024
TL = 2048
S = 8192
EPS = 1e-6
N_IN = 5648
XF_ROWS = 516
XB_ROWS = 256


class T:
    def __init__(self, name, handle, shape, space):
        self.name, self.h, self.shape, self.space = name, handle, shape, space
        self.recs = []
        self.track = True

    def v(self, p0=0, p1=None, f0=0, f1=None, fn=None):
        P = self.shape[0]
        F = int(np.prod(self.shape[1:]))
        p1 = P if p1 is None else p1
        f1 = F if f1 is None else f1
        ap = self.h[p0:p1, f0:f1]
        if fn is not None:
            ap = fn(ap)
        if self.space == 'psum':
            reg = (0, 128, 0, 1 << 30)
        else:
            reg = (p0, p1, f0, f1)
        return V(self, ap, reg)


class V:
    def __init__(self, t, ap, reg):
        self.t, self.ap, self.reg = t, ap, reg

    def f(self, fn):
        return V(self.t, fn(self.ap), self.reg)


def _ov(a, b):
    return a[0] < b[1] and b[0] < a[1] and a[2] < b[3] and b[2] < a[3]


def _cov(a, b):
    return a[0] <= b[0] and a[1] >= b[1] and a[2] <= b[2] and a[3] >= b[3]


class Sched:
    def __init__(self, nc):
        self.nc = nc
        self.ops = {e: [] for e in ENGS}
        self.cnt = {e: 0 for e in ENGS}
        self.waited = {e: {} for e in ENGS}
        self.dma_n = {e: 0 for e in ENGS}
        self.dma_last = {}
        self.ctx = []
        self.rr = 0
        self.epoch = {}
        self.final = {}
        self.dma_ep = {}
        self.prep_n = 0

    def mark(self):
        return len(self.ctx)

    def barrier(self):
        deps = dict(self.final)
        deps.update({k: v for k, v in self.dma_last.items() if k[2] < 100})
        for e in ENGS:
            w = self._waits(e, dict(deps))
            if w:
                self.ops[e].append((w, None, None, 0))

    def release(self, mark):
        self.barrier()
        while len(self.ctx) > mark:
            self.ctx.pop().__exit__(None, None, None)

    def sb(self, name, shape, dt):
        self.uid = getattr(self, 'uid', 0) + 1
        cm = self.nc.sbuf_tensor('sb%d_%s' % (self.uid, name), list(shape), dt)
        h = cm.__enter__()
        self.ctx.append(cm)
        return T(name, h, shape, 'sbuf')

    def ps(self, name, shape=(128, 512), dt=F32):
        cm = self.nc.psum_tensor('pp_' + name, list(shape), dt)
        h = cm.__enter__()
        self.ctx.append(cm)
        return T(name, h, shape, 'psum')

    def dram(self, name, shape, dt, kind):
        h = self.nc.dram_tensor(name, list(shape), dt, kind=kind)
        t = T(name, h.ap(), shape, 'dram')
        t.track = (kind == 'Internal')
        return t

    def _deps(self, reads, writes):
        deps = {}
        reads = [v for v in reads if v.t.track]
        writes = [v for v in writes if v.t.track]
        for vw in reads:
            for r in vw.t.recs:
                if r[0] and _ov(r[1:5], vw.reg) and deps.get(r[5], 0) < r[6]:
                    deps[r[5]] = r[6]
        for vw in writes:
            for r in vw.t.recs:
                if _ov(r[1:5], vw.reg) and deps.get(r[5], 0) < r[6]:
                    deps[r[5]] = r[6]
        return deps

    def _record(self, reads, writes, semkey, val):
        reads = [v for v in reads if v.t.track]
        writes = [v for v in writes if v.t.track]
        for vw in writes:
            t = vw.t
            reg = tuple(vw.reg)
            t.recs = [r for r in t.recs if not _cov(reg, r[1:5])]
            t.recs.append((True,) + reg + (semkey, val))
        for vw in reads:
            t = vw.t
            reg = tuple(vw.reg)
            t.recs = [r for r in t.recs if r[0] or r[5] != semkey or r[1:5] != reg]
            t.recs.append((False,) + reg + (semkey, val))

    def _waits(self, eng, deps):
        w = []
        for k, v in deps.items():
            if k[0] == 'c' and k[1] == eng and (eng == 'pe' or not SAME_ENGINE_SYNC):
                continue
            if self.waited[eng].get(k, 0) >= v:
                continue
            self.waited[eng][k] = v
            w.append((k, v))
        return w

    def op(self, eng, fn, reads=(), writes=()):
        writes = list(writes) + [v for v in reads if v.t.space == 'psum']
        reads = [v for v in reads if v.t.space != 'psum']
        deps = self._deps(reads, writes)
        waits = self._waits(eng, deps)
        if self.cnt[eng] >= SEM_EPOCH:
            self.epoch[eng] = self.epoch.get(eng, 0) + 1
            self.cnt[eng] = 0
        self.cnt[eng] += 1
        key = ('c', eng, self.epoch.get(eng, 0))
        self.final[key] = self.cnt[eng]
        self._record(reads, writes, key, self.cnt[eng])
        self.ops[eng].append((waits, fn, key, 1))

    def dma(self, eng, out, in_, prep=False, **kw):
        deps = self._deps([in_], [out])
        if prep:
            j = self.prep_n
            self.prep_n += 1
            slot = 100 + j % DMA_POOL
        else:
            j = self.dma_n[eng]
            self.dma_n[eng] += 1
            slot = j % DMA_POOL
        if self.dma_last.get(('d', eng, slot, self.dma_ep.get((eng, slot), 0)), 0) >= SEM_EPOCH:
            self.dma_ep[(eng, slot)] = self.dma_ep.get((eng, slot), 0) + 1
        key = ('d', eng, slot, self.dma_ep.get((eng, slot), 0))
        prev = self.dma_last.get(key, 0)
        if not prev and self.dma_ep.get((eng, slot), 0) > 0:
            pk = ('d', eng, slot, self.dma_ep[(eng, slot)] - 1)
            deps[pk] = max(deps.get(pk, 0), self.dma_last[pk])
        if prev:
            deps[key] = max(deps.get(key, 0), prev)
        waits = self._waits(eng, deps)
        val = prev + 16
        self.dma_last[key] = val
        self._record([in_], [out], key, val)
        oa, ia = out.ap, in_.ap
        self.ops[eng].append((waits, lambda e: e.dma_start(out=oa, in_=ia, **kw), key, 16))

    def finish(self):
        waits = self._waits('sp', dict(self.dma_last))
        self.ops['sp'].append((waits, None, None, 0))
        nc = self.nc
        semkeys = list(self.final.keys()) + list(self.dma_last.keys())
        sems = {}
        cms = []
        for k in semkeys:
            cm = nc.semaphore('s_' + '_'.join(str(x) for x in k))
            sems[k] = cm.__enter__()
            cms.append(cm)
        with nc.Block() as block:
            deco = dict(pe=block.tensor, act=block.scalar, pool=block.gpsimd, dve=block.vector, sp=block.sync)
            for e in ENGS:
                ops = self.ops[e]

                def body(engine, ops=ops):
                    for waits, fn, key, inc in ops:
                        for k, v in waits:
                            engine.wait_ge(sems[k], v)
                        if fn is not None:
                            fn(engine).then_inc(sems[key], inc)
                if ops:
                    deco[e](body)
        for cm in reversed(cms):
            cm.__exit__(None, None, None)
        for cm in reversed(self.ctx):
            cm.__exit__(None, None, None)

    def mm(self, out, lhsT, rhs, start=True, stop=True):
        self.op('pe', lambda e: e.matmul(out.ap, lhsT.ap, rhs.ap, start=start, stop=stop),
                reads=[lhsT, rhs], writes=[out])

    def tr(self, out, in_, ident):
        self.op('pe', lambda e: e.transpose(out.ap, in_.ap, ident.ap), reads=[in_, ident], writes=[out])

    def act(self, out, in_, func, bias=None, scale=None, eng='act'):
        kw = {}
        rd = [in_]
        if bias is not None:
            if isinstance(bias, V):
                kw['bias'] = bias.ap
                rd.append(bias)
            else:
                kw['bias'] = bias
        if scale is not None:
            if isinstance(scale, V):
                kw['scale'] = scale.ap
                rd.append(scale)
            else:
                kw['scale'] = scale
        self.op('act', lambda e: e.activation(out=out.ap, in_=in_.ap, func=func, **kw), reads=rd, writes=[out])

    def tt(self, out, in0, in1, op, eng='dve'):
        self.op(eng, lambda e: e.tensor_tensor(out=out.ap, in0=in0.ap, in1=in1.ap, op=op), reads=[in0, in1], writes=[out])

    def ts(self, out, in0, s1, op0, s2=None, op1=None, eng='dve'):
        rd = [in0]
        a1 = s1
        a2 = s2
        if isinstance(s1, V):
            rd.append(s1)
            a1 = s1.ap
        if isinstance(s2, V):
            rd.append(s2)
            a2 = s2.ap
        if op1 is None:
            self.op(eng, lambda e: e.tensor_scalar(out=out.ap, in0=in0.ap, scalar1=a1, scalar2=None, op0=op0), reads=rd, writes=[out])
        else:
            self.op(eng, lambda e: e.tensor_scalar(out=out.ap, in0=in0.ap, scalar1=a1, scalar2=a2, op0=op0, op1=op1), reads=rd, writes=[out])

    def stt(self, out, in0, sc, in1, op0, op1):
        rd = [in0, in1]
        a = sc
        if isinstance(sc, V):
            rd.append(sc)
            a = sc.ap
        self.op('dve', lambda e: e.scalar_tensor_tensor(out=out.ap, in0=in0.ap, scalar=a, in1=in1.ap, op0=op0, op1=op1),
                reads=rd, writes=[out])

    def copy(self, out, in_, eng='dve'):
        if eng == 'act':
            self.op('act', lambda e: e.activation(out=out.ap, in_=in_.ap, func=AF.Identity), reads=[in_], writes=[out])
        else:
            self.op(eng, lambda e: e.tensor_copy(out=out.ap, in_=in_.ap), reads=[in_], writes=[out])

    def evac_eng(self):
        self.rr += 1
        return 'act' if self.rr % 2 else 'dve'


class Ctx:
    pass


OFF_Q, OFF_K, OFF_V, OFF_O, OFF_G16, OFF_F, OFF_S5, OFF_GATE = 0, 512, 1024, 1536, 2048, 2064, 2320, 2576


def common_setup(s, C, ident_d):
    C.ident = s.sb('ident', [128, 128], F32)
    s.dma('sp', C.ident.v(), ident_d.v())
    C.identb = s.sb('identb', [128, 128], BF16)
    s.copy(C.identb.v(), C.ident.v())
    C.ones = s.sb('ones', [128, 128], BF16)
    s.op('dve', lambda e: e.memset(C.ones.h[:, :], 1.0), writes=[C.ones.v()])
    C.ps = [s.ps('ps%d' % i) for i in range(8)]
    C.psi = 0


def next_ps(C):
    C.psi = (C.psi + 1) % 8
    return C.ps[C.psi]


def load_xT(s, C, x_d, ntok):
    C.xT = [s.sb('xT%d' % k, [128, ntok], F32) for k in range(8)]
    stage = [s.sb('xst%d' % i, [128, 1024], F32) for i in range(8)]
    for tg in range(ntok // 512):
        for i in range(4):
            tt = tg * 4 + i
            s.dma('sp' if i % 2 == 0 else 'act', stage[(tg % 2) * 4 + i].v(), x_d.v(128 * tt, 128 * tt + 128))
        for kt in range(8):
            ps = next_ps(C)
            for i in range(4):
                s.tr(ps.v(0, 128, 128 * i, 128 * i + 128), stage[(tg % 2) * 4 + i].v(0, 128, 128 * kt, 128 * kt + 128), C.ident.v())
            s.copy(C.xT[kt].v(0, 128, 512 * tg, 512 * tg + 512), ps.v(), eng=s.evac_eng())


def rms_rstd(s, C, src, ntok, name):
    rstd = s.sb('rstd_' + name, [128, ntok], F32)
    sq = [s.sb('sq_%s%d' % (name, i), [128, 512], BF16) for i in range(2)]
    n = 0
    for tc in range(ntok // 512):
        ps = next_ps(C)
        for kt in range(8):
            q = sq[n % 2]
            n += 1
            s.act(q.v(), src[kt].v(0, 128, 512 * tc, 512 * tc + 512), AF.Square)
            s.mm(ps.v(), C.ones.v(), q.v(), start=(kt == 0), stop=(kt == 7))
        r = rstd.v(0, 128, 512 * tc, 512 * tc + 512)
        s.act(r, ps.v(), AF.Sqrt, bias=C.epsb.v(), scale=1.0 / D)
        s.op('dve', lambda e, r=r: e.reciprocal(out=r.ap, in_=r.ap), reads=[r], writes=[r])
    return rstd


def load_vec_cols(s, C, name, d_t, n):
    t = s.sb(name, [128, n], F32)
    for k in range(n):
        s.dma('sp', t.v(0, 128, k, k + 1), d_t.v(128 * k, 128 * k + 128))
    return t


def make_h(s, C, gcols, rstd, ntok, name):
    hT = [s.sb('%s%d' % (name, k), [128, ntok], BF16) for k in range(8)]
    for kt in range(8):
        s.stt(hT[kt].v(), C.xT[kt].v(), gcols.v(0, 128, kt, kt + 1), rstd.v(), ALU.mult, ALU.mult)
    return hT


class PW:
    def __init__(self, s, name, w_d, K, specs):
        self.w_d, self.K, self.specs = w_d, K, specs
        width = K * max(n for _, n in specs)
        self.t = s.dram(name, [len(specs) * 128, width], BF16, 'Internal')
        self.idx = {c0: i for i, (c0, n) in enumerate(specs)}

    def prep(self, s, i):
        c0, ncols = self.specs[i]
        K = self.K
        out = self.t.v(i * 128, i * 128 + 128, 0, K * ncols, fn=lambda a: a.rearrange('p (k n) -> p k n', k=K))
        in_ = self.w_d.v(0, K * 128, c0, c0 + ncols, fn=lambda a: a.rearrange('(k p) n -> p k n', p=128))
        s.dma('pool', out, in_, prep=True)


def load_w_cols(s, C, wt, w_d, K, c0, ncols, q='pool'):
    if isinstance(w_d, PW):
        i = w_d.idx[c0]
        s.dma('sp', wt.v(0, 128, 0, K * ncols), w_d.t.v(i * 128, i * 128 + 128, 0, K * ncols))
        return
    out = wt.v(0, 128, 0, K * ncols, fn=lambda a: a.rearrange('p (k n) -> p k n', k=K))
    in_ = w_d.v(0, K * 128, c0, c0 + ncols, fn=lambda a: a.rearrange('(k p) n -> p k n', p=128))
    s.dma(q, out, in_)


def build_T1():
    nc = bass.Bass("TRN2", target_bir_lowering=False)
    s = Sched(nc)
    C = Ctx()
    x_d = s.dram('x', [TL, D], F32, 'ExternalInput')
    w_d = s.dram('w_in', [D, N_IN], F32, 'ExternalInput')
    b_d = s.dram('b_in', [N_IN, 1], F32, 'ExternalInput')
    g_d = s.dram('g_pre', [D, 1], F32, 'ExternalInput')
    id_d = s.dram('ident', [128, 128], F32, 'ExternalInput')
    xf_d = s.dram('xf', [4 * XF_ROWS, TL], F32, 'ExternalOutput')
    common_setup(s, C, id_d)
    C.epsb = s.sb('epsb', [128, 1], F32)
    s.op('dve', lambda e: e.memset(C.epsb.h[:, :], EPS), writes=[C.epsb.v()])
    load_xT(s, C, x_d, TL)
    gcols = load_vec_cols(s, C, 'gpre', g_d, 8)
    rstd = rms_rstd(s, C, C.xT, TL, 'pre')
    hT = make_h(s, C, gcols, rstd, TL, 'hT')
    phase_T1_proj(s, C, hT, w_d, b_d, xf_d)
    s.finish()
    return nc


def phase_T1_proj(s, C, hT, w_d, b_d, xf_d):
    tiles = []
    for i in range(4):
        tiles.append((OFF_Q + 128 * i, 128, [(0, 128, i, 0)]))
        tiles.append((OFF_K + 128 * i, 128, [(0, 128, i, 128)]))
        tiles.append((OFF_V + 128 * i, 128, [(0, 128, i, 256)]))
    tiles.append((OFF_G16, 16, [(0, 16, None, 384)]))
    for i in range(2):
        tiles.append((OFF_F + 128 * i, 128, [(0, 64, 2 * i, 388), (64, 64, 2 * i + 1, 388)]))
        tiles.append((OFF_S5 + 128 * i, 128, [(0, 64, 2 * i, 452), (64, 64, 2 * i + 1, 452)]))
    wts = [s.sb('wT1_%d' % i, [128, 8 * 128], BF16) for i in range(6)]
    bts = [s.sb('bT1_%d' % i, [128, 1], F32) for i in range(6)]
    stg = [s.sb('zst%d' % i, [128, 512], F32) for i in range(8)]
    n = 0
    for ti, (c0, ncols, dsts) in enumerate(tiles):
        wt = wts[ti % 6]
        bt = bts[ti % 6]
        load_w_cols(s, C, wt, w_d, 8, c0, ncols)
        s.dma('sp', bt.v(0, ncols), b_d.v(c0, c0 + ncols))
        for tc in range(TL // 512):
            ps = next_ps(C)
            for kt in range(8):
                s.mm(ps.v(0, ncols), wt.v(0, 128, kt * ncols, (kt + 1) * ncols), hT[kt].v(0, 128, 512 * tc, 512 * tc + 512),
                     start=(kt == 0), stop=(kt == 7))
            st = stg[n % 8]
            n += 1
            s.act(st.v(0, ncols), ps.v(0, ncols), AF.Identity, bias=bt.v(0, ncols))
            for (r0, nr, dst, dr0) in dsts:
                if dst is None:
                    for dd in range(4):
                        o = xf_d.v(dd * XF_ROWS + dr0, dd * XF_ROWS + dr0 + 4, 512 * tc, 512 * tc + 512)
                        s.dma('sp' if dd % 2 else 'act', o, st.v(0, 16, fn=lambda a, dd=dd: a[dd::4, :]))
                else:
                    o = xf_d.v(dst * XF_ROWS + dr0, dst * XF_ROWS + dr0 + nr, 512 * tc, 512 * tc + 512)
                    s.dma('sp' if n % 2 else 'act', o, st.v(r0, r0 + nr))


class WPool:
    def __init__(self, s, name, ncols, n):
        self.b = [s.sb('%s%d' % (name, i), [128, ncols], BF16) for i in range(n)]
        self.i = 0

    def get(self):
        self.i += 1
        return self.b[self.i % len(self.b)]


def bias_cols(s, name, d_t, offs, n=128):
    t = s.sb(name, [128, len(offs)], F32)
    for k, o in enumerate(offs):
        s.dma('sp' if k % 2 else 'act', t.v(0, n, k, k + 1), d_t.v(o, o + n))
    return t


def store_xT(s, C, out_d, tok0, ntok, stage):
    for tt in range(ntok // 128):
        st = stage[tt % 2]
        for kg in range(2):
            ps = next_ps(C)
            for i in range(4):
                kt = kg * 4 + i
                s.tr(ps.v(0, 128, 128 * i, 128 * i + 128), C.xT[kt].v(0, 128, 128 * tt, 128 * tt + 128), C.ident.v())
            s.copy(st.v(0, 128, 512 * kg, 512 * kg + 512), ps.v(), eng=s.evac_eng())
        s.dma('sp' if tt % 2 else 'act', out_d.v(tok0 + 128 * tt, tok0 + 128 * tt + 128), st.v())


def t2_setup(s, C, W):
    NT = C.NT
    C.wp = WPool(s, 'wp', 1024, getattr(C, 'n_wp', 8))
    C.wbig = WPool(s, 'wbig', 4096, getattr(C, 'n_wbig', 2))
    C.big = s.sb('big', [128, 32 * NT], BF16)
    C.h2T = s.sb('h2T', [128, 8 * NT], BF16)
    C.tmpf = [s.sb('tmpf%d' % i, [128, 512], F32) for i in range(6)]
    C.tmpi = 0
    C.hnst = [s.sb('hnst%d' % i, [128, NT], F32) for i in range(2)]
    C.b_o = bias_cols(s, 'b_o', W['b_in'], [OFF_O + 128 * i for i in range(4)])
    C.b_g = bias_cols(s, 'b_g', W['b_in'], [OFF_GATE + 128 * i for i in range(24)])
    C.b_glu = bias_cols(s, 'b_glu', W['b_glu'], [128 * i for i in range(16)])
    C.normg = bias_cols(s, 'normg', W['norm_g'], [128 * i for i in range(4)])
    C.g_post = bias_cols(s, 'g_post', W['g_post'], [128 * i for i in range(8)])
    C.g_f1 = bias_cols(s, 'g_f1', W['g_ffn_pre'], [128 * i for i in range(8)])
    C.g_f2 = bias_cols(s, 'g_f2', W['g_ffn_post'], [128 * i for i in range(8)])


def tmpf(C):
    C.tmpi += 1
    return C.tmpf[C.tmpi % len(C.tmpf)]


def ps6(C):
    C.psi = (C.psi + 1) % 6
    return C.ps[C.psi]


def big_view(C, tile, c0, c1):
    NT = C.NT
    return C.big.v(0, 128, tile * NT + c0, tile * NT + c1)


def phase_T2(s, C, W, xb_d, tok0):
    NT = C.NT
    NTC = NT // 512
    hT = C.hT
    MIX0, HG0, YF0, YS0, OB0 = 8, 16, 20, 22, 24

    def chunk(tc):
        return 512 * tc, 512 * tc + 512

    for (row0, dst0) in ((128, YF0), (192, YS0)):
        for half in range(2):
            st = C.hnst[half]
            for q in range(2):
                src = 2 * half + q
                s.dma('sp' if q else 'act', st.v(64 * q, 64 * q + 64),
                      xb_d.v(src * XB_ROWS + row0, src * XB_ROWS + row0 + 64, tok0, tok0 + NT))
            s.copy(big_view(C, dst0 + half, 0, NT), st.v(), eng='pool')
    for i in range(4):
        st = C.hnst[i % 2]
        s.dma('sp', st.v(), xb_d.v(i * XB_ROWS, i * XB_ROWS + 128, tok0, tok0 + NT))
        wt = C.wp.get()
        load_w_cols(s, C, wt, W['w_in'], 8, OFF_O + 128 * i, 128)
        for tc in range(NTC):
            c0, c1 = chunk(tc)
            ps = ps6(C)
            for kt in range(8):
                s.mm(ps.v(), wt.v(0, 128, 128 * kt, 128 * kt + 128), hT[kt].v(0, 128, c0, c1), start=(kt == 0), stop=(kt == 7))
            sg = tmpf(C)
            s.act(sg.v(), ps.v(), AF.Sigmoid, bias=C.b_o.v(0, 128, i, i + 1))
            s.stt(big_view(C, HG0 + i, c0, c1), st.v(0, 128, c0, c1), C.normg.v(0, 128, i, i + 1), sg.v(), ALU.mult, ALU.mult)
    for j in range(8):
        w_um = C.wp.get(); load_w_cols(s, C, w_um, W['w_up_m'], 4, 128 * j, 128)
        w_uf = C.wp.get(); load_w_cols(s, C, w_uf, W['w_up_f'], 2, 128 * j, 128)
        w_ga = C.wp.get(); load_w_cols(s, C, w_ga, W['w_glu'], 2, 128 * j, 128)
        w_gb = C.wp.get(); load_w_cols(s, C, w_gb, W['w_glu'], 2, 1024 + 128 * j, 128)
        w_g = []
        for b in range(3):
            w = C.wp.get(); load_w_cols(s, C, w, W['w_in'], 8, OFF_GATE + 1024 * b + 128 * j, 128)
            w_g.append(w)
        for tc in range(NTC):
            c0, c1 = chunk(tc)

            def gate(b):
                ps = ps6(C)
                for kt in range(8):
                    s.mm(ps.v(), w_g[b].v(0, 128, 128 * kt, 128 * kt + 128), hT[kt].v(0, 128, c0, c1), start=(kt == 0), stop=(kt == 7))
                g = tmpf(C)
                s.act(g.v(), ps.v(), AF.Sigmoid, bias=C.b_g.v(0, 128, 8 * b + j, 8 * b + j + 1))
                return g

            def small(wt, K, src0):
                ps = ps6(C)
                for kt in range(K):
                    s.mm(ps.v(), wt.v(0, 128, 128 * kt, 128 * kt + 128), big_view(C, src0 + kt, c0, c1), start=(kt == 0), stop=(kt == K - 1))
                return ps
            ps_ym = small(w_um, 4, HG0)
            g0 = gate(0)
            acc = tmpf(C)
            s.tt(acc.v(), g0.v(), ps_ym.v(), ALU.mult)
            ps_yf = small(w_uf, 2, YF0)
            g1 = gate(1)
            t1 = tmpf(C)
            s.tt(t1.v(), g1.v(), ps_yf.v(), ALU.mult)
            s.tt(acc.v(), acc.v(), t1.v(), ALU.add, eng='pool')
            ps_za = small(w_ga, 2, YS0)
            ps_zb = small(w_gb, 2, YS0)
            sb_ = tmpf(C)
            s.act(sb_.v(), ps_zb.v(), AF.Sigmoid, bias=C.b_glu.v(0, 128, 8 + j, 8 + j + 1))
            ys = tmpf(C)
            s.stt(ys.v(), ps_za.v(), C.b_glu.v(0, 128, j, j + 1), sb_.v(), ALU.add, ALU.mult)
            g2 = gate(2)
            s.tt(ys.v(), ys.v(), g2.v(), ALU.mult, eng='pool')
            s.tt(big_view(C, MIX0 + j, c0, c1), acc.v(), ys.v(), ALU.add)
    ss = [C.ps[6], C.ps[7]]
    sqb = [s_ for s_ in C.sqb]

    def proj_norm_add(wname, K, src_view, dst0_tile, dst_T, gcols, wpool, kcols):
        n = 0
        for j in range(8):
            wt = wpool.get()
            load_w_cols(s, C, wt, W[wname], K, 128 * j, 128)
            for tc in range(NTC):
                c0, c1 = chunk(tc)
                ps = ps6(C)
                for kt in range(K):
                    s.mm(ps.v(), wt.v(0, 128, 128 * kt, 128 * kt + 128), src_view(kt, c0, c1), start=(kt == 0), stop=(kt == K - 1))
                if dst_T is None:
                    ov = big_view(C, dst0_tile + j, c0, c1)
                else:
                    ov = dst_T.v(0, 128, j * NT + c0, j * NT + c1)
                s.copy(ov, ps.v(), eng='dve')
                q = sqb[n % 2]
                n += 1
                s.act(q.v(), ps.v(), AF.Square)
                s.mm(ss[tc].v(), C.ones.v(), q.v(), start=(j == 0), stop=(j == 7))
        for tc in range(NTC):
            c0, c1 = chunk(tc)
            r = C.rstd2.v(0, 128, c0, c1)
            s.act(r, ss[tc].v(), AF.Sqrt, bias=C.epsb.v(), scale=1.0 / D)
            s.op('dve', lambda e, r=r: e.reciprocal(out=r.ap, in_=r.ap), reads=[r], writes=[r])
        for j in range(8):
            for tc in range(NTC):
                c0, c1 = chunk(tc)
                if dst_T is None:
                    ov = big_view(C, dst0_tile + j, c0, c1)
                else:
                    ov = dst_T.v(0, 128, j * NT + c0, j * NT + c1)
                t = tmpf(C)
                s.stt(t.v(), ov, gcols.v(0, 128, j, j + 1), C.rstd2.v(0, 128, c0, c1), ALU.mult, ALU.mult)
                xv = C.xT[j].v(0, 128, C.xoff + c0, C.xoff + c1)
                s.tt(xv, xv, t.v(), ALU.add, eng='pool')

    proj_norm_add('w_out', 8, lambda kt, c0, c1: big_view(C, MIX0 + kt, c0, c1), OB0, None, C.g_post, C.wp, 128)
    for tc in range(NTC):
        c0, c1 = chunk(tc)
        ps = ps6(C)
        for kt in range(8):
            q = sqb[kt % 2]
            s.act(q.v(), C.xT[kt].v(0, 128, C.xoff + c0, C.xoff + c1), AF.Square)
            s.mm(ps.v(), C.ones.v(), q.v(), start=(kt == 0), stop=(kt == 7))
        r = C.rstd2.v(0, 128, c0, c1)
        s.act(r, ps.v(), AF.Sqrt, bias=C.epsb.v(), scale=1.0 / D)
        s.op('dve', lambda e, r=r: e.reciprocal(out=r.ap, in_=r.ap), reads=[r], writes=[r])
    for kt in range(8):
        s.stt(C.h2T.v(0, 128, kt * NT, kt * NT + NT), C.xT[kt].v(0, 128, C.xoff, C.xoff + NT), C.g_f1.v(0, 128, kt, kt + 1),
              C.rstd2.v(0, 128, 0, NT), ALU.mult, ALU.mult)
    for ng in range(8):
        wt = C.wbig.get()
        load_w_cols(s, C, wt, W['w_ffn1'], 8, 512 * ng, 512)
        for nn in range(4):
            n = 4 * ng + nn
            for tc in range(NTC):
                c0, c1 = chunk(tc)
                ps = ps6(C)
                for kt in range(8):
                    s.mm(ps.v(), wt.v(0, 128, 512 * kt + 128 * nn, 512 * kt + 128 * nn + 128), C.h2T.v(0, 128, kt * NT + c0, kt * NT + c1),
                         start=(kt == 0), stop=(kt == 7))
                t = tmpf(C)
                s.act(t.v(), ps.v(), AF.Relu)
                s.tt(big_view(C, n, c0, c1), t.v(), t.v(), ALU.mult, eng='pool' if (n + tc) % 2 else 'dve')
    proj_norm_add('w_ffn2', 32, lambda kt, c0, c1: big_view(C, kt, c0, c1), None, C.h2T, C.g_f2, C.wbig, 128)


def build_T2(NT=1024):
    nc = bass.Bass("TRN2", target_bir_lowering=False)
    s = Sched(nc)
    C = Ctx()
    C.NT = NT
    x_d = s.dram('x', [TL, D], F32, 'ExternalInput')
    xb_d = s.dram('xb', [4 * XB_ROWS, TL], F32, 'ExternalInput')
    W = {}
    for name, shape in [('w_in', [D, N_IN]), ('b_in', [N_IN, 1]), ('g_pre', [D, 1]), ('norm_g', [512, 1]), ('w_up_m', [512, D]),
                        ('w_up_f', [256, D]), ('w_glu', [256, 2 * D]), ('b_glu', [2 * D, 1]), ('w_out', [D, D]), ('g_post', [D, 1]),
                        ('g_ffn_pre', [D, 1]), ('g_ffn_post', [D, 1]), ('w_ffn1', [D, 4 * D]), ('w_ffn2', [4 * D, D])]:
        W[name] = s.dram(name, shape, F32, 'ExternalInput')
    id_d = s.dram('ident', [128, 128], F32, 'ExternalInput')
    out_d = s.dram('x_out', [TL, D], F32, 'ExternalOutput')
    common_setup(s, C, id_d)
    C.epsb = s.sb('epsb', [128, 1], F32)
    s.op('dve', lambda e: e.memset(C.epsb.h[:, :], EPS), writes=[C.epsb.v()])
    C.sqb = [s.sb('sqb%d' % i, [128, 512], BF16) for i in range(2)]
    C.rstd2 = s.sb('rstd2', [128, NT], F32)
    gcols = load_vec_cols(s, C, 'gpre', W['g_pre'], 8)
    t2_setup(s, C, W)
    C.xT = [s.sb('xT%d' % k, [128, NT], F32) for k in range(8)]
    C.hT = [s.sb('hT%d' % k, [128, NT], BF16) for k in range(8)]
    C.xoff = 0
    xstage = [s.sb('xst%d' % i, [128, 1024], F32) for i in range(4)]
    for hf in range(TL // NT):
        tok0 = hf * NT
        for tg in range(NT // 512):
            for i in range(4):
                tt = tg * 4 + i
                s.dma('sp' if i % 2 == 0 else 'act', xstage[i].v(), x_d.v(tok0 + 128 * tt, tok0 + 128 * tt + 128))
            for kt in range(8):
                ps = ps6(C)
                for i in range(4):
                    s.tr(ps.v(0, 128, 128 * i, 128 * i + 128), xstage[i].v(0, 128, 128 * kt, 128 * kt + 128), C.ident.v())
                s.copy(C.xT[kt].v(0, 128, 512 * tg, 512 * tg + 512), ps.v(), eng=s.evac_eng())
        for tc in range(NT // 512):
            ps = ps6(C)
            for kt in range(8):
                q = C.sqb[kt % 2]
                s.act(q.v(), C.xT[kt].v(0, 128, 512 * tc, 512 * tc + 512), AF.Square)
                s.mm(ps.v(), C.ones.v(), q.v(), start=(kt == 0), stop=(kt == 7))
            r = C.rstd2.v(0, 128, 512 * tc, 512 * tc + 512)
            s.act(r, ps.v(), AF.Sqrt, bias=C.epsb.v(), scale=1.0 / D)
            s.op('dve', lambda e, r=r: e.reciprocal(out=r.ap, in_=r.ap), reads=[r], writes=[r])
        for kt in range(8):
            s.stt(C.hT[kt].v(), C.xT[kt].v(), gcols.v(0, 128, kt, kt + 1), C.rstd2.v(), ALU.mult, ALU.mult)
        phase_T2(s, C, W, xb_d, tok0)
        store_xT(s, C, out_d, tok0, NT, xstage)
    s.finish()
    return nc


def fourier_tables():
    c = np.arange(64)
    a64 = 2 * np.pi * np.outer(c, c) / 64.0
    s1 = np.arange(128)
    a128 = 2 * np.pi * np.outer(s1, s1) / 128.0
    atw = 2 * np.pi * np.outer(s1, c) / 8192.0
    sc = 1.0 / np.sqrt(8192.0 * 64.0)
    z = np.zeros((64, 64))
    tabs = dict(
        f_f64=np.concatenate([np.cos(a64), -np.sin(a64)], 1),
        f_c128=np.cos(a128), f_s128=np.sin(a128), f_ns128=-np.sin(a128),
        f_tw=np.concatenate([np.cos(atw), -np.sin(atw)], 1),
        f_bdc=np.block([[np.cos(a64), z], [z, np.cos(a64)]]) * sc,
        f_bds=np.block([[np.sin(a64), z], [z, np.sin(a64)]]) * sc,
    )
    return {k: np.ascontiguousarray(v.astype(np.float32)) for k, v in tabs.items()}


FT_SHAPES = dict(f_f64=[64, 128], f_c128=[128, 128], f_s128=[128, 128], f_ns128=[128, 128], f_tw=[128, 128],
                 f_bdc=[128, 128], f_bds=[128, 128])


def phase_U_fourier(s, C, xf_d, xb_d, tabs_d):
    tb = {}
    for k in ['f_f64', 'f_c128', 'f_s128', 'f_ns128', 'f_bdc', 'f_bds']:
        tb[k] = s.sb(k, FT_SHAPES[k], BF16)
        s.dma('pool', tb[k].v(), tabs_d[k].v())
    tw = s.sb('f_tw', [128, 128], F32)
    s.dma('sp', tw.v(), tabs_d['f_tw'].v())
    UTb = s.sb('f_UTb', [64, S], BF16)
    for src in range(4):
        s.dma('pool', UTb.v(0, 64, 2048 * src, 2048 * src + 2048), xf_d.v(src * XF_ROWS + 388, src * XF_ROWS + 452))
    Zre = s.sb('f_Zre', [128, 4096], BF16)
    Zim = s.sb('f_Zim', [128, 4096], BF16)
    for g in range(16):
        ps = next_ps(C)
        for i in range(4):
            s2 = 4 * g + i
            lhsT = V(UTb, UTb.h[0:64, s2::64], (0, 64, 0, S))
            s.mm(ps.v(0, 128, 128 * i, 128 * i + 128), lhsT, tb['f_f64'].v())
        pv = lambda lo: ps.v(fn=lambda a: a.rearrange('p (s c) -> p s c', c=128)[:, :, lo:lo + 64])
        zv = lambda Zt: Zt.v(0, 128, 256 * g, 256 * g + 256, fn=lambda a: a.rearrange('p (s c) -> p s c', c=64))
        s.copy(zv(Zre), pv(0), eng='act')
        s.copy(zv(Zim), pv(64), eng='dve')
    ArP = s.sb('f_ArP', [128, 4096], BF16)
    AiP = s.sb('f_AiP', [128, 4096], BF16)
    tmp = [s.sb('f_tmp%d' % i, [128, 512], F32) for i in range(4)]
    for ch in range(8):
        c0, c1 = 512 * ch, 512 * ch + 512
        pr = next_ps(C)
        s.mm(pr.v(), tb['f_c128'].v(), Zre.v(0, 128, c0, c1), start=True, stop=False)
        s.mm(pr.v(), tb['f_s128'].v(), Zim.v(0, 128, c0, c1), start=False, stop=True)
        pi = next_ps(C)
        s.mm(pi.v(), tb['f_c128'].v(), Zim.v(0, 128, c0, c1), start=True, stop=False)
        s.mm(pi.v(), tb['f_ns128'].v(), Zre.v(0, 128, c0, c1), start=False, stop=True)
        r3 = lambda a: a.rearrange('p (s c) -> p s c', c=64)
        tre = tw.v(0, 128, 8 * ch, 8 * ch + 8, fn=lambda a: a.unsqueeze(2).to_broadcast([128, 8, 64]))
        tim = tw.v(0, 128, 64 + 8 * ch, 64 + 8 * ch + 8, fn=lambda a: a.unsqueeze(2).to_broadcast([128, 8, 64]))
        t = [x.v(fn=r3) for x in tmp]
        s.tt(t[0], pr.v(fn=r3), tre, ALU.mult)
        s.tt(t[1], pi.v(fn=r3), tim, ALU.mult)
        s.tt(t[2], pr.v(fn=r3), tim, ALU.mult)
        s.tt(t[3], pi.v(fn=r3), tre, ALU.mult)
        perm = lambda a: a.rearrange('p (c s) -> p s c', s=64)[:, 8 * ch:8 * ch + 8, :]
        s.tt(ArP.v(fn=perm), t[0], t[1], ALU.subtract, eng='pool')
        s.tt(AiP.v(fn=perm), t[2], t[3], ALU.add, eng='pool')
    ArT = s.sb('f_ArT', [128, 4096], BF16)
    AiT = s.sb('f_AiT', [128, 4096], BF16)
    for (src, dst) in ((ArP, ArT), (AiP, AiT)):
        for g in range(8):
            ps = next_ps(C)
            pb = lambda lo, hi: ps.v(fn=lambda a: a.bitcast(BF16)[:, lo:hi])
            for i in range(4):
                blk = 4 * g + i
                s.tr(pb(128 * i, 128 * i + 128), src.v(0, 128, 128 * blk, 128 * blk + 128), C.identb.v())
            s.copy(dst.v(0, 128, 512 * g, 512 * g + 512), pb(0, 512), eng=s.evac_eng())
    Y = s.sb('f_Y', [128, 4096], F32)
    for g in range(8):
        c0, c1 = 512 * g, 512 * g + 512
        ps = next_ps(C)
        s.mm(ps.v(), tb['f_bdc'].v(), ArT.v(0, 128, c0, c1), start=True, stop=False)
        s.mm(ps.v(), tb['f_bds'].v(), AiT.v(0, 128, c0, c1), start=False, stop=True)
        s.copy(Y.v(0, 128, c0, c1), ps.v(), eng=s.evac_eng())
    n = 0
    for cp in range(2):
        for dst in range(4):
            r0 = dst * XB_ROWS + 128 + cp
            o = xb_d.v(r0, r0 + 64, 0, TL, fn=lambda a: a[::2, :].rearrange('b (s j) -> s b j', j=128))
            i_ = Y.v(64 * cp + 16 * dst, 64 * cp + 16 * dst + 16, 0, 4096, fn=lambda a: a.rearrange('p (b j) -> p b j', j=128))
            s.dma('sp' if n % 2 else 'act', o, i_)
            n += 1


def build_U(parts=('fourier',)):
    nc = bass.Bass("TRN2", target_bir_lowering=False)
    s = Sched(nc)
    C = Ctx()
    xf_d = s.dram('xf', [4 * XF_ROWS, TL], F32, 'ExternalInput')
    id_d = s.dram('ident', [128, 128], F32, 'ExternalInput')
    xb_d = s.dram('xb', [4 * XB_ROWS, TL], F32, 'ExternalOutput')
    tabs_d = {k: s.dram(k, v, F32, 'ExternalInput') for k, v in FT_SHAPES.items()}
    md = {k: s.dram(k, v, F32, 'ExternalInput') for k, v in MT_SHAPES.items()}
    sd = {k: s.dram(k, v, F32, 'ExternalInput') for k, v in ST_SHAPES.items()}
    common_setup(s, C, id_d)
    C.epsb = s.sb('epsb', [128, 1], F32)
    s.op('dve', lambda e: e.memset(C.epsb.h[:, :], EPS), writes=[C.epsb.v()])
    C.oneb = s.sb('oneb', [128, 1], F32)
    s.op('dve', lambda e: e.memset(C.oneb.h[:, :], 1.0), writes=[C.oneb.v()])
    if 'dbg' in parts:
        C.dbg = {k: s.dram(k, sh, F32, 'ExternalOutput') for k, sh in [('d_q', [128, S]), ('d_k', [128, S]), ('d_H', [128, S]),
                                                                        ('d_sm', [128, 512]), ('d_va', [128, 64 * 129])]}
    mk = s.mark()
    if 'fourier' in parts:
        phase_U_fourier(s, C, xf_d, xb_d, tabs_d)
        s.release(mk)
    if 's5' in parts:
        phase_U_s5(s, C, xf_d, xb_d, sd)
        s.release(mk)
    if 'mlstm' in parts:
        phase_U_mlstm(s, C, xf_d, xb_d, md)
    s.finish()
    return nc


def mlstm_tables():
    j = np.arange(128)
    trif = (j[:, None] <= j[None, :]).astype(np.float32)
    sc = 128.0 ** -0.5
    return dict(m_trif=trif, m_trib=np.ascontiguousarray(trif.T), m_maskf=trif * sc, m_maskb=np.ascontiguousarray(trif.T) * sc,
                m_onesf=np.ones((128, 128), np.float32))


MT_SHAPES = dict(m_trif=[128, 128], m_trib=[128, 128], m_maskf=[128, 128], m_maskb=[128, 128], m_onesf=[128, 128],
                 m_cwq=[128, 5], m_cwk=[128, 5], m_cbq=[128, 1], m_cbk=[128, 1])


def phase_U_mlstm(s, C, xf_d, xb_d, md):
    NCH = S // 128
    SC = 128.0 ** -0.5
    tb = {}
    for k in MT_SHAPES:
        tb[k] = s.sb(k, MT_SHAPES[k], F32)
        s.dma('sp', tb[k].v(), md[k].v())
    G4 = s.sb('m_G4', [4, S], F32)
    for src in range(4):
        s.dma('act', G4.v(0, 4, 2048 * src, 2048 * src + 2048), xf_d.v(src * XF_ROWS + 384, src * XF_ROWS + 388))
    gT = s.sb('m_gT', [128, NCH * 4], F32)
    ps = next_ps(C)
    for c in range(NCH):
        s.tr(ps.v(0, 128, 4 * c, 4 * c + 4), G4.v(0, 4, 128 * c, 128 * c + 128), C.ident.v(0, 4, 0, 4))
    s.copy(gT.v(), ps.v(0, 128, 0, 4 * NCH))
    gcol = lambda g: gT.v(fn=lambda a: a.rearrange('p (c g) -> p c g', g=4)[:, :, g])
    sm = {}
    for nm in ['lf_f', 'lf_b', 'b_f', 'b_b', 'w_f', 'w_b', 'enb_f', 'enb_b', 'eg_f', 'eg_b', 'egs_f', 'egs_b', 'tmp']:
        sm[nm] = s.sb('m_' + nm, [128, NCH], F32)
    for d, gi in (('f', 2), ('b', 3)):
        lf = sm['lf_' + d]
        s.act(sm['tmp'].v(), gcol(gi), AF.Exp, scale=-1.0)
        s.act(lf.v(), sm['tmp'].v(), AF.Ln, bias=C.oneb.v(), scale=1.0)
        s.ts(lf.v(), lf.v(), -1.0, ALU.mult)
        pb = next_ps(C)
        s.mm(pb.v(0, 128, 0, NCH), tb['m_tri' + d].v(), lf.v())
        s.copy(sm['b_' + d].v(), pb.v(0, 128, 0, NCH))
        pg = next_ps(C)
        s.mm(pg.v(0, 128, 0, NCH), tb['m_onesf'].v(), lf.v())
        s.act(sm['eg_' + d].v(), pg.v(0, 128, 0, NCH), AF.Exp)
        s.ts(sm['egs_' + d].v(), sm['eg_' + d].v(), SC, ALU.mult)
        s.act(sm['enb_' + d].v(), sm['b_' + d].v(), AF.Exp, scale=-1.0)
        s.tt(sm['tmp'].v(), gcol(0 if d == 'f' else 1), sm['b_' + d].v(), ALU.subtract)
        s.act(sm['w_' + d].v(), sm['tmp'].v(), AF.Exp)
    zpb = s.sb('m_zpb', [128, S + 4], BF16)
    s.op('dve', lambda e: e.memset(zpb.h[:, 0:2], 0.0), writes=[zpb.v(0, 128, 0, 2)])
    s.op('dve', lambda e: e.memset(zpb.h[:, S + 2:S + 4], 0.0), writes=[zpb.v(0, 128, S + 2, S + 4)])
    acc0 = s.sb('m_acc0', [128, 2048], F32)
    vst = s.sb('m_vst', [128, 4096], F32)
    dg = s.sb('m_dg', [128, 10 * 128], BF16)
    qT = s.sb('m_qT', [128, S], BF16)
    kT = s.sb('m_kT', [128, S], BF16)
    for (qi, row0, cw, cb, dstT) in ((0, 0, tb['m_cwq'], tb['m_cbq'], qT), (1, 128, tb['m_cwk'], tb['m_cbk'], kT)):
        for k in range(5):
            s.ts(dg.v(0, 128, 128 * (5 * qi + k), 128 * (5 * qi + k) + 128), C.ident.v(), cw.v(0, 128, k, k + 1), ALU.mult)
        for src in range(4):
            s.dma('pool', zpb.v(0, 128, 2 + 2048 * src, 2 + 2048 * src + 2048),
                  xf_d.v(src * XF_ROWS + row0, src * XF_ROWS + row0 + 128))
        for ch in range(S // 512):
            ps = next_ps(C)
            for k in range(5):
                s.mm(ps.v(), dg.v(0, 128, 128 * (5 * qi + k), 128 * (5 * qi + k) + 128),
                     zpb.v(0, 128, 512 * ch + k, 512 * ch + k + 512), start=(k == 0), stop=(k == 4))
            s.act(dstT.v(0, 128, 512 * ch, 512 * ch + 512), ps.v(), AF.Silu, bias=cb.v())
    ktok = s.sb('m_ktok', [128, S], BF16)
    for g in range(NCH // 4):
        ps = next_ps(C)
        pb = lambda lo, hi: ps.v(fn=lambda a: a.bitcast(BF16)[:, lo:hi])
        for i in range(4):
            c = 4 * g + i
            s.tr(pb(128 * i, 128 * i + 128), kT.v(0, 128, 128 * c, 128 * c + 128), C.identb.v())
        s.copy(ktok.v(0, 128, 512 * g, 512 * g + 512), pb(0, 512), eng=s.evac_eng())
    vaug = {d: s.sb('m_vaug_' + d, [128, NCH * 129], BF16) for d in 'fb'}
    for g in range(NCH // 4):
        if g % 8 == 0:
            hv_ = g // 8
            for q2 in range(2):
                src = 2 * hv_ + q2
                s.dma('sp' if q2 else 'act', vst.v(0, 128, 2048 * q2, 2048 * q2 + 2048),
                      xf_d.v(src * XF_ROWS + 256, src * XF_ROWS + 384))
        ps = next_ps(C)
        for i in range(4):
            c = 4 * g + i
            cl = c % 32
            s.tr(ps.v(0, 128, 128 * i, 128 * i + 128), vst.v(0, 128, 128 * cl, 128 * cl + 128), C.ident.v())
        for i in range(4):
            c = 4 * g + i
            for d in 'fb':
                s.ts(vaug[d].v(0, 128, 129 * c, 129 * c + 128), ps.v(0, 128, 128 * i, 128 * i + 128), sm['w_' + d].v(0, 128, c, c + 1), ALU.mult)
    for d in 'fb':
        s.copy(vaug[d].v(fn=lambda a: a.rearrange('p (c e) -> p c e', e=129)[:, :, 128]), sm['w_' + d].v(), eng='pool')
    H = s.sb('m_H', [128, S], F32)
    P = {d: s.sb('m_P_' + d, [128, 129], F32) for d in 'fb'}
    Cb = {d: [s.sb('m_Cb_%s%d' % (d, i), [128, 129], BF16) for i in range(2)] for d in 'fb'}
    Sm = {d: [s.sb('m_Sm_%s%d' % (d, i), [128, 128], BF16) for i in range(3)] for d in 'fb'}
    den = {d: [s.sb('m_den_%s%d' % (d, i), [128, 1], F32) for i in range(3)] for d in 'fb'}
    mask = {'f': tb['m_maskf'], 'b': tb['m_maskb']}
    def dir_gen(d):
        for step in range(NCH):
            c = step if d == 'f' else NCH - 1 - step
            cprev = c - 1 if d == 'f' else c + 1
            va = vaug[d].v(0, 128, 129 * c, 129 * c + 129)
            if step < NCH - 1:
                ps_d = next_ps(C)
                s.mm(ps_d.v(0, 128, 0, 129), ktok.v(0, 128, 128 * c, 128 * c + 128), va)
                if step == 0:
                    s.copy(P[d].v(), ps_d.v(0, 128, 0, 129))
                else:
                    s.stt(P[d].v(), P[d].v(), sm['eg_' + d].v(0, 128, cprev, cprev + 1), ps_d.v(0, 128, 0, 129), ALU.mult, ALU.add)
                yield
                s.act(Cb[d][(step + 1) % 2].v(), P[d].v(), AF.Copy, scale=sm['egs_' + d].v(0, 128, c, c + 1))
                yield
            ps_s = next_ps(C)
            s.mm(ps_s.v(0, 128, 0, 128), kT.v(0, 128, 128 * c, 128 * c + 128), qT.v(0, 128, 128 * c, 128 * c + 128))
            smt = Sm[d][step % 3]
            s.tt(smt.v(), ps_s.v(0, 128, 0, 128), mask[d].v(), ALU.mult)
            yield
            ps_o = next_ps(C)
            s.mm(ps_o.v(0, 128, 0, 129), smt.v(), va, start=True, stop=(step == 0))
            if step > 0:
                s.mm(ps_o.v(0, 128, 0, 129), qT.v(0, 128, 128 * c, 128 * c + 128), Cb[d][step % 2].v(), start=False, stop=True)
            dn = den[d][step % 3]
            s.ts(dn.v(), ps_o.v(0, 128, 128, 129), -1.0, ALU.mult, sm['enb_' + d].v(0, 128, c, c + 1), ALU.max)
            yield
            s.tt(dn.v(), dn.v(), ps_o.v(0, 128, 128, 129), ALU.max)
            yield
            s.op('dve', lambda e, dn=dn: e.reciprocal(out=dn.h[:, :], in_=dn.h[:, :]), reads=[dn.v()], writes=[dn.v()])
            yield
            hv = H.v(0, 128, 128 * c, 128 * c + 128)
            if step < NCH // 2:
                s.act(hv, ps_o.v(0, 128, 0, 128), AF.Copy, scale=dn.v())
            else:
                s.stt(hv, ps_o.v(0, 128, 0, 128), dn.v(), hv, ALU.mult, ALU.add)
            yield

    gens = [dir_gen('f'), dir_gen('b')]
    while gens:
        for gnr in list(gens):
            try:
                next(gnr)
            except StopIteration:
                gens.remove(gnr)
    if getattr(C, 'dbg', None) is not None:
        s.dma('pool', C.dbg['d_q'].v(), qT.v())
        s.dma('pool', C.dbg['d_k'].v(), kT.v())
        s.dma('sp', C.dbg['d_H'].v(), H.v())
        for i, nm in enumerate(['lf_f', 'b_f', 'w_f', 'enb_f', 'eg_f', 'lf_b', 'b_b', 'w_b']):
            s.dma('sp', C.dbg['d_sm'].v(0, 128, 64 * i, 64 * i + 64), sm[nm].v())
        s.dma('pool', C.dbg['d_va'].v(), vaug['f'].v())
    H3 = lambda lo, hi: H.v(0, 128, 128 * lo, 128 * hi, fn=lambda a: a.rearrange('p (c e) -> p c e', e=128))
    mu = sm['tmp']
    s.op('dve', lambda e: e.tensor_reduce(out=mu.h[:, :], in_=H.h[:, :].rearrange('p (c e) -> p c e', e=128), axis=AX.X, op=ALU.add),
         reads=[H.v()], writes=[mu.v()])
    s.ts(mu.v(), mu.v(), 1.0 / 128, ALU.mult)
    bc = lambda t, lo, hi: t.v(0, 128, lo, hi, fn=lambda a: a.unsqueeze(2).to_broadcast([128, hi - lo, 128]))
    var = sm['lf_f']
    sq = vst
    for pc in range(4):
        lo, hi = 16 * pc, 16 * pc + 16
        s.tt(H3(lo, hi), H3(lo, hi), bc(mu, lo, hi), ALU.subtract)
        s.act(sq.v(0, 128, 0, 2048), H.v(0, 128, 128 * lo, 128 * hi), AF.Square)
        s.op('dve', lambda e, lo=lo, hi=hi: e.tensor_reduce(out=var.h[:, lo:hi], in_=sq.h[:, 0:2048].rearrange('p (c e) -> p c e', e=128),
                                                           axis=AX.X, op=ALU.add),
             reads=[sq.v(0, 128, 0, 2048)], writes=[var.v(0, 128, lo, hi)])
    s.act(var.v(), var.v(), AF.Sqrt, bias=C.epsb.v(), scale=1.0 / 128)
    s.op('dve', lambda e: e.reciprocal(out=var.h[:, :], in_=var.h[:, :]), reads=[var.v()], writes=[var.v()])
    for pc in range(4):
        lo, hi = 16 * pc, 16 * pc + 16
        s.tt(H3(lo, hi), H3(lo, hi), bc(var, lo, hi), ALU.mult, eng='pool' if pc % 2 else 'dve')
    for g in range(NCH // 4):
        ps = next_ps(C)
        for i in range(4):
            c = 4 * g + i
            s.tr(ps.v(0, 128, 128 * i, 128 * i + 128), H.v(0, 128, 128 * c, 128 * c + 128), C.ident.v())
        stv = acc0.v(0, 128, 512 * (g % 4), 512 * (g % 4) + 512)
        s.copy(stv, ps.v(), eng=s.evac_eng())
        dst = g // 4
        col = 512 * (g % 4)
        s.dma('sp' if g % 2 else 'act', xb_d.v(dst * XB_ROWS, dst * XB_ROWS + 128, col, col + 512), stv)


def u_extra_inputs(d, l, c):
    rp = c % 4
    m = {}
    m.update(mlstm_tables())
    cw = d['conv_w'][l]
    cb = d['conv_b'][l]
    m['m_cwq'] = np.ascontiguousarray(cw[:, 128 * rp:128 * rp + 128].T)
    m['m_cwk'] = np.ascontiguousarray(cw[:, 512 + 128 * rp:512 + 128 * rp + 128].T)
    m['m_cbq'] = np.ascontiguousarray(cb[128 * rp:128 * rp + 128].reshape(128, 1))
    m['m_cbk'] = np.ascontiguousarray(cb[512 + 128 * rp:512 + 128 * rp + 128].reshape(128, 1))
    m.update(s5_tables())
    m.update(s5_inputs(d, l, c))
    return m


NTAU = 152


def s5_tables():
    n72 = np.arange(72.0)
    n8 = np.arange(8.0)
    n64 = np.arange(64.0)
    dpow = 64.0 * 2.0 ** np.arange(7)
    tf = np.concatenate([n72 - 7, 7 - n8, 63 - n64, dpow, [1.0]])
    tbk = np.concatenate([64 - n72, n8, n64, dpow, [1.0]])
    tau = np.tile(np.concatenate([tf, tbk])[None, :], (128, 1))
    p = np.arange(128)
    imask = np.eye(128)
    jmask = np.roll(np.eye(128), 64, axis=1)
    blk = p // 16
    bmf = (blk[:, None] <= blk[None, :]) * 1.0
    bmb = (blk[:, None] >= blk[None, :]) * 1.0
    sg = np.concatenate([-np.ones(64), np.ones(64)])[:, None]
    sel = np.zeros((64, 4, 8, 128))
    for g in range(4):
        for i in range(8):
            for c in range(16):
                sel[16 * g + c, g, i, 16 * i + c] = 1.0
    selT = sel.transpose(3, 1, 2, 0).reshape(128, 4 * 8 * 64)
    tabs = dict(s_tau=tau, s_imask=imask, s_jmask=jmask, s_bmf=bmf, s_bmb=bmb, s_sg=sg, s_nsg=-sg,
                s_sel=sel.reshape(64, 4096), s_selT=selT)
    return {k: np.ascontiguousarray(v.astype(np.float32)) for k, v in tabs.items()}


ST_SHAPES = dict(s_tau=[128, 2 * NTAU], s_imask=[128, 128], s_jmask=[128, 128], s_bmf=[128, 128], s_bmb=[128, 128],
                 s_sg=[128, 1], s_nsg=[128, 1], s_sel=[64, 4096], s_selT=[128, 2048],
                 s_lam=[8 * 128, 2], s_logdt=[8 * 128, 1], s_X1=[8 * 128, 16], s_X2=[8 * 128, 16], s_Y1=[8 * 128, 16],
                 s_Y2=[8 * 128, 16], s_d=[4 * 128, 1])


def s5_inputs(d, l, c):
    rp = c % 4
    lam = np.zeros((8, 128, 2), np.float32)
    logdt = np.zeros((8, 128, 1), np.float32)
    X1 = np.zeros((8, 128, 16), np.float32)
    X2 = np.zeros((8, 128, 16), np.float32)
    Y1 = np.zeros((8, 128, 16), np.float32)
    Y2 = np.zeros((8, 128, 16), np.float32)
    dd = np.zeros((4, 128, 1), np.float32)
    for gl in range(4):
        g = 4 * rp + gl
        dd[gl, :, 0] = np.tile(d['s5_d'][l, g], 8)
        for di in range(2):
            u = 2 * gl + di
            lam[u, :, 0] = np.tile(d['s5_lam_re'][l, di, g], 2)
            lam[u, :, 1] = np.tile(d['s5_lam_im'][l, di, g], 2)
            logdt[u, :, 0] = d['s5_log_dt'][l, di, g]
            bre, bim = d['s5_b_re'][l, di, g], d['s5_b_im'][l, di, g]
            cre, cim = d['s5_c_re'][l, di, g].T, d['s5_c_im'][l, di, g].T
            X1[u] = np.concatenate([bre, bim], 0)
            X2[u] = np.concatenate([bim, bre], 0)
            Y1[u] = np.concatenate([cre, cim], 0)
            Y2[u] = np.concatenate([cim, cre], 0)
    return dict(s_lam=lam.reshape(1024, 2), s_logdt=logdt.reshape(1024, 1), s_X1=X1.reshape(1024, 16), s_X2=X2.reshape(1024, 16),
                s_Y1=Y1.reshape(1024, 16), s_Y2=Y2.reshape(1024, 16), s_d=dd.reshape(512, 1))


def phase_U_s5(s, C, xf_d, xb_d, sd):
    TWO_PI = 2.0 * np.pi
    cst = {}
    for k in ['s_tau', 's_imask', 's_jmask', 's_bmf', 's_bmb', 's_sg', 's_nsg']:
        cst[k] = s.sb(k, ST_SHAPES[k], F32)
        s.dma('sp', cst[k].v(), sd[k].v())
    sel = s.sb('s_sel', [64, 4096], BF16)
    s.dma('pool', sel.v(), sd['s_sel'].v())
    selT = s.sb('s_selT', [128, 2048], BF16)
    s.dma('pool', selT.v(), sd['s_selT'].v())
    zsb = s.sb('s_zsb', [64, S], BF16)
    for src in range(4):
        s.dma('pool', zsb.v(0, 64, 2048 * src, 2048 * src + 2048), xf_d.v(src * XF_ROWS + 452, src * XF_ROWS + 516))
    U8 = [s.sb('s_U8_%d' % g, [128, 1024], BF16) for g in range(4)]
    for g in range(4):
        for half in range(2):
            ps = next_ps(C)
            for i in range(8):
                rhs = V(zsb, zsb.h[0:64, 4096 * half + i:4096 * (half + 1):8], (0, 64, 4096 * half, 4096 * (half + 1)))
                s.mm(ps.v(0, 128, 0, 512), sel.v(0, 64, 128 * (8 * g + i), 128 * (8 * g + i) + 128), rhs, start=(i == 0), stop=(i == 7))
            s.copy(U8[g].v(0, 128, 512 * half, 512 * half + 512), ps.v(), eng=s.evac_eng())
    NS = 4

    def mk(nm, shape, dt=F32):
        return [s.sb('s_%s_%d' % (nm, b), shape, dt) for b in range(NS)]
    lamt_, ldt_ = mk('lamt', [128, 2]), mk('ldt', [128, 1])
    v1_ = {nm: mk(nm, [128, 1]) for nm in ['dt', 'lr', 'th', 'den', 'am1', 't1', 't2', 'fre', 'fim', 'fis', 'frs', 'dvec']}
    X1_, X2_, Y1_, Y2_ = [mk(nm, [128, 16]) for nm in ['X1', 'X2', 'Y1', 'Y2']]
    BA_, BB_, CA_, CB_, t16_ = [mk(nm, [128, 16]) for nm in ['BA', 'BB', 'CA', 'CB', 't16']]
    mag_, ang_, kf_, al_, be_ = [mk(nm, [128, NTAU]) for nm in ['mag', 'ang', 'kf', 'al', 'be']]
    ki_ = mk('ki', [128, NTAU], mybir.dt.int32)
    bsn_ = mk('bsn', [128, 7])
    Gt_, Ht_, tG_ = mk('G', [128, 72 * 16]), mk('H', [128, 72 * 16]), mk('tG', [128, 72 * 16])
    Gb_ = mk('Gb', [128, 72 * 16], BF16)
    Tm_, Wm_ = mk('T', [128, 1024], BF16), mk('W', [128, 1024], BF16)
    Dm_ = mk('D', [128, 7 * 128], BF16)
    tmp128_ = mk('tmp128', [128, 128])
    Xf_, Xb_, Xp_ = mk('Xf', [128, 128]), mk('Xb', [128, 128], BF16), mk('Xp', [128, 130], BF16)
    for b in range(NS):
        s.op('dve', lambda e, b=b: e.memset(Xp_[b].h[:, :], 0.0), writes=[Xp_[b].v()])
    Y8 = [s.sb('s_Y8_%d' % g, [128, 1024], BF16) for g in range(4)]
    gx = [s.sb('s_gx%d' % i, [128, 512], F32) for i in range(6)]

    def col(t, i):
        return t.v(0, 128, i, i + 1)

    def unit_gen(g, d):
        b = 2 * (g % 2) + d
        lamt, ldt = lamt_[b], ldt_[b]
        v1 = {k: v[b] for k, v in v1_.items()}
        X1, X2, Y1, Y2 = X1_[b], X2_[b], Y1_[b], Y2_[b]
        BA, BB, CA, CB, t16 = BA_[b], BB_[b], CA_[b], CB_[b], t16_[b]
        mag, ang, kf, al, be, ki, bsn = mag_[b], ang_[b], kf_[b], al_[b], be_[b], ki_[b], bsn_[b]
        Gt, Ht, tG, Gb, Tm, Wm, Dm, tmp128 = Gt_[b], Ht_[b], tG_[b], Gb_[b], Tm_[b], Wm_[b], Dm_[b], tmp128_[b]
        Xf, Xb, Xp = Xf_[b], Xb_[b], Xp_[b]
        u = 2 * g + d
        r0 = 128 * u
        s.dma('sp', lamt.v(), sd['s_lam'].v(r0, r0 + 128))
        s.dma('act', ldt.v(), sd['s_logdt'].v(r0, r0 + 128))
        s.dma('sp', X1.v(), sd['s_X1'].v(r0, r0 + 128))
        s.dma('act', X2.v(), sd['s_X2'].v(r0, r0 + 128))
        s.dma('sp', Y1.v(), sd['s_Y1'].v(r0, r0 + 128))
        s.dma('act', Y2.v(), sd['s_Y2'].v(r0, r0 + 128))
        if d == 0:
            s.dma('sp', v1['dvec'].v(), sd['s_d'].v(128 * g, 128 * g + 128))
        yield
        lre, lim = col(lamt, 0), col(lamt, 1)
        s.act(v1['dt'].v(), ldt.v(), AF.Exp); yield
        s.tt(v1['lr'].v(), lre, v1['dt'].v(), ALU.mult); yield
        s.tt(v1['th'].v(), lim, v1['dt'].v(), ALU.mult); yield
        tau = cst['s_tau'].v(0, 128, NTAU * d, NTAU * d + NTAU)
        s.ts(ang.v(), tau, v1['th'].v(), ALU.mult); yield
        s.ts(kf.v(), ang.v(), 1.0 / TWO_PI, ALU.mult); yield
        s.copy(ki.v(), kf.v()); yield
        s.copy(kf.v(), ki.v()); yield
        s.stt(ang.v(), kf.v(), -TWO_PI, ang.v(), ALU.mult, ALU.add); yield

        def wrap(dst, y):
            s.ts(mag.v(), y.v(), float(np.pi), ALU.is_gt, -TWO_PI, ALU.mult); yield
            s.tt(dst.v(), mag.v(), y.v(), ALU.add); yield
            s.ts(mag.v(), y.v(), -float(np.pi), ALU.is_lt, TWO_PI, ALU.mult); yield
            s.tt(dst.v(), dst.v(), mag.v(), ALU.add); yield
        yield from wrap(kf, ang)
        s.act(be.v(), kf.v(), AF.Sin); yield
        s.ts(ang.v(), ang.v(), float(np.pi / 2), ALU.add); yield
        yield from wrap(kf, ang)
        s.act(al.v(), kf.v(), AF.Sin); yield
        s.act(mag.v(), tau, AF.Exp, scale=v1['lr'].v()); yield
        s.tt(al.v(), al.v(), mag.v(), ALU.mult); yield
        s.tt(be.v(), be.v(), mag.v(), ALU.mult); yield
        a1, b1 = col(al, NTAU - 1), col(be, NTAU - 1)
        s.tt(v1['t1'].v(), lre, lre, ALU.mult); yield
        s.stt(v1['den'].v(), lim, lim, v1['t1'].v(), ALU.mult, ALU.add); yield
        s.op('dve', lambda e: e.reciprocal(out=v1['den'].h[:, :], in_=v1['den'].h[:, :]), reads=[v1['den'].v()], writes=[v1['den'].v()]); yield
        s.ts(v1['am1'].v(), a1, -1.0, ALU.add); yield
        s.tt(v1['t1'].v(), v1['am1'].v(), lre, ALU.mult); yield
        s.stt(v1['t1'].v(), b1, lim, v1['t1'].v(), ALU.mult, ALU.add); yield
        s.tt(v1['fre'].v(), v1['t1'].v(), v1['den'].v(), ALU.mult); yield
        s.tt(v1['t2'].v(), v1['am1'].v(), lim, ALU.mult); yield
        s.stt(v1['t2'].v(), b1, lre, v1['t2'].v(), ALU.mult, ALU.subtract); yield
        s.tt(v1['fim'].v(), v1['t2'].v(), v1['den'].v(), ALU.mult); yield
        s.tt(v1['fis'].v(), v1['fim'].v(), cst['s_sg'].v(), ALU.mult); yield
        s.tt(v1['frs'].v(), v1['fre'].v(), cst['s_sg'].v(), ALU.mult); yield
        s.ts(t16.v(), X1.v(), v1['fre'].v(), ALU.mult); yield
        s.stt(BA.v(), X2.v(), v1['fis'].v(), t16.v(), ALU.mult, ALU.add); yield
        s.ts(t16.v(), X1.v(), v1['fim'].v(), ALU.mult); yield
        s.stt(BB.v(), X2.v(), v1['frs'].v(), t16.v(), ALU.mult, ALU.subtract); yield
        s.ts(CA.v(), Y1.v(), cst['s_nsg'].v(), ALU.mult); yield
        s.ts(CB.v(), Y2.v(), -1.0, ALU.mult); yield
        b3 = lambda t: t.v(fn=lambda a: a.unsqueeze(1).to_broadcast([128, 72, 16]))
        a3 = lambda t, lo: t.v(0, 128, lo, lo + 72, fn=lambda a: a.unsqueeze(2).to_broadcast([128, 72, 16]))
        r3 = lambda t: t.v(fn=lambda a: a.rearrange('p (n c) -> p n c', c=16))
        s.tt(r3(Gt), b3(CA), a3(al, 0), ALU.mult); yield
        s.tt(r3(tG), b3(CB), a3(be, 0), ALU.mult); yield
        s.tt(Gt.v(), Gt.v(), tG.v(), ALU.add, eng='pool'); yield
        s.tt(r3(Ht), b3(BA), a3(al, 72), ALU.mult); yield
        s.tt(r3(tG), b3(BB), a3(be, 72), ALU.mult); yield
        s.tt(Ht.v(), Ht.v(), tG.v(), ALU.add, eng='dve'); yield
        s.copy(Gb.v(), Gt.v(), eng='act'); yield
        s.ts(bsn.v(), be.v(0, 128, 144, 151), cst['s_nsg'].v(), ALU.mult); yield
        for m in range(7):
            s.ts(tmp128.v(), cst['s_imask'].v(), col(al, 144 + m), ALU.mult, eng='pool' if m % 2 else 'dve'); yield
            s.stt(Dm.v(0, 128, 128 * m, 128 * m + 128), cst['s_jmask'].v(), col(bsn, m), tmp128.v(), ALU.mult, ALU.add); yield
        for dl in range(8):
            gs = 128 * dl if d == 0 else 128 * (8 - dl)
            ps = next_ps(C)
            s.mm(ps.v(0, 128, 0, 128), Ht.v(0, 128, 0, 128), Gt.v(0, 128, gs, gs + 128))
            tv = Tm.v(0, 128, 128 * dl, 128 * dl + 128)
            if dl == 0:
                bm = cst['s_bmf'] if d == 0 else cst['s_bmb']
                if d == 0:
                    s.tt(tmp128.v(), ps.v(0, 128, 0, 128), bm.v(), ALU.mult); yield
                    s.stt(tv, cst['s_imask'].v(), v1['dvec'].v(), tmp128.v(), ALU.mult, ALU.add)
                else:
                    s.tt(tv, ps.v(0, 128, 0, 128), bm.v(), ALU.mult)
            else:
                s.copy(tv, ps.v(0, 128, 0, 128), eng=s.evac_eng())
            yield
        for hh in range(2):
            ps = next_ps(C)
            for i in range(4):
                ih = 4 * hh + i
                s.tr(ps.v(0, 128, 128 * i, 128 * i + 128), Ht.v(0, 128, 128 + 128 * ih, 128 + 128 * ih + 128), C.ident.v())
            s.copy(Wm.v(0, 128, 512 * hh, 512 * hh + 512), ps.v(), eng=s.evac_eng()); yield
        ps = next_ps(C)
        for ih in range(8):
            rhs = V(U8[g], U8[g].h[:, ih::8], (0, 128, 0, 1024))
            s.mm(ps.v(0, 128, 0, 128), Wm.v(0, 128, 128 * ih, 128 * ih + 128), rhs, start=(ih == 0), stop=(ih == 7))
        s.copy(Xf.v(), ps.v(0, 128, 0, 128), eng='dve')
        s.copy(Xb.v(), ps.v(0, 128, 0, 128), eng='act'); yield
        for m in range(7):
            sh = 2 ** m
            ps = next_ps(C)
            if d == 0:
                s.mm(ps.v(0, 128, 0, 128 - sh), Dm.v(0, 128, 128 * m, 128 * m + 128), Xb.v(0, 128, 0, 128 - sh))
                xv = Xf.v(0, 128, sh, 128)
            else:
                s.mm(ps.v(0, 128, 0, 128 - sh), Dm.v(0, 128, 128 * m, 128 * m + 128), Xb.v(0, 128, sh, 128))
                xv = Xf.v(0, 128, 0, 128 - sh)
            s.tt(xv, xv, ps.v(0, 128, 0, 128 - sh), ALU.add); yield
            if m < 6:
                s.copy(Xb.v(), Xf.v(), eng='act'); yield
        s.copy(Xp.v(0, 128, 1, 129), Xf.v(), eng='act'); yield

    def out_gen(g):
        bf, bb = 2 * (g % 2), 2 * (g % 2) + 1
        for hh in range(2):
            ps = next_ps(C)
            for q in range(4):
                jh = 4 * hh + q
                o = ps.v(0, 128, 128 * q, 128 * q + 128)
                first = True
                for ih in range(jh + 1):
                    rhs = V(U8[g], U8[g].h[:, ih::8], (0, 128, 0, 1024))
                    s.mm(o, Tm_[bf].v(0, 128, 128 * (jh - ih), 128 * (jh - ih) + 128), rhs, start=first, stop=False)
                    first = False
                for ih in range(jh, 8):
                    rhs = V(U8[g], U8[g].h[:, ih::8], (0, 128, 0, 1024))
                    s.mm(o, Tm_[bb].v(0, 128, 128 * (ih - jh), 128 * (ih - jh) + 128), rhs, start=False, stop=False)
                s.mm(o, Gb_[bf].v(0, 128, 128 * (jh + 1), 128 * (jh + 1) + 128), Xp_[bf].v(0, 128, 0, 128), start=False, stop=False)
                s.mm(o, Gb_[bb].v(0, 128, 128 * jh, 128 * jh + 128), Xp_[bb].v(0, 128, 2, 130), start=False, stop=True)
            k3 = 3 * (g % 2)
            x_, t_, u_ = gx[k3], gx[k3 + 1], gx[k3 + 2]
            s.copy(x_.v(), ps.v(), eng='dve'); yield
            s.act(t_.v(), x_.v(), AF.Square); yield
            s.ts(t_.v(), t_.v(), 0.044715, ALU.mult, 1.0, ALU.add, eng='pool'); yield
            s.tt(t_.v(), t_.v(), x_.v(), ALU.mult, eng='pool'); yield
            s.act(u_.v(), t_.v(), AF.Sigmoid, scale=1.5957691216057308); yield
            ov = Y8[g].v(fn=lambda a, hh=hh: a.rearrange('p (k j) -> p j k', j=8)[:, 4 * hh:4 * hh + 4, :])
            s.tt(ov, u_.v(fn=lambda a: a.rearrange('p (j k) -> p j k', k=128)), x_.v(fn=lambda a: a.rearrange('p (j k) -> p j k', k=128)), ALU.mult); yield

    def run_interleaved(gens):
        gens = list(gens)
        while gens:
            for gnr in list(gens):
                try:
                    next(gnr)
                except StopIteration:
                    gens.remove(gnr)

    for gp in range(2):
        run_interleaved([unit_gen(2 * gp + gg, d) for gg in range(2) for d in range(2)])
        run_interleaved([out_gen(2 * gp + gg) for gg in range(2)])
    yst = [s.sb('s_yst%d' % i, [64, 512], F32) for i in range(2)]
    for tb_ in range(16):
        ps = next_ps(C)
        for j in range(8):
            for g in range(4):
                s.mm(ps.v(0, 64, 64 * j, 64 * j + 64), selT.v(0, 128, 64 * (8 * g + j), 64 * (8 * g + j) + 64),
                     Y8[g].v(0, 128, 64 * tb_, 64 * tb_ + 64), start=(g == 0), stop=(g == 3))
        st = yst[tb_ % 2]
        s.copy(st.v(fn=lambda a: a.rearrange('p (k j) -> p j k', j=8)), ps.v(0, 64, 0, 512, fn=lambda a: a.rearrange('p (j k) -> p j k', k=64)),
               eng=s.evac_eng())
        dst = tb_ // 4
        cc = 512 * (tb_ % 4)
        s.dma('sp' if tb_ % 2 else 'act', xb_d.v(dst * XB_ROWS + 192, dst * XB_ROWS + 256, cc, cc + 512), st.v())


def _col(a):
    return np.ascontiguousarray(np.asarray(a, np.float32).reshape(-1, 1))


def _c(a):
    return np.ascontiguousarray(np.asarray(a, np.float32))


def _layer_weights(inp, l):
    return dict(w_in=_c(inp['w_in'][l]), b_in=_col(inp['b_in'][l]), g_pre=_col(inp['g_mix_pre'][l]),
                norm_g=_col(inp['mlstm_norm_g'][l]), w_up_m=_c(inp['w_up_mlstm'][l]), w_up_f=_c(inp['w_up_fourier'][l]),
                w_glu=_c(inp['w_glu'][l]), b_glu=_col(inp['b_glu'][l]), w_out=_c(inp['w_out'][l]), g_post=_col(inp['g_mix_post'][l]),
                g_ffn_pre=_col(inp['g_ffn_pre'][l]), g_ffn_post=_col(inp['g_ffn_post'][l]), w_ffn1=_c(inp['w_ffn1'][l]),
                w_ffn2=_c(inp['w_ffn2'][l]))


def kernel_unfused(**inputs):
    inp = {k: np.asarray(v) for k, v in inputs.items()}
    ident = np.eye(128, dtype=np.float32)
    cores = list(range(8))
    xs = [_c(inp['x'][c // 4, TL * (c % 4):TL * (c % 4 + 1)]) for c in cores]
    ftab = fourier_tables()
    for l in range(2):
        Wl = _layer_weights(inp, l)
        nc = build_T1()
        maps = [dict(x=xs[c], w_in=Wl['w_in'], b_in=Wl['b_in'], g_pre=Wl['g_pre'], ident=ident) for c in cores]
        res = run_bass_kernel_spmd(nc, maps, core_ids=cores)
        xf_sent = [np.asarray(res.results[c]['xf']).reshape(4, XF_ROWS, TL) for c in cores]
        maps = []
        for c in cores:
            b, r = c // 4, c % 4
            xf = np.stack([xf_sent[4 * b + src][r] for src in range(4)], 0).reshape(4 * XF_ROWS, TL)
            m = dict(xf=np.ascontiguousarray(xf), ident=ident)
            m.update(ftab)
            m.update(u_extra_inputs(inp, l, c))
            maps.append(m)
        nc = build_U(('fourier', 'mlstm', 's5'))
        res = run_bass_kernel_spmd(nc, maps, core_ids=cores)
        xb_sent = [np.asarray(res.results[c]['xb']).reshape(4, XB_ROWS, TL) for c in cores]
        maps = []
        for c in cores:
            b, r = c // 4, c % 4
            xb = np.stack([xb_sent[4 * b + src][r] for src in range(4)], 0).reshape(4 * XB_ROWS, TL)
            m = dict(x=xs[c], xb=np.ascontiguousarray(xb), ident=ident)
            m.update(Wl)
            maps.append(m)
        nc = build_T2()
        res = run_bass_kernel_spmd(nc, maps, core_ids=cores)
        xs = [np.asarray(res.results[c]['x_out']) for c in cores]
    out = np.zeros((2, S, D), np.float32)
    for c in cores:
        out[c // 4, TL * (c % 4):TL * (c % 4 + 1)] = xs[c]
    return out


class BlockT:
    def __init__(self, parent, bases, B):
        self.parent, self.bases, self.B = parent, bases, B
        self.space = parent.space

    def v(self, p0=0, p1=None, f0=0, f1=None, fn=None):
        if p1 is None:
            p1 = self.B * len(self.bases)
        blk = p0 // self.B
        assert (p1 - 1) // self.B == blk, (p0, p1, self.B)
        off = self.bases[blk] - blk * self.B
        return self.parent.v(off + p0, off + p1, f0, f1, fn)


W_SHAPES = [('w_in', [D, N_IN]), ('b_in', [N_IN, 1]), ('g_pre', [D, 1]), ('norm_g', [512, 1]), ('w_up_m', [512, D]),
            ('w_up_f', [256, D]), ('w_glu', [256, 2 * D]), ('b_glu', [2 * D, 1]), ('w_out', [D, D]), ('g_post', [D, 1]),
            ('g_ffn_pre', [D, 1]), ('g_ffn_post', [D, 1]), ('w_ffn1', [D, 4 * D]), ('w_ffn2', [4 * D, D])]
M_CONST = ['m_trif', 'm_trib', 'm_maskf', 'm_maskb', 'm_onesf']
M_PARAM = ['m_cwq', 'm_cwk', 'm_cbq', 'm_cbk']
S_CONST = ['s_tau', 's_imask', 's_jmask', 's_bmf', 's_bmb', 's_sg', 's_nsg', 's_sel', 's_selT']
S_PARAM = ['s_lam', 's_logdt', 's_X1', 's_X2', 's_Y1', 's_Y2', 's_d']


class ColT:
    def __init__(self, parent, col0, n):
        self.parent, self.col0, self.n = parent, col0, n
        self.space = parent.space

    def v(self, p0=0, p1=None, f0=0, f1=None, fn=None):
        f1 = self.n if f1 is None else f1
        return self.parent.v(p0, p1, self.col0 + f0, self.col0 + f1, fn)


def fused_T1(s, C, x_src, W, xf_view):
    mk = s.mark()
    load_xT(s, C, x_src, TL)
    gcols = load_vec_cols(s, C, 'gpre', W['g_pre'], 8)
    rstd = rms_rstd(s, C, C.xT, TL, 'pre')
    hT = make_h(s, C, gcols, rstd, TL, 'hT')
    phase_T1_proj(s, C, hT, W['w_in'], W['b_in'], xf_view)
    s.release(mk)


def fused_T2(s, C, x_src, W, xb_view, out_view, NT=1024):
    mk = s.mark()
    C.NT = NT
    C.n_wbig = 3
    C.n_wp = 11
    C.sqb = [s.sb('sqb%d' % i, [128, 512], BF16) for i in range(2)]
    C.rstd2 = s.sb('rstd2', [128, NT], F32)
    gcols = load_vec_cols(s, C, 'gpre', W['g_pre'], 8)
    t2_setup(s, C, W)
    C.xT = [s.sb('xT%d' % k, [128, NT], F32) for k in range(8)]
    C.hT = [ColT(C.h2T, k * NT, NT) for k in range(8)]
    C.xoff = 0
    xstage = [s.sb('xst%d' % i, [128, 1024], F32) for i in range(4)]
    for hf in range(TL // NT):
        tok0 = hf * NT
        for tg in range(NT // 512):
            for i in range(4):
                tt = tg * 4 + i
                s.dma('sp' if i % 2 == 0 else 'act', xstage[i].v(), x_src.v(tok0 + 128 * tt, tok0 + 128 * tt + 128))
            for kt in range(8):
                ps = ps6(C)
                for i in range(4):
                    s.tr(ps.v(0, 128, 128 * i, 128 * i + 128), xstage[i].v(0, 128, 128 * kt, 128 * kt + 128), C.ident.v())
                s.copy(C.xT[kt].v(0, 128, 512 * tg, 512 * tg + 512), ps.v(), eng=s.evac_eng())
        for tc in range(NT // 512):
            ps = ps6(C)
            for kt in range(8):
                q = C.sqb[kt % 2]
                s.act(q.v(), C.xT[kt].v(0, 128, 512 * tc, 512 * tc + 512), AF.Square)
                s.mm(ps.v(), C.ones.v(), q.v(), start=(kt == 0), stop=(kt == 7))
            r = C.rstd2.v(0, 128, 512 * tc, 512 * tc + 512)
            s.act(r, ps.v(), AF.Sqrt, bias=C.epsb.v(), scale=1.0 / D)
            s.op('dve', lambda e, r=r: e.reciprocal(out=r.ap, in_=r.ap), reads=[r], writes=[r])
        for kt in range(8):
            s.stt(C.hT[kt].v(), C.xT[kt].v(), gcols.v(0, 128, kt, kt + 1), C.rstd2.v(), ALU.mult, ALU.mult)
        phase_T2(s, C, W, xb_view, tok0)
        store_xT(s, C, out_view, tok0, NT, xstage)
    s.release(mk)


def fused_U(s, C, xf_view, xb_view, tabs_d, md, sd):
    mk = s.mark()
    phase_U_fourier(s, C, xf_view, xb_view, tabs_d)
    s.release(mk)
    phase_U_s5(s, C, xf_view, xb_view, sd)
    s.release(mk)
    phase_U_mlstm(s, C, xf_view, xb_view, md)
    s.release(mk)


def build_fused(n_layers=2):
    nc = bass.Bass("TRN2", target_bir_lowering=False)
    s = Sched(nc)
    C = Ctx()
    x_d = s.dram('x', [S, D], F32, 'ExternalInput')
    id_d = s.dram('ident', [128, 128], F32, 'ExternalInput')
    out_d = s.dram('out', [S, D], F32, 'ExternalOutput')
    Wl = [{n: s.dram('%s_l%d' % (n, l), sh, F32, 'ExternalInput') for n, sh in W_SHAPES} for l in range(n_layers)]
    tabs_d = {k: s.dram(k, v, F32, 'ExternalInput') for k, v in FT_SHAPES.items()}
    mconst = {k: s.dram(k, MT_SHAPES[k], F32, 'ExternalInput') for k in M_CONST}
    sconst = {k: s.dram(k, ST_SHAPES[k], F32, 'ExternalInput') for k in S_CONST}
    mpar = [{k: s.dram('%s_l%d' % (k, l), [4 * MT_SHAPES[k][0], MT_SHAPES[k][1]], F32, 'ExternalInput') for k in M_PARAM}
            for l in range(n_layers)]
    spar = [{k: s.dram('%s_l%d' % (k, l), [4 * ST_SHAPES[k][0], ST_SHAPES[k][1]], F32, 'ExternalInput') for k in S_PARAM}
            for l in range(n_layers)]
    xf_all = s.dram('xf_all', [16 * XF_ROWS, TL], F32, 'Internal')
    xb_all = s.dram('xb_all', [16 * XB_ROWS, TL], F32, 'Internal')
    xs1 = s.dram('xs1', [S, D], F32, 'Internal')
    common_setup(s, C, id_d)
    C.epsb = s.sb('epsb', [128, 1], F32)
    s.op('dve', lambda e: e.memset(C.epsb.h[:, :], EPS), writes=[C.epsb.v()])
    C.oneb = s.sb('oneb', [128, 1], F32)
    s.op('dve', lambda e: e.memset(C.oneb.h[:, :], 1.0), writes=[C.oneb.v()])
    for l in range(n_layers):
        xsrc = x_d if l == 0 else xs1
        xdst = out_d if l == n_layers - 1 else xs1
        Wf = Wl[l]
        t1cols = [(OFF_Q + 128 * i, 128) for i in range(4)] + [(OFF_K + 128 * i, 128) for i in range(4)] + \
                 [(OFF_V + 128 * i, 128) for i in range(4)] + [(OFF_G16, 16)] + [(OFF_F + 128 * i, 128) for i in range(2)] + \
                 [(OFF_S5 + 128 * i, 128) for i in range(2)]
        t2cols = [(OFF_O + 128 * i, 128) for i in range(4)] + [(OFF_GATE + 1024 * b + 128 * j, 128) for j in range(8) for b in range(3)]
        pw = dict(
            w_in=PW(s, 'pw_in_l%d' % l, Wf['w_in'], 8, t1cols + t2cols),
            w_up_m=PW(s, 'pw_um_l%d' % l, Wf['w_up_m'], 4, [(128 * j, 128) for j in range(8)]),
            w_up_f=PW(s, 'pw_uf_l%d' % l, Wf['w_up_f'], 2, [(128 * j, 128) for j in range(8)]),
            w_glu=PW(s, 'pw_glu_l%d' % l, Wf['w_glu'], 2, [(128 * j, 128) for j in range(16)]),
            w_out=PW(s, 'pw_out_l%d' % l, Wf['w_out'], 8, [(128 * j, 128) for j in range(8)]),
            w_ffn1=PW(s, 'pw_f1_l%d' % l, Wf['w_ffn1'], 8, [(512 * j, 512) for j in range(8)]),
            w_ffn2=PW(s, 'pw_f2_l%d' % l, Wf['w_ffn2'], 32, [(128 * j, 128) for j in range(8)]))
        for i in range(len(t1cols) + 4):
            pw['w_in'].prep(s, i)
        for j in range(8):
            pw['w_up_m'].prep(s, j)
            pw['w_up_f'].prep(s, j)
            pw['w_glu'].prep(s, j)
            pw['w_glu'].prep(s, 8 + j)
            for b in range(3):
                pw['w_in'].prep(s, len(t1cols) + 4 + 3 * j + b)
        for nm in ['w_out', 'w_ffn1', 'w_ffn2']:
            for j in range(8):
                pw[nm].prep(s, j)
        Wp = dict(Wf)
        Wp.update(pw)
        Wl[l] = Wp
        for q in range(4):
            fused_T1(s, C, BlockT(xsrc, [TL * q], TL), Wl[l], BlockT(xf_all, [(dst * 4 + q) * XF_ROWS for dst in range(4)], XF_ROWS))
        for u in range(4):
            md = dict(mconst)
            md.update({k: BlockT(mpar[l][k], [u * MT_SHAPES[k][0]], MT_SHAPES[k][0]) for k in M_PARAM})
            sd = dict(sconst)
            sd.update({k: BlockT(spar[l][k], [u * ST_SHAPES[k][0]], ST_SHAPES[k][0]) for k in S_PARAM})
            fused_U(s, C, BlockT(xf_all, [(u * 4 + src) * XF_ROWS for src in range(4)], XF_ROWS),
                    BlockT(xb_all, [(dst * 4 + u) * XB_ROWS for dst in range(4)], XB_ROWS), tabs_d, md, sd)
        for q in range(4):
            fused_T2(s, C, BlockT(xsrc, [TL * q], TL), Wl[l], BlockT(xb_all, [(q * 4 + src) * XB_ROWS for src in range(4)], XB_ROWS),
                     BlockT(xdst, [TL * q], TL))
    s.finish()
    return nc


def fused_inputs(inp, b, n_layers=2):
    m = dict(x=_c(inp['x'][b]), ident=np.eye(128, dtype=np.float32))
    m.update(fourier_tables())
    mt = mlstm_tables()
    m.update({k: mt[k] for k in M_CONST})
    st = s5_tables()
    m.update({k: st[k] for k in S_CONST})
    for l in range(n_layers):
        for k, v in _layer_weights(inp, l).items():
            m['%s_l%d' % (k, l)] = v
        units = [u_extra_inputs(inp, l, u) for u in range(4)]
        for k in M_PARAM + S_PARAM:
            m['%s_l%d' % (k, l)] = np.ascontiguousarray(np.concatenate([units[u][k] for u in range(4)], 0))
    return m


def kernel(**inputs):
    inp = {k: np.asarray(v) for k, v in inputs.items()}
    nc = build_fused()
    per_batch = [fused_inputs(inp, b) for b in range(2)]
    maps = [per_batch[c // 4] for c in range(8)]
    res = run_bass_kernel_spmd(nc, maps, core_ids=list(range(8)))
    out = np.stack([np.asarray(res.results[0]['out']), np.asarray(res.results[4]['out'])], 0)
    return out.astype(np.float32)
```
